# Optimizing a Trainium2 kernel written in Bass

```python
import math
import jax, jax.numpy as jnp
from jax import lax
import numpy as np

D_MODEL = 2048
BATCH = 2
SEQ = 4096
DEPTH = 1
DEC_BATCH = 128
DEC_SEQ = 8
PAST_LEN = 16384
PAGE_SIZE = 128

N_HEADS = 32
N_KV_HEADS = 8
HEAD_DIM = 64
Q_PER_KV = N_HEADS // N_KV_HEADS
ATTN_WIDTH = N_HEADS * HEAD_DIM
KV_WIDTH = N_KV_HEADS * HEAD_DIM
WINDOW = 128
SSM_EXPAND = 2
SSM_WIDTH = SSM_EXPAND * D_MODEL
SSM_HEAD_DIM = 64
SSM_HEADS = SSM_WIDTH // SSM_HEAD_DIM
SSM_GROUPS = 8
HEADS_PER_GROUP = SSM_HEADS // SSM_GROUPS
D_STATE = 128
CONV_W = 4
CONV_DIM = SSM_WIDTH + 2 * SSM_GROUPS * D_STATE
SSD_CHUNK = 128
NORM_EPS = 1e-6
IN_SIZES = (ATTN_WIDTH, KV_WIDTH, KV_WIDTH, ATTN_WIDTH, CONV_DIM, SSM_WIDTH, SSM_HEADS, D_MODEL, D_MODEL)
IN_DIM = sum(IN_SIZES)

kernel_name = "hybrid_swa_sink_alibi_mamba2_gated_merge_step"


def _offsets(sizes):
    out, acc = [], 0
    for s in sizes[:-1]:
        acc += s
        out.append(acc)
    return out


def rmsnorm(x, g):
    xf = x.astype(jnp.float32)
    xf = xf * lax.rsqrt(jnp.mean(xf * xf, axis=-1, keepdims=True) + NORM_EPS)
    return (xf * g.astype(jnp.float32)).astype(x.dtype)


def alibi_slopes():
    h = jnp.arange(1, N_HEADS + 1, dtype=jnp.float32)
    return jnp.exp2(-8.0 * h / N_HEADS)


def window_attention(q, k, v, q_pos, k_pos, sinks):
    scale = HEAD_DIM ** -0.5
    s = jnp.einsum('bnqkgd,bnskd->bnkgqs', q, k).astype(jnp.float32) * scale
    dist = q_pos[:, :, None] - k_pos[:, None, :]
    valid = (dist >= 0) & (dist < WINDOW) & (k_pos[:, None, :] >= 0)
    slopes = alibi_slopes().reshape(N_KV_HEADS, Q_PER_KV)
    s = s - slopes[None, None, :, :, None, None] * dist.astype(jnp.float32)[None, :, None, None]
    s = jnp.where(valid[None, :, None, None], s, -jnp.inf)
    sink = sinks.astype(jnp.float32).reshape(N_KV_HEADS, Q_PER_KV)[None, None, :, :, None, None]
    m = jnp.maximum(jnp.max(s, axis=-1, keepdims=True), sink)
    p = jnp.exp(s - m)
    denom = jnp.sum(p, axis=-1, keepdims=True) + jnp.exp(sink - m)
    o = jnp.einsum('bnkgqs,bnskd->bnqkgd', (p / denom).astype(v.dtype), v)
    b, n, lq = o.shape[:3]
    return o.reshape(b, n * lq, ATTN_WIDTH)


def attn_prompt(q, k, v, sinks):
    b, L = q.shape[:2]
    nb = L // WINDOW
    qb = q.reshape(b, nb, WINDOW, N_KV_HEADS, Q_PER_KV, HEAD_DIM)
    kb = k.reshape(b, nb, WINDOW, N_KV_HEADS, HEAD_DIM)
    vb = v.reshape(b, nb, WINDOW, N_KV_HEADS, HEAD_DIM)
    pad = ((0, 0), (1, 0), (0, 0), (0, 0), (0, 0))
    kwin = jnp.concatenate([jnp.pad(kb, pad)[:, :-1], kb], axis=2)
    vwin = jnp.concatenate([jnp.pad(vb, pad)[:, :-1], vb], axis=2)
    q_pos = jnp.arange(L, dtype=jnp.int32).reshape(nb, WINDOW)
    k_pos = (jnp.arange(nb, dtype=jnp.int32) * WINDOW - WINDOW)[:, None] + jnp.arange(2 * WINDOW, dtype=jnp.int32)[None]
    return window_attention(qb, kwin, vwin, q_pos, k_pos, sinks)


def attn_sample(q, all_k, all_v, sinks):
    L = q.shape[1]
    past = all_k.shape[1] - L
    q_pos = (PAST_LEN + jnp.arange(L, dtype=jnp.int32))[None]
    k_pos = (PAST_LEN - past + jnp.arange(past + L, dtype=jnp.int32))[None]
    return window_attention(q[:, None], all_k[:, None], all_v[:, None], q_pos, k_pos, sinks)


def causal_conv(xbc, buf, w, bias):
    full = jnp.concatenate([buf, xbc], axis=1)
    y = lax.conv_general_dilated(full, w.astype(full.dtype)[:, None, :], window_strides=(1,),
                                 padding='VALID', dimension_numbers=('NWC', 'WIO', 'NWC'),
                                 feature_group_count=CONV_DIM)
    return jax.nn.silu(y + bias.astype(y.dtype)), full[:, -(CONV_W - 1):]


def ssd_scan(x, dt, a, bm, cm, s0, chunk):
    b, L = x.shape[:2]
    nc = L // chunk
    G, R, P, N = SSM_GROUPS, HEADS_PER_GROUP, SSM_HEAD_DIM, D_STATE
    xr = x.astype(jnp.float32).reshape(b, nc, chunk, G, R, P)
    dtr = dt.reshape(b, nc, chunk, G, R)
    br = bm.astype(jnp.float32).reshape(b, nc, chunk, G, N)
    cr = cm.astype(jnp.float32).reshape(b, nc, chunk, G, N)
    acum = jnp.cumsum(dtr * a.reshape(G, R), axis=2)
    at = jnp.moveaxis(acum, 2, -1)
    causal = jnp.tril(jnp.ones((chunk, chunk), dtype=bool))
    decay = jnp.exp(jnp.where(causal, at[..., :, None] - at[..., None, :], -jnp.inf))
    cb = jnp.einsum('bclgn,bcsgn->bcgls', cr, br)
    y_intra = jnp.einsum('bcgls,bcgrls,bcsgr,bcsgrp->bclgrp', cb, decay, dtr, xr)
    decay_end = jnp.exp(at[..., -1:] - at)
    states = jnp.einsum('bcsgn,bcgrs,bcsgr,bcsgrp->bcgrpn', br, decay_end, dtr, xr)
    chunk_decay = jnp.exp(at[..., -1])

    def step(s, inp):
        st, dec = inp
        return dec[..., None, None] * s + st, s

    s_init = s0.astype(jnp.float32).reshape(b, G, R, P, N)
    s_final, s_in = lax.scan(step, s_init, (jnp.moveaxis(states, 1, 0), jnp.moveaxis(chunk_decay, 1, 0)))
    s_in = jnp.moveaxis(s_in, 0, 1)
    y_inter = jnp.einsum('bclgn,bcgrpn,bcgrl->bclgrp', cr, s_in, jnp.exp(at))
    y = (y_intra + y_inter).reshape(b, L, SSM_HEADS, P)
    return y, s_final.reshape(b, SSM_HEADS, P, N)


def gated_group_rmsnorm(y, z, g):
    b, L, _ = y.shape
    u = (y.astype(jnp.float32) * jax.nn.silu(z.astype(jnp.float32))).reshape(b, L, SSM_GROUPS, -1)
    u = u * lax.rsqrt(jnp.mean(u * u, axis=-1, keepdims=True) + NORM_EPS)
    return (u.reshape(b, L, SSM_WIDTH) * g.astype(jnp.float32)).astype(z.dtype)


def hybrid_layer(x, win_k, win_v, ssm_state, conv_buf, norm_pre, w_in, conv_w, conv_b, dt_bias,
                 a_log, d_skip, ssm_norm, attn_sinks, w_attn_br, w_ssm_br, w_out, norm_post):
    b, L, _ = x.shape
    prompt = win_k is None
    h = rmsnorm(x, norm_pre)
    q, k, v, z_a, xbc, z_s, dt_raw, g_a, g_s = jnp.split(h @ w_in, _offsets(IN_SIZES), axis=-1)
    q = q.reshape(b, L, N_KV_HEADS, Q_PER_KV, HEAD_DIM)
    k = k.reshape(b, L, N_KV_HEADS, HEAD_DIM)
    v = v.reshape(b, L, N_KV_HEADS, HEAD_DIM)
    if prompt:
        attn = attn_prompt(q, k, v, attn_sinks)
        new_k, new_v = k[:, L - WINDOW:], v[:, L - WINDOW:]
        conv_buf = jnp.zeros((b, CONV_W - 1, CONV_DIM), x.dtype)
        ssm_state = jnp.zeros((b, SSM_HEADS, SSM_HEAD_DIM, D_STATE), jnp.float32)
        chunk = SSD_CHUNK
    else:
        all_k = jnp.concatenate([win_k, k], axis=1)
        all_v = jnp.concatenate([win_v, v], axis=1)
        attn = attn_sample(q, all_k, all_v, attn_sinks)
        new_k, new_v = all_k[:, L:], all_v[:, L:]
        chunk = L
    attn_out = attn * jax.nn.silu(z_a)

    xbc_c, new_conv = causal_conv(xbc, conv_buf, conv_w, conv_b)
    xs, bm, cm = jnp.split(xbc_c, _offsets((SSM_WIDTH, SSM_GROUPS * D_STATE, SSM_GROUPS * D_STATE)), axis=-1)
    xs = xs.reshape(b, L, SSM_HEADS, SSM_HEAD_DIM)
    dt = jax.nn.softplus(dt_raw.astype(jnp.float32) + dt_bias.astype(jnp.float32))
    a = -jnp.exp(a_log.astype(jnp.float32))
    y, s_final = ssd_scan(xs, dt, a, bm.reshape(b, L, SSM_GROUPS, D_STATE),
                          cm.reshape(b, L, SSM_GROUPS, D_STATE), ssm_state, chunk)
    y = y + d_skip.astype(jnp.float32)[:, None] * xs.astype(jnp.float32)
    ssm_out = gated_group_rmsnorm(y.reshape(b, L, SSM_WIDTH), z_s, ssm_norm)

    merged = jax.nn.sigmoid(g_a) * (attn_out @ w_attn_br) + jax.nn.sigmoid(g_s) * (ssm_out @ w_ssm_br)
    out = x + rmsnorm(merged @ w_out, norm_post)
    return out, new_k, new_v, s_final.astype(x.dtype), new_conv


def setup_inputs(seed: int = 0) -> dict:
    key = jax.random.key(seed)
    ks = jax.random.split(key, 20)
    f32 = jnp.float32
    nrm = lambda k, shape, s=1.0: jax.random.normal(k, shape, f32) * s
    dt0 = jnp.exp(jax.random.uniform(ks[10], (DEPTH, SSM_HEADS), f32, math.log(1e-3), math.log(1e-1)))
    return {
        "x_prompt": nrm(ks[0], (BATCH, SEQ, D_MODEL)),
        "x_sample": nrm(ks[1], (DEC_BATCH, DEC_SEQ, D_MODEL)),
        "cache_k": nrm(ks[2], (DEPTH, DEC_BATCH, WINDOW, N_KV_HEADS, HEAD_DIM)),
        "cache_v": nrm(ks[3], (DEPTH, DEC_BATCH, WINDOW, N_KV_HEADS, HEAD_DIM)),
        "state_ssm": nrm(ks[4], (DEPTH, DEC_BATCH, SSM_HEADS, SSM_HEAD_DIM, D_STATE), 0.5),
        "state_conv": nrm(ks[5], (DEPTH, DEC_BATCH, CONV_W - 1, CONV_DIM)),
        "norm_pre": 1.0 + nrm(ks[6], (DEPTH, D_MODEL), 0.05),
        "w_in": nrm(ks[7], (DEPTH, D_MODEL, IN_DIM), D_MODEL ** -0.5),
        "conv_w": nrm(ks[8], (DEPTH, CONV_W, CONV_DIM), CONV_W ** -0.5),
        "conv_b": nrm(ks[9], (DEPTH, CONV_DIM), 0.05),
        "dt_bias": dt0 + jnp.log(-jnp.expm1(-dt0)),
        "a_log": jnp.log(jax.random.uniform(ks[11], (DEPTH, SSM_HEADS), f32, 1.0, 16.0)),
        "d_skip": 1.0 + nrm(ks[12], (DEPTH, SSM_HEADS), 0.1),
        "ssm_norm": 1.0 + nrm(ks[13], (DEPTH, SSM_WIDTH), 0.05),
        "attn_sinks": nrm(ks[14], (DEPTH, N_HEADS), 0.5),
        "w_attn_br": nrm(ks[15], (DEPTH, ATTN_WIDTH, D_MODEL), ATTN_WIDTH ** -0.5),
        "w_ssm_br": nrm(ks[16], (DEPTH, SSM_WIDTH, D_MODEL), SSM_WIDTH ** -0.5),
        "w_out": nrm(ks[17], (DEPTH, D_MODEL, D_MODEL), D_MODEL ** -0.5),
        "norm_post": 1.0 + nrm(ks[18], (DEPTH, D_MODEL), 0.05),
    }


def reference(x_prompt, x_sample, cache_k, cache_v, state_ssm, state_conv, norm_pre, w_in, conv_w,
              conv_b, dt_bias, a_log, d_skip, ssm_norm, attn_sinks, w_attn_br, w_ssm_br, w_out, norm_post):
    yp, ys = x_prompt, x_sample
    outs = []
    for layer in range(DEPTH):
        w = (norm_pre[layer], w_in[layer], conv_w[layer], conv_b[layer], dt_bias[layer], a_log[layer],
             d_skip[layer], ssm_norm[layer], attn_sinks[layer], w_attn_br[layer], w_ssm_br[layer],
             w_out[layer], norm_post[layer])
        yp, kp, vp, sp, cp = hybrid_layer(yp, None, None, None, None, *w)
        ys, kq, vq, sq, cq = hybrid_layer(ys, cache_k[layer], cache_v[layer], state_ssm[layer],
                                          state_conv[layer], *w)
        outs.append((kp, vp, sp, cp, kq, vq, sq, cq))
    k_p, v_p, s_p, c_p, k_s, v_s, s_s, c_s = [jnp.stack(arrs, axis=0) for arrs in zip(*outs)]
    return (yp, ys, k_p, v_p, s_p, c_p, k_s, v_s, s_s, c_s)
```

```python
import numpy as np
import concourse.bass as bass
import concourse.mybir as mybir
from concourse.bass_utils import run_bass_kernel_spmd

F32 = mybir.dt.float32
BF16 = mybir.dt.bfloat16
AF = mybir.ActivationFunctionType
ALU = mybir.AluOpType
AX = mybir.AxisListType

NCORES = 8
D = 2048
NT = 10
TOK = NT * 128
NPT = 1152
IN_DIM = 19520
Q0, K0, V0, ZA0, XS0, B0, C0, ZS0, DT0, GA0, GS0 = 0, 2048, 2560, 3072, 5120, 9216, 10240, 11264, 15360, 15424, 17472
EPS = 1e-6
NEG = -30000.0
EPOCH = 60000


class _Stop(Exception):
    pass


class Res:
    __slots__ = ("name", "w", "r", "excl")

    def __init__(self, name="", excl=False):
        self.name = name
        self.w = None
        self.r = {}
        self.excl = excl


class Tok:
    __slots__ = ("key", "val", "clock")

    def __init__(self, key, val, clock):
        self.key, self.val, self.clock = key, val, clock


class DSem:
    def __init__(self, name, inc=16):
        self.key = ("d", name)
        self.count = 0
        self.inc = inc


class Prog:
    ENG = ("pe", "act", "dve", "pool", "sp")

    def __init__(self):
        self.streams = {e: [] for e in self.ENG}
        self.seq = {e: 0 for e in self.ENG}
        self.known = {e: {} for e in self.ENG}
        self.keys = {}
        self.out_toks = []
        self.all_dma = []

    def _deps(self, eng, r, w):
        toks = []
        for res in r:
            if res.w is not None:
                toks.append(res.w)
            if res.excl:
                for k, t in res.r.items():
                    if k[0] != eng:
                        toks.append(t)
        for res in w:
            if res.w is not None:
                toks.append(res.w)
            toks.extend(res.r.values())
        return toks

    def _waits(self, eng, toks):
        known = self.known[eng]
        need = {}
        for t in toks:
            if t.key[0] == "pe" and eng == "pe":
                continue
            if known.get(t.key, 0) >= t.val:
                continue
            if need.get(t.key, 0) < t.val:
                need[t.key] = t.val
        for t in toks:
            for k, v in t.clock.items():
                if known.get(k, 0) < v:
                    known[k] = v
        return list(need.items())

    def add(self, eng, fn, r=(), w=(), dsem=None, out=False):
        toks = self._deps(eng, r, w)
        waits = self._waits(eng, toks)
        if dsem is not None:
            dsem.count += dsem.inc
            key, val, inc = dsem.key, dsem.count, dsem.inc
        else:
            s = self.seq[eng]
            self.seq[eng] = s + 1
            key, val, inc = (eng, s // EPOCH), s % EPOCH + 1, 1
        self.keys[key] = True
        clock = dict(self.known[eng])
        clock[key] = val
        tok = Tok(key, val, clock)
        if eng == "pe" and dsem is None:
            self.known[eng][key] = val
        for res in w:
            res.w = tok
            res.r = {}
        for res in r:
            old = res.r.get(key)
            if old is None or old.val < val:
                res.r[key] = tok
        self.streams[eng].append((waits, fn, (key, inc)))
        if dsem is not None:
            self.all_dma.append(tok)
        if out:
            self.out_toks.append(tok)
        return tok

    def wait_tokens(self, eng, toks):
        waits = self._waits(eng, toks)
        if waits:
            self.streams[eng].append((waits, None, None))

    def barrier(self):
        last = []
        for e in self.ENG:
            if e == "sp":
                continue
            s = self.seq[e]
            if s > 0:
                s -= 1
                last.append(Tok((e, s // EPOCH), s % EPOCH + 1, {}))
        toks = last + self.all_dma
        self.all_dma = []
        for e in self.ENG:
            self.wait_tokens(e, toks)

    def emit(self, nc, sems):
        semmap = {}
        keys = list(self.keys.keys())
        assert len(keys) <= len(sems), (len(keys), len(sems))
        for k, s in zip(keys, sems):
            semmap[k] = s
        engobj = {"pe": "tensor", "act": "scalar", "dve": "vector", "pool": "gpsimd", "sp": "sync"}
        with nc.Block() as block:
            def mk(ename):
                def body(e):
                    for waits, fn, inc in self.streams[ename]:
                        for k, v in waits:
                            e.wait_ge(semmap[k], v)
                        if fn is not None:
                            ins = fn(e)
                            ins.then_inc(semmap[inc[0]], inc[1])
                return body
            block.tensor(mk("pe"))
            block.scalar(mk("act"))
            block.vector(mk("dve"))
            block.gpsimd(mk("pool"))
            block.sync(mk("sp"))


CST_LAYOUT = {}


def _build_consts():
    cols = []
    off = 0

    def put(name, arr):
        nonlocal off
        arr = np.asarray(arr, np.float32)
        assert arr.shape[0] == 128
        CST_LAYOUT[name] = (off, arr.shape[1])
        cols.append(arr)
        off += arr.shape[1]

    i = np.arange(128)
    put("ident", np.eye(128))
    put("tri", (i[:, None] <= i[None, :]).astype(np.float32))
    same = (i[:, None] // 8 == i[None, :] // 8)
    put("tris", ((i[:, None] <= i[None, :]) & same).astype(np.float32))
    put("mgt", (i[:, None] > i[None, :]).astype(np.float32))
    put("same", same.astype(np.float32))
    put("ones", np.ones((128, 128)))
    put("m16", (i[:, None] // 8 == np.arange(16)[None, :]).astype(np.float32))
    return np.concatenate(cols, axis=1)


def _alibi_tables():
    slopes = np.exp2(-8.0 * np.arange(1, 33, dtype=np.float32) / 32.0).astype(np.float32)
    ql = np.arange(128)[:, None]
    kl = np.arange(256)[None, :]
    dist = (128 + ql) - kl
    valid = (dist >= 0) & (dist < 128)
    bp = np.where(valid[None], -slopes[:, None, None] * dist[None].astype(np.float32), NEG).astype(np.float32)
    t = (np.arange(128) % 8)[:, None]
    s = (np.arange(128) // 8)[:, None]
    j = np.arange(128)[None, :]
    dist_c = 128 + t - j
    valid_c = dist_c < 128
    t2 = (np.arange(128) % 8)[None, :]
    s2 = (np.arange(128) // 8)[None, :]
    dist_n = t - t2
    valid_n = (s == s2) & (dist_n >= 0)
    dist_s = np.concatenate([dist_c, dist_n], axis=1)
    valid_s = np.concatenate([valid_c, valid_n], axis=1)
    bs = np.where(valid_s[None], -slopes[:, None, None] * dist_s[None].astype(np.float32), NEG).astype(np.float32)
    return bp, bs


class Arena:
    def __init__(self, t_f32, nbytes):
        self.t = t_f32
        self.n = nbytes
        self.off = 0
        self.marks = []

    def push(self):
        self.marks.append(self.off)

    def pop(self):
        self.off = self.marks.pop()

    def alloc(self, shape, dt):
        esz = 4 if dt == F32 else 2
        free = 1
        for s in shape[1:]:
            free *= s
        nb = (free * esz + 63) // 64 * 64
        assert self.off + nb <= self.n, ("SBUF arena overflow", self.off, nb, self.n)
        a = self.t[0:shape[0], self.off // 4:(self.off + nb) // 4]
        self.off += nb
        if dt != F32:
            a = a.bitcast(dt)
        a = a[:, 0:free]
        if len(shape) == 3:
            a = a.rearrange("p (a b) -> p a b", a=shape[1])
        elif len(shape) == 4:
            a = a.rearrange("p (a b c) -> p a b c", a=shape[1], b=shape[2])
        return a


def build_nc(stop_after=None, debug=False, nocc=False, prefix=True):
    nc = bass.Bass("TRN2", target_bir_lowering=False)
    cst_np = _build_consts()
    NCST = cst_np.shape[1]

    def din(name, shape):
        return nc.dram_tensor(name, list(shape), F32, kind="ExternalInput").ap()

    def dout(name, shape):
        return nc.dram_tensor(name, list(shape), F32, kind="ExternalOutput").ap()

    xin = din("xin", [TOK, D])
    w_in = din("w_in", [D, IN_DIM])
    w_ab = din("w_attn_br", [D, D])
    w_sb = din("w_ssm_br", [2 * D, D])
    w_o = din("w_out", [D, D])
    cache_k = din("cache_k", [16, 128, 512])
    cache_v = din("cache_v", [16, 128, 512])
    state_ssm = din("state_ssm", [16, 4096, 128])
    state_conv = din("state_conv", [48, 6144])
    norm_pre = din("norm_pre", [D])
    conv_w = din("conv_w", [4, 6144])
    conv_b = din("conv_b", [6144])
    dt_bias = din("dt_bias", [64])
    a_log = din("a_log", [64])
    d_skip = din("d_skip", [64])
    ssm_norm = din("ssm_norm", [4096])
    sinks = din("attn_sinks", [32])
    norm_post = din("norm_post", [D])
    cst = din("cst", [128, NCST])
    bias_p = din("bias_p", [32, 128, 256])
    bias_s = din("bias_s", [32, 128, 256])
    halo_mask = din("halo_mask", [128, 256])
    alpha = din("alpha", [8])
    xprev = din("xprev", [3072, D])
    pflag = din("pflag", [24])

    y_out = dout("y", [NPT, D])
    kp_out = dout("kp", [128, 512])
    vp_out = dout("vp", [128, 512])
    sp_out = dout("sp", [4096, 128])
    cp_out = dout("cp", [3, 6144])
    ks_out = dout("ks", [16, 128, 512])
    vs_out = dout("vs", [16, 128, 512])
    ss_out = dout("ss", [16, 4096, 128])
    cs_out = dout("cs", [48, 6144])

    if debug:
        AT_d = nc.dram_tensor("AT_d", [16, 128, NPT], BF16, kind="ExternalOutput").ap()
        ST_d = nc.dram_tensor("ST_d", [32, 128, NPT], BF16, kind="ExternalOutput").ap()
    else:
        AT_d = nc.dram_tensor("AT_d", [16, 128, NPT], BF16).ap()
        ST_d = nc.dram_tensor("ST_d", [32, 128, NPT], BF16).ap()
    Sst_d = [nc.dram_tensor(f"Sst_d{g}", [128, 512], F32) for g in range(8)]
    ag_in = [nc.dram_tensor(f"ag_in{g}", [128, 520], F32) for g in range(8)]
    ag_out = [nc.dram_tensor(f"ag_out{g}", [NCORES * 128, 520], F32) for g in range(8)]

    P = Prog()
    ARENA_BYTES = 207 * 1024

    from contextlib import ExitStack
    with ExitStack() as estack:
        arena_t = estack.enter_context(nc.sbuf_tensor("arena", [128, ARENA_BYTES // 4], F32))
        psum = [estack.enter_context(nc.psum_tensor(f"ps{i}", [128, 512], F32)) for i in range(8)]
        A = Arena(arena_t, ARENA_BYTES)
        PS = [p[:] for p in psum]
        RPS = [Res(f"ps{i}", excl=True) for i in range(8)]

        def PSb(i):
            return PS[i].bitcast(BF16)

        def dma(q, out, in_, r=(), w=(), dsem=None, outflag=False, slow=False):
            if slow:
                return P.add(q, lambda e: e.dma_start(out=out, in_=in_, allow_slow_non_contiguous=True), r=r, w=w, dsem=dsem, out=outflag)
            return P.add(q, lambda e: e.dma_start(out=out, in_=in_), r=r, w=w, dsem=dsem, out=outflag)

        def mm(out, lhsT, rhs, start=True, stop=True, r=(), w=()):
            return P.add("pe", lambda e: e.matmul(out, lhsT=lhsT, rhs=rhs, start=start, stop=stop), r=r, w=w)

        def tr(out, in_, ident, r=(), w=()):
            return P.add("pe", lambda e: e.transpose(out=out, in_=in_, identity=ident), r=r, w=w)

        def act(out, in_, func, r=(), w=(), bias=None, scale=None, accum_out=None):
            kw = {}
            if bias is not None:
                kw["bias"] = bias
            if scale is not None:
                kw["scale"] = scale
            if accum_out is not None:
                kw["accum_out"] = accum_out
            return P.add("act", lambda e: e.activation(out=out, in_=in_, func=func, **kw), r=r, w=w)

        def tt(eng, out, in0, in1, op, r=(), w=()):
            return P.add(eng, lambda e: e.tensor_tensor(out=out, in0=in0, in1=in1, op=op), r=r, w=w)

        def ts(eng, out, in0, s1, s2, op0, op1=None, r=(), w=(), accum_out=None):
            kw = {}
            if op1 is not None:
                kw["op1"] = op1
            if accum_out is not None:
                kw["accum_out"] = accum_out
            return P.add(eng, lambda e: e.tensor_scalar(out=out, in0=in0, scalar1=s1, scalar2=s2, op0=op0, **kw), r=r, w=w)

        def stt(eng, out, in0, scalar, in1, op0, op1, r=(), w=()):
            return P.add(eng, lambda e: e.scalar_tensor_tensor(out=out, in0=in0, scalar=scalar, in1=in1, op0=op0, op1=op1), r=r, w=w)

        def cp(eng, out, in_, r=(), w=()):
            if eng == "act":
                return P.add("act", lambda e: e.copy(out=out, in_=in_), r=r, w=w)
            return P.add(eng, lambda e: e.tensor_copy(out=out, in_=in_), r=r, w=w)

        def run_interleaved(gens, width):
            it_ = iter(gens)
            active = []
            while True:
                while len(active) < width:
                    try:
                        active.append(next(it_))
                    except StopIteration:
                        break
                if not active:
                    break
                for g_ in list(active):
                    try:
                        next(g_)
                    except StopIteration:
                        active.remove(g_)

        def memset(eng, ap, val, w=()):
            return P.add(eng, lambda e: e.memset(ap, val), w=w)

        R_const = Res("const")
        ds_const = DSem("const")
        cstt = A.alloc([128, NCST], F32)
        dma("sp", cstt, cst[:, :], dsem=ds_const)

        def C(name):
            o, n = CST_LAYOUT[name]
            return cstt[:, o:o + n]
        ident_f, TRI, TRIS, MGT, SAME, ONES, M16 = C("ident"), C("tri"), C("tris"), C("mgt"), C("same"), C("ones"), C("m16")
        ident_b = A.alloc([128, 128], BF16)
        o_id = CST_LAYOUT["ident"][0]
        ds_idb = DSem("idb")
        R_idb = Res("idb")
        dma("pool", ident_b, cst[:, o_id:o_id + 128], w=[R_idb], dsem=ds_idb)
        npre_pk = A.alloc([128, 16], F32)
        cw_pk = A.alloc([128, 48, 4], F32)
        cb_pk = A.alloc([128, 48], F32)
        ssn_pk = A.alloc([128, 32], F32)
        npost_pk = None
        pk_srcs = [(norm_pre, 16), (conv_b, 48), (ssm_norm, 32), (conv_w[0], 48), (conv_w[1], 48), (conv_w[2], 48), (conv_w[3], 48)]
        dtb_bc = A.alloc([128, 64], F32)
        dma("sp", dtb_bc, dt_bias.partition_broadcast(128), dsem=ds_const)
        alog_bc = A.alloc([128, 64], F32)
        dma("sp", alog_bc, a_log.partition_broadcast(128), dsem=ds_const)
        dsk_bc = A.alloc([128, 64], F32)
        dma("sp", dsk_bc, d_skip.partition_broadcast(128), dsem=ds_const)
        sink_bc = A.alloc([128, 32], F32)
        dma("sp", sink_bc, sinks.partition_broadcast(128), dsem=ds_const)
        alpha_bc = A.alloc([128, 8], F32)
        dma("sp", alpha_bc, alpha.partition_broadcast(128), dsem=ds_const)
        hmask = A.alloc([128, 256], F32)
        dma("sp", hmask, halo_mask[:, :], dsem=ds_const)
        R_const.w = Tok(ds_const.key, ds_const.count, {ds_const.key: ds_const.count})
        RC = [R_const]
        A.push()
        pk_tmp = A.alloc([128, 6, 128], F32)
        R_pk = Res("pk")
        ds_pk = DSem("pk")
        pk_dst = [npre_pk, cb_pk, ssn_pk] + [cw_pk[:, :, j] for j in range(4)]
        for i, ((src, nb), dst) in enumerate(zip(pk_srcs, pk_dst)):
            slot = i % 6
            if i == 6:
                P.barrier()
            dma("sp", pk_tmp[0:nb, slot, :], src.rearrange("(b p) -> b p", p=128), w=[R_pk], dsem=ds_pk)
            mm(PS[0][:, 0:nb], pk_tmp[0:nb, slot, :], ident_f[0:nb, 0:nb], r=[R_pk] + RC, w=[RPS[0]])
            cp("dve", dst, PS[0][:, 0:nb], r=[RPS[0]], w=[R_pk])
        RC = [R_const, R_pk, R_idb]
        A.pop()
        P.barrier()

        NPF = 24
        ibank = {"b": 0}

        def ipbank():
            b = ibank["b"]
            ibank["b"] = 1 - b
            return b

        def norm_tile(src_rows, xb, rx, dsx, dst, Rdst, stcol, Rst, junk_, Rjunk):
            dma("sp", xb, src_rows, w=[rx], dsem=dsx)
            act(junk_, xb, AF.Square, r=[rx], w=[Rjunk, Rst], accum_out=stcol)
            ts("dve", stcol, stcol, 1.0 / D, EPS, ALU.mult, ALU.add, r=[Rst], w=[Rst])
            act(stcol, stcol, AF.Sqrt, r=[Rst], w=[Rst])
            P.add("dve", lambda e, o=stcol: e.reciprocal(out=o, in_=o), r=[Rst], w=[Rst])
            tt("pool", xb, xb, stcol.broadcast_to([128, D]), ALU.mult, r=[rx, Rst], w=[rx])
            for b in range(4):
                bank = b % 2
                for q in range(4):
                    kb = b * 4 + q
                    mm(PS[bank][:, q * 128:(q + 1) * 128], xb[:, kb * 128:(kb + 1) * 128], ident_f,
                       r=[rx] + RC, w=[RPS[bank]])
                src = PS[bank].rearrange("p (a b) -> p a b", a=4)
                sc_ = npre_pk[:, b * 4:(b + 1) * 4].unsqueeze(2).broadcast_to([128, 4, 128])
                tt("dve", dst(b), src, sc_, ALU.mult, r=[RPS[bank]] + RC, w=[Rdst])

        R_SstD = [Res(f"SstD{g}") for g in range(8)]
        if prefix:
            A.push()
            hTp = A.alloc([128, 16, NPF * 128], BF16)
            R_hTp = [Res(f"hTp{t}") for t in range(NPF)]
            pw = [A.alloc([128, 16, 512], BF16) for _ in range(2)]
            R_pw, ds_pw = [Res("pw0"), Res("pw1")], [DSem("pw0"), DSem("pw1")]
            pwn = {"n": 0}

            def load_p(src, n):
                i = pwn["n"] % 2
                pwn["n"] += 1
                dma("pool", pw[i][:, :, 0:n], src.rearrange("(k p) n -> p k n", p=128), w=[R_pw[i]], dsem=ds_pw[i])
                return pw[i], R_pw[i]

            pxb = [A.alloc([128, D], F32) for _ in range(2)]
            R_px, ds_px = [Res("px0"), Res("px1")], [DSem("px0"), DSem("px1")]
            pjunk = A.alloc([128, D], BF16)
            R_pj = Res("pjunk")
            pst = A.alloc([128, NPF], F32)
            R_pst = [Res(f"pst{t}") for t in range(NPF)]
            pflag_bc = A.alloc([128, NPF], F32)
            R_pf, ds_pf = Res("pflag"), DSem("pflag")
            dma("sp", pflag_bc, pflag.partition_broadcast(128), w=[R_pf], dsem=ds_pf)
            dtdP = A.alloc([128, NPF, 64], F32)
            eaeP = A.alloc([128, NPF, 64], F32)
            R_pdt = [Res(f"pdt{t}") for t in range(NPF)]
            pA_bc = A.alloc([128, 64], F32)
            R_pA = Res("pA")
            ptmp = A.alloc([128, 8, 64], F32)
            R_pt = Res("ptmp")
            for t in range(NPF):
                norm_tile(xprev[t * 128:(t + 1) * 128, :], pxb[t % 2], R_px[t % 2], ds_px[t % 2],
                          lambda b, t=t: hTp[:, b * 4:(b + 1) * 4, t * 128:(t + 1) * 128], R_hTp[t],
                          pst[:, t:t + 1], R_pst[t], pjunk, R_pj)
            act(pA_bc, alog_bc, AF.Exp, r=RC, w=[R_pA])
            ts("dve", pA_bc, pA_bc, -1.0, None, ALU.mult, r=[R_pA], w=[R_pA])
            wtd, Rwd = load_p(w_in[:, DT0:DT0 + 64], 64)
            for t in range(NPF):
                bank = ipbank()
                for k in range(16):
                    mm(PS[bank][:, 0:64], hTp[:, k, t * 128:(t + 1) * 128], wtd[:, k, 0:64],
                       start=(k == 0), stop=(k == 15), r=[Rwd, R_hTp[t]], w=[RPS[bank]])
                xx, ax, ee, ll, tmp, dtp, adtp, at_ = [ptmp[:, n, :] for n in range(8)]
                tt("dve", xx, PS[bank][:, 0:64], dtb_bc, ALU.add, r=[RPS[bank]] + RC, w=[R_pt])
                act(ax, xx, AF.Abs, r=[R_pt], w=[R_pt])
                act(ee, ax, AF.Exp, scale=-1.0, r=[R_pt], w=[R_pt])
                ts("dve", ee, ee, 1.0, None, ALU.add, r=[R_pt], w=[R_pt])
                act(ll, ee, AF.Ln, r=[R_pt], w=[R_pt])
                stt("dve", dtp, xx, 0.0, ll, ALU.max, ALU.add, r=[R_pt], w=[R_pt])
                ts("dve", dtp, dtp, pflag_bc[:, t:t + 1], None, ALU.mult, r=[R_pt, R_pf], w=[R_pt])
                tt("dve", adtp, dtp, pA_bc, ALU.mult, r=[R_pt, R_pA], w=[R_pt])
                b2 = ipbank()
                mm(PS[b2][:, 0:64], TRI, adtp, r=[R_pt] + RC, w=[RPS[b2]])
                mm(PS[b2][:, 64:128], ONES, adtp, r=[R_pt] + RC, w=[RPS[b2]])
                cp("dve", at_, PS[b2][:, 0:64], r=[RPS[b2]], w=[R_pt])
                tt("dve", tmp, PS[b2][:, 64:128], at_, ALU.subtract, r=[RPS[b2], R_pt], w=[R_pt])
                act(tmp, tmp, AF.Exp, r=[R_pt], w=[R_pt])
                tt("dve", dtdP[:, t, :], dtp, tmp, ALU.mult, r=[R_pt], w=[R_pdt[t]])
                act(eaeP[:, t, :], PS[b2][:, 64:128], AF.Exp, r=[RPS[b2]], w=[R_pdt[t]])
            pxp = A.alloc([128, 5, 515], F32)
            R_pxp = [Res(f"pxp{b}") for b in range(5)]
            pacc2 = [A.alloc([128, 512], F32) for _ in range(5)]
            R_pacc2 = [Res(f"pacc{i}") for i in range(5)]
            pcv2 = [A.alloc([128, 5, 512], BF16) for _ in range(2)]
            R_pcv2 = [[Res(f"pcv{i}_{b}") for b in range(5)] for i in range(2)]
            pxt = [A.alloc([128, 640], BF16) for _ in range(2)]
            R_pxt = [Res("pxt0"), Res("pxt1")]
            pxd = [A.alloc([128, 512], BF16) for _ in range(2)]
            R_pxd = [Res("pxd0"), Res("pxd1")]
            pST = [A.alloc([128, 512], F32)] * 2
            R_pST, ds_pST = [Res("pST0")] * 2, [DSem("pST0")] * 2
            pc = {"x": 0}
            for g in range(8):
                wxa, Rxa = load_p(w_in[:, XS0 + 512 * g:XS0 + 512 * (g + 1)], 512)
                wxb, Rxb = load_p(w_in[:, B0 + 128 * g:B0 + 128 * (g + 1)], 128)
                STp, RSTp = pST[g % 2], R_pST[g % 2]
                memset("pool", STp, 0.0, w=[RSTp])
                for blk in range(5):
                    memset("pool", pxp[:, blk, 0:3], 0.0, w=[R_pxp[blk]])
                blocks_done = {}
                blk_active = {}
                free_pacc = [0, 1]
                free_tail = [0, 1]
                tails_tr = {}
                upd_done = {-1: True}

                def blk_gen(s, blk):
                    tiles = [R_hTp[t] for t in range(4 * s, 4 * s + 4)]
                    wsl, Rws, c0 = (wxa, Rxa, blk * 128) if blk < 4 else (wxb, Rxb, 0)
                    cbi = (4 * g + blk) if blk < 4 else (32 + g)
                    while blk_active.get(blk):
                        yield
                    blk_active[blk] = True
                    pslot = blk
                    pacc_, R_pacc_ = pacc2[pslot], R_pacc2[pslot]
                    pcv_, R_pcv_ = pcv2[s % 2], R_pcv2[s % 2]
                    bank = (0, 1, 3, 4, 5)[blk]
                    for k in range(16):
                        mm(PS[bank], wsl[:, k, c0:c0 + 128], hTp[:, k, s * 512:(s + 1) * 512],
                           start=(k == 0), stop=(k == 15), r=[Rws] + tiles, w=[RPS[bank]])
                    xp = pxp[:, blk, :]
                    cp("act", xp[:, 3:515], PS[bank], r=[RPS[bank]], w=[R_pxp[blk]])
                    yield
                    ts("dve", pacc_, xp[:, 0:512], cw_pk[:, cbi, 0:1], None, ALU.mult, r=[R_pxp[blk]] + RC, w=[R_pacc_])
                    yield
                    for j in range(1, 4):
                        stt("dve", pacc_, xp[:, j:j + 512], cw_pk[:, cbi, j:j + 1], pacc_, ALU.mult, ALU.add,
                            r=[R_pxp[blk], R_pacc_] + RC, w=[R_pacc_])
                        yield
                    while s >= 2 and tails_tr.get(s - 2, 0) < 4:
                        yield
                    act(pcv_[:, blk, :], pacc_, AF.Silu, bias=cb_pk[:, cbi:cbi + 1], r=[R_pacc_] + RC, w=[R_pcv_[blk]])
                    cp("pool", xp[:, 0:3], xp[:, 512:515], r=[R_pxp[blk]], w=[R_pxp[blk]])
                    blocks_done[s] = blocks_done.get(s, 0) + 1
                    blk_active[blk] = False

                def tail_gen(s, q):
                    t = 4 * s + q
                    pcv_, R_pcv_ = pcv2[s % 2], R_pcv2[s % 2]
                    while blocks_done.get(s, 0) < 5 or not free_tail:
                        yield
                    xi = free_tail.pop()
                    bT = 7 if xi == 0 else 2
                    for blk in range(5):
                        tr(PSb(bT)[:, blk * 128:(blk + 1) * 128], pcv_[:, blk, q * 128:(q + 1) * 128], ident_b,
                           r=[R_pcv_[blk]] + RC, w=[RPS[bT]])
                    tails_tr[s] = tails_tr.get(s, 0) + 1
                    cp("act", pxt[xi], PSb(bT)[:, 0:640], r=[RPS[bT]], w=[R_pxt[xi]])
                    yield
                    tt("pool", pxd[xi].rearrange("p (h d) -> p h d", h=8), pxt[xi][:, 0:512].rearrange("p (h d) -> p h d", h=8),
                       dtdP[:, t, 8 * g:8 * g + 8].unsqueeze(2).broadcast_to([128, 8, 64]), ALU.mult,
                       r=[R_pxt[xi], R_pdt[t]], w=[R_pxd[xi]])
                    yield
                    while not upd_done.get(t - 1):
                        yield
                    mm(PS[6], pxt[xi][:, 512:640], pxd[xi], r=[R_pxt[xi], R_pxd[xi]], w=[RPS[6]])
                    tt("dve", STp.rearrange("p (h d) -> p h d", h=8), STp.rearrange("p (h d) -> p h d", h=8),
                       eaeP[:, t, 8 * g:8 * g + 8].unsqueeze(2).broadcast_to([128, 8, 64]), ALU.mult,
                       r=[RSTp, R_pdt[t]], w=[RSTp])
                    tt("dve", STp, STp, PS[6], ALU.add, r=[RSTp, RPS[6]], w=[RSTp])
                    upd_done[t] = True
                    free_tail.append(xi)

                def all_gens():
                    for s in range(NPF // 4):
                        for blk in range(5):
                            yield blk_gen(s, blk)
                        for q in range(4):
                            yield tail_gen(s, q)

                run_interleaved(all_gens(), 6)
                dma("sp", Sst_d[g].ap(), STp, r=[RSTp], w=[R_SstD[g]], dsem=ds_pST[g % 2])
            A.pop()
            P.barrier()

        hT = A.alloc([128, 16, TOK], BF16)
        R_hT = [Res(f"hT{t}") for t in range(NT)]

        NSLOT = 2
        wslot = [A.alloc([128, 16, 512], BF16) for _ in range(NSLOT)]
        R_w = [Res(f"w{i}") for i in range(NSLOT)]
        ds_w = [DSem(f"w{i}") for i in range(NSLOT)]
        wstate = {"n": 0}

        def load_w(segs):
            i = wstate["n"] % NSLOT
            wstate["n"] += 1
            for k, (src, off) in enumerate(segs):
                n = src.shape[1]
                dma("pool", wslot[i][:, :, off:off + n], src.rearrange("(k p) n -> p k n", p=128),
                    w=[R_w[i]] if k == 0 else [], dsem=ds_w[i])
                if k > 0:
                    R_w[i].w = P.all_dma[-1]
            return wslot[i], R_w[i]

        def CK(name):
            if stop_after == name:
                raise _Stop()

        try:
            A.push()
            xbuf = [A.alloc([128, D], F32) for _ in range(2)]
            R_x = [Res("x0"), Res("x1")]
            ds_x = [DSem("x0"), DSem("x1")]
            junk = A.alloc([128, D], F32)
            R_junk = Res("junk")
            ssq = A.alloc([128, NT], F32)
            rstd = A.alloc([128, NT], F32)
            R_st = [Res(f"st{t}") for t in range(NT)]
            for t in range(NT):
                xb, rx = xbuf[t % 2], R_x[t % 2]
                dma("sp", xb, xin[t * 128:(t + 1) * 128, :], w=[rx], dsem=ds_x[t % 2])
                act(junk, xb, AF.Square, r=[rx], w=[R_junk, R_st[t]], accum_out=ssq[:, t:t + 1])
                ts("dve", rstd[:, t:t + 1], ssq[:, t:t + 1], 1.0 / D, EPS, ALU.mult, ALU.add, r=[R_st[t]], w=[R_st[t]])
                act(rstd[:, t:t + 1], rstd[:, t:t + 1], AF.Sqrt, r=[R_st[t]], w=[R_st[t]])
                P.add("dve", lambda e, o=rstd[:, t:t + 1]: e.reciprocal(out=o, in_=o), r=[R_st[t]], w=[R_st[t]])
                tt("pool", xb, xb, rstd[:, t:t + 1].broadcast_to([128, D]), ALU.mult, r=[rx, R_st[t]], w=[rx])
                for b in range(4):
                    bank = b % 2
                    for q in range(4):
                        kb = b * 4 + q
                        mm(PS[bank][:, q * 128:(q + 1) * 128], xb[:, kb * 128:(kb + 1) * 128], ident_f,
                           r=[rx] + RC, w=[RPS[bank]])
                    eng = "dve" if b % 2 == 0 else "pool"
                    src = PS[bank].rearrange("p (a b) -> p a b", a=4)
                    dst = hT[:, b * 4:(b + 1) * 4, t * 128:(t + 1) * 128]
                    sc = npre_pk[:, b * 4:(b + 1) * 4].unsqueeze(2).broadcast_to([128, 4, 128])
                    tt("dve", dst, src, sc, ALU.mult, r=[RPS[bank]] + RC, w=[R_hT[t]])
            A.pop()
            P.barrier()
            CK("p0")
            A.push()
            qT = A.alloc([128, 2, NPT], BF16)
            R_qT = Res("qT")
            kT = A.alloc([128, TOK], BF16)
            R_kT = Res("kT")
            vb = A.alloc([128, NT, 64], BF16)
            R_vb = [Res(f"vb{t}") for t in range(NT)]
            sz = A.alloc([128, NT, 256], BF16)
            R_sz = [Res(f"sz{t}") for t in range(NT)]
            kvnew = A.alloc([128, 2, 2, 512], F32)
            R_kvnew = Res("kvnew")
            biasP = A.alloc([128, 4, 256], F32)
            biasS = A.alloc([128, 4, 256], F32)
            R_bias = Res("bias")
            ds_bias = DSem("bias")
            Sb = [A.alloc([128, 4, 256], F32) for _ in range(2)]
            Pb = [A.alloc([128, 4, 256], BF16) for _ in range(2)]
            PTs = [A.alloc([128, 8, 128], BF16) for _ in range(2)]
            stat = [A.alloc([128, 8, 4], F32) for _ in range(2)]
            An = [A.alloc([128, 256], F32) for _ in range(2)]
            Ag = [A.alloc([128, 256], BF16) for _ in range(2)]
            R_it = [[Res(f"it{i}_{n}") for n in range(8)] for i in range(2)]
            ATs = A.alloc([128, 2, NPT], BF16)
            R_ATs = Res("ATs")
            ds_AT = DSem("AT")
            R_ATd = Res("ATd")
            kc = A.alloc([128, 16, 128], BF16)
            vc = A.alloc([128, 16, 64], BF16)
            R_kc, R_vc = Res("kc"), Res("vc")
            ds_kc = DSem("kc")
            ds_vc = DSem("vc")
            KcT = A.alloc([128, 16, 128], BF16)
            R_KcT = Res("KcT")
            Zq = [A.alloc([128, 16, 248], BF16) for _ in range(2)]
            R_Zq = [Res("Zq0"), Res("Zq1")]
            ZP = A.alloc([128, 2, 16, 248], BF16)
            R_ZP = Res("ZP")
            memset("pool", Zq[0], 0.0, w=[R_Zq[0]])
            memset("pool", Zq[1], 0.0, w=[R_Zq[1]])
            memset("pool", ZP, 0.0, w=[R_ZP])
            itc = {"n": 0, "bank": 0}

            for g in range(8):
                wt, Rw = load_w([(w_in[:, Q0 + 256 * g:Q0 + 256 * (g + 1)], 0),
                                 (w_in[:, K0 + 64 * g:K0 + 64 * (g + 1)], 256),
                                 (w_in[:, K0 + 64 * g:K0 + 64 * (g + 1)], 320),
                                 (w_in[:, V0 + 64 * g:V0 + 64 * (g + 1)], 384)])
                wt2, Rw2 = load_w([(w_in[:, ZA0 + 256 * g:ZA0 + 256 * (g + 1)], 0)])
                dma("sp", biasP, bias_p[4 * g:4 * g + 4].rearrange("h p k -> p h k"), w=[R_bias], dsem=ds_bias)
                dma("sp", biasS, bias_s[4 * g:4 * g + 4].rearrange("h p k -> p h k"), dsem=ds_bias)
                R_bias.w = P.all_dma[-1]
                dma("pool", kc[:, :, 0:64], cache_k[:, :, 64 * g:64 * (g + 1)].rearrange("s k d -> k s d"), w=[R_kc], dsem=ds_kc)
                dma("pool", kc[:, :, 64:128], cache_k[:, :, 64 * g:64 * (g + 1)].rearrange("s k d -> k s d"), dsem=ds_kc)
                R_kc.w = P.all_dma[-1]
                dma("pool", vc, cache_v[:, :, 64 * g:64 * (g + 1)].rearrange("s k d -> k s d"), w=[R_vc], dsem=ds_vc)

                if g == 0:
                    CK("a0")
                for blk in range(3):
                    ranges = [(128, 512), (640, 512), (1152, 128)] if blk < 2 else [(0, 512), (512, 512), (1024, 256)]
                    for (t0, n) in ranges:
                        bank = ipbank()
                        tiles = [R_hT[t] for t in range(t0 // 128, (t0 + n) // 128)]
                        for k in range(16):
                            mm(PS[bank][:, 0:n], wt[:, k, blk * 128:(blk + 1) * 128], hT[:, k, t0:t0 + n],
                               start=(k == 0), stop=(k == 15), r=[Rw] + tiles, w=[RPS[bank]])
                        if blk < 2:
                            ts("dve", qT[:, blk, t0 - 128:t0 - 128 + n], PS[bank][:, 0:n], 0.125, None, ALU.mult,
                               r=[RPS[bank]], w=[R_qT])
                        else:
                            cp("act", kT[:, t0:t0 + n], PS[bank][:, 0:n], r=[RPS[bank]], w=[R_kT])
                if g == 0:
                    CK("a1")
                for t in range(NT):
                    bank = ipbank()
                    for k in range(16):
                        mm(PS[bank][:, 0:128], hT[:, k, t * 128:(t + 1) * 128], wt[:, k, 320:448],
                           start=(k == 0), stop=(k == 15), r=[Rw, R_hT[t]], w=[RPS[bank]])
                    if t >= 1:
                        for k in range(16):
                            mm(PS[bank][:, 128:384], hT[:, k, t * 128:(t + 1) * 128], wt2[:, k, 0:256],
                               start=(k == 0), stop=(k == 15), r=[Rw2, R_hT[t]], w=[RPS[bank]])
                    cp("dve", vb[:, t, :], PS[bank][:, 64:128], r=[RPS[bank]], w=[R_vb[t]])
                    if t >= 8:
                        cp("dve", kvnew[:, t - 8, :, 64 * g:64 * (g + 1)], PS[bank][:, 0:128].rearrange("p (a b) -> p a b", a=2),
                           r=[RPS[bank]], w=[R_kvnew])
                    if t >= 1:
                        act(sz[:, t, :], PS[bank][:, 128:384], AF.Silu, r=[RPS[bank]], w=[R_sz[t]])

                if g == 0:
                    CK("a_inproj")
                for hf in range(2):
                    for s8 in range(8):
                        s = hf * 8 + s8
                        tr(PSb(7)[:, s8 * 128:(s8 + 1) * 128], kc[:, s, :], ident_b, r=[R_kc] + RC, w=[RPS[7]])
                    cp("act", KcT[:, hf * 8:(hf + 1) * 8, :], PSb(7).rearrange("p (a b) -> p a b", a=8), r=[RPS[7]], w=[R_KcT])
                for b in range(2):
                    cp("pool", Zq[b][:, :, 120:128], qT[:, b, 1024:1152].rearrange("p (s t) -> p s t", s=16),
                       r=[R_qT], w=[R_Zq[b]])

                if g == 0:
                    CK("a3")
                def attn_gen(i):
                    sample = (i == 9)
                    it = (i - 1) % 2
                    bS = (2, 3) if it == 0 else (0, 1)
                    bPT = 4 if it == 0 else 7
                    bO = 5 if it == 0 else 6
                    RS, RP, RPT, RST, RAN, RAG = R_it[it][0:6]
                    S_, P_, PT_, st_, An_, Ag_ = Sb[it], Pb[it], PTs[it], stat[it], An[it], Ag[it]
                    rmax, mx, negm, rsum, smm, es, den, rec = [st_[:, n, :] for n in range(8)]
                    for j in range(4):
                        b, jj = j // 2, j % 2
                        pr = slice(jj * 64, (jj + 1) * 64)
                        reg = PS[bS[jj]][:, b * 256:(b + 1) * 256]
                        if not sample:
                            mm(reg, qT[pr, b, (i - 1) * 128:i * 128], kT[pr, (i - 1) * 128:(i + 1) * 128],
                               r=[R_qT, R_kT], w=[RPS[bS[jj]]])
                        else:
                            for s in range(16):
                                mm(reg[:, 0:128], Zq[b][pr, s, 120 - 8 * s:248 - 8 * s], KcT[pr, s, :],
                                   start=(s == 0), stop=(s == 15), r=[R_Zq[b], R_KcT], w=[RPS[bS[jj]]])
                            mm(reg[:, 128:256], qT[pr, b, 1024:1152], kT[pr, 1152:1280], r=[R_qT, R_kT], w=[RPS[bS[jj]]])
                    bias_t = biasS if sample else biasP
                    for jj in range(2):
                        tt("dve", S_.rearrange("p (b j) k -> p b j k", j=2)[:, :, jj, :], PS[bS[jj]].rearrange("p (a b) -> p a b", a=2),
                           bias_t.rearrange("p (b j) k -> p b j k", j=2)[:, :, jj, :], ALU.add, r=[RPS[bS[jj]], R_bias], w=[RS])
                    yield
                    if i == 1:
                        tt("dve", S_, S_, hmask.unsqueeze(1).broadcast_to([128, 4, 256]), ALU.add, r=[RS] + RC, w=[RS])
                    P.add("dve", lambda e, o=rmax, s_=S_: e.tensor_reduce(out=o, in_=s_, axis=AX.X, op=ALU.max), r=[RS], w=[RST])
                    tt("dve", mx, rmax, sink_bc[:, 4 * g:4 * g + 4], ALU.max, r=[RST] + RC, w=[RST])
                    ts("dve", negm, mx, -1.0, None, ALU.mult, r=[RST], w=[RST])
                    tt("dve", smm, sink_bc[:, 4 * g:4 * g + 4], mx, ALU.subtract, r=[RST] + RC, w=[RST])
                    yield
                    for j in range(4):
                        act(P_[:, j, :], S_[:, j, :], AF.Exp, bias=negm[:, j:j + 1], accum_out=rsum[:, j:j + 1],
                            r=[RS, RST], w=[RP, RST])
                    act(es, smm, AF.Exp, r=[RST], w=[RST])
                    yield
                    tt("dve", den, rsum, es, ALU.add, r=[RST], w=[RST])
                    P.add("dve", lambda e, o=rec, d_=den: e.reciprocal(out=o, in_=d_), r=[RST], w=[RST])
                    yield
                    for j in range(4):
                        for hf in range(2):
                            tr(PSb(bPT)[:, (2 * j + hf) * 128:(2 * j + hf + 1) * 128], P_[:, j, hf * 128:(hf + 1) * 128], ident_b,
                               r=[RP] + RC, w=[RPS[bPT]])
                    cp("act", PT_, PSb(bPT).rearrange("p (a b) -> p a b", a=8), r=[RPS[bPT]], w=[RPT])
                    yield
                    if not sample:
                        for j in range(4):
                            mm(PS[bO][:, j * 64:(j + 1) * 64], PT_[:, 2 * j, :], vb[:, i - 1, :], start=True, stop=False,
                               r=[RPT, R_vb[i - 1]], w=[RPS[bO]])
                            mm(PS[bO][:, j * 64:(j + 1) * 64], PT_[:, 2 * j + 1, :], vb[:, i, :], start=False, stop=True,
                               r=[RPT, R_vb[i]], w=[RPS[bO]])
                    else:
                        for b in range(2):
                            src = PT_.rearrange("p (j h) k -> p j h k", h=2)[:, 2 * b:2 * b + 2, 0, :]
                            cp("pool", ZP[:, :, :, 120:128], src.rearrange("p j (s t) -> p j s t", s=16), r=[RPT], w=[R_ZP])
                            for jj in range(2):
                                j = 2 * b + jj
                                for s in range(16):
                                    mm(PS[bO][:, j * 64:(j + 1) * 64], ZP[:, jj, s, 120 - 8 * s:248 - 8 * s], vc[:, s, :],
                                       start=(s == 0), stop=False, r=[R_ZP, R_vc], w=[RPS[bO]])
                                mm(PS[bO][:, j * 64:(j + 1) * 64], PT_[:, 2 * j + 1, :], vb[:, 9, :], start=False, stop=True,
                                   r=[RPT, R_vb[9]], w=[RPS[bO]])
                    tt("dve", An_.rearrange("p (a b) -> p a b", a=4), PS[bO][:, 0:256].rearrange("p (a b) -> p a b", a=4),
                       rec.unsqueeze(2).broadcast_to([128, 4, 64]), ALU.mult, r=[RPS[bO], RST], w=[RAN])
                    tt("pool", Ag_, An_, sz[:, i, :], ALU.mult, r=[RAN, R_sz[i]], w=[RAG])
                    yield
                    for b in range(2):
                        tr(PSb(bO)[:, 512 + b * 128:512 + (b + 1) * 128], Ag_[:, b * 128:(b + 1) * 128], ident_b, r=[RAG] + RC, w=[RPS[bO]])
                    cp("act", ATs[:, :, (i - 1) * 128:i * 128], PSb(bO)[:, 512:768].rearrange("p (a b) -> p a b", a=2),
                       r=[RPS[bO]], w=[R_ATs])
                run_interleaved((attn_gen(i) for i in range(1, NT)), 2)
                dma("sp", AT_d[2 * g:2 * g + 2].rearrange("b p t -> p b t"), ATs, r=[R_ATs], w=[R_ATd], dsem=ds_AT)
                if g == 0:
                    CK("a_g0")

            CK("a_attn")
            ds_kv = DSem("kvout")
            dma("sp", kp_out[:, :], kvnew[:, 0, 0, :], r=[R_kvnew], dsem=ds_kv, outflag=True)
            dma("sp", vp_out[:, :], kvnew[:, 0, 1, :], r=[R_kvnew], dsem=ds_kv, outflag=True)
            for s in range(16):
                dma("sp", ks_out[s, 120:128, :], kvnew[8 * s:8 * s + 8, 1, 0, :], r=[R_kvnew], dsem=ds_kv, outflag=True)
                dma("sp", vs_out[s, 120:128, :], kvnew[8 * s:8 * s + 8, 1, 1, :], r=[R_kvnew], dsem=ds_kv, outflag=True)
            dma("sp", ks_out[:, 0:120, :], cache_k[:, 8:128, :], dsem=ds_kv, outflag=True)
            dma("sp", vs_out[:, 0:120, :], cache_v[:, 8:128, :], dsem=ds_kv, outflag=True)
            A.pop()
            P.barrier()
            A.push()
            NTT = 9
            dt_all = A.alloc([128, NTT, 64], F32)
            adt_all = A.alloc([128, NTT, 64], F32)
            eat_all = A.alloc([128, NTT, 64], F32)
            dtd_all = A.alloc([128, NTT, 64], F32)
            eaend_all = A.alloc([128, NTT, 64], F32)
            eatg_all = None if prefix else A.alloc([128, NTT, 64], F32)
            R_dt = [Res(f"dt{t}") for t in range(NTT)]
            A_bc = A.alloc([128, 64], F32)
            cum_bc = A.alloc([128, 64], F32)
            R_cum = Res("cum")
            E_samp = A.alloc([128, 512], F32)
            R_Es = Res("Es")
            A.push()
            b0tmp = A.alloc([128, 8, 64], F32)
            R_b0 = Res("b0tmp")
            Xeo = A.alloc([128, 2, 512], F32)
            R_Xeo = Res("Xeo")

            act(A_bc, alog_bc, AF.Exp, r=RC, w=[R_cum])
            ts("dve", A_bc, A_bc, -1.0, None, ALU.mult, r=[R_cum], w=[R_cum])
            memset("pool", cum_bc, 0.0, w=[R_cum])
            wt, Rw = load_w([(w_in[:, DT0:DT0 + 64], 0)])
            for t in range(1, NT):
                ti = t - 1
                smp = (t == 9)
                bank = ipbank()
                for k in range(16):
                    mm(PS[bank][:, 0:64], hT[:, k, t * 128:(t + 1) * 128], wt[:, k, 0:64],
                       start=(k == 0), stop=(k == 15), r=[Rw, R_hT[t]], w=[RPS[bank]])
                xx, ax, ee, ll, tmp = [b0tmp[:, n, :] for n in range(5)]
                tt("dve", xx, PS[bank][:, 0:64], dtb_bc, ALU.add, r=[RPS[bank]] + RC, w=[R_b0])
                act(ax, xx, AF.Abs, r=[R_b0], w=[R_b0])
                act(ee, ax, AF.Exp, scale=-1.0, r=[R_b0], w=[R_b0])
                ts("dve", ee, ee, 1.0, None, ALU.add, r=[R_b0], w=[R_b0])
                act(ll, ee, AF.Ln, r=[R_b0], w=[R_b0])
                stt("dve", dt_all[:, ti, :], xx, 0.0, ll, ALU.max, ALU.add, r=[R_b0], w=[R_dt[ti]])
                tt("dve", adt_all[:, ti, :], dt_all[:, ti, :], A_bc, ALU.mult, r=[R_dt[ti], R_cum], w=[R_dt[ti]])
                b2 = ipbank()
                mm(PS[b2][:, 0:64], TRIS if smp else TRI, adt_all[:, ti, :], r=[R_dt[ti]] + RC, w=[RPS[b2]])
                mm(PS[b2][:, 64:128], SAME if smp else ONES, adt_all[:, ti, :], r=[R_dt[ti]] + RC, w=[RPS[b2]])
                at_ = b0tmp[:, 5, :]
                cp("dve", at_, PS[b2][:, 0:64], r=[RPS[b2]], w=[R_b0])
                act(eat_all[:, ti, :], PS[b2][:, 0:64], AF.Exp, r=[RPS[b2]], w=[R_dt[ti]])
                tt("dve", tmp, PS[b2][:, 64:128], at_, ALU.subtract, r=[RPS[b2], R_b0], w=[R_b0])
                act(tmp, tmp, AF.Exp, r=[R_b0], w=[R_b0])
                tt("dve", dtd_all[:, ti, :], dt_all[:, ti, :], tmp, ALU.mult, r=[R_b0, R_dt[ti]], w=[R_dt[ti]])
                act(eaend_all[:, ti, :], PS[b2][:, 64:128], AF.Exp, r=[RPS[b2]], w=[R_dt[ti]])
                if not smp and prefix:
                    tt("dve", cum_bc, PS[b2][:, 64:128], cum_bc, ALU.add, r=[RPS[b2], R_cum], w=[R_cum])
                elif not smp:
                    atg = b0tmp[:, 6, :]
                    tt("dve", atg, at_, cum_bc, ALU.add, r=[R_b0, R_cum], w=[R_b0])
                    act(eatg_all[:, ti, :], atg, AF.Exp, r=[R_b0], w=[R_dt[ti]])
                    tt("dve", cum_bc, PS[b2][:, 64:128], cum_bc, ALU.add, r=[RPS[b2], R_cum], w=[R_cum])
                else:
                    adv = adt_all[:, ti, :].rearrange("p (i two) -> p i two", two=2)
                    for par in range(2):
                        tt("pool", Xeo[:, par, :].rearrange("p (s i) -> p s i", s=16),
                           adv[:, :, par].unsqueeze(1).broadcast_to([128, 16, 32]),
                           M16.unsqueeze(2).broadcast_to([128, 16, 32]), ALU.mult, r=[R_dt[ti]] + RC, w=[R_Xeo])
                    b3 = ipbank()
                    mm(PS[b3][0:64, :], ONES[:, 0:64], Xeo[:, 0, :], r=[R_Xeo] + RC, w=[RPS[b3]])
                    mm(PS[b3][64:128, :], ONES[:, 0:64], Xeo[:, 1, :], r=[R_Xeo] + RC, w=[RPS[b3]])
                    act(E_samp, PS[b3], AF.Exp, r=[RPS[b3]], w=[R_Es])
            A.pop()
            P.barrier()
            CK("b0")

            sc = A.alloc([128, 768], F32)
            R_sc, ds_sc = Res("sc"), DSem("sc")
            xpP = [A.alloc([128, 1027], F32) for _ in range(2)]
            xpS = [A.alloc([128, 16, 11], F32) for _ in range(2)]
            R_xp = [Res("xp0"), Res("xp1")]
            accP2 = [A.alloc([128, 1024], F32) for _ in range(2)]
            accS2 = [A.alloc([128, 16, 8], F32) for _ in range(2)]
            R_accP2, R_accS2 = [Res("accP0"), Res("accP1")], [Res("accS0"), Res("accS1")]
            convT = A.alloc([128, 5, NPT], BF16)
            R_cv = [Res(f"cv{b}") for b in range(5)]
            CTk = [A.alloc([128, NPT], BF16) for _ in range(2)]
            R_CT = [Res("CT0"), Res("CT1")]
            crow = sc
            R_crow, ds_crow = R_sc, DSem("crow")
            xtok = [A.alloc([128, 640], BF16) for _ in range(2)]
            R_xtok = [Res("xtok0"), Res("xtok1")]
            ylocal = A.alloc([128, NTT, 512], F32)
            R_yl = [Res(f"yl{t}") for t in range(NTT)]
            cbm = [A.alloc([128, 128], F32) for _ in range(2)]
            R_cbm = [Res("cbm0"), Res("cbm1")]
            xdt = [A.alloc([128, 512], BF16) for _ in range(2)]
            xdd = [A.alloc([128, 512], BF16) for _ in range(2)]
            R_xd = [Res("xd0"), Res("xd1")]
            Lt = [A.alloc([128, 4, 128], F32) for _ in range(2)]
            R_L = [Res("L0"), Res("L1")]
            Et = [A.alloc([128, 512], F32) for _ in range(2)]
            R_E = [Res("E0"), Res("E1")]
            MT = [A.alloc([128, 4, 128], BF16) for _ in range(2)]
            R_MT = [Res("MT0"), Res("MT1")]
            t12 = [A.alloc([128, 512], F32) for _ in range(2)]
            t3 = [None, None]
            R_t12, R_t3 = [Res("t12a"), Res("t12b")], [Res("t3a"), Res("t3b")]
            STf = [A.alloc([128, 520], F32) for _ in range(1 if prefix else 2)] * (2 if prefix else 1)
            R_STf = [Res("STf0")] * 2 if prefix else [Res("STf0"), Res("STf1")]
            STb = A.alloc([128, 512], BF16)
            R_STb = Res("STb")
            S0nat = [A.alloc([128, 4, 128], F32) for _ in range(2)]
            R_S0, ds_S0 = [Res("S00"), Res("S01")], [DSem("S00"), DSem("S01")]
            S0T = [A.alloc([128, 512], BF16) for _ in range(2)]
            R_S0T = [Res("S0T0"), Res("S0T1")]
            ZC = A.alloc([128, 16, 248], BF16)
            R_ZC = Res("ZC")
            Bm = A.alloc([128, 16, 128], BF16)
            R_Bm = Res("Bm")
            snew = [A.alloc([128, 4, 128], F32)] * 2
            R_sn, ds_sn = [Res("sn0")] * 2, [DSem("sn0")] * 2
            agl = None if prefix else A.alloc([128, 520], F32)
            R_agl, ds_agl = Res("agl"), DSem("agl")
            Sst = None if prefix else A.alloc([128, 512], F32)
            Sstb = None if prefix else A.alloc([128, 512], BF16)
            R_Sst, R_Sstb = Res("Sst"), Res("Sstb")
            cci = A.alloc([128, 4, 8], F32)
            R_cci = Res("cci")
            szs2 = [A.alloc([128, 512], F32) for _ in range(2)]
            yb2 = [A.alloc([128, 512], F32) for _ in range(2)]
            R_szs2, R_yb2 = [Res("szsa"), Res("szsb")], [Res("yba"), Res("ybb")]
            gst2 = [A.alloc([128, 4], F32) for _ in range(2)]
            R_gst2 = [Res("gsta"), Res("gstb")]
            szs, yb = szs2[0], yb2[0]
            gn = [A.alloc([128, 512], BF16) for _ in range(2)]
            R_szs, R_yb, R_gn = Res("szs"), Res("yb"), [Res("gn0"), Res("gn1")]
            gst = A.alloc([128, 4], F32)
            R_gst = Res("gst")
            ssTs = [A.alloc([128, 4, 128], BF16) for _ in range(2)]
            R_ssT, ds_ssT = [Res("ssT0"), Res("ssT1")], [DSem("ssT0"), DSem("ssT1")]
            spst = snew[0]
            R_spst, ds_spst = R_sn[0], DSem("spst")
            R_agin = [Res(f"agin{g}") for g in range(8)]
            R_agout = [Res(f"agout{g}") for g in range(8)]
            ds_agin = DSem("agin")
            R_STd = Res("STd")
            ds_stin = DSem("stin")
            memset("pool", ZC, 0.0, w=[R_ZC])
            hsel = A.alloc([128, 16, 48], BF16)
            R_hsel = Res("hsel")
            cp("pool", hsel.rearrange("p k (s j) -> p k s j", s=16),
               hT[:, :, 1152:1280].rearrange("p k (s t) -> p k s t", s=16)[:, :, :, 5:8], r=[R_hT[9]], w=[R_hsel])
            cnt = {"x": 0, "mt": 0, "s0": 0, "sn": 0, "gn": 0, "ss": 0}

            def bcast_h(ap_h8):
                return ap_h8.unsqueeze(2).broadcast_to([128, 8, 64])

            def v864(ap):
                return ap.rearrange("p (h d) -> p h d", h=8)

            def B1a(g):
                wa, Rwa = load_w([(w_in[:, XS0 + 512 * g:XS0 + 512 * (g + 1)], 0)])
                wb_, Rwb = load_w([(w_in[:, B0 + 128 * g:B0 + 128 * (g + 1)], 0),
                                   (w_in[:, C0 + 128 * g:C0 + 128 * (g + 1)], 128)])
                dma("sp", sc[0:48, 0:512], state_conv[:, 512 * g:512 * (g + 1)], w=[R_sc], dsem=ds_sc)
                dma("sp", sc[0:48, 512:640], state_conv[:, 4096 + 128 * g:4096 + 128 * (g + 1)], dsem=ds_sc)
                dma("sp", sc[0:48, 640:768], state_conv[:, 5120 + 128 * g:5120 + 128 * (g + 1)], dsem=ds_sc)
                R_sc.w = P.all_dma[-1]
                def blk_gen_b(blk):
                    wsl, Rws, c0 = (wa, Rwa, blk * 128) if blk < 4 else (wb_, Rwb, (blk - 4) * 128)
                    cbi = (4 * g + blk) if blk < 4 else (32 + g if blk == 4 else 40 + g)
                    xi = blk % 2
                    xp, xs_, Rxp = xpP[xi], xpS[xi], R_xp[xi]
                    accP, accS, R_aP, R_aS = accP2[xi], accS2[xi], R_accP2[xi], R_accS2[xi]
                    for ri, (t0, n) in enumerate([(0, 512), (512, 512), (1024, 256)]):
                        bank = ipbank()
                        tiles = [R_hT[t] for t in range(t0 // 128, (t0 + n) // 128)]
                        for k in range(16):
                            mm(PS[bank][:, 0:n], wsl[:, k, c0:c0 + 128], hT[:, k, t0:t0 + n],
                               start=(k == 0), stop=(k == 15), r=[Rws] + tiles, w=[RPS[bank]])
                        if ri == 0:
                            cp("act", xp[:, 0:387], PS[bank][:, 125:512], r=[RPS[bank]], w=[Rxp])
                        elif ri == 1:
                            cp("act", xp[:, 387:899], PS[bank][:, 0:512], r=[RPS[bank]], w=[Rxp])
                        else:
                            cp("act", xp[:, 899:1027], PS[bank][:, 0:128], r=[RPS[bank]], w=[Rxp])
                            cp("act", xs_[:, :, 3:11], PS[bank][:, 128:256].rearrange("p (s t) -> p s t", s=16),
                               r=[RPS[bank]], w=[Rxp])
                    bank = ipbank()
                    mm(PS[bank][:, 0:48], sc[0:48, blk * 128:(blk + 1) * 128], ident_f[0:48, 0:48], r=[R_sc] + RC, w=[RPS[bank]])
                    cp("act", xs_[:, :, 0:3], PS[bank][:, 0:48].rearrange("p (s t) -> p s t", s=16), r=[RPS[bank]], w=[Rxp])
                    yield
                    ts("dve", accP, xp[:, 0:1024], cw_pk[:, cbi, 0:1], None, ALU.mult, r=[Rxp] + RC, w=[R_aP])
                    ts("dve", accS, xs_[:, :, 0:8], cw_pk[:, cbi, 0:1], None, ALU.mult, r=[Rxp] + RC, w=[R_aS])
                    yield
                    for j in range(1, 4):
                        stt("dve", accP, xp[:, j:j + 1024], cw_pk[:, cbi, j:j + 1], accP, ALU.mult, ALU.add, r=[Rxp, R_aP] + RC, w=[R_aP])
                        stt("dve", accS, xs_[:, :, j:j + 8], cw_pk[:, cbi, j:j + 1], accS, ALU.mult, ALU.add, r=[Rxp, R_aS] + RC, w=[R_aS])
                        yield
                    if blk < 5:
                        dst, Rd = convT[:, blk, :], R_cv[blk]
                    else:
                        dst, Rd = CTk[g % 2], R_CT[g % 2]
                    act(dst[:, 0:1024], accP, AF.Silu, bias=cb_pk[:, cbi:cbi + 1], r=[R_aP] + RC, w=[Rd])
                    act(dst[:, 1024:1152].rearrange("p (s t) -> p s t", s=16), accS, AF.Silu, bias=cb_pk[:, cbi:cbi + 1],
                        r=[R_aS] + RC, w=[Rd])

                run_interleaved((blk_gen_b(blk) for blk in range(6)), 2)
                for (lhs_of_k, nrow, dst_out, Rt) in (
                        (lambda k: hT[:, k, 1149:1152], 3, cp_out, R_hT[8]),
                        (lambda k: hsel[:, k, :], 48, cs_out, R_hsel)):
                    bank = ipbank()
                    for k in range(16):
                        mm(PS[bank][0:nrow, 0:512], lhs_of_k(k), wa[:, k, 0:512], start=(k == 0), stop=(k == 15),
                           r=[Rwa, Rt], w=[RPS[bank]])
                    cp("dve", crow[0:nrow, 0:512], PS[bank][0:nrow, 0:512], r=[RPS[bank]], w=[R_crow])
                    bank = ipbank()
                    for k in range(16):
                        mm(PS[bank][0:nrow, 0:256], lhs_of_k(k), wb_[:, k, 0:256], start=(k == 0), stop=(k == 15),
                           r=[Rwb, Rt], w=[RPS[bank]])
                    cp("dve", crow[0:nrow, 512:768], PS[bank][0:nrow, 0:256], r=[RPS[bank]], w=[R_crow])
                    dma("sp", dst_out[:, 512 * g:512 * (g + 1)], crow[0:nrow, 0:512], r=[R_crow], dsem=ds_crow, outflag=True)
                    dma("sp", dst_out[:, 4096 + 128 * g:4096 + 128 * (g + 1)], crow[0:nrow, 512:640], r=[R_crow], dsem=ds_crow, outflag=True)
                    dma("sp", dst_out[:, 5120 + 128 * g:5120 + 128 * (g + 1)], crow[0:nrow, 640:768], r=[R_crow], dsem=ds_crow, outflag=True)

            def B1b(g):
                hs = slice(8 * g, 8 * g + 8)
                CT, RCT = CTk[g % 2], R_CT[g % 2]
                STc, RST_ = STf[g % 2], R_STf[g % 2]
                cp("pool", ZC[:, :, 120:128], CT[:, 1024:1152].rearrange("p (s t) -> p s t", s=16), r=[RCT], w=[R_ZC])
                if prefix:
                    dma("sp", STc[:, 0:512], Sst_d[g].ap(), r=[R_SstD[g]], w=[RST_], dsem=ds_stin)
                    cp("act", STb, STc[:, 0:512], r=[RST_], w=[R_STb])
                st_done = {0: True}

                def tile_gen(t):
                    ti = t - 1
                    smp = (t == 9)
                    par = ti % 2
                    bT = 7 if par == 0 else 2
                    bY = 4 if par == 0 else 0
                    bYI = 5 if par == 0 else 1
                    Lt_, R_L_, Et_, R_E_ = Lt[par], R_L[par], Et[par], R_E[par]
                    t12_, R_t12_, t3_, R_t3_ = t12[par], R_t12[par], t3[par], R_t3[par]
                    tsl = slice(ti * 128, (ti + 1) * 128)
                    xi = par
                    xk, Rxk = xtok[xi], R_xtok[xi]
                    for blk in range(5):
                        tr(PSb(bT)[:, blk * 128:(blk + 1) * 128], convT[:, blk, tsl], ident_b, r=[R_cv[blk]] + RC, w=[RPS[bT]])
                    cp("act", xk, PSb(bT)[:, 0:640], r=[RPS[bT]], w=[Rxk])
                    yield
                    mm(PS[bT][:, 384:512], convT[:, 4, tsl], CT[:, tsl], r=[R_cv[4], RCT], w=[RPS[bT]])
                    cb_, Rcb = cbm[ti % 2], R_cbm[ti % 2]
                    tt("dve", cb_, PS[bT][:, 384:512], TRIS if smp else TRI, ALU.mult, r=[RPS[bT]] + RC, w=[Rcb])
                    yield
                    xdt_, xdd_, Rxd = xdt[ti % 2], xdd[ti % 2], R_xd[ti % 2]
                    tt("pool", v864(xdt_), v864(xk[:, 0:512]), bcast_h(dt_all[:, ti, hs]), ALU.mult, r=[Rxk, R_dt[ti]], w=[Rxd])
                    tt("pool", v864(xdd_), v864(xk[:, 0:512]), bcast_h(dtd_all[:, ti, hs]), ALU.mult, r=[Rxk, R_dt[ti]], w=[Rxd])
                    yield
                    for hb in range(2):
                        h0 = 8 * g + 4 * hb
                        tt("pool", Lt_, MGT.unsqueeze(1).broadcast_to([128, 4, 128]),
                           adt_all[:, ti, h0:h0 + 4].unsqueeze(2).broadcast_to([128, 4, 128]), ALU.mult,
                           r=[R_dt[ti]] + RC, w=[R_L_])
                        yield
                        for r_ in range(4):
                            mm(PS[3][:, r_ * 128:(r_ + 1) * 128], Lt_[:, r_, :], TRI, r=[R_L_] + RC, w=[RPS[3]])
                        act(Et_, PS[3], AF.Exp, r=[RPS[3]], w=[R_E_])
                        yield
                        mi = par
                        tt("dve", MT[mi], Et_.rearrange("p (a b) -> p a b", a=4), cb_.unsqueeze(1).broadcast_to([128, 4, 128]),
                           ALU.mult, r=[R_E_, Rcb], w=[R_MT[mi]])
                        yield
                        for r_ in range(4):
                            hh = 4 * hb + r_
                            mm(PS[bY][:, hh * 64:(hh + 1) * 64], MT[mi][:, r_, :], xdt_[:, hh * 64:(hh + 1) * 64],
                               r=[R_MT[mi], Rxd], w=[RPS[bY]])
                        yield
                    have_inter = smp or t >= 2 or prefix
                    while not smp and not st_done.get(ti):
                        yield
                    if smp:
                        s0toks = []
                        for s in range(16):
                            si = cnt["s0"] % 2
                            cnt["s0"] += 1
                            dma("sp", S0nat[si], state_ssm[s, 512 * g:512 * (g + 1), :].rearrange("(q p) n -> p q n", p=128),
                                w=[R_S0[si]], dsem=ds_S0[si])
                            for q in range(4):
                                mm(PS[7][:, q * 128:(q + 1) * 128], S0nat[si][:, q, :], ident_f, r=[R_S0[si]] + RC, w=[RPS[7]])
                            cp("act", S0T[si], PS[7], r=[RPS[7]], w=[R_S0T[si]])
                            mm(PS[bYI], ZC[:, s, 120 - 8 * s:248 - 8 * s], S0T[si], start=(s == 0), stop=(s == 15),
                               r=[R_ZC, R_S0T[si]], w=[RPS[bYI]])
                            if s == 0:
                                tt("pool", Bm, xk[:, 512:640].unsqueeze(1).broadcast_to([128, 16, 128]),
                                   M16.unsqueeze(2).broadcast_to([128, 16, 128]), ALU.mult, r=[Rxk] + RC, w=[R_Bm])
                            for q in range(4):
                                mm(PS[6][:, q * 128:(q + 1) * 128], xdd_[:, q * 128:(q + 1) * 128], Bm[:, s, :],
                                   r=[Rxd, R_Bm], w=[RPS[6]])
                            ni = cnt["sn"] % 2
                            cnt["sn"] += 1
                            ecol = E_samp[:, s * 32 + 4 * g:s * 32 + 4 * g + 4]
                            tt("dve", snew[ni], S0nat[si], ecol.unsqueeze(2).broadcast_to([128, 4, 128]), ALU.mult,
                               r=[R_S0[si], R_Es], w=[R_sn[ni]])
                            tt("dve", snew[ni], snew[ni], PS[6].rearrange("p (q n) -> p q n", q=4), ALU.add,
                               r=[R_sn[ni], RPS[6]], w=[R_sn[ni]])
                            dma("sp", ss_out[s, 512 * g:512 * (g + 1), :].rearrange("(q p) n -> p q n", p=128), snew[ni],
                                r=[R_sn[ni]], dsem=ds_sn[ni], outflag=True)
                            yield
                    elif t >= 2 or prefix:
                        mm(PS[bYI], CT[:, tsl], STb, r=[RCT, R_STb], w=[RPS[bYI]])
                    if have_inter:
                        yield
                        tt("dve", v864(t12_), v864(PS[bYI]), bcast_h(eat_all[:, ti, hs]), ALU.mult, r=[RPS[bYI], R_dt[ti]], w=[R_t12_])
                        tt("dve", t12_, t12_, PS[bY], ALU.add, r=[R_t12_, RPS[bY]], w=[R_t12_])
                    else:
                        cp("dve", t12_, PS[bY], r=[RPS[bY]], w=[R_t12_])
                    tt("pool", v864(ylocal[:, ti, :]), v864(xk[:, 0:512]), bcast_h(dsk_bc[:, hs]), ALU.mult, r=[Rxk] + RC, w=[R_yl[ti]])
                    yield
                    tt("pool", ylocal[:, ti, :], ylocal[:, ti, :], t12_, ALU.add, r=[R_t12_, R_yl[ti]], w=[R_yl[ti]])
                    if not smp:
                        mm(PS[6], xk[:, 512:640], xdd_, r=[Rxk, Rxd], w=[RPS[6]])
                        if t == 1 and not prefix:
                            cp("dve", STc[:, 0:512], PS[6], r=[RPS[6]], w=[RST_])
                        else:
                            tt("dve", v864(STc[:, 0:512]), v864(STc[:, 0:512]), bcast_h(eaend_all[:, ti, hs]), ALU.mult,
                               r=[RST_, R_dt[ti]], w=[RST_])
                            tt("dve", STc[:, 0:512], STc[:, 0:512], PS[6], ALU.add, r=[RST_, RPS[6]], w=[RST_])
                        if t < 8:
                            cp("act", STb, STc[:, 0:512], r=[RST_], w=[R_STb])
                        st_done[t] = True

                run_interleaved((tile_gen(t) for t in range(1, NT)), 2)
                if prefix:
                    for q in range(4):
                        mm(PS[7][:, q * 128:(q + 1) * 128], STc[:, q * 128:(q + 1) * 128], ident_f, r=[RST_] + RC, w=[RPS[7]])
                    cp("act", spst, PS[7].rearrange("p (q n) -> p q n", q=4), r=[RPS[7]], w=[R_spst])
                    dma("sp", sp_out[512 * g:512 * (g + 1), :].rearrange("(q p) n -> p q n", p=128), spst, r=[R_spst],
                        dsem=ds_spst, outflag=True)
                    return
                cp("dve", STc[:, 512:520], cum_bc[:, hs], r=[R_cum], w=[RST_])
                dma("sp", ag_in[g].ap(), STc, r=[RST_], w=[R_agin[g]], dsem=ds_agin)
                dcc = DSem(f"cc{g}", inc=1)
                if nocc:
                    dma("sp", ag_out[g].ap()[0:128, :], ag_in[g].ap(), r=[R_agin[g]], w=[R_agout[g]], dsem=DSem(f"ccx{g}"))
                else:
                  P.add("pool", lambda e, g=g: e.collective_compute(
                    "AllGather", ALU.bypass, replica_groups=[list(range(NCORES))],
                    ins=[ag_in[g].ap().opt()], outs=[ag_out[g].ap().opt()]),
                    r=[R_agin[g]], w=[R_agout[g]], dsem=dcc)

            def B2(g):
                hs = slice(8 * g, 8 * g + 8)
                CT, RCT = CTk[g % 2], R_CT[g % 2]
                STc, RST_ = STf[g % 2], R_STf[g % 2]
                wz, Rwz = load_w([(w_in[:, ZS0 + 512 * g:ZS0 + 512 * (g + 1)], 0)])
                if not prefix:
                    memset("pool", Sst, 0.0, w=[R_Sst])
                for i in range(0 if prefix else NCORES):
                    dma("sp", agl, ag_out[g].ap()[i * 128:(i + 1) * 128, :], r=[R_agout[g]], w=[R_agl], dsem=ds_agl)
                    ei, ci = cci[:, 0, :], cci[:, 1, :]
                    act(ei, agl[:, 512:520], AF.Exp, r=[R_agl], w=[R_cci])
                    ts("dve", ci, ei, -1.0, None, ALU.add, r=[R_cci], w=[R_cci])
                    ts("dve", ci, ci, alpha_bc[:, i:i + 1], 1.0, ALU.mult, ALU.add, r=[R_cci] + RC, w=[R_cci])
                    tt("dve", v864(Sst), v864(Sst), bcast_h(ci), ALU.mult, r=[R_Sst, R_cci], w=[R_Sst])
                    stt("dve", Sst, agl[:, 0:512], alpha_bc[:, i:i + 1], Sst, ALU.mult, ALU.add, r=[R_agl, R_Sst] + RC, w=[R_Sst])
                if not prefix:
                    cp("act", Sstb, Sst, r=[R_Sst], w=[R_Sstb])
                eo = cci[:, 2, :]
                if not prefix:
                    act(eo, STc[:, 512:520], AF.Exp, r=[RST_], w=[R_cci])
                    tt("dve", v864(Sst), v864(Sst), bcast_h(eo), ALU.mult, r=[R_Sst, R_cci, R_Sstb], w=[R_Sst])
                    tt("dve", Sst, Sst, STc[:, 0:512], ALU.add, r=[R_Sst, RST_], w=[R_Sst])
                    for q in range(4):
                        mm(PS[7][:, q * 128:(q + 1) * 128], Sst[:, q * 128:(q + 1) * 128], ident_f, r=[R_Sst] + RC, w=[RPS[7]])
                    cp("act", spst, PS[7].rearrange("p (q n) -> p q n", q=4), r=[RPS[7]], w=[R_spst])
                    dma("sp", sp_out[512 * g:512 * (g + 1), :].rearrange("(q p) n -> p q n", p=128), spst, r=[R_spst],
                        dsem=ds_spst, outflag=True)
                def tile_gen2(t):
                    ti = t - 1
                    par = ti % 2
                    bT = 7 if par == 0 else 2
                    szs, R_szs, yb, R_yb, gst, R_gst = szs2[par], R_szs2[par], yb2[par], R_yb2[par], gst2[par], R_gst2[par]
                    tsl = slice(ti * 128, (ti + 1) * 128)
                    bank = ipbank()
                    for k in range(16):
                        mm(PS[bank], hT[:, k, t * 128:(t + 1) * 128], wz[:, k, 0:512], start=(k == 0), stop=(k == 15),
                           r=[Rwz, R_hT[t]], w=[RPS[bank]])
                    act(szs, PS[bank], AF.Silu, r=[RPS[bank]], w=[R_szs])
                    yield
                    if t <= 8 and not prefix:
                        mm(PS[5], CT[:, tsl], Sstb, r=[RCT, R_Sstb], w=[RPS[5]])
                        tt("dve", v864(yb), v864(PS[5]), bcast_h(eatg_all[:, ti, hs]), ALU.mult, r=[RPS[5], R_dt[ti]], w=[R_yb])
                        tt("pool", yb, yb, ylocal[:, ti, :], ALU.add, r=[R_yb, R_yl[ti]], w=[R_yb])
                        tt("pool", yb, yb, szs, ALU.mult, r=[R_yb, R_szs], w=[R_yb])
                    else:
                        tt("pool", yb, ylocal[:, ti, :], szs, ALU.mult, r=[R_yl[ti], R_szs], w=[R_yb])
                    yield
                    act(szs, yb, AF.Square, accum_out=gst[:, 0:1], r=[R_yb], w=[R_szs, R_gst])
                    yield
                    ts("dve", gst[:, 1:2], gst[:, 0:1], 1.0 / 512.0, EPS, ALU.mult, ALU.add, r=[R_gst], w=[R_gst])
                    yield
                    act(gst[:, 1:2], gst[:, 1:2], AF.Sqrt, r=[R_gst], w=[R_gst])
                    yield
                    P.add("dve", lambda e, o=gst[:, 2:3], i_=gst[:, 1:2]: e.reciprocal(out=o, in_=i_), r=[R_gst], w=[R_gst])
                    yield
                    gi = par
                    ts("dve", gn[gi], yb, gst[:, 2:3], None, ALU.mult, r=[R_yb, R_gst], w=[R_gn[gi]])
                    yield
                    for q in range(4):
                        tr(PSb(bT)[:, q * 128:(q + 1) * 128], gn[gi][:, q * 128:(q + 1) * 128], ident_b, r=[R_gn[gi]] + RC, w=[RPS[bT]])
                    yield
                    oi = cnt["ss"] % 2
                    cnt["ss"] += 1
                    tt("dve", ssTs[oi], PSb(bT)[:, 0:512].rearrange("p (q n) -> p q n", q=4),
                       ssn_pk[:, 4 * g:4 * g + 4].unsqueeze(2).broadcast_to([128, 4, 128]), ALU.mult,
                       r=[RPS[bT]] + RC, w=[R_ssT[oi]])
                    dma("sp", ST_d[4 * g:4 * g + 4, :, tsl].rearrange("b p t -> p b t"), ssTs[oi], r=[R_ssT[oi]], w=[R_STd],
                        dsem=ds_ssT[oi])

                run_interleaved((tile_gen2(t) for t in range(1, NT)), 2)

            for g in range(8):
                B1a(g)
                if g > 0:
                    B2(g - 1)
                B1b(g)
                if g == 0:
                    CK("b_g0")
            B2(7)
            A.pop()
            P.barrier()
            CK("b")
            A.push()
            mergedT = A.alloc([128, 16, NPT], BF16)
            R_mg = [Res(f"mg{t}") for t in range(9)]
            A.push()
            cslots = [wslot[0][:, :, 0:256], wslot[0][:, :, 256:512], wslot[1][:, :, 0:256], wslot[1][:, :, 256:512]]
            cslots += [A.alloc([128, 16, 256], BF16) for _ in range(4)]
            NCS = len(cslots)
            R_cs = [Res(f"cs{i}") for i in range(NCS)]
            ds_cs = [DSem(f"cs{i}") for i in range(NCS)]
            cstate = {"n": 0}

            def load_c(src):
                i = cstate["n"] % NCS
                cstate["n"] += 1
                dma("pool", cslots[i], src.rearrange("(k p) n -> p k n", p=128), w=[R_cs[i]], dsem=ds_cs[i])
                return cslots[i], R_cs[i]

            ATg = [A.alloc([128, 16, 256], BF16) for _ in range(2)]
            STg = [A.alloc([128, 32, 256], BF16) for _ in range(2)]
            R_xg, ds_xg = [Res("xg0"), Res("xg1")], [DSem("xg0"), DSem("xg1")]
            sg = [A.alloc([128, 512], F32) for _ in range(2)]
            R_sg = [Res("sg0"), Res("sg1")]
            mtmp = [A.alloc([128, 512], F32) for _ in range(2)]
            R_mt = [Res("mt0"), Res("mt1")]
            TGS = [(0, 256), (256, 256), (512, 256), (768, 256), (1024, 128)]
            ccnt = {"x": 0, "m": 0}
            for fc in range(8):
                f0 = 256 * fc
                wga, Rga = load_c(w_in[:, GA0 + f0:GA0 + f0 + 256])
                wgs, Rgs = load_c(w_in[:, GS0 + f0:GS0 + f0 + 256])
                wab, Rab = load_c(w_ab[:, f0:f0 + 256])
                ws0, Rs0 = load_c(w_sb[0:2048, f0:f0 + 256])
                ws1, Rs1 = load_c(w_sb[2048:4096, f0:f0 + 256])
                for (o0, n) in TGS:
                    xi = ccnt["x"] % 2
                    ccnt["x"] += 1
                    dma("sp", ATg[xi][:, :, 0:n], AT_d[:, :, o0:o0 + n].rearrange("b p t -> p b t"), r=[R_ATd], w=[R_xg[xi]], dsem=ds_xg[xi])
                    dma("sp", STg[xi][:, :, 0:n], ST_d[:, :, o0:o0 + n].rearrange("b p t -> p b t"), r=[R_STd], dsem=ds_xg[xi])
                    R_xg[xi].w = P.all_dma[-1]
                    htiles = [R_hT[t] for t in range((o0 + 128) // 128, (o0 + 128 + n) // 128)]
                    for sb in range(2):
                        cs_ = slice(sb * 128, (sb + 1) * 128)
                        bG, bP = (2, 3) if ccnt["m"] % 2 == 0 else (4, 5)
                        for k in range(16):
                            mm(PS[bG][:, 0:n], wga[:, k, cs_], hT[:, k, o0 + 128:o0 + 128 + n], start=(k == 0), stop=(k == 15),
                               r=[Rga] + htiles, w=[RPS[bG]])
                        for k in range(16):
                            mm(PS[bG][:, 256:256 + n], wgs[:, k, cs_], hT[:, k, o0 + 128:o0 + 128 + n], start=(k == 0), stop=(k == 15),
                               r=[Rgs] + htiles, w=[RPS[bG]])
                        for k in range(16):
                            mm(PS[bP][:, 0:n], wab[:, k, cs_], ATg[xi][:, k, 0:n], start=(k == 0), stop=(k == 15),
                               r=[Rab, R_xg[xi]], w=[RPS[bP]])
                        for k in range(32):
                            wsx, Rsx = (ws0, Rs0) if k < 16 else (ws1, Rs1)
                            mm(PS[bP][:, 256:256 + n], wsx[:, k % 16, cs_], STg[xi][:, k, 0:n], start=(k == 0), stop=(k == 31),
                               r=[Rsx, R_xg[xi]], w=[RPS[bP]])
                        mi = ccnt["m"] % 2
                        ccnt["m"] += 1
                        act(sg[mi], PS[bG], AF.Sigmoid, r=[RPS[bG]], w=[R_sg[mi]])
                        tt("dve", mtmp[mi], sg[mi], PS[bP], ALU.mult, r=[R_sg[mi], RPS[bP]], w=[R_mt[mi]])
                        mts = [R_mg[t] for t in range(o0 // 128, (o0 + n) // 128)]
                        tt("pool", mergedT[:, 2 * fc + sb, o0:o0 + n], mtmp[mi][:, 0:n], mtmp[mi][:, 256:256 + n], ALU.add,
                           r=[R_mt[mi]], w=mts)
            A.pop()
            P.barrier()
            CK("c")
            hT_flat = hT.rearrange("p a b -> p (a b)")
            wo = [hT_flat[:, 0:8192].rearrange("p (k n) -> p k n", k=16),
                  hT_flat[:, 8192:16384].rearrange("p (k n) -> p k n", k=16), wslot[0], wslot[1]]
            R_wo, ds_wo = [Res(f"wo{i}") for i in range(4)], [DSem(f"wo{i}") for i in range(4)]
            for c in range(4):
                dma("pool", wo[c], w_o[:, 512 * c:512 * (c + 1)].rearrange("(k p) n -> p k n", p=128), w=[R_wo[c]], dsem=ds_wo[c])
            npost_bc = A.alloc([128, D], F32)
            R_np, ds_np = Res("np"), DSem("np")
            dma("sp", npost_bc, norm_post.partition_broadcast(128), w=[R_np], dsem=ds_np)
            o32 = [A.alloc([128, D], F32) for _ in range(2)]
            xr = [A.alloc([128, D], F32) for _ in range(2)]
            R_o32, R_xr = [Res("o0"), Res("o1")], [Res("xr0"), Res("xr1")]
            ds_xr, ds_y = [DSem("xr0"), DSem("xr1")], [DSem("y0"), DSem("y1")]
            dst_ = A.alloc([128, 2, 8], F32)
            R_dst = [Res("dst0"), Res("dst1")]
            djunk = A.alloc([128, 512], BF16)
            R_dj = Res("dj")
            def d_gen(t):
                ti = t - 1
                oi = ti % 2
                bb = 4 if oi == 0 else 0
                dma("sp", xr[oi], xin[t * 128:(t + 1) * 128, :], w=[R_xr[oi]], dsem=ds_xr[oi])
                st_ = dst_[:, oi, :]
                for c in range(4):
                    for k in range(16):
                        mm(PS[bb + c], mergedT[:, k, ti * 128:(ti + 1) * 128], wo[c][:, k, :], start=(k == 0), stop=(k == 15),
                           r=[R_mg[ti], R_wo[c]], w=[RPS[bb + c]])
                    act(djunk, PS[bb + c], AF.Square, accum_out=st_[:, c:c + 1], r=[RPS[bb + c]], w=[R_dj, R_dst[oi]])
                    cp("dve", o32[oi][:, 512 * c:512 * (c + 1)], PS[bb + c], r=[RPS[bb + c]], w=[R_o32[oi]])
                    yield
                tt("dve", st_[:, 4:5], st_[:, 0:1], st_[:, 1:2], ALU.add, r=[R_dst[oi]], w=[R_dst[oi]])
                tt("dve", st_[:, 5:6], st_[:, 2:3], st_[:, 3:4], ALU.add, r=[R_dst[oi]], w=[R_dst[oi]])
                tt("dve", st_[:, 4:5], st_[:, 4:5], st_[:, 5:6], ALU.add, r=[R_dst[oi]], w=[R_dst[oi]])
                ts("dve", st_[:, 4:5], st_[:, 4:5], 1.0 / D, EPS, ALU.mult, ALU.add, r=[R_dst[oi]], w=[R_dst[oi]])
                yield
                act(st_[:, 4:5], st_[:, 4:5], AF.Sqrt, r=[R_dst[oi]], w=[R_dst[oi]])
                P.add("dve", lambda e, o=st_[:, 6:7], i_=st_[:, 4:5]: e.reciprocal(out=o, in_=i_), r=[R_dst[oi]], w=[R_dst[oi]])
                yield
                ts("dve", o32[oi], o32[oi], st_[:, 6:7], None, ALU.mult, r=[R_o32[oi], R_dst[oi]], w=[R_o32[oi]])
                yield
                tt("pool", o32[oi], o32[oi], npost_bc, ALU.mult, r=[R_o32[oi], R_np], w=[R_o32[oi]])
                tt("pool", o32[oi], o32[oi], xr[oi], ALU.add, r=[R_o32[oi], R_xr[oi]], w=[R_o32[oi]])
                dma("sp", y_out[ti * 128:(ti + 1) * 128, :], o32[oi], r=[R_o32[oi]], dsem=ds_y[oi], outflag=True)
            run_interleaved((d_gen(t) for t in range(1, NT)), 2)
            A.pop()
        except _Stop:
            pass
        P.wait_tokens("sp", P.out_toks + P.all_dma)
        sems = [estack.enter_context(nc.semaphore(f"s{i}")) for i in range(len(P.keys))]
        P.emit(nc, sems)
    return nc, P


_NC_CACHE = {}


def make_in_maps(x_prompt, x_sample, cache_k, cache_v, state_ssm, state_conv, norm_pre, w_in, conv_w,
                 conv_b, dt_bias, a_log, d_skip, ssm_norm, attn_sinks, w_attn_br, w_ssm_br, w_out, norm_post):
    f = lambda a: np.ascontiguousarray(np.asarray(a, dtype=np.float32))
    cst_np = _build_consts()
    bp, bs = _alibi_tables()
    shared = {
        "w_in": f(w_in[0]), "w_attn_br": f(w_attn_br[0]), "w_ssm_br": f(w_ssm_br[0]), "w_out": f(w_out[0]),
        "norm_pre": f(norm_pre[0]), "conv_w": f(conv_w[0]), "conv_b": f(conv_b[0]), "dt_bias": f(dt_bias[0]),
        "a_log": f(a_log[0]), "d_skip": f(d_skip[0]), "ssm_norm": f(ssm_norm[0]), "attn_sinks": f(attn_sinks[0]),
        "norm_post": f(norm_post[0]), "cst": cst_np, "bias_p": bp, "bias_s": bs,
    }
    in_maps = []
    for c in range(NCORES):
        b, j = c // 4, c % 4
        xin = np.zeros((TOK, D), np.float32)
        if j > 0:
            xin[0:128] = x_prompt[b, 1024 * j - 128:1024 * j]
        xin[128:1152] = x_prompt[b, 1024 * j:1024 * (j + 1)]
        xin[1152:1280] = np.asarray(x_sample[16 * c:16 * (c + 1)]).reshape(128, D)
        hm = np.zeros((128, 256), np.float32)
        if j == 0:
            hm[:, 0:128] = NEG
        al = np.zeros((8,), np.float32)
        for i in range(4 * b, c):
            al[i] = 1.0
        xprev = np.zeros((3072, D), np.float32)
        pflag = np.zeros((24,), np.float32)
        if j > 0:
            xprev[3072 - 1024 * j:] = x_prompt[b, 0:1024 * j]
            pflag[24 - 8 * j:] = 1.0
        m = dict(shared)
        m.update({
            "xprev": xprev, "pflag": pflag,
            "xin": xin,
            "cache_k": f(np.asarray(cache_k[0, 16 * c:16 * (c + 1)]).reshape(16, 128, 512)),
            "cache_v": f(np.asarray(cache_v[0, 16 * c:16 * (c + 1)]).reshape(16, 128, 512)),
            "state_ssm": f(np.asarray(state_ssm[0, 16 * c:16 * (c + 1)]).reshape(16, 4096, 128)),
            "state_conv": f(np.asarray(state_conv[0, 16 * c:16 * (c + 1)]).reshape(48, 6144)),
            "halo_mask": hm, "alpha": al,
        })
        in_maps.append(m)
    return in_maps


def kernel(x_prompt, x_sample, cache_k, cache_v, state_ssm, state_conv, norm_pre, w_in, conv_w,
           conv_b, dt_bias, a_log, d_skip, ssm_norm, attn_sinks, w_attn_br, w_ssm_br, w_out, norm_post):
    in_maps = make_in_maps(x_prompt, x_sample, cache_k, cache_v, state_ssm, state_conv, norm_pre, w_in, conv_w,
                           conv_b, dt_bias, a_log, d_skip, ssm_norm, attn_sinks, w_attn_br, w_ssm_br, w_out, norm_post)
    if "nc" not in _NC_CACHE:
        _NC_CACHE["nc"] = build_nc()[0]
    nc = _NC_CACHE["nc"]
    res = run_bass_kernel_spmd(nc, in_maps, core_ids=list(range(NCORES)))
    return assemble(res.results)


def assemble(r):
    f32 = np.float32
    y_prompt = np.zeros((2, 4096, D), f32)
    y_sample = np.zeros((128, 8, D), f32)
    k_p = np.zeros((1, 2, 128, 8, 64), f32)
    v_p = np.zeros((1, 2, 128, 8, 64), f32)
    s_p = np.zeros((1, 2, 64, 64, 128), f32)
    c_p = np.zeros((1, 2, 3, 6144), f32)
    k_s = np.zeros((1, 128, 128, 8, 64), f32)
    v_s = np.zeros((1, 128, 128, 8, 64), f32)
    s_s = np.zeros((1, 128, 64, 64, 128), f32)
    c_s = np.zeros((1, 128, 3, 6144), f32)
    for c in range(NCORES):
        b, j = c // 4, c % 4
        o = r[c]
        y = np.asarray(o["y"], f32)
        y_prompt[b, 1024 * j:1024 * (j + 1)] = y[0:1024]
        y_sample[16 * c:16 * (c + 1)] = y[1024:1152].reshape(16, 8, D)
        k_s[0, 16 * c:16 * (c + 1)] = np.asarray(o["ks"], f32).reshape(16, 128, 8, 64)
        v_s[0, 16 * c:16 * (c + 1)] = np.asarray(o["vs"], f32).reshape(16, 128, 8, 64)
        s_s[0, 16 * c:16 * (c + 1)] = np.asarray(o["ss"], f32).reshape(16, 64, 64, 128)
        c_s[0, 16 * c:16 * (c + 1)] = np.asarray(o["cs"], f32).reshape(16, 3, 6144)
        if j == 3:
            k_p[0, b] = np.asarray(o["kp"], f32).reshape(128, 8, 64)
            v_p[0, b] = np.asarray(o["vp"], f32).reshape(128, 8, 64)
            s_p[0, b] = np.asarray(o["sp"], f32).reshape(64, 64, 128)
            c_p[0, b] = np.asarray(o["cp"], f32)
    return (y_prompt, y_sample, k_p, v_p, s_p, c_p, k_s, v_s, s_s, c_s)
```

```python
import numpy as np
import concourse.bass as bass
import concourse.mybir as mybir
from concourse.bass_utils import run_bass_kernel_spmd

F32 = mybir.dt.float32
BF16 = mybir.dt.bfloat16
AF = mybir.ActivationFunctionType
ALU = mybir.AluOpType
AX = mybir.AxisListType

NCORES = 8
D = 2048
NT = 10
TOK = NT * 128
NPT = 1152
IN_DIM = 19520
Q0, K0, V0, ZA0, XS0, B0, C0, ZS0, DT0, GA0, GS0 = 0, 2048, 2560, 3072, 5120, 9216, 10240, 11264, 15360, 15424, 17472
EPS = 1e-6
NEG = -30000.0
EPOCH = 60000


class _Stop(Exception):
    pass


class Res:
    __slots__ = ("name", "w", "r", "excl")

    def __init__(self, name="", excl=False):
        self.name = name
        self.w = None
        self.r = {}
        self.excl = excl


class Tok:
    __slots__ = ("key", "val", "clock")

    def __init__(self, key, val, clock):
        self.key, self.val, self.clock = key, val, clock


class DSem:
    def __init__(self, name, inc=16):
        self.key = ("d", name)
        self.count = 0
        self.inc = inc


class Prog:
    ENG = ("pe", "act", "dve", "pool", "sp")

    def __init__(self):
        self.streams = {e: [] for e in self.ENG}
        self.seq = {e: 0 for e in self.ENG}
        self.known = {e: {} for e in self.ENG}
        self.keys = {}
        self.out_toks = []
        self.all_dma = []

    def _deps(self, eng, r, w):
        toks = []
        for res in r:
            if res.w is not None:
                toks.append(res.w)
            if res.excl:
                for k, t in res.r.items():
                    if k[0] != eng:
                        toks.append(t)
        for res in w:
            if res.w is not None:
                toks.append(res.w)
            toks.extend(res.r.values())
        return toks

    def _waits(self, eng, toks):
        known = self.known[eng]
        need = {}
        for t in toks:
            if t.key[0] == "pe" and eng == "pe":
                continue
            if known.get(t.key, 0) >= t.val:
                continue
            if need.get(t.key, 0) < t.val:
                need[t.key] = t.val
        waited = [t for t in toks if need.get(t.key) == t.val and not (t.key[0] == "pe" and eng == "pe")]
        for k in list(need.keys()):
            v = need[k]
            for t in waited:
                if t.key != k and t.clock.get(k, 0) >= v and need.get(t.key) == t.val:
                    del need[k]
                    break
        for t in toks:
            for k, v in t.clock.items():
                if known.get(k, 0) < v:
                    known[k] = v
        return list(need.items())

    def add(self, eng, fn, r=(), w=(), dsem=None, out=False):
        toks = self._deps(eng, r, w)
        waits = self._waits(eng, toks)
        if dsem is not None:
            dsem.count += dsem.inc
            key, val, inc = dsem.key, dsem.count, dsem.inc
        else:
            s = self.seq[eng]
            self.seq[eng] = s + 1
            key, val, inc = (eng, s // EPOCH), s % EPOCH + 1, 1
        self.keys[key] = True
        clock = dict(self.known[eng])
        clock[key] = val
        tok = Tok(key, val, clock)
        if eng == "pe" and dsem is None:
            self.known[eng][key] = val
        for res in w:
            res.w = tok
            res.r = {}
        for res in r:
            old = res.r.get(key)
            if old is None or old.val < val:
                res.r[key] = tok
        self.streams[eng].append((waits, fn, (key, inc)))
        if dsem is not None:
            self.all_dma.append(tok)
        if out:
            self.out_toks.append(tok)
        return tok

    def wait_tokens(self, eng, toks):
        waits = self._waits(eng, toks)
        if waits:
            self.streams[eng].append((waits, None, None))

    def barrier(self):
        last = []
        for e in self.ENG:
            if e == "sp":
                continue
            s = self.seq[e]
            if s > 0:
                s -= 1
                last.append(Tok((e, s // EPOCH), s % EPOCH + 1, {}))
        toks = last + self.all_dma
        self.all_dma = []
        for e in self.ENG:
            self.wait_tokens(e, toks)

    def emit(self, nc, sems):
        semmap = {}
        keys = list(self.keys.keys())
        assert len(keys) <= len(sems), (len(keys), len(sems))
        for k, s in zip(keys, sems):
            semmap[k] = s
        engobj = {"pe": "tensor", "act": "scalar", "dve": "vector", "pool": "gpsimd", "sp": "sync"}
        with nc.Block() as block:
            def mk(ename):
                def body(e):
                    for waits, fn, inc in self.streams[ename]:
                        for k, v in waits:
                            e.wait_ge(semmap[k], v)
                        if fn is not None:
                            ins = fn(e)
                            ins.then_inc(semmap[inc[0]], inc[1])
                return body
            block.tensor(mk("pe"))
            block.scalar(mk("act"))
            block.vector(mk("dve"))
            block.gpsimd(mk("pool"))
            block.sync(mk("sp"))


CST_LAYOUT = {}


def _build_consts():
    cols = []
    off = 0

    def put(name, arr):
        nonlocal off
        arr = np.asarray(arr, np.float32)
        assert arr.shape[0] == 128
        CST_LAYOUT[name] = (off, arr.shape[1])
        cols.append(arr)
        off += arr.shape[1]

    i = np.arange(128)
    put("ident", np.eye(128))
    put("tri", (i[:, None] <= i[None, :]).astype(np.float32))
    same = (i[:, None] // 8 == i[None, :] // 8)
    put("tris", ((i[:, None] <= i[None, :]) & same).astype(np.float32))
    put("mgt", (i[:, None] > i[None, :]).astype(np.float32))
    put("same", same.astype(np.float32))
    put("ones", np.ones((128, 128)))
    put("m16", (i[:, None] // 8 == np.arange(16)[None, :]).astype(np.float32))
    return np.concatenate(cols, axis=1)


def _alibi_tables():
    slopes = np.exp2(-8.0 * np.arange(1, 33, dtype=np.float32) / 32.0).astype(np.float32)
    ql = np.arange(128)[:, None]
    kl = np.arange(256)[None, :]
    dist = (128 + ql) - kl
    valid = (dist >= 0) & (dist < 128)
    bp = np.where(valid[None], -slopes[:, None, None] * dist[None].astype(np.float32), NEG).astype(np.float32)
    t = (np.arange(128) % 8)[:, None]
    s = (np.arange(128) // 8)[:, None]
    j = np.arange(128)[None, :]
    dist_c = 128 + t - j
    valid_c = dist_c < 128
    t2 = (np.arange(128) % 8)[None, :]
    s2 = (np.arange(128) // 8)[None, :]
    dist_n = t - t2
    valid_n = (s == s2) & (dist_n >= 0)
    dist_s = np.concatenate([dist_c, dist_n], axis=1)
    valid_s = np.concatenate([valid_c, valid_n], axis=1)
    bs = np.where(valid_s[None], -slopes[:, None, None] * dist_s[None].astype(np.float32), NEG).astype(np.float32)
    return bp, bs


class Arena:
    def __init__(self, t_f32, nbytes):
        self.t = t_f32
        self.n = nbytes
        self.off = 0
        self.marks = []

    def push(self):
        self.marks.append(self.off)

    def pop(self):
        self.off = self.marks.pop()

    def alloc(self, shape, dt):
        esz = 4 if dt == F32 else 2
        free = 1
        for s in shape[1:]:
            free *= s
        nb = (free * esz + 63) // 64 * 64
        assert self.off + nb <= self.n, ("SBUF arena overflow", self.off, nb, self.n)
        a = self.t[0:shape[0], self.off // 4:(self.off + nb) // 4]
        self.off += nb
        if dt != F32:
            a = a.bitcast(dt)
        a = a[:, 0:free]
        if len(shape) == 3:
            a = a.rearrange("p (a b) -> p a b", a=shape[1])
        elif len(shape) == 4:
            a = a.rearrange("p (a b c) -> p a b c", a=shape[1], b=shape[2])
        return a


def build_nc(stop_after=None, debug=False, nocc=False, prefix=True):
    nc = bass.Bass("TRN2", target_bir_lowering=False)
    cst_np = _build_consts()
    NCST = cst_np.shape[1]

    def din(name, shape):
        return nc.dram_tensor(name, list(shape), F32, kind="ExternalInput").ap()

    def dout(name, shape):
        return nc.dram_tensor(name, list(shape), F32, kind="ExternalOutput").ap()

    xin = din("xin", [TOK, D])
    w_in = din("w_in", [D, IN_DIM])
    w_ab = din("w_attn_br", [D, D])
    w_sb = din("w_ssm_br", [2 * D, D])
    w_o = din("w_out", [D, D])
    cache_k = din("cache_k", [16, 128, 512])
    cache_v = din("cache_v", [16, 128, 512])
    state_ssm = din("state_ssm", [16, 4096, 128])
    state_conv = din("state_conv", [48, 6144])
    norm_pre = din("norm_pre", [D])
    conv_w = din("conv_w", [4, 6144])
    conv_b = din("conv_b", [6144])
    dt_bias = din("dt_bias", [64])
    a_log = din("a_log", [64])
    d_skip = din("d_skip", [64])
    ssm_norm = din("ssm_norm", [4096])
    sinks = din("attn_sinks", [32])
    norm_post = din("norm_post", [D])
    cst = din("cst", [128, NCST])
    bias_p = din("bias_p", [32, 128, 256])
    bias_s = din("bias_s", [32, 128, 256])
    halo_mask = din("halo_mask", [128, 256])
    alpha = din("alpha", [8])
    xprev = din("xprev", [3072, D])
    pflag = din("pflag", [24])

    y_out = dout("y", [NPT, D])
    kp_out = dout("kp", [128, 512])
    vp_out = dout("vp", [128, 512])
    sp_out = dout("sp", [4096, 128])
    cp_out = dout("cp", [3, 6144])
    ks_out = dout("ks", [16, 128, 512])
    vs_out = dout("vs", [16, 128, 512])
    ss_out = dout("ss", [16, 4096, 128])
    cs_out = dout("cs", [48, 6144])

    if debug:
        AT_d = nc.dram_tensor("AT_d", [16, 128, NPT], BF16, kind="ExternalOutput").ap()
        ST_d = nc.dram_tensor("ST_d", [32, 128, NPT], BF16, kind="ExternalOutput").ap()
    else:
        AT_d = nc.dram_tensor("AT_d", [16, 128, NPT], BF16).ap()
        ST_d = nc.dram_tensor("ST_d", [32, 128, NPT], BF16).ap()
    Sst_d = [nc.dram_tensor(f"Sst_d{g}", [128, 512], F32) for g in range(8)]
    ag_in = [nc.dram_tensor(f"ag_in{g}", [128, 520], F32) for g in range(8)]
    ag_out = [nc.dram_tensor(f"ag_out{g}", [NCORES * 128, 520], F32) for g in range(8)]

    P = Prog()
    ARENA_BYTES = 207 * 1024

    from contextlib import ExitStack
    with ExitStack() as estack:
        arena_t = estack.enter_context(nc.sbuf_tensor("arena", [128, ARENA_BYTES // 4], F32))
        psum = [estack.enter_context(nc.psum_tensor(f"ps{i}", [128, 512], F32)) for i in range(8)]
        A = Arena(arena_t, ARENA_BYTES)
        PS = [p[:] for p in psum]
        RPS = [Res(f"ps{i}", excl=True) for i in range(8)]

        def PSb(i):
            return PS[i].bitcast(BF16)

        def dma(q, out, in_, r=(), w=(), dsem=None, outflag=False, slow=False):
            if slow:
                return P.add(q, lambda e: e.dma_start(out=out, in_=in_, allow_slow_non_contiguous=True), r=r, w=w, dsem=dsem, out=outflag)
            return P.add(q, lambda e: e.dma_start(out=out, in_=in_), r=r, w=w, dsem=dsem, out=outflag)

        def mm(out, lhsT, rhs, start=True, stop=True, r=(), w=()):
            return P.add("pe", lambda e: e.matmul(out, lhsT=lhsT, rhs=rhs, start=start, stop=stop), r=r, w=w)

        def tr(out, in_, ident, r=(), w=()):
            return P.add("pe", lambda e: e.transpose(out=out, in_=in_, identity=ident), r=r, w=w)

        def act(out, in_, func, r=(), w=(), bias=None, scale=None, accum_out=None):
            kw = {}
            if bias is not None:
                kw["bias"] = bias
            if scale is not None:
                kw["scale"] = scale
            if accum_out is not None:
                kw["accum_out"] = accum_out
            return P.add("act", lambda e: e.activation(out=out, in_=in_, func=func, **kw), r=r, w=w)

        def tt(eng, out, in0, in1, op, r=(), w=()):
            return P.add(eng, lambda e: e.tensor_tensor(out=out, in0=in0, in1=in1, op=op), r=r, w=w)

        def ts(eng, out, in0, s1, s2, op0, op1=None, r=(), w=(), accum_out=None):
            kw = {}
            if op1 is not None:
                kw["op1"] = op1
            if accum_out is not None:
                kw["accum_out"] = accum_out
            return P.add(eng, lambda e: e.tensor_scalar(out=out, in0=in0, scalar1=s1, scalar2=s2, op0=op0, **kw), r=r, w=w)

        def stt(eng, out, in0, scalar, in1, op0, op1, r=(), w=()):
            return P.add(eng, lambda e: e.scalar_tensor_tensor(out=out, in0=in0, scalar=scalar, in1=in1, op0=op0, op1=op1), r=r, w=w)

        def cp(eng, out, in_, r=(), w=()):
            if eng == "act":
                return P.add("act", lambda e: e.copy(out=out, in_=in_), r=r, w=w)
            return P.add(eng, lambda e: e.tensor_copy(out=out, in_=in_), r=r, w=w)

        def run_interleaved(gens, width):
            it_ = iter(gens)
            active = []
            while True:
                while len(active) < width:
                    try:
                        active.append(next(it_))
                    except StopIteration:
                        break
                if not active:
                    break
                for g_ in list(active):
                    try:
                        next(g_)
                    except StopIteration:
                        active.remove(g_)

        def memset(eng, ap, val, w=()):
            return P.add(eng, lambda e: e.memset(ap, val), w=w)

        R_const = Res("const")
        ds_const = DSem("const")
        cstt = A.alloc([128, NCST], F32)
        dma("sp", cstt, cst[:, :], dsem=ds_const)

        def C(name):
            o, n = CST_LAYOUT[name]
            return cstt[:, o:o + n]
        ident_f, TRI, TRIS, MGT, SAME, ONES, M16 = C("ident"), C("tri"), C("tris"), C("mgt"), C("same"), C("ones"), C("m16")
        ident_b = A.alloc([128, 128], BF16)
        o_id = CST_LAYOUT["ident"][0]
        ds_idb = DSem("idb")
        R_idb = Res("idb")
        dma("pool", ident_b, cst[:, o_id:o_id + 128], w=[R_idb], dsem=ds_idb)
        npre_pk = A.alloc([128, 16], F32)
        cw_pk = A.alloc([128, 48, 4], F32)
        cb_pk = A.alloc([128, 48], F32)
        ssn_pk = A.alloc([128, 32], F32)
        npost_pk = None
        pk_srcs = [(norm_pre, 16), (conv_b, 48), (ssm_norm, 32), (conv_w[0], 48), (conv_w[1], 48), (conv_w[2], 48), (conv_w[3], 48)]
        dtb_bc = A.alloc([128, 64], F32)
        dma("sp", dtb_bc, dt_bias.partition_broadcast(128), dsem=ds_const)
        alog_bc = A.alloc([128, 64], F32)
        dma("sp", alog_bc, a_log.partition_broadcast(128), dsem=ds_const)
        dsk_bc = A.alloc([128, 64], F32)
        dma("sp", dsk_bc, d_skip.partition_broadcast(128), dsem=ds_const)
        sink_bc = A.alloc([128, 32], F32)
        dma("sp", sink_bc, sinks.partition_broadcast(128), dsem=ds_const)
        alpha_bc = A.alloc([128, 8], F32)
        dma("sp", alpha_bc, alpha.partition_broadcast(128), dsem=ds_const)
        hmask = A.alloc([128, 256], F32)
        dma("sp", hmask, halo_mask[:, :], dsem=ds_const)
        R_const.w = Tok(ds_const.key, ds_const.count, {ds_const.key: ds_const.count})
        RC = [R_const]
        A.push()
        pk_tmp = A.alloc([128, 6, 128], F32)
        R_pk = Res("pk")
        ds_pk = DSem("pk")
        pk_dst = [npre_pk, cb_pk, ssn_pk] + [cw_pk[:, :, j] for j in range(4)]
        for i, ((src, nb), dst) in enumerate(zip(pk_srcs, pk_dst)):
            slot = i % 6
            if i == 6:
                P.barrier()
            dma("sp", pk_tmp[0:nb, slot, :], src.rearrange("(b p) -> b p", p=128), w=[R_pk], dsem=ds_pk)
            mm(PS[0][:, 0:nb], pk_tmp[0:nb, slot, :], ident_f[0:nb, 0:nb], r=[R_pk] + RC, w=[RPS[0]])
            cp("dve", dst, PS[0][:, 0:nb], r=[RPS[0]], w=[R_pk])
        RC = [R_const, R_pk, R_idb]
        A.pop()
        P.barrier()

        NPF = 24
        ibank = {"b": 0}

        def ipbank():
            b = ibank["b"]
            ibank["b"] = 1 - b
            return b

        def norm_tile(src_rows, xb, rx, dsx, dst, Rdst, stcol, Rst, junk_, Rjunk):
            dma("sp", xb, src_rows, w=[rx], dsem=dsx)
            act(junk_, xb, AF.Square, r=[rx], w=[Rjunk, Rst], accum_out=stcol)
            ts("dve", stcol, stcol, 1.0 / D, EPS, ALU.mult, ALU.add, r=[Rst], w=[Rst])
            act(stcol, stcol, AF.Sqrt, r=[Rst], w=[Rst])
            P.add("dve", lambda e, o=stcol: e.reciprocal(out=o, in_=o), r=[Rst], w=[Rst])
            tt("pool", xb, xb, stcol.broadcast_to([128, D]), ALU.mult, r=[rx, Rst], w=[rx])
            for b in range(4):
                bank = b % 2
                for q in range(4):
                    kb = b * 4 + q
                    mm(PS[bank][:, q * 128:(q + 1) * 128], xb[:, kb * 128:(kb + 1) * 128], ident_f,
                       r=[rx] + RC, w=[RPS[bank]])
                src = PS[bank].rearrange("p (a b) -> p a b", a=4)
                sc_ = npre_pk[:, b * 4:(b + 1) * 4].unsqueeze(2).broadcast_to([128, 4, 128])
                tt("dve", dst(b), src, sc_, ALU.mult, r=[RPS[bank]] + RC, w=[Rdst])

        R_SstD = [Res(f"SstD{g}") for g in range(8)]
        if prefix:
            A.push()
            hTp = A.alloc([128, 16, NPF * 128], BF16)
            R_hTp = [Res(f"hTp{t}") for t in range(NPF)]
            pw = [A.alloc([128, 16, 512], BF16) for _ in range(2)]
            R_pw, ds_pw = [Res("pw0"), Res("pw1")], [DSem("pw0"), DSem("pw1")]
            pwn = {"n": 0}

            def load_p(src, n):
                i = pwn["n"] % 2
                pwn["n"] += 1
                dma("pool", pw[i][:, :, 0:n], src.rearrange("(k p) n -> p k n", p=128), w=[R_pw[i]], dsem=ds_pw[i])
                return pw[i], R_pw[i]

            pxb = [A.alloc([128, D], F32) for _ in range(2)]
            R_px, ds_px = [Res("px0"), Res("px1")], [DSem("px0"), DSem("px1")]
            pjunk = A.alloc([128, D], BF16)
            R_pj = Res("pjunk")
            pst = A.alloc([128, NPF], F32)
            R_pst = [Res(f"pst{t}") for t in range(NPF)]
            pflag_bc = A.alloc([128, NPF], F32)
            R_pf, ds_pf = Res("pflag"), DSem("pflag")
            dma("sp", pflag_bc, pflag.partition_broadcast(128), w=[R_pf], dsem=ds_pf)
            dtdP = A.alloc([128, NPF, 64], F32)
            eaeP = A.alloc([128, NPF, 64], F32)
            R_pdt = [Res(f"pdt{t}") for t in range(NPF)]
            pA_bc = A.alloc([128, 64], F32)
            R_pA = Res("pA")
            ptmp = A.alloc([128, 8, 64], F32)
            R_pt = Res("ptmp")
            for t in range(NPF):
                norm_tile(xprev[t * 128:(t + 1) * 128, :], pxb[t % 2], R_px[t % 2], ds_px[t % 2],
                          lambda b, t=t: hTp[:, b * 4:(b + 1) * 4, t * 128:(t + 1) * 128], R_hTp[t],
                          pst[:, t:t + 1], R_pst[t], pjunk, R_pj)
            act(pA_bc, alog_bc, AF.Exp, r=RC, w=[R_pA])
            ts("dve", pA_bc, pA_bc, -1.0, None, ALU.mult, r=[R_pA], w=[R_pA])
            wtd, Rwd = load_p(w_in[:, DT0:DT0 + 64], 64)
            for t in range(NPF):
                bank = ipbank()
                for k in range(16):
                    mm(PS[bank][:, 0:64], hTp[:, k, t * 128:(t + 1) * 128], wtd[:, k, 0:64],
                       start=(k == 0), stop=(k == 15), r=[Rwd, R_hTp[t]], w=[RPS[bank]])
                xx, ax, ee, ll, tmp, dtp, adtp, at_ = [ptmp[:, n, :] for n in range(8)]
                tt("dve", xx, PS[bank][:, 0:64], dtb_bc, ALU.add, r=[RPS[bank]] + RC, w=[R_pt])
                act(ax, xx, AF.Abs, r=[R_pt], w=[R_pt])
                act(ee, ax, AF.Exp, scale=-1.0, r=[R_pt], w=[R_pt])
                ts("dve", ee, ee, 1.0, None, ALU.add, r=[R_pt], w=[R_pt])
                act(ll, ee, AF.Ln, r=[R_pt], w=[R_pt])
                stt("dve", dtp, xx, 0.0, ll, ALU.max, ALU.add, r=[R_pt], w=[R_pt])
                ts("dve", dtp, dtp, pflag_bc[:, t:t + 1], None, ALU.mult, r=[R_pt, R_pf], w=[R_pt])
                tt("dve", adtp, dtp, pA_bc, ALU.mult, r=[R_pt, R_pA], w=[R_pt])
                b2 = ipbank()
                mm(PS[b2][:, 0:64], TRI, adtp, r=[R_pt] + RC, w=[RPS[b2]])
                mm(PS[b2][:, 64:128], ONES, adtp, r=[R_pt] + RC, w=[RPS[b2]])
                cp("dve", at_, PS[b2][:, 0:64], r=[RPS[b2]], w=[R_pt])
                tt("dve", tmp, PS[b2][:, 64:128], at_, ALU.subtract, r=[RPS[b2], R_pt], w=[R_pt])
                act(tmp, tmp, AF.Exp, r=[R_pt], w=[R_pt])
                tt("dve", dtdP[:, t, :], dtp, tmp, ALU.mult, r=[R_pt], w=[R_pdt[t]])
                act(eaeP[:, t, :], PS[b2][:, 64:128], AF.Exp, r=[RPS[b2]], w=[R_pdt[t]])
            pxp = A.alloc([128, 5, 515], F32)
            R_pxp = [Res(f"pxp{b}") for b in range(5)]
            pacc2 = [A.alloc([128, 512], F32) for _ in range(5)]
            R_pacc2 = [Res(f"pacc{i}") for i in range(5)]
            pcv2 = [A.alloc([128, 5, 512], BF16) for _ in range(2)]
            R_pcv2 = [[Res(f"pcv{i}_{b}") for b in range(5)] for i in range(2)]
            pxt = [A.alloc([128, 640], BF16) for _ in range(2)]
            R_pxt = [Res("pxt0"), Res("pxt1")]
            pxd = [A.alloc([128, 512], BF16) for _ in range(2)]
            R_pxd = [Res("pxd0"), Res("pxd1")]
            pST = [A.alloc([128, 512], F32)] * 2
            R_pST, ds_pST = [Res("pST0")] * 2, [DSem("pST0")] * 2
            pc = {"x": 0}
            for g in range(8):
                wxa, Rxa = load_p(w_in[:, XS0 + 512 * g:XS0 + 512 * (g + 1)], 512)
                wxb, Rxb = load_p(w_in[:, B0 + 128 * g:B0 + 128 * (g + 1)], 128)
                STp, RSTp = pST[g % 2], R_pST[g % 2]
                memset("pool", STp, 0.0, w=[RSTp])
                for blk in range(5):
                    memset("pool", pxp[:, blk, 0:3], 0.0, w=[R_pxp[blk]])
                blocks_done = {}
                blk_active = {}
                free_pacc = [0, 1]
                free_tail = [0, 1]
                tails_tr = {}
                upd_done = {-1: True}

                def blk_gen(s, blk):
                    tiles = [R_hTp[t] for t in range(4 * s, 4 * s + 4)]
                    wsl, Rws, c0 = (wxa, Rxa, blk * 128) if blk < 4 else (wxb, Rxb, 0)
                    cbi = (4 * g + blk) if blk < 4 else (32 + g)
                    while blk_active.get(blk):
                        yield
                    blk_active[blk] = True
                    pslot = blk
                    pacc_, R_pacc_ = pacc2[pslot], R_pacc2[pslot]
                    pcv_, R_pcv_ = pcv2[s % 2], R_pcv2[s % 2]
                    bank = (0, 1, 3, 4, 5)[blk]
                    for k in range(16):
                        mm(PS[bank], wsl[:, k, c0:c0 + 128], hTp[:, k, s * 512:(s + 1) * 512],
                           start=(k == 0), stop=(k == 15), r=[Rws] + tiles, w=[RPS[bank]])
                    xp = pxp[:, blk, :]
                    cp("act", xp[:, 3:515], PS[bank], r=[RPS[bank]], w=[R_pxp[blk]])
                    yield
                    ts("dve", pacc_, xp[:, 0:512], cw_pk[:, cbi, 0:1], None, ALU.mult, r=[R_pxp[blk]] + RC, w=[R_pacc_])
                    yield
                    for j in range(1, 4):
                        stt("dve", pacc_, xp[:, j:j + 512], cw_pk[:, cbi, j:j + 1], pacc_, ALU.mult, ALU.add,
                            r=[R_pxp[blk], R_pacc_] + RC, w=[R_pacc_])
                        yield
                    while s >= 2 and tails_tr.get(s - 2, 0) < 4:
                        yield
                    act(pcv_[:, blk, :], pacc_, AF.Silu, bias=cb_pk[:, cbi:cbi + 1], r=[R_pacc_] + RC, w=[R_pcv_[blk]])
                    cp("pool", xp[:, 0:3], xp[:, 512:515], r=[R_pxp[blk]], w=[R_pxp[blk]])
                    blocks_done[s] = blocks_done.get(s, 0) + 1
                    blk_active[blk] = False

                def tail_gen(s, q):
                    t = 4 * s + q
                    pcv_, R_pcv_ = pcv2[s % 2], R_pcv2[s % 2]
                    while blocks_done.get(s, 0) < 5 or not free_tail:
                        yield
                    xi = free_tail.pop()
                    bT = 7 if xi == 0 else 2
                    for blk in range(5):
                        tr(PSb(bT)[:, blk * 128:(blk + 1) * 128], pcv_[:, blk, q * 128:(q + 1) * 128], ident_b,
                           r=[R_pcv_[blk]] + RC, w=[RPS[bT]])
                    tails_tr[s] = tails_tr.get(s, 0) + 1
                    cp("act", pxt[xi], PSb(bT)[:, 0:640], r=[RPS[bT]], w=[R_pxt[xi]])
                    yield
                    tt("pool", pxd[xi].rearrange("p (h d) -> p h d", h=8), pxt[xi][:, 0:512].rearrange("p (h d) -> p h d", h=8),
                       dtdP[:, t, 8 * g:8 * g + 8].unsqueeze(2).broadcast_to([128, 8, 64]), ALU.mult,
                       r=[R_pxt[xi], R_pdt[t]], w=[R_pxd[xi]])
                    yield
                    while not upd_done.get(t - 1):
                        yield
                    mm(PS[6], pxt[xi][:, 512:640], pxd[xi], r=[R_pxt[xi], R_pxd[xi]], w=[RPS[6]])
                    tt("dve", STp.rearrange("p (h d) -> p h d", h=8), STp.rearrange("p (h d) -> p h d", h=8),
                       eaeP[:, t, 8 * g:8 * g + 8].unsqueeze(2).broadcast_to([128, 8, 64]), ALU.mult,
                       r=[RSTp, R_pdt[t]], w=[RSTp])
                    tt("dve", STp, STp, PS[6], ALU.add, r=[RSTp, RPS[6]], w=[RSTp])
                    upd_done[t] = True
                    free_tail.append(xi)

                def all_gens():
                    for s in range(NPF // 4):
                        for blk in range(5):
                            yield blk_gen(s, blk)
                        for q in range(4):
                            yield tail_gen(s, q)

                run_interleaved(all_gens(), 6)
                dma("sp", Sst_d[g].ap(), STp, r=[RSTp], w=[R_SstD[g]], dsem=ds_pST[g % 2])
            A.pop()
            P.barrier()

        hT = A.alloc([128, 16, TOK], BF16)
        R_hT = [Res(f"hT{t}") for t in range(NT)]

        NSLOT = 2
        wslot = [A.alloc([128, 16, 512], BF16) for _ in range(NSLOT)]
        R_w = [Res(f"w{i}") for i in range(NSLOT)]
        ds_w = [DSem(f"w{i}") for i in range(NSLOT)]
        wstate = {"n": 0}

        def load_w(segs):
            i = wstate["n"] % NSLOT
            wstate["n"] += 1
            for k, (src, off) in enumerate(segs):
                n = src.shape[1]
                dma("pool", wslot[i][:, :, off:off + n], src.rearrange("(k p) n -> p k n", p=128),
                    w=[R_w[i]] if k == 0 else [], dsem=ds_w[i])
                if k > 0:
                    R_w[i].w = P.all_dma[-1]
            return wslot[i], R_w[i]

        def CK(name):
            if stop_after == name:
                raise _Stop()

        try:
            A.push()
            xbuf = [A.alloc([128, D], F32) for _ in range(2)]
            R_x = [Res("x0"), Res("x1")]
            ds_x = [DSem("x0"), DSem("x1")]
            junk = A.alloc([128, D], F32)
            R_junk = Res("junk")
            ssq = A.alloc([128, NT], F32)
            rstd = A.alloc([128, NT], F32)
            R_st = [Res(f"st{t}") for t in range(NT)]
            for t in range(NT):
                xb, rx = xbuf[t % 2], R_x[t % 2]
                dma("sp", xb, xin[t * 128:(t + 1) * 128, :], w=[rx], dsem=ds_x[t % 2])
                act(junk, xb, AF.Square, r=[rx], w=[R_junk, R_st[t]], accum_out=ssq[:, t:t + 1])
                ts("dve", rstd[:, t:t + 1], ssq[:, t:t + 1], 1.0 / D, EPS, ALU.mult, ALU.add, r=[R_st[t]], w=[R_st[t]])
                act(rstd[:, t:t + 1], rstd[:, t:t + 1], AF.Sqrt, r=[R_st[t]], w=[R_st[t]])
                P.add("dve", lambda e, o=rstd[:, t:t + 1]: e.reciprocal(out=o, in_=o), r=[R_st[t]], w=[R_st[t]])
                tt("pool", xb, xb, rstd[:, t:t + 1].broadcast_to([128, D]), ALU.mult, r=[rx, R_st[t]], w=[rx])
                for b in range(4):
                    bank = b % 2
                    for q in range(4):
                        kb = b * 4 + q
                        mm(PS[bank][:, q * 128:(q + 1) * 128], xb[:, kb * 128:(kb + 1) * 128], ident_f,
                           r=[rx] + RC, w=[RPS[bank]])
                    eng = "dve" if b % 2 == 0 else "pool"
                    src = PS[bank].rearrange("p (a b) -> p a b", a=4)
                    dst = hT[:, b * 4:(b + 1) * 4, t * 128:(t + 1) * 128]
                    sc = npre_pk[:, b * 4:(b + 1) * 4].unsqueeze(2).broadcast_to([128, 4, 128])
                    tt("dve", dst, src, sc, ALU.mult, r=[RPS[bank]] + RC, w=[R_hT[t]])
            A.pop()
            P.barrier()
            CK("p0")
            A.push()
            qT = A.alloc([128, 2, NPT], BF16)
            R_qT = Res("qT")
            kT = A.alloc([128, TOK], BF16)
            R_kT = Res("kT")
            vb = A.alloc([128, NT, 64], BF16)
            R_vb = [Res(f"vb{t}") for t in range(NT)]
            sz = A.alloc([128, NT, 256], BF16)
            R_sz = [Res(f"sz{t}") for t in range(NT)]
            kvnew = A.alloc([128, 2, 2, 512], F32)
            R_kvnew = Res("kvnew")
            biasP = A.alloc([128, 4, 256], F32)
            biasS = A.alloc([128, 4, 256], F32)
            R_bias = Res("bias")
            ds_bias = DSem("bias")
            Sb = [A.alloc([128, 4, 256], F32) for _ in range(2)]
            Pb = [A.alloc([128, 4, 256], BF16) for _ in range(2)]
            PTs = [A.alloc([128, 8, 128], BF16) for _ in range(2)]
            stat = [A.alloc([128, 8, 4], F32) for _ in range(2)]
            An = [A.alloc([128, 256], F32) for _ in range(2)]
            Ag = [A.alloc([128, 256], BF16) for _ in range(2)]
            R_it = [[Res(f"it{i}_{n}") for n in range(8)] for i in range(2)]
            ATs = A.alloc([128, 2, NPT], BF16)
            R_ATs = Res("ATs")
            ds_AT = DSem("AT")
            R_ATd = Res("ATd")
            kc = A.alloc([128, 16, 128], BF16)
            vc = A.alloc([128, 16, 64], BF16)
            R_kc, R_vc = Res("kc"), Res("vc")
            ds_kc = DSem("kc")
            ds_vc = DSem("vc")
            KcT = A.alloc([128, 16, 128], BF16)
            R_KcT = Res("KcT")
            Zq = [A.alloc([128, 16, 248], BF16) for _ in range(2)]
            R_Zq = [Res("Zq0"), Res("Zq1")]
            ZP = A.alloc([128, 2, 16, 248], BF16)
            R_ZP = Res("ZP")
            memset("pool", Zq[0], 0.0, w=[R_Zq[0]])
            memset("pool", Zq[1], 0.0, w=[R_Zq[1]])
            memset("pool", ZP, 0.0, w=[R_ZP])
            itc = {"n": 0, "bank": 0}

            for g in range(8):
                wt, Rw = load_w([(w_in[:, Q0 + 256 * g:Q0 + 256 * (g + 1)], 0),
                                 (w_in[:, K0 + 64 * g:K0 + 64 * (g + 1)], 256),
                                 (w_in[:, K0 + 64 * g:K0 + 64 * (g + 1)], 320),
                                 (w_in[:, V0 + 64 * g:V0 + 64 * (g + 1)], 384)])
                wt2, Rw2 = load_w([(w_in[:, ZA0 + 256 * g:ZA0 + 256 * (g + 1)], 0)])
                dma("sp", biasP, bias_p[4 * g:4 * g + 4].rearrange("h p k -> p h k"), w=[R_bias], dsem=ds_bias)
                dma("sp", biasS, bias_s[4 * g:4 * g + 4].rearrange("h p k -> p h k"), dsem=ds_bias)
                R_bias.w = P.all_dma[-1]
                dma("pool", kc[:, :, 0:64], cache_k[:, :, 64 * g:64 * (g + 1)].rearrange("s k d -> k s d"), w=[R_kc], dsem=ds_kc)
                dma("pool", kc[:, :, 64:128], cache_k[:, :, 64 * g:64 * (g + 1)].rearrange("s k d -> k s d"), dsem=ds_kc)
                R_kc.w = P.all_dma[-1]
                dma("pool", vc, cache_v[:, :, 64 * g:64 * (g + 1)].rearrange("s k d -> k s d"), w=[R_vc], dsem=ds_vc)

                if g == 0:
                    CK("a0")
                for blk in range(3):
                    ranges = [(128, 512), (640, 512), (1152, 128)] if blk < 2 else [(0, 512), (512, 512), (1024, 256)]
                    for (t0, n) in ranges:
                        bank = ipbank()
                        tiles = [R_hT[t] for t in range(t0 // 128, (t0 + n) // 128)]
                        for k in range(16):
                            mm(PS[bank][:, 0:n], wt[:, k, blk * 128:(blk + 1) * 128], hT[:, k, t0:t0 + n],
                               start=(k == 0), stop=(k == 15), r=[Rw] + tiles, w=[RPS[bank]])
                        if blk < 2:
                            ts("dve", qT[:, blk, t0 - 128:t0 - 128 + n], PS[bank][:, 0:n], 0.125, None, ALU.mult,
                               r=[RPS[bank]], w=[R_qT])
                        else:
                            cp("act", kT[:, t0:t0 + n], PS[bank][:, 0:n], r=[RPS[bank]], w=[R_kT])
                if g == 0:
                    CK("a1")
                for t in range(NT):
                    bank = ipbank()
                    for k in range(16):
                        mm(PS[bank][:, 0:128], hT[:, k, t * 128:(t + 1) * 128], wt[:, k, 320:448],
                           start=(k == 0), stop=(k == 15), r=[Rw, R_hT[t]], w=[RPS[bank]])
                    if t >= 1:
                        for k in range(16):
                            mm(PS[bank][:, 128:384], hT[:, k, t * 128:(t + 1) * 128], wt2[:, k, 0:256],
                               start=(k == 0), stop=(k == 15), r=[Rw2, R_hT[t]], w=[RPS[bank]])
                    cp("dve", vb[:, t, :], PS[bank][:, 64:128], r=[RPS[bank]], w=[R_vb[t]])
                    if t >= 8:
                        cp("dve", kvnew[:, t - 8, :, 64 * g:64 * (g + 1)], PS[bank][:, 0:128].rearrange("p (a b) -> p a b", a=2),
                           r=[RPS[bank]], w=[R_kvnew])
                    if t >= 1:
                        act(sz[:, t, :], PS[bank][:, 128:384], AF.Silu, r=[RPS[bank]], w=[R_sz[t]])

                if g == 0:
                    CK("a_inproj")
                for hf in range(2):
                    for s8 in range(8):
                        s = hf * 8 + s8
                        tr(PSb(7)[:, s8 * 128:(s8 + 1) * 128], kc[:, s, :], ident_b, r=[R_kc] + RC, w=[RPS[7]])
                    cp("act", KcT[:, hf * 8:(hf + 1) * 8, :], PSb(7).rearrange("p (a b) -> p a b", a=8), r=[RPS[7]], w=[R_KcT])
                for b in range(2):
                    cp("pool", Zq[b][:, :, 120:128], qT[:, b, 1024:1152].rearrange("p (s t) -> p s t", s=16),
                       r=[R_qT], w=[R_Zq[b]])

                if g == 0:
                    CK("a3")
                def attn_gen(i):
                    sample = (i == 9)
                    it = (i - 1) % 2
                    bS = (2, 3) if it == 0 else (0, 1)
                    bPT = 4 if it == 0 else 7
                    bO = 5 if it == 0 else 6
                    RS, RP, RPT, RST, RAN, RAG = R_it[it][0:6]
                    S_, P_, PT_, st_, An_, Ag_ = Sb[it], Pb[it], PTs[it], stat[it], An[it], Ag[it]
                    rmax, mx, negm, rsum, smm, es, den, rec = [st_[:, n, :] for n in range(8)]
                    for j in range(4):
                        b, jj = j // 2, j % 2
                        pr = slice(jj * 64, (jj + 1) * 64)
                        reg = PS[bS[jj]][:, b * 256:(b + 1) * 256]
                        if not sample:
                            mm(reg, qT[pr, b, (i - 1) * 128:i * 128], kT[pr, (i - 1) * 128:(i + 1) * 128],
                               r=[R_qT, R_kT], w=[RPS[bS[jj]]])
                        else:
                            for s in range(16):
                                mm(reg[:, 0:128], Zq[b][pr, s, 120 - 8 * s:248 - 8 * s], KcT[pr, s, :],
                                   start=(s == 0), stop=(s == 15), r=[R_Zq[b], R_KcT], w=[RPS[bS[jj]]])
                            mm(reg[:, 128:256], qT[pr, b, 1024:1152], kT[pr, 1152:1280], r=[R_qT, R_kT], w=[RPS[bS[jj]]])
                    bias_t = biasS if sample else biasP
                    for jj in range(2):
                        tt("dve", S_.rearrange("p (b j) k -> p b j k", j=2)[:, :, jj, :], PS[bS[jj]].rearrange("p (a b) -> p a b", a=2),
                           bias_t.rearrange("p (b j) k -> p b j k", j=2)[:, :, jj, :], ALU.add, r=[RPS[bS[jj]], R_bias], w=[RS])
                    yield
                    if i == 1:
                        tt("dve", S_, S_, hmask.unsqueeze(1).broadcast_to([128, 4, 256]), ALU.add, r=[RS] + RC, w=[RS])
                    P.add("dve", lambda e, o=rmax, s_=S_: e.tensor_reduce(out=o, in_=s_, axis=AX.X, op=ALU.max), r=[RS], w=[RST])
                    tt("dve", mx, rmax, sink_bc[:, 4 * g:4 * g + 4], ALU.max, r=[RST] + RC, w=[RST])
                    ts("dve", negm, mx, -1.0, None, ALU.mult, r=[RST], w=[RST])
                    tt("dve", smm, sink_bc[:, 4 * g:4 * g + 4], mx, ALU.subtract, r=[RST] + RC, w=[RST])
                    yield
                    for j in range(4):
                        act(P_[:, j, :], S_[:, j, :], AF.Exp, bias=negm[:, j:j + 1], accum_out=rsum[:, j:j + 1],
                            r=[RS, RST], w=[RP, RST])
                    act(es, smm, AF.Exp, r=[RST], w=[RST])
                    yield
                    tt("dve", den, rsum, es, ALU.add, r=[RST], w=[RST])
                    P.add("dve", lambda e, o=rec, d_=den: e.reciprocal(out=o, in_=d_), r=[RST], w=[RST])
                    yield
                    for j in range(4):
                        for hf in range(2):
                            tr(PSb(bPT)[:, (2 * j + hf) * 128:(2 * j + hf + 1) * 128], P_[:, j, hf * 128:(hf + 1) * 128], ident_b,
                               r=[RP] + RC, w=[RPS[bPT]])
                    cp("act", PT_, PSb(bPT).rearrange("p (a b) -> p a b", a=8), r=[RPS[bPT]], w=[RPT])
                    yield
                    if not sample:
                        for j in range(4):
                            mm(PS[bO][:, j * 64:(j + 1) * 64], PT_[:, 2 * j, :], vb[:, i - 1, :], start=True, stop=False,
                               r=[RPT, R_vb[i - 1]], w=[RPS[bO]])
                            mm(PS[bO][:, j * 64:(j + 1) * 64], PT_[:, 2 * j + 1, :], vb[:, i, :], start=False, stop=True,
                               r=[RPT, R_vb[i]], w=[RPS[bO]])
                    else:
                        for b in range(2):
                            src = PT_.rearrange("p (j h) k -> p j h k", h=2)[:, 2 * b:2 * b + 2, 0, :]
                            cp("pool", ZP[:, :, :, 120:128], src.rearrange("p j (s t) -> p j s t", s=16), r=[RPT], w=[R_ZP])
                            for jj in range(2):
                                j = 2 * b + jj
                                for s in range(16):
                                    mm(PS[bO][:, j * 64:(j + 1) * 64], ZP[:, jj, s, 120 - 8 * s:248 - 8 * s], vc[:, s, :],
                                       start=(s == 0), stop=False, r=[R_ZP, R_vc], w=[RPS[bO]])
                                mm(PS[bO][:, j * 64:(j + 1) * 64], PT_[:, 2 * j + 1, :], vb[:, 9, :], start=False, stop=True,
                                   r=[RPT, R_vb[9]], w=[RPS[bO]])
                    tt("dve", An_.rearrange("p (a b) -> p a b", a=4), PS[bO][:, 0:256].rearrange("p (a b) -> p a b", a=4),
                       rec.unsqueeze(2).broadcast_to([128, 4, 64]), ALU.mult, r=[RPS[bO], RST], w=[RAN])
                    tt("pool", Ag_, An_, sz[:, i, :], ALU.mult, r=[RAN, R_sz[i]], w=[RAG])
                    yield
                    for b in range(2):
                        tr(PSb(bO)[:, 512 + b * 128:512 + (b + 1) * 128], Ag_[:, b * 128:(b + 1) * 128], ident_b, r=[RAG] + RC, w=[RPS[bO]])
                    cp("act", ATs[:, :, (i - 1) * 128:i * 128], PSb(bO)[:, 512:768].rearrange("p (a b) -> p a b", a=2),
                       r=[RPS[bO]], w=[R_ATs])
                run_interleaved((attn_gen(i) for i in range(1, NT)), 2)
                dma("sp", AT_d[2 * g:2 * g + 2].rearrange("b p t -> p b t"), ATs, r=[R_ATs], w=[R_ATd], dsem=ds_AT)
                if g == 0:
                    CK("a_g0")

            CK("a_attn")
            ds_kv = DSem("kvout")
            dma("sp", kp_out[:, :], kvnew[:, 0, 0, :], r=[R_kvnew], dsem=ds_kv, outflag=True)
            dma("sp", vp_out[:, :], kvnew[:, 0, 1, :], r=[R_kvnew], dsem=ds_kv, outflag=True)
            for s in range(16):
                dma("sp", ks_out[s, 120:128, :], kvnew[8 * s:8 * s + 8, 1, 0, :], r=[R_kvnew], dsem=ds_kv, outflag=True)
                dma("sp", vs_out[s, 120:128, :], kvnew[8 * s:8 * s + 8, 1, 1, :], r=[R_kvnew], dsem=ds_kv, outflag=True)
            dma("sp", ks_out[:, 0:120, :], cache_k[:, 8:128, :], dsem=ds_kv, outflag=True)
            dma("sp", vs_out[:, 0:120, :], cache_v[:, 8:128, :], dsem=ds_kv, outflag=True)
            A.pop()
            P.barrier()
            A.push()
            NTT = 9
            dt_all = A.alloc([128, NTT, 64], F32)
            adt_all = A.alloc([128, NTT, 64], F32)
            eat_all = A.alloc([128, NTT, 64], F32)
            dtd_all = A.alloc([128, NTT, 64], F32)
            eaend_all = A.alloc([128, NTT, 64], F32)
            eatg_all = None if prefix else A.alloc([128, NTT, 64], F32)
            R_dt = [Res(f"dt{t}") for t in range(NTT)]
            A_bc = A.alloc([128, 64], F32)
            cum_bc = A.alloc([128, 64], F32)
            R_cum = Res("cum")
            E_samp = A.alloc([128, 512], F32)
            R_Es = Res("Es")
            A.push()
            b0tmp = A.alloc([128, 8, 64], F32)
            R_b0 = Res("b0tmp")
            Xeo = A.alloc([128, 2, 512], F32)
            R_Xeo = Res("Xeo")

            act(A_bc, alog_bc, AF.Exp, r=RC, w=[R_cum])
            ts("dve", A_bc, A_bc, -1.0, None, ALU.mult, r=[R_cum], w=[R_cum])
            memset("pool", cum_bc, 0.0, w=[R_cum])
            wt, Rw = load_w([(w_in[:, DT0:DT0 + 64], 0)])
            for t in range(1, NT):
                ti = t - 1
                smp = (t == 9)
                bank = ipbank()
                for k in range(16):
                    mm(PS[bank][:, 0:64], hT[:, k, t * 128:(t + 1) * 128], wt[:, k, 0:64],
                       start=(k == 0), stop=(k == 15), r=[Rw, R_hT[t]], w=[RPS[bank]])
                xx, ax, ee, ll, tmp = [b0tmp[:, n, :] for n in range(5)]
                tt("dve", xx, PS[bank][:, 0:64], dtb_bc, ALU.add, r=[RPS[bank]] + RC, w=[R_b0])
                act(ax, xx, AF.Abs, r=[R_b0], w=[R_b0])
                act(ee, ax, AF.Exp, scale=-1.0, r=[R_b0], w=[R_b0])
                ts("dve", ee, ee, 1.0, None, ALU.add, r=[R_b0], w=[R_b0])
                act(ll, ee, AF.Ln, r=[R_b0], w=[R_b0])
                stt("dve", dt_all[:, ti, :], xx, 0.0, ll, ALU.max, ALU.add, r=[R_b0], w=[R_dt[ti]])
                tt("dve", adt_all[:, ti, :], dt_all[:, ti, :], A_bc, ALU.mult, r=[R_dt[ti], R_cum], w=[R_dt[ti]])
                b2 = ipbank()
                mm(PS[b2][:, 0:64], TRIS if smp else TRI, adt_all[:, ti, :], r=[R_dt[ti]] + RC, w=[RPS[b2]])
                mm(PS[b2][:, 64:128], SAME if smp else ONES, adt_all[:, ti, :], r=[R_dt[ti]] + RC, w=[RPS[b2]])
                at_ = b0tmp[:, 5, :]
                cp("dve", at_, PS[b2][:, 0:64], r=[RPS[b2]], w=[R_b0])
                act(eat_all[:, ti, :], PS[b2][:, 0:64], AF.Exp, r=[RPS[b2]], w=[R_dt[ti]])
                tt("dve", tmp, PS[b2][:, 64:128], at_, ALU.subtract, r=[RPS[b2], R_b0], w=[R_b0])
                act(tmp, tmp, AF.Exp, r=[R_b0], w=[R_b0])
                tt("dve", dtd_all[:, ti, :], dt_all[:, ti, :], tmp, ALU.mult, r=[R_b0, R_dt[ti]], w=[R_dt[ti]])
                act(eaend_all[:, ti, :], PS[b2][:, 64:128], AF.Exp, r=[RPS[b2]], w=[R_dt[ti]])
                if not smp and prefix:
                    tt("dve", cum_bc, PS[b2][:, 64:128], cum_bc, ALU.add, r=[RPS[b2], R_cum], w=[R_cum])
                elif not smp:
                    atg = b0tmp[:, 6, :]
                    tt("dve", atg, at_, cum_bc, ALU.add, r=[R_b0, R_cum], w=[R_b0])
                    act(eatg_all[:, ti, :], atg, AF.Exp, r=[R_b0], w=[R_dt[ti]])
                    tt("dve", cum_bc, PS[b2][:, 64:128], cum_bc, ALU.add, r=[RPS[b2], R_cum], w=[R_cum])
                else:
                    adv = adt_all[:, ti, :].rearrange("p (i two) -> p i two", two=2)
                    for par in range(2):
                        tt("pool", Xeo[:, par, :].rearrange("p (s i) -> p s i", s=16),
                           adv[:, :, par].unsqueeze(1).broadcast_to([128, 16, 32]),
                           M16.unsqueeze(2).broadcast_to([128, 16, 32]), ALU.mult, r=[R_dt[ti]] + RC, w=[R_Xeo])
                    b3 = ipbank()
                    mm(PS[b3][0:64, :], ONES[:, 0:64], Xeo[:, 0, :], r=[R_Xeo] + RC, w=[RPS[b3]])
                    mm(PS[b3][64:128, :], ONES[:, 0:64], Xeo[:, 1, :], r=[R_Xeo] + RC, w=[RPS[b3]])
                    act(E_samp, PS[b3], AF.Exp, r=[RPS[b3]], w=[R_Es])
            A.pop()
            P.barrier()
            CK("b0")

            sc = A.alloc([128, 768], F32)
            R_sc, ds_sc = Res("sc"), DSem("sc")
            xpP = [A.alloc([128, 1027], F32) for _ in range(2)]
            xpS = [A.alloc([128, 16, 11], F32) for _ in range(2)]
            R_xp = [Res("xp0"), Res("xp1")]
            accP2 = [A.alloc([128, 1024], F32) for _ in range(2)]
            accS2 = [A.alloc([128, 16, 8], F32) for _ in range(2)]
            R_accP2, R_accS2 = [Res("accP0"), Res("accP1")], [Res("accS0"), Res("accS1")]
            convT = A.alloc([128, 5, NPT], BF16)
            R_cv = [Res(f"cv{b}") for b in range(5)]
            CTk = [A.alloc([128, NPT], BF16) for _ in range(2)]
            R_CT = [Res("CT0"), Res("CT1")]
            crow = sc
            R_crow, ds_crow = R_sc, DSem("crow")
            xtok = [A.alloc([128, 640], BF16) for _ in range(2)]
            R_xtok = [Res("xtok0"), Res("xtok1")]
            ylocal = A.alloc([128, NTT, 512], F32)
            R_yl = [Res(f"yl{t}") for t in range(NTT)]
            cbm = [A.alloc([128, 128], F32) for _ in range(2)]
            R_cbm = [Res("cbm0"), Res("cbm1")]
            xdt = [A.alloc([128, 512], BF16) for _ in range(2)]
            xdd = [A.alloc([128, 512], BF16) for _ in range(2)]
            R_xd = [Res("xd0"), Res("xd1")]
            Lt = [A.alloc([128, 4, 128], F32) for _ in range(2)]
            R_L = [Res("L0"), Res("L1")]
            Et = [A.alloc([128, 512], F32) for _ in range(2)]
            R_E = [Res("E0"), Res("E1")]
            MT = [A.alloc([128, 4, 128], BF16) for _ in range(2)]
            R_MT = [Res("MT0"), Res("MT1")]
            t12 = [A.alloc([128, 512], F32) for _ in range(2)]
            t3 = [None, None]
            R_t12, R_t3 = [Res("t12a"), Res("t12b")], [Res("t3a"), Res("t3b")]
            STf = [A.alloc([128, 520], F32) for _ in range(1 if prefix else 2)] * (2 if prefix else 1)
            R_STf = [Res("STf0")] * 2 if prefix else [Res("STf0"), Res("STf1")]
            STb = A.alloc([128, 512], BF16)
            R_STb = Res("STb")
            S0nat = [A.alloc([128, 4, 128], F32) for _ in range(2)]
            R_S0, ds_S0 = [Res("S00"), Res("S01")], [DSem("S00"), DSem("S01")]
            S0T = [A.alloc([128, 512], BF16) for _ in range(2)]
            R_S0T = [Res("S0T0"), Res("S0T1")]
            ZC = A.alloc([128, 16, 248], BF16)
            R_ZC = Res("ZC")
            Bm = A.alloc([128, 16, 128], BF16)
            R_Bm = Res("Bm")
            snew = [A.alloc([128, 4, 128], F32)] * 2
            R_sn, ds_sn = [Res("sn0")] * 2, [DSem("sn0")] * 2
            agl = None if prefix else A.alloc([128, 520], F32)
            R_agl, ds_agl = Res("agl"), DSem("agl")
            Sst = None if prefix else A.alloc([128, 512], F32)
            Sstb = None if prefix else A.alloc([128, 512], BF16)
            R_Sst, R_Sstb = Res("Sst"), Res("Sstb")
            cci = A.alloc([128, 4, 8], F32)
            R_cci = Res("cci")
            szs2 = [A.alloc([128, 512], F32) for _ in range(2)]
            yb2 = [A.alloc([128, 512], F32) for _ in range(2)]
            R_szs2, R_yb2 = [Res("szsa"), Res("szsb")], [Res("yba"), Res("ybb")]
            gst2 = [A.alloc([128, 4], F32) for _ in range(2)]
            R_gst2 = [Res("gsta"), Res("gstb")]
            szs, yb = szs2[0], yb2[0]
            gn = [A.alloc([128, 512], BF16) for _ in range(2)]
            R_szs, R_yb, R_gn = Res("szs"), Res("yb"), [Res("gn0"), Res("gn1")]
            gst = A.alloc([128, 4], F32)
            R_gst = Res("gst")
            ssTs = [A.alloc([128, 4, 128], BF16) for _ in range(2)]
            R_ssT, ds_ssT = [Res("ssT0"), Res("ssT1")], [DSem("ssT0"), DSem("ssT1")]
            spst = snew[0]
            R_spst, ds_spst = R_sn[0], DSem("spst")
            R_agin = [Res(f"agin{g}") for g in range(8)]
            R_agout = [Res(f"agout{g}") for g in range(8)]
            ds_agin = DSem("agin")
            R_STd = Res("STd")
            ds_stin = DSem("stin")
            memset("pool", ZC, 0.0, w=[R_ZC])
            hsel = A.alloc([128, 16, 48], BF16)
            R_hsel = Res("hsel")
            cp("pool", hsel.rearrange("p k (s j) -> p k s j", s=16),
               hT[:, :, 1152:1280].rearrange("p k (s t) -> p k s t", s=16)[:, :, :, 5:8], r=[R_hT[9]], w=[R_hsel])
            cnt = {"x": 0, "mt": 0, "s0": 0, "sn": 0, "gn": 0, "ss": 0}

            def bcast_h(ap_h8):
                return ap_h8.unsqueeze(2).broadcast_to([128, 8, 64])

            def v864(ap):
                return ap.rearrange("p (h d) -> p h d", h=8)

            def B1a(g):
                wa, Rwa = load_w([(w_in[:, XS0 + 512 * g:XS0 + 512 * (g + 1)], 0)])
                wb_, Rwb = load_w([(w_in[:, B0 + 128 * g:B0 + 128 * (g + 1)], 0),
                                   (w_in[:, C0 + 128 * g:C0 + 128 * (g + 1)], 128)])
                dma("sp", sc[0:48, 0:512], state_conv[:, 512 * g:512 * (g + 1)], w=[R_sc], dsem=ds_sc)
                dma("sp", sc[0:48, 512:640], state_conv[:, 4096 + 128 * g:4096 + 128 * (g + 1)], dsem=ds_sc)
                dma("sp", sc[0:48, 640:768], state_conv[:, 5120 + 128 * g:5120 + 128 * (g + 1)], dsem=ds_sc)
                R_sc.w = P.all_dma[-1]
                def blk_gen_b(blk):
                    wsl, Rws, c0 = (wa, Rwa, blk * 128) if blk < 4 else (wb_, Rwb, (blk - 4) * 128)
                    cbi = (4 * g + blk) if blk < 4 else (32 + g if blk == 4 else 40 + g)
                    xi = blk % 2
                    xp, xs_, Rxp = xpP[xi], xpS[xi], R_xp[xi]
                    accP, accS, R_aP, R_aS = accP2[xi], accS2[xi], R_accP2[xi], R_accS2[xi]
                    for ri, (t0, n) in enumerate([(0, 512), (512, 512), (1024, 256)]):
                        bank = ipbank()
                        tiles = [R_hT[t] for t in range(t0 // 128, (t0 + n) // 128)]
                        for k in range(16):
                            mm(PS[bank][:, 0:n], wsl[:, k, c0:c0 + 128], hT[:, k, t0:t0 + n],
                               start=(k == 0), stop=(k == 15), r=[Rws] + tiles, w=[RPS[bank]])
                        if ri == 0:
                            cp("act", xp[:, 0:387], PS[bank][:, 125:512], r=[RPS[bank]], w=[Rxp])
                        elif ri == 1:
                            cp("act", xp[:, 387:899], PS[bank][:, 0:512], r=[RPS[bank]], w=[Rxp])
                        else:
                            cp("act", xp[:, 899:1027], PS[bank][:, 0:128], r=[RPS[bank]], w=[Rxp])
                            cp("act", xs_[:, :, 3:11], PS[bank][:, 128:256].rearrange("p (s t) -> p s t", s=16),
                               r=[RPS[bank]], w=[Rxp])
                    bank = ipbank()
                    mm(PS[bank][:, 0:48], sc[0:48, blk * 128:(blk + 1) * 128], ident_f[0:48, 0:48], r=[R_sc] + RC, w=[RPS[bank]])
                    cp("act", xs_[:, :, 0:3], PS[bank][:, 0:48].rearrange("p (s t) -> p s t", s=16), r=[RPS[bank]], w=[Rxp])
                    yield
                    ts("dve", accP, xp[:, 0:1024], cw_pk[:, cbi, 0:1], None, ALU.mult, r=[Rxp] + RC, w=[R_aP])
                    ts("dve", accS, xs_[:, :, 0:8], cw_pk[:, cbi, 0:1], None, ALU.mult, r=[Rxp] + RC, w=[R_aS])
                    yield
                    for j in range(1, 4):
                        stt("dve", accP, xp[:, j:j + 1024], cw_pk[:, cbi, j:j + 1], accP, ALU.mult, ALU.add, r=[Rxp, R_aP] + RC, w=[R_aP])
                        stt("dve", accS, xs_[:, :, j:j + 8], cw_pk[:, cbi, j:j + 1], accS, ALU.mult, ALU.add, r=[Rxp, R_aS] + RC, w=[R_aS])
                        yield
                    if blk < 5:
                        dst, Rd = convT[:, blk, :], R_cv[blk]
                    else:
                        dst, Rd = CTk[g % 2], R_CT[g % 2]
                    act(dst[:, 0:1024], accP, AF.Silu, bias=cb_pk[:, cbi:cbi + 1], r=[R_aP] + RC, w=[Rd])
                    act(dst[:, 1024:1152].rearrange("p (s t) -> p s t", s=16), accS, AF.Silu, bias=cb_pk[:, cbi:cbi + 1],
                        r=[R_aS] + RC, w=[Rd])

                run_interleaved((blk_gen_b(blk) for blk in range(6)), 2)
                for (lhs_of_k, nrow, dst_out, Rt) in (
                        (lambda k: hT[:, k, 1149:1152], 3, cp_out, R_hT[8]),
                        (lambda k: hsel[:, k, :], 48, cs_out, R_hsel)):
                    bank = ipbank()
                    for k in range(16):
                        mm(PS[bank][0:nrow, 0:512], lhs_of_k(k), wa[:, k, 0:512], start=(k == 0), stop=(k == 15),
                           r=[Rwa, Rt], w=[RPS[bank]])
                    cp("dve", crow[0:nrow, 0:512], PS[bank][0:nrow, 0:512], r=[RPS[bank]], w=[R_crow])
                    bank = ipbank()
                    for k in range(16):
                        mm(PS[bank][0:nrow, 0:256], lhs_of_k(k), wb_[:, k, 0:256], start=(k == 0), stop=(k == 15),
                           r=[Rwb, Rt], w=[RPS[bank]])
                    cp("dve", crow[0:nrow, 512:768], PS[bank][0:nrow, 0:256], r=[RPS[bank]], w=[R_crow])
                    dma("sp", dst_out[:, 512 * g:512 * (g + 1)], crow[0:nrow, 0:512], r=[R_crow], dsem=ds_crow, outflag=True)
                    dma("sp", dst_out[:, 4096 + 128 * g:4096 + 128 * (g + 1)], crow[0:nrow, 512:640], r=[R_crow], dsem=ds_crow, outflag=True)
                    dma("sp", dst_out[:, 5120 + 128 * g:5120 + 128 * (g + 1)], crow[0:nrow, 640:768], r=[R_crow], dsem=ds_crow, outflag=True)

            def B1b(g):
                hs = slice(8 * g, 8 * g + 8)
                CT, RCT = CTk[g % 2], R_CT[g % 2]
                STc, RST_ = STf[g % 2], R_STf[g % 2]
                cp("pool", ZC[:, :, 120:128], CT[:, 1024:1152].rearrange("p (s t) -> p s t", s=16), r=[RCT], w=[R_ZC])
                if prefix:
                    dma("sp", STc[:, 0:512], Sst_d[g].ap(), r=[R_SstD[g]], w=[RST_], dsem=ds_stin)
                    cp("act", STb, STc[:, 0:512], r=[RST_], w=[R_STb])
                st_done = {0: True}

                def tile_gen(t):
                    ti = t - 1
                    smp = (t == 9)
                    par = ti % 2
                    bT = 7 if par == 0 else 2
                    bY = 4 if par == 0 else 0
                    bYI = 5 if par == 0 else 1
                    Lt_, R_L_, Et_, R_E_ = Lt[par], R_L[par], Et[par], R_E[par]
                    t12_, R_t12_, t3_, R_t3_ = t12[par], R_t12[par], t3[par], R_t3[par]
                    tsl = slice(ti * 128, (ti + 1) * 128)
                    xi = par
                    xk, Rxk = xtok[xi], R_xtok[xi]
                    for blk in range(5):
                        tr(PSb(bT)[:, blk * 128:(blk + 1) * 128], convT[:, blk, tsl], ident_b, r=[R_cv[blk]] + RC, w=[RPS[bT]])
                    cp("act", xk, PSb(bT)[:, 0:640], r=[RPS[bT]], w=[Rxk])
                    yield
                    mm(PS[bT][:, 384:512], convT[:, 4, tsl], CT[:, tsl], r=[R_cv[4], RCT], w=[RPS[bT]])
                    cb_, Rcb = cbm[ti % 2], R_cbm[ti % 2]
                    tt("dve", cb_, PS[bT][:, 384:512], TRIS if smp else TRI, ALU.mult, r=[RPS[bT]] + RC, w=[Rcb])
                    yield
                    xdt_, xdd_, Rxd = xdt[ti % 2], xdd[ti % 2], R_xd[ti % 2]
                    tt("pool", v864(xdt_), v864(xk[:, 0:512]), bcast_h(dt_all[:, ti, hs]), ALU.mult, r=[Rxk, R_dt[ti]], w=[Rxd])
                    tt("pool", v864(xdd_), v864(xk[:, 0:512]), bcast_h(dtd_all[:, ti, hs]), ALU.mult, r=[Rxk, R_dt[ti]], w=[Rxd])
                    yield
                    for hb in range(2):
                        h0 = 8 * g + 4 * hb
                        tt("pool", Lt_, MGT.unsqueeze(1).broadcast_to([128, 4, 128]),
                           adt_all[:, ti, h0:h0 + 4].unsqueeze(2).broadcast_to([128, 4, 128]), ALU.mult,
                           r=[R_dt[ti]] + RC, w=[R_L_])
                        yield
                        for r_ in range(4):
                            mm(PS[3][:, r_ * 128:(r_ + 1) * 128], Lt_[:, r_, :], TRI, r=[R_L_] + RC, w=[RPS[3]])
                        act(Et_, PS[3], AF.Exp, r=[RPS[3]], w=[R_E_])
                        yield
                        mi = par
                        tt("dve", MT[mi], Et_.rearrange("p (a b) -> p a b", a=4), cb_.unsqueeze(1).broadcast_to([128, 4, 128]),
                           ALU.mult, r=[R_E_, Rcb], w=[R_MT[mi]])
                        yield
                        for r_ in range(4):
                            hh = 4 * hb + r_
                            mm(PS[bY][:, hh * 64:(hh + 1) * 64], MT[mi][:, r_, :], xdt_[:, hh * 64:(hh + 1) * 64],
                               r=[R_MT[mi], Rxd], w=[RPS[bY]])
                        yield
                    have_inter = smp or t >= 2 or prefix
                    while not smp and not st_done.get(ti):
                        yield
                    if smp:
                        s0toks = []
                        for s in range(16):
                            si = cnt["s0"] % 2
                            cnt["s0"] += 1
                            dma("sp", S0nat[si], state_ssm[s, 512 * g:512 * (g + 1), :].rearrange("(q p) n -> p q n", p=128),
                                w=[R_S0[si]], dsem=ds_S0[si])
                            for q in range(4):
                                mm(PS[7][:, q * 128:(q + 1) * 128], S0nat[si][:, q, :], ident_f, r=[R_S0[si]] + RC, w=[RPS[7]])
                            cp("act", S0T[si], PS[7], r=[RPS[7]], w=[R_S0T[si]])
                            mm(PS[bYI], ZC[:, s, 120 - 8 * s:248 - 8 * s], S0T[si], start=(s == 0), stop=(s == 15),
                               r=[R_ZC, R_S0T[si]], w=[RPS[bYI]])
                            if s == 0:
                                tt("pool", Bm, xk[:, 512:640].unsqueeze(1).broadcast_to([128, 16, 128]),
                                   M16.unsqueeze(2).broadcast_to([128, 16, 128]), ALU.mult, r=[Rxk] + RC, w=[R_Bm])
                            for q in range(4):
                                mm(PS[6][:, q * 128:(q + 1) * 128], xdd_[:, q * 128:(q + 1) * 128], Bm[:, s, :],
                                   r=[Rxd, R_Bm], w=[RPS[6]])
                            ni = cnt["sn"] % 2
                            cnt["sn"] += 1
                            ecol = E_samp[:, s * 32 + 4 * g:s * 32 + 4 * g + 4]
                            tt("dve", snew[ni], S0nat[si], ecol.unsqueeze(2).broadcast_to([128, 4, 128]), ALU.mult,
                               r=[R_S0[si], R_Es], w=[R_sn[ni]])
                            tt("dve", snew[ni], snew[ni], PS[6].rearrange("p (q n) -> p q n", q=4), ALU.add,
                               r=[R_sn[ni], RPS[6]], w=[R_sn[ni]])
                            dma("sp", ss_out[s, 512 * g:512 * (g + 1), :].rearrange("(q p) n -> p q n", p=128), snew[ni],
                                r=[R_sn[ni]], dsem=ds_sn[ni], outflag=True)
                            yield
                    elif t >= 2 or prefix:
                        mm(PS[bYI], CT[:, tsl], STb, r=[RCT, R_STb], w=[RPS[bYI]])
                    if have_inter:
                        yield
                        tt("dve", v864(t12_), v864(PS[bYI]), bcast_h(eat_all[:, ti, hs]), ALU.mult, r=[RPS[bYI], R_dt[ti]], w=[R_t12_])
                        tt("dve", t12_, t12_, PS[bY], ALU.add, r=[R_t12_, RPS[bY]], w=[R_t12_])
                    else:
                        cp("dve", t12_, PS[bY], r=[RPS[bY]], w=[R_t12_])
                    tt("pool", v864(ylocal[:, ti, :]), v864(xk[:, 0:512]), bcast_h(dsk_bc[:, hs]), ALU.mult, r=[Rxk] + RC, w=[R_yl[ti]])
                    yield
                    tt("pool", ylocal[:, ti, :], ylocal[:, ti, :], t12_, ALU.add, r=[R_t12_, R_yl[ti]], w=[R_yl[ti]])
                    if not smp:
                        mm(PS[6], xk[:, 512:640], xdd_, r=[Rxk, Rxd], w=[RPS[6]])
                        if t == 1 and not prefix:
                            cp("dve", STc[:, 0:512], PS[6], r=[RPS[6]], w=[RST_])
                        else:
                            tt("dve", v864(STc[:, 0:512]), v864(STc[:, 0:512]), bcast_h(eaend_all[:, ti, hs]), ALU.mult,
                               r=[RST_, R_dt[ti]], w=[RST_])
                            tt("dve", STc[:, 0:512], STc[:, 0:512], PS[6], ALU.add, r=[RST_, RPS[6]], w=[RST_])
                        if t < 8:
                            cp("act", STb, STc[:, 0:512], r=[RST_], w=[R_STb])
                        st_done[t] = True

                run_interleaved((tile_gen(t) for t in range(1, NT)), 2)
                if prefix:
                    for q in range(4):
                        mm(PS[7][:, q * 128:(q + 1) * 128], STc[:, q * 128:(q + 1) * 128], ident_f, r=[RST_] + RC, w=[RPS[7]])
                    cp("act", spst, PS[7].rearrange("p (q n) -> p q n", q=4), r=[RPS[7]], w=[R_spst])
                    dma("sp", sp_out[512 * g:512 * (g + 1), :].rearrange("(q p) n -> p q n", p=128), spst, r=[R_spst],
                        dsem=ds_spst, outflag=True)
                    return
                cp("dve", STc[:, 512:520], cum_bc[:, hs], r=[R_cum], w=[RST_])
                dma("sp", ag_in[g].ap(), STc, r=[RST_], w=[R_agin[g]], dsem=ds_agin)
                dcc = DSem(f"cc{g}", inc=1)
                if nocc:
                    dma("sp", ag_out[g].ap()[0:128, :], ag_in[g].ap(), r=[R_agin[g]], w=[R_agout[g]], dsem=DSem(f"ccx{g}"))
                else:
                  P.add("pool", lambda e, g=g: e.collective_compute(
                    "AllGather", ALU.bypass, replica_groups=[list(range(NCORES))],
                    ins=[ag_in[g].ap().opt()], outs=[ag_out[g].ap().opt()]),
                    r=[R_agin[g]], w=[R_agout[g]], dsem=dcc)

            def B2(g):
                hs = slice(8 * g, 8 * g + 8)
                CT, RCT = CTk[g % 2], R_CT[g % 2]
                STc, RST_ = STf[g % 2], R_STf[g % 2]
                wz, Rwz = load_w([(w_in[:, ZS0 + 512 * g:ZS0 + 512 * (g + 1)], 0)])
                if not prefix:
                    memset("pool", Sst, 0.0, w=[R_Sst])
                for i in range(0 if prefix else NCORES):
                    dma("sp", agl, ag_out[g].ap()[i * 128:(i + 1) * 128, :], r=[R_agout[g]], w=[R_agl], dsem=ds_agl)
                    ei, ci = cci[:, 0, :], cci[:, 1, :]
                    act(ei, agl[:, 512:520], AF.Exp, r=[R_agl], w=[R_cci])
                    ts("dve", ci, ei, -1.0, None, ALU.add, r=[R_cci], w=[R_cci])
                    ts("dve", ci, ci, alpha_bc[:, i:i + 1], 1.0, ALU.mult, ALU.add, r=[R_cci] + RC, w=[R_cci])
                    tt("dve", v864(Sst), v864(Sst), bcast_h(ci), ALU.mult, r=[R_Sst, R_cci], w=[R_Sst])
                    stt("dve", Sst, agl[:, 0:512], alpha_bc[:, i:i + 1], Sst, ALU.mult, ALU.add, r=[R_agl, R_Sst] + RC, w=[R_Sst])
                if not prefix:
                    cp("act", Sstb, Sst, r=[R_Sst], w=[R_Sstb])
                eo = cci[:, 2, :]
                if not prefix:
                    act(eo, STc[:, 512:520], AF.Exp, r=[RST_], w=[R_cci])
                    tt("dve", v864(Sst), v864(Sst), bcast_h(eo), ALU.mult, r=[R_Sst, R_cci, R_Sstb], w=[R_Sst])
                    tt("dve", Sst, Sst, STc[:, 0:512], ALU.add, r=[R_Sst, RST_], w=[R_Sst])
                    for q in range(4):
                        mm(PS[7][:, q * 128:(q + 1) * 128], Sst[:, q * 128:(q + 1) * 128], ident_f, r=[R_Sst] + RC, w=[RPS[7]])
                    cp("act", spst, PS[7].rearrange("p (q n) -> p q n", q=4), r=[RPS[7]], w=[R_spst])
                    dma("sp", sp_out[512 * g:512 * (g + 1), :].rearrange("(q p) n -> p q n", p=128), spst, r=[R_spst],
                        dsem=ds_spst, outflag=True)
                def tile_gen2(t):
                    ti = t - 1
                    par = ti % 2
                    bT = 7 if par == 0 else 2
                    szs, R_szs, yb, R_yb, gst, R_gst = szs2[par], R_szs2[par], yb2[par], R_yb2[par], gst2[par], R_gst2[par]
                    tsl = slice(ti * 128, (ti + 1) * 128)
                    bank = ipbank()
                    for k in range(16):
                        mm(PS[bank], hT[:, k, t * 128:(t + 1) * 128], wz[:, k, 0:512], start=(k == 0), stop=(k == 15),
                           r=[Rwz, R_hT[t]], w=[RPS[bank]])
                    act(szs, PS[bank], AF.Silu, r=[RPS[bank]], w=[R_szs])
                    yield
                    if t <= 8 and not prefix:
                        mm(PS[5], CT[:, tsl], Sstb, r=[RCT, R_Sstb], w=[RPS[5]])
                        tt("dve", v864(yb), v864(PS[5]), bcast_h(eatg_all[:, ti, hs]), ALU.mult, r=[RPS[5], R_dt[ti]], w=[R_yb])
                        tt("pool", yb, yb, ylocal[:, ti, :], ALU.add, r=[R_yb, R_yl[ti]], w=[R_yb])
                        tt("pool", yb, yb, szs, ALU.mult, r=[R_yb, R_szs], w=[R_yb])
                    else:
                        tt("pool", yb, ylocal[:, ti, :], szs, ALU.mult, r=[R_yl[ti], R_szs], w=[R_yb])
                    yield
                    act(szs, yb, AF.Square, accum_out=gst[:, 0:1], r=[R_yb], w=[R_szs, R_gst])
                    yield
                    ts("dve", gst[:, 1:2], gst[:, 0:1], 1.0 / 512.0, EPS, ALU.mult, ALU.add, r=[R_gst], w=[R_gst])
                    yield
                    act(gst[:, 1:2], gst[:, 1:2], AF.Sqrt, r=[R_gst], w=[R_gst])
                    yield
                    P.add("dve", lambda e, o=gst[:, 2:3], i_=gst[:, 1:2]: e.reciprocal(out=o, in_=i_), r=[R_gst], w=[R_gst])
                    yield
                    gi = par
                    ts("dve", gn[gi], yb, gst[:, 2:3], None, ALU.mult, r=[R_yb, R_gst], w=[R_gn[gi]])
                    yield
                    for q in range(4):
                        tr(PSb(bT)[:, q * 128:(q + 1) * 128], gn[gi][:, q * 128:(q + 1) * 128], ident_b, r=[R_gn[gi]] + RC, w=[RPS[bT]])
                    yield
                    oi = cnt["ss"] % 2
                    cnt["ss"] += 1
                    tt("dve", ssTs[oi], PSb(bT)[:, 0:512].rearrange("p (q n) -> p q n", q=4),
                       ssn_pk[:, 4 * g:4 * g + 4].unsqueeze(2).broadcast_to([128, 4, 128]), ALU.mult,
                       r=[RPS[bT]] + RC, w=[R_ssT[oi]])
                    dma("sp", ST_d[4 * g:4 * g + 4, :, tsl].rearrange("b p t -> p b t"), ssTs[oi], r=[R_ssT[oi]], w=[R_STd],
                        dsem=ds_ssT[oi])

                run_interleaved((tile_gen2(t) for t in range(1, NT)), 2)

            for g in range(8):
                B1a(g)
                if g > 0:
                    B2(g - 1)
                B1b(g)
                if g == 0:
                    CK("b_g0")
            B2(7)
            A.pop()
            P.barrier()
            CK("b")
            A.push()
            mergedT = A.alloc([128, 16, NPT], BF16)
            R_mg = [Res(f"mg{t}") for t in range(9)]
            A.push()
            cslots = [wslot[0][:, :, 0:256], wslot[0][:, :, 256:512], wslot[1][:, :, 0:256], wslot[1][:, :, 256:512]]
            cslots += [A.alloc([128, 16, 256], BF16) for _ in range(4)]
            NCS = len(cslots)
            R_cs = [Res(f"cs{i}") for i in range(NCS)]
            ds_cs = [DSem(f"cs{i}") for i in range(NCS)]
            cstate = {"n": 0}

            def load_c(src):
                i = cstate["n"] % NCS
                cstate["n"] += 1
                dma("pool", cslots[i], src.rearrange("(k p) n -> p k n", p=128), w=[R_cs[i]], dsem=ds_cs[i])
                return cslots[i], R_cs[i]

            ATg = [A.alloc([128, 16, 256], BF16) for _ in range(2)]
            STg = [A.alloc([128, 32, 256], BF16) for _ in range(2)]
            R_xg, ds_xg = [Res("xg0"), Res("xg1")], [DSem("xg0"), DSem("xg1")]
            sg = [A.alloc([128, 512], F32) for _ in range(2)]
            R_sg = [Res("sg0"), Res("sg1")]
            mtmp = [A.alloc([128, 512], F32) for _ in range(2)]
            R_mt = [Res("mt0"), Res("mt1")]
            TGS = [(0, 256), (256, 256), (512, 256), (768, 256), (1024, 128)]
            ccnt = {"x": 0, "m": 0}
            for fc in range(8):
                f0 = 256 * fc
                wga, Rga = load_c(w_in[:, GA0 + f0:GA0 + f0 + 256])
                wgs, Rgs = load_c(w_in[:, GS0 + f0:GS0 + f0 + 256])
                wab, Rab = load_c(w_ab[:, f0:f0 + 256])
                ws0, Rs0 = load_c(w_sb[0:2048, f0:f0 + 256])
                ws1, Rs1 = load_c(w_sb[2048:4096, f0:f0 + 256])
                for (o0, n) in TGS:
                    xi = ccnt["x"] % 2
                    ccnt["x"] += 1
                    dma("sp", ATg[xi][:, :, 0:n], AT_d[:, :, o0:o0 + n].rearrange("b p t -> p b t"), r=[R_ATd], w=[R_xg[xi]], dsem=ds_xg[xi])
                    dma("sp", STg[xi][:, :, 0:n], ST_d[:, :, o0:o0 + n].rearrange("b p t -> p b t"), r=[R_STd], dsem=ds_xg[xi])
                    R_xg[xi].w = P.all_dma[-1]
                    htiles = [R_hT[t] for t in range((o0 + 128) // 128, (o0 + 128 + n) // 128)]
                    for sb in range(2):
                        cs_ = slice(sb * 128, (sb + 1) * 128)
                        bG, bP = (2, 3) if ccnt["m"] % 2 == 0 else (4, 5)
                        for k in range(16):
                            mm(PS[bG][:, 0:n], wga[:, k, cs_], hT[:, k, o0 + 128:o0 + 128 + n], start=(k == 0), stop=(k == 15),
                               r=[Rga] + htiles, w=[RPS[bG]])
                        for k in range(16):
                            mm(PS[bG][:, 256:256 + n], wgs[:, k, cs_], hT[:, k, o0 + 128:o0 + 128 + n], start=(k == 0), stop=(k == 15),
                               r=[Rgs] + htiles, w=[RPS[bG]])
                        for k in range(16):
                            mm(PS[bP][:, 0:n], wab[:, k, cs_], ATg[xi][:, k, 0:n], start=(k == 0), stop=(k == 15),
                               r=[Rab, R_xg[xi]], w=[RPS[bP]])
                        for k in range(32):
                            wsx, Rsx = (ws0, Rs0) if k < 16 else (ws1, Rs1)
                            mm(PS[bP][:, 256:256 + n], wsx[:, k % 16, cs_], STg[xi][:, k, 0:n], start=(k == 0), stop=(k == 31),
                               r=[Rsx, R_xg[xi]], w=[RPS[bP]])
                        mi = ccnt["m"] % 2
                        ccnt["m"] += 1
                        act(sg[mi], PS[bG], AF.Sigmoid, r=[RPS[bG]], w=[R_sg[mi]])
                        tt("dve", mtmp[mi], sg[mi], PS[bP], ALU.mult, r=[R_sg[mi], RPS[bP]], w=[R_mt[mi]])
                        mts = [R_mg[t] for t in range(o0 // 128, (o0 + n) // 128)]
                        tt("pool", mergedT[:, 2 * fc + sb, o0:o0 + n], mtmp[mi][:, 0:n], mtmp[mi][:, 256:256 + n], ALU.add,
                           r=[R_mt[mi]], w=mts)
            A.pop()
            P.barrier()
            CK("c")
            hT_flat = hT.rearrange("p a b -> p (a b)")
            wo = [hT_flat[:, 0:8192].rearrange("p (k n) -> p k n", k=16),
                  hT_flat[:, 8192:16384].rearrange("p (k n) -> p k n", k=16), wslot[0], wslot[1]]
            R_wo, ds_wo = [Res(f"wo{i}") for i in range(4)], [DSem(f"wo{i}") for i in range(4)]
            for c in range(4):
                dma("pool", wo[c], w_o[:, 512 * c:512 * (c + 1)].rearrange("(k p) n -> p k n", p=128), w=[R_wo[c]], dsem=ds_wo[c])
            npost_bc = A.alloc([128, D], F32)
            R_np, ds_np = Res("np"), DSem("np")
            dma("sp", npost_bc, norm_post.partition_broadcast(128), w=[R_np], dsem=ds_np)
            o32 = [A.alloc([128, D], F32) for _ in range(2)]
            xr = [A.alloc([128, D], F32) for _ in range(2)]
            R_o32, R_xr = [Res("o0"), Res("o1")], [Res("xr0"), Res("xr1")]
            ds_xr, ds_y = [DSem("xr0"), DSem("xr1")], [DSem("y0"), DSem("y1")]
            dst_ = A.alloc([128, 2, 8], F32)
            R_dst = [Res("dst0"), Res("dst1")]
            djunk = A.alloc([128, 512], BF16)
            R_dj = Res("dj")
            def d_gen(t):
                ti = t - 1
                oi = ti % 2
                bb = 4 if oi == 0 else 0
                dma("sp", xr[oi], xin[t * 128:(t + 1) * 128, :], w=[R_xr[oi]], dsem=ds_xr[oi])
                st_ = dst_[:, oi, :]
                for c in range(4):
                    for k in range(16):
                        mm(PS[bb + c], mergedT[:, k, ti * 128:(ti + 1) * 128], wo[c][:, k, :], start=(k == 0), stop=(k == 15),
                           r=[R_mg[ti], R_wo[c]], w=[RPS[bb + c]])
                    act(djunk, PS[bb + c], AF.Square, accum_out=st_[:, c:c + 1], r=[RPS[bb + c]], w=[R_dj, R_dst[oi]])
                    cp("dve", o32[oi][:, 512 * c:512 * (c + 1)], PS[bb + c], r=[RPS[bb + c]], w=[R_o32[oi]])
                    yield
                tt("dve", st_[:, 4:5], st_[:, 0:1], st_[:, 1:2], ALU.add, r=[R_dst[oi]], w=[R_dst[oi]])
                tt("dve", st_[:, 5:6], st_[:, 2:3], st_[:, 3:4], ALU.add, r=[R_dst[oi]], w=[R_dst[oi]])
                tt("dve", st_[:, 4:5], st_[:, 4:5], st_[:, 5:6], ALU.add, r=[R_dst[oi]], w=[R_dst[oi]])
                ts("dve", st_[:, 4:5], st_[:, 4:5], 1.0 / D, EPS, ALU.mult, ALU.add, r=[R_dst[oi]], w=[R_dst[oi]])
                yield
                act(st_[:, 4:5], st_[:, 4:5], AF.Sqrt, r=[R_dst[oi]], w=[R_dst[oi]])
                P.add("dve", lambda e, o=st_[:, 6:7], i_=st_[:, 4:5]: e.reciprocal(out=o, in_=i_), r=[R_dst[oi]], w=[R_dst[oi]])
                yield
                ts("dve", o32[oi], o32[oi], st_[:, 6:7], None, ALU.mult, r=[R_o32[oi], R_dst[oi]], w=[R_o32[oi]])
                yield
                tt("pool", o32[oi], o32[oi], npost_bc, ALU.mult, r=[R_o32[oi], R_np], w=[R_o32[oi]])
                tt("pool", o32[oi], o32[oi], xr[oi], ALU.add, r=[R_o32[oi], R_xr[oi]], w=[R_o32[oi]])
                dma("sp", y_out[ti * 128:(ti + 1) * 128, :], o32[oi], r=[R_o32[oi]], dsem=ds_y[oi], outflag=True)
            run_interleaved((d_gen(t) for t in range(1, NT)), 2)
            A.pop()
        except _Stop:
            pass
        P.wait_tokens("sp", P.out_toks + P.all_dma)
        sems = [estack.enter_context(nc.semaphore(f"s{i}")) for i in range(len(P.keys))]
        P.emit(nc, sems)
    return nc, P


_NC_CACHE = {}


def make_in_maps(x_prompt, x_sample, cache_k, cache_v, state_ssm, state_conv, norm_pre, w_in, conv_w,
                 conv_b, dt_bias, a_log, d_skip, ssm_norm, attn_sinks, w_attn_br, w_ssm_br, w_out, norm_post):
    f = lambda a: np.ascontiguousarray(np.asarray(a, dtype=np.float32))
    cst_np = _build_consts()
    bp, bs = _alibi_tables()
    shared = {
        "w_in": f(w_in[0]), "w_attn_br": f(w_attn_br[0]), "w_ssm_br": f(w_ssm_br[0]), "w_out": f(w_out[0]),
        "norm_pre": f(norm_pre[0]), "conv_w": f(conv_w[0]), "conv_b": f(conv_b[0]), "dt_bias": f(dt_bias[0]),
        "a_log": f(a_log[0]), "d_skip": f(d_skip[0]), "ssm_norm": f(ssm_norm[0]), "attn_sinks": f(attn_sinks[0]),
        "norm_post": f(norm_post[0]), "cst": cst_np, "bias_p": bp, "bias_s": bs,
    }
    in_maps = []
    for c in range(NCORES):
        b, j = c // 4, c % 4
        xin = np.zeros((TOK, D), np.float32)
        if j > 0:
            xin[0:128] = x_prompt[b, 1024 * j - 128:1024 * j]
        xin[128:1152] = x_prompt[b, 1024 * j:1024 * (j + 1)]
        xin[1152:1280] = np.asarray(x_sample[16 * c:16 * (c + 1)]).reshape(128, D)
        hm = np.zeros((128, 256), np.float32)
        if j == 0:
            hm[:, 0:128] = NEG
        al = np.zeros((8,), np.float32)
        for i in range(4 * b, c):
            al[i] = 1.0
        xprev = np.zeros((3072, D), np.float32)
        pflag = np.zeros((24,), np.float32)
        if j > 0:
            xprev[3072 - 1024 * j:] = x_prompt[b, 0:1024 * j]
            pflag[24 - 8 * j:] = 1.0
        m = dict(shared)
        m.update({
            "xprev": xprev, "pflag": pflag,
            "xin": xin,
            "cache_k": f(np.asarray(cache_k[0, 16 * c:16 * (c + 1)]).reshape(16, 128, 512)),
            "cache_v": f(np.asarray(cache_v[0, 16 * c:16 * (c + 1)]).reshape(16, 128, 512)),
            "state_ssm": f(np.asarray(state_ssm[0, 16 * c:16 * (c + 1)]).reshape(16, 4096, 128)),
            "state_conv": f(np.asarray(state_conv[0, 16 * c:16 * (c + 1)]).reshape(48, 6144)),
            "halo_mask": hm, "alpha": al,
        })
        in_maps.append(m)
    return in_maps


def kernel(x_prompt, x_sample, cache_k, cache_v, state_ssm, state_conv, norm_pre, w_in, conv_w,
           conv_b, dt_bias, a_log, d_skip, ssm_norm, attn_sinks, w_attn_br, w_ssm_br, w_out, norm_post):
    in_maps = make_in_maps(x_prompt, x_sample, cache_k, cache_v, state_ssm, state_conv, norm_pre, w_in, conv_w,
                           conv_b, dt_bias, a_log, d_skip, ssm_norm, attn_sinks, w_attn_br, w_ssm_br, w_out, norm_post)
    if "nc" not in _NC_CACHE:
        _NC_CACHE["nc"] = build_nc()[0]
    nc = _NC_CACHE["nc"]
    res = run_bass_kernel_spmd(nc, in_maps, core_ids=list(range(NCORES)))
    return assemble(res.results)


def assemble(r):
    f32 = np.float32
    y_prompt = np.zeros((2, 4096, D), f32)
    y_sample = np.zeros((128, 8, D), f32)
    k_p = np.zeros((1, 2, 128, 8, 64), f32)
    v_p = np.zeros((1, 2, 128, 8, 64), f32)
    s_p = np.zeros((1, 2, 64, 64, 128), f32)
    c_p = np.zeros((1, 2, 3, 6144), f32)
    k_s = np.zeros((1, 128, 128, 8, 64), f32)
    v_s = np.zeros((1, 128, 128, 8, 64), f32)
    s_s = np.zeros((1, 128, 64, 64, 128), f32)
    c_s = np.zeros((1, 128, 3, 6144), f32)
    for c in range(NCORES):
        b, j = c // 4, c % 4
        o = r[c]
        y = np.asarray(o["y"], f32)
        y_prompt[b, 1024 * j:1024 * (j + 1)] = y[0:1024]
        y_sample[16 * c:16 * (c + 1)] = y[1024:1152].reshape(16, 8, D)
        k_s[0, 16 * c:16 * (c + 1)] = np.asarray(o["ks"], f32).reshape(16, 128, 8, 64)
        v_s[0, 16 * c:16 * (c + 1)] = np.asarray(o["vs"], f32).reshape(16, 128, 8, 64)
        s_s[0, 16 * c:16 * (c + 1)] = np.asarray(o["ss"], f32).reshape(16, 64, 64, 128)
        c_s[0, 16 * c:16 * (c + 1)] = np.asarray(o["cs"], f32).reshape(16, 3, 6144)
        if j == 3:
            k_p[0, b] = np.asarray(o["kp"], f32).reshape(128, 8, 64)
            v_p[0, b] = np.asarray(o["vp"], f32).reshape(128, 8, 64)
            s_p[0, b] = np.asarray(o["sp"], f32).reshape(64, 64, 128)
            c_p[0, b] = np.asarray(o["cp"], f32)
    return (y_prompt, y_sample, k_p, v_p, s_p, c_p, k_s, v_s, s_s, c_s)
```

```python
import numpy as np
import concourse.bass as bass
import concourse.mybir as mybir
from concourse.bass_utils import run_bass_kernel_spmd

F32 = mybir.dt.float32
BF16 = mybir.dt.bfloat16
AF = mybir.ActivationFunctionType
ALU = mybir.AluOpType
AX = mybir.AxisListType

NCORES = 8
D = 2048
NT = 10
TOK = NT * 128
NPT = 1152
IN_DIM = 19520
Q0, K0, V0, ZA0, XS0, B0, C0, ZS0, DT0, GA0, GS0 = 0, 2048, 2560, 3072, 5120, 9216, 10240, 11264, 15360, 15424, 17472
EPS = 1e-6
NEG = -30000.0
EPOCH = 60000


class _Stop(Exception):
    pass


class Res:
    __slots__ = ("name", "w", "r", "excl")

    def __init__(self, name="", excl=False):
        self.name = name
        self.w = None
        self.r = {}
        self.excl = excl


class Tok:
    __slots__ = ("key", "val", "clock")

    def __init__(self, key, val, clock):
        self.key, self.val, self.clock = key, val, clock


class DSem:
    def __init__(self, name, inc=16):
        self.key = ("d", name)
        self.count = 0
        self.inc = inc


class Prog:
    ENG = ("pe", "act", "dve", "pool", "sp")

    def __init__(self):
        self.streams = {e: [] for e in self.ENG}
        self.seq = {e: 0 for e in self.ENG}
        self.known = {e: {} for e in self.ENG}
        self.keys = {}
        self.out_toks = []
        self.all_dma = []

    def _deps(self, eng, r, w):
        toks = []
        for res in r:
            if res.w is not None:
                toks.append(res.w)
            if res.excl:
                for k, t in res.r.items():
                    if k[0] != eng:
                        toks.append(t)
        for res in w:
            if res.w is not None:
                toks.append(res.w)
            toks.extend(res.r.values())
        return toks

    def _waits(self, eng, toks):
        known = self.known[eng]
        need = {}
        for t in toks:
            if t.key[0] == "pe" and eng == "pe":
                continue
            if known.get(t.key, 0) >= t.val:
                continue
            if need.get(t.key, 0) < t.val:
                need[t.key] = t.val
        for t in toks:
            for k, v in t.clock.items():
                if known.get(k, 0) < v:
                    known[k] = v
        return list(need.items())

    def add(self, eng, fn, r=(), w=(), dsem=None, out=False):
        toks = self._deps(eng, r, w)
        waits = self._waits(eng, toks)
        if dsem is not None:
            dsem.count += dsem.inc
            key, val, inc = dsem.key, dsem.count, dsem.inc
        else:
            s = self.seq[eng]
            self.seq[eng] = s + 1
            key, val, inc = (eng, s // EPOCH), s % EPOCH + 1, 1
        self.keys[key] = True
        clock = dict(self.known[eng])
        clock[key] = val
        tok = Tok(key, val, clock)
        if eng == "pe" and dsem is None:
            self.known[eng][key] = val
        for res in w:
            res.w = tok
            res.r = {}
        for res in r:
            old = res.r.get(key)
            if old is None or old.val < val:
                res.r[key] = tok
        self.streams[eng].append((waits, fn, (key, inc)))
        if dsem is not None:
            self.all_dma.append(tok)
        if out:
            self.out_toks.append(tok)
        return tok

    def wait_tokens(self, eng, toks):
        waits = self._waits(eng, toks)
        if waits:
            self.streams[eng].append((waits, None, None))

    def barrier(self):
        last = []
        for e in self.ENG:
            if e == "sp":
                continue
            s = self.seq[e]
            if s > 0:
                s -= 1
                last.append(Tok((e, s // EPOCH), s % EPOCH + 1, {}))
        toks = last + self.all_dma
        self.all_dma = []
        for e in self.ENG:
            self.wait_tokens(e, toks)

    def emit(self, nc, sems):
        semmap = {}
        keys = list(self.keys.keys())
        assert len(keys) <= len(sems), (len(keys), len(sems))
        for k, s in zip(keys, sems):
            semmap[k] = s
        engobj = {"pe": "tensor", "act": "scalar", "dve": "vector", "pool": "gpsimd", "sp": "sync"}
        with nc.Block() as block:
            def mk(ename):
                def body(e):
                    for waits, fn, inc in self.streams[ename]:
                        for k, v in waits:
                            e.wait_ge(semmap[k], v)
                        if fn is not None:
                            ins = fn(e)
                            ins.then_inc(semmap[inc[0]], inc[1])
                return body
            block.tensor(mk("pe"))
            block.scalar(mk("act"))
            block.vector(mk("dve"))
            block.gpsimd(mk("pool"))
            block.sync(mk("sp"))


CST_LAYOUT = {}


def _build_consts():
    cols = []
    off = 0

    def put(name, arr):
        nonlocal off
        arr = np.asarray(arr, np.float32)
        assert arr.shape[0] == 128
        CST_LAYOUT[name] = (off, arr.shape[1])
        cols.append(arr)
        off += arr.shape[1]

    i = np.arange(128)
    put("ident", np.eye(128))
    put("tri", (i[:, None] <= i[None, :]).astype(np.float32))
    same = (i[:, None] // 8 == i[None, :] // 8)
    put("tris", ((i[:, None] <= i[None, :]) & same).astype(np.float32))
    put("mgt", (i[:, None] > i[None, :]).astype(np.float32))
    put("same", same.astype(np.float32))
    put("ones", np.ones((128, 128)))
    put("m16", (i[:, None] // 8 == np.arange(16)[None, :]).astype(np.float32))
    return np.concatenate(cols, axis=1)


def _alibi_tables():
    slopes = np.exp2(-8.0 * np.arange(1, 33, dtype=np.float32) / 32.0).astype(np.float32)
    ql = np.arange(128)[:, None]
    kl = np.arange(256)[None, :]
    dist = (128 + ql) - kl
    valid = (dist >= 0) & (dist < 128)
    bp = np.where(valid[None], -slopes[:, None, None] * dist[None].astype(np.float32), NEG).astype(np.float32)
    t = (np.arange(128) % 8)[:, None]
    s = (np.arange(128) // 8)[:, None]
    j = np.arange(128)[None, :]
    dist_c = 128 + t - j
    valid_c = dist_c < 128
    t2 = (np.arange(128) % 8)[None, :]
    s2 = (np.arange(128) // 8)[None, :]
    dist_n = t - t2
    valid_n = (s == s2) & (dist_n >= 0)
    dist_s = np.concatenate([dist_c, dist_n], axis=1)
    valid_s = np.concatenate([valid_c, valid_n], axis=1)
    bs = np.where(valid_s[None], -slopes[:, None, None] * dist_s[None].astype(np.float32), NEG).astype(np.float32)
    return bp, bs


class Arena:
    def __init__(self, t_f32, nbytes):
        self.t = t_f32
        self.n = nbytes
        self.off = 0
        self.marks = []

    def push(self):
        self.marks.append(self.off)

    def pop(self):
        self.off = self.marks.pop()

    def alloc(self, shape, dt):
        esz = 4 if dt == F32 else 2
        free = 1
        for s in shape[1:]:
            free *= s
        nb = (free * esz + 63) // 64 * 64
        assert self.off + nb <= self.n, ("SBUF arena overflow", self.off, nb, self.n)
        a = self.t[0:shape[0], self.off // 4:(self.off + nb) // 4]
        self.off += nb
        if dt != F32:
            a = a.bitcast(dt)
        a = a[:, 0:free]
        if len(shape) == 3:
            a = a.rearrange("p (a b) -> p a b", a=shape[1])
        elif len(shape) == 4:
            a = a.rearrange("p (a b c) -> p a b c", a=shape[1], b=shape[2])
        return a


def build_nc(stop_after=None, debug=False, nocc=False, prefix=True):
    nc = bass.Bass("TRN2", target_bir_lowering=False)
    cst_np = _build_consts()
    NCST = cst_np.shape[1]

    def din(name, shape):
        return nc.dram_tensor(name, list(shape), F32, kind="ExternalInput").ap()

    def dout(name, shape):
        return nc.dram_tensor(name, list(shape), F32, kind="ExternalOutput").ap()

    xin = din("xin", [TOK, D])
    w_in = din("w_in", [D, IN_DIM])
    w_ab = din("w_attn_br", [D, D])
    w_sb = din("w_ssm_br", [2 * D, D])
    w_o = din("w_out", [D, D])
    cache_k = din("cache_k", [16, 128, 512])
    cache_v = din("cache_v", [16, 128, 512])
    state_ssm = din("state_ssm", [16, 4096, 128])
    state_conv = din("state_conv", [48, 6144])
    norm_pre = din("norm_pre", [D])
    conv_w = din("conv_w", [4, 6144])
    conv_b = din("conv_b", [6144])
    dt_bias = din("dt_bias", [64])
    a_log = din("a_log", [64])
    d_skip = din("d_skip", [64])
    ssm_norm = din("ssm_norm", [4096])
    sinks = din("attn_sinks", [32])
    norm_post = din("norm_post", [D])
    cst = din("cst", [128, NCST])
    bias_p = din("bias_p", [32, 128, 256])
    bias_s = din("bias_s", [32, 128, 256])
    halo_mask = din("halo_mask", [128, 256])
    alpha = din("alpha", [8])
    xprev = din("xprev", [3072, D])
    pflag = din("pflag", [24])

    y_out = dout("y", [NPT, D])
    kp_out = dout("kp", [128, 512])
    vp_out = dout("vp", [128, 512])
    sp_out = dout("sp", [4096, 128])
    cp_out = dout("cp", [3, 6144])
    ks_out = dout("ks", [16, 128, 512])
    vs_out = dout("vs", [16, 128, 512])
    ss_out = dout("ss", [16, 4096, 128])
    cs_out = dout("cs", [48, 6144])

    if debug:
        AT_d = nc.dram_tensor("AT_d", [16, 128, NPT], BF16, kind="ExternalOutput").ap()
        ST_d = nc.dram_tensor("ST_d", [32, 128, NPT], BF16, kind="ExternalOutput").ap()
    else:
        AT_d = nc.dram_tensor("AT_d", [16, 128, NPT], BF16).ap()
        ST_d = nc.dram_tensor("ST_d", [32, 128, NPT], BF16).ap()
    Sst_d = [nc.dram_tensor(f"Sst_d{g}", [128, 512], F32) for g in range(8)]
    ag_in = [nc.dram_tensor(f"ag_in{g}", [128, 520], F32) for g in range(8)]
    ag_out = [nc.dram_tensor(f"ag_out{g}", [NCORES * 128, 520], F32) for g in range(8)]

    P = Prog()
    ARENA_BYTES = 207 * 1024

    from contextlib import ExitStack
    with ExitStack() as estack:
        arena_t = estack.enter_context(nc.sbuf_tensor("arena", [128, ARENA_BYTES // 4], F32))
        psum = [estack.enter_context(nc.psum_tensor(f"ps{i}", [128, 512], F32)) for i in range(8)]
        A = Arena(arena_t, ARENA_BYTES)
        PS = [p[:] for p in psum]
        RPS = [Res(f"ps{i}", excl=True) for i in range(8)]

        def PSb(i):
            return PS[i].bitcast(BF16)

        def dma(q, out, in_, r=(), w=(), dsem=None, outflag=False, slow=False):
            if slow:
                return P.add(q, lambda e: e.dma_start(out=out, in_=in_, allow_slow_non_contiguous=True), r=r, w=w, dsem=dsem, out=outflag)
            return P.add(q, lambda e: e.dma_start(out=out, in_=in_), r=r, w=w, dsem=dsem, out=outflag)

        def mm(out, lhsT, rhs, start=True, stop=True, r=(), w=()):
            return P.add("pe", lambda e: e.matmul(out, lhsT=lhsT, rhs=rhs, start=start, stop=stop), r=r, w=w)

        def tr(out, in_, ident, r=(), w=()):
            return P.add("pe", lambda e: e.transpose(out=out, in_=in_, identity=ident), r=r, w=w)

        def act(out, in_, func, r=(), w=(), bias=None, scale=None, accum_out=None):
            kw = {}
            if bias is not None:
                kw["bias"] = bias
            if scale is not None:
                kw["scale"] = scale
            if accum_out is not None:
                kw["accum_out"] = accum_out
            return P.add("act", lambda e: e.activation(out=out, in_=in_, func=func, **kw), r=r, w=w)

        def tt(eng, out, in0, in1, op, r=(), w=()):
            return P.add(eng, lambda e: e.tensor_tensor(out=out, in0=in0, in1=in1, op=op), r=r, w=w)

        def ts(eng, out, in0, s1, s2, op0, op1=None, r=(), w=(), accum_out=None):
            kw = {}
            if op1 is not None:
                kw["op1"] = op1
            if accum_out is not None:
                kw["accum_out"] = accum_out
            return P.add(eng, lambda e: e.tensor_scalar(out=out, in0=in0, scalar1=s1, scalar2=s2, op0=op0, **kw), r=r, w=w)

        def stt(eng, out, in0, scalar, in1, op0, op1, r=(), w=()):
            return P.add(eng, lambda e: e.scalar_tensor_tensor(out=out, in0=in0, scalar=scalar, in1=in1, op0=op0, op1=op1), r=r, w=w)

        def cp(eng, out, in_, r=(), w=()):
            if eng == "act":
                return P.add("act", lambda e: e.copy(out=out, in_=in_), r=r, w=w)
            return P.add(eng, lambda e: e.tensor_copy(out=out, in_=in_), r=r, w=w)

        def run_interleaved(gens, width):
            it_ = iter(gens)
            active = []
            while True:
                while len(active) < width:
                    try:
                        active.append(next(it_))
                    except StopIteration:
                        break
                if not active:
                    break
                for g_ in list(active):
                    try:
                        next(g_)
                    except StopIteration:
                        active.remove(g_)

        def memset(eng, ap, val, w=()):
            return P.add(eng, lambda e: e.memset(ap, val), w=w)

        R_const = Res("const")
        ds_const = DSem("const")
        cstt = A.alloc([128, NCST], F32)
        dma("sp", cstt, cst[:, :], dsem=ds_const)

        def C(name):
            o, n = CST_LAYOUT[name]
            return cstt[:, o:o + n]
        ident_f, TRI, TRIS, MGT, SAME, ONES, M16 = C("ident"), C("tri"), C("tris"), C("mgt"), C("same"), C("ones"), C("m16")
        ident_b = A.alloc([128, 128], BF16)
        o_id = CST_LAYOUT["ident"][0]
        ds_idb = DSem("idb")
        R_idb = Res("idb")
        dma("pool", ident_b, cst[:, o_id:o_id + 128], w=[R_idb], dsem=ds_idb)
        npre_pk = A.alloc([128, 16], F32)
        cw_pk = A.alloc([128, 48, 4], F32)
        cb_pk = A.alloc([128, 48], F32)
        ssn_pk = A.alloc([128, 32], F32)
        npost_pk = None
        pk_srcs = [(norm_pre, 16), (conv_b, 48), (ssm_norm, 32), (conv_w[0], 48), (conv_w[1], 48), (conv_w[2], 48), (conv_w[3], 48)]
        dtb_bc = A.alloc([128, 64], F32)
        dma("sp", dtb_bc, dt_bias.partition_broadcast(128), dsem=ds_const)
        alog_bc = A.alloc([128, 64], F32)
        dma("sp", alog_bc, a_log.partition_broadcast(128), dsem=ds_const)
        dsk_bc = A.alloc([128, 64], F32)
        dma("sp", dsk_bc, d_skip.partition_broadcast(128), dsem=ds_const)
        sink_bc = A.alloc([128, 32], F32)
        dma("sp", sink_bc, sinks.partition_broadcast(128), dsem=ds_const)
        alpha_bc = A.alloc([128, 8], F32)
        dma("sp", alpha_bc, alpha.partition_broadcast(128), dsem=ds_const)
        hmask = A.alloc([128, 256], F32)
        dma("sp", hmask, halo_mask[:, :], dsem=ds_const)
        R_const.w = Tok(ds_const.key, ds_const.count, {ds_const.key: ds_const.count})
        RC = [R_const]
        A.push()
        pk_tmp = A.alloc([128, 6, 128], F32)
        R_pk = Res("pk")
        ds_pk = DSem("pk")
        pk_dst = [npre_pk, cb_pk, ssn_pk] + [cw_pk[:, :, j] for j in range(4)]
        for i, ((src, nb), dst) in enumerate(zip(pk_srcs, pk_dst)):
            slot = i % 6
            if i == 6:
                P.barrier()
            dma("sp", pk_tmp[0:nb, slot, :], src.rearrange("(b p) -> b p", p=128), w=[R_pk], dsem=ds_pk)
            mm(PS[0][:, 0:nb], pk_tmp[0:nb, slot, :], ident_f[0:nb, 0:nb], r=[R_pk] + RC, w=[RPS[0]])
            cp("dve", dst, PS[0][:, 0:nb], r=[RPS[0]], w=[R_pk])
        RC = [R_const, R_pk, R_idb]
        A.pop()
        P.barrier()

        NPF = 24
        ibank = {"b": 0}

        def ipbank():
            b = ibank["b"]
            ibank["b"] = 1 - b
            return b

        def norm_tile(src_rows, xb, rx, dsx, dst, Rdst, stcol, Rst, junk_, Rjunk):
            dma("sp", xb, src_rows, w=[rx], dsem=dsx)
            act(junk_, xb, AF.Square, r=[rx], w=[Rjunk, Rst], accum_out=stcol)
            ts("dve", stcol, stcol, 1.0 / D, EPS, ALU.mult, ALU.add, r=[Rst], w=[Rst])
            act(stcol, stcol, AF.Sqrt, r=[Rst], w=[Rst])
            P.add("dve", lambda e, o=stcol: e.reciprocal(out=o, in_=o), r=[Rst], w=[Rst])
            tt("pool", xb, xb, stcol.broadcast_to([128, D]), ALU.mult, r=[rx, Rst], w=[rx])
            for b in range(4):
                bank = b % 2
                for q in range(4):
                    kb = b * 4 + q
                    mm(PS[bank][:, q * 128:(q + 1) * 128], xb[:, kb * 128:(kb + 1) * 128], ident_f,
                       r=[rx] + RC, w=[RPS[bank]])
                src = PS[bank].rearrange("p (a b) -> p a b", a=4)
                sc_ = npre_pk[:, b * 4:(b + 1) * 4].unsqueeze(2).broadcast_to([128, 4, 128])
                tt("dve", dst(b), src, sc_, ALU.mult, r=[RPS[bank]] + RC, w=[Rdst])

        R_SstD = [Res(f"SstD{g}") for g in range(8)]
        if prefix:
            A.push()
            hTp = A.alloc([128, 16, NPF * 128], BF16)
            R_hTp = [Res(f"hTp{t}") for t in range(NPF)]
            pw = [A.alloc([128, 16, 512], BF16) for _ in range(2)]
            R_pw, ds_pw = [Res("pw0"), Res("pw1")], [DSem("pw0"), DSem("pw1")]
            pwn = {"n": 0}

            def load_p(src, n):
                i = pwn["n"] % 2
                pwn["n"] += 1
                dma("pool", pw[i][:, :, 0:n], src.rearrange("(k p) n -> p k n", p=128), w=[R_pw[i]], dsem=ds_pw[i])
                return pw[i], R_pw[i]

            pxb = [A.alloc([128, D], F32) for _ in range(2)]
            R_px, ds_px = [Res("px0"), Res("px1")], [DSem("px0"), DSem("px1")]
            pjunk = A.alloc([128, D], BF16)
            R_pj = Res("pjunk")
            pst = A.alloc([128, NPF], F32)
            R_pst = [Res(f"pst{t}") for t in range(NPF)]
            pflag_bc = A.alloc([128, NPF], F32)
            R_pf, ds_pf = Res("pflag"), DSem("pflag")
            dma("sp", pflag_bc, pflag.partition_broadcast(128), w=[R_pf], dsem=ds_pf)
            dtdP = A.alloc([128, NPF, 64], F32)
            eaeP = A.alloc([128, NPF, 64], F32)
            R_pdt = [Res(f"pdt{t}") for t in range(NPF)]
            pA_bc = A.alloc([128, 64], F32)
            R_pA = Res("pA")
            ptmp = A.alloc([128, 8, 64], F32)
            R_pt = Res("ptmp")
            for t in range(NPF):
                norm_tile(xprev[t * 128:(t + 1) * 128, :], pxb[t % 2], R_px[t % 2], ds_px[t % 2],
                          lambda b, t=t: hTp[:, b * 4:(b + 1) * 4, t * 128:(t + 1) * 128], R_hTp[t],
                          pst[:, t:t + 1], R_pst[t], pjunk, R_pj)
            act(pA_bc, alog_bc, AF.Exp, r=RC, w=[R_pA])
            ts("dve", pA_bc, pA_bc, -1.0, None, ALU.mult, r=[R_pA], w=[R_pA])
            wtd, Rwd = load_p(w_in[:, DT0:DT0 + 64], 64)
            for t in range(NPF):
                bank = ipbank()
                for k in range(16):
                    mm(PS[bank][:, 0:64], hTp[:, k, t * 128:(t + 1) * 128], wtd[:, k, 0:64],
                       start=(k == 0), stop=(k == 15), r=[Rwd, R_hTp[t]], w=[RPS[bank]])
                xx, ax, ee, ll, tmp, dtp, adtp, at_ = [ptmp[:, n, :] for n in range(8)]
                tt("dve", xx, PS[bank][:, 0:64], dtb_bc, ALU.add, r=[RPS[bank]] + RC, w=[R_pt])
                act(ax, xx, AF.Abs, r=[R_pt], w=[R_pt])
                act(ee, ax, AF.Exp, scale=-1.0, r=[R_pt], w=[R_pt])
                ts("dve", ee, ee, 1.0, None, ALU.add, r=[R_pt], w=[R_pt])
                act(ll, ee, AF.Ln, r=[R_pt], w=[R_pt])
                stt("dve", dtp, xx, 0.0, ll, ALU.max, ALU.add, r=[R_pt], w=[R_pt])
                ts("dve", dtp, dtp, pflag_bc[:, t:t + 1], None, ALU.mult, r=[R_pt, R_pf], w=[R_pt])
                tt("dve", adtp, dtp, pA_bc, ALU.mult, r=[R_pt, R_pA], w=[R_pt])
                b2 = ipbank()
                mm(PS[b2][:, 0:64], TRI, adtp, r=[R_pt] + RC, w=[RPS[b2]])
                mm(PS[b2][:, 64:128], ONES, adtp, r=[R_pt] + RC, w=[RPS[b2]])
                cp("dve", at_, PS[b2][:, 0:64], r=[RPS[b2]], w=[R_pt])
                tt("dve", tmp, PS[b2][:, 64:128], at_, ALU.subtract, r=[RPS[b2], R_pt], w=[R_pt])
                act(tmp, tmp, AF.Exp, r=[R_pt], w=[R_pt])
                tt("dve", dtdP[:, t, :], dtp, tmp, ALU.mult, r=[R_pt], w=[R_pdt[t]])
                act(eaeP[:, t, :], PS[b2][:, 64:128], AF.Exp, r=[RPS[b2]], w=[R_pdt[t]])
            pxp = A.alloc([128, 5, 515], F32)
            R_pxp = [Res(f"pxp{b}") for b in range(5)]
            pacc2 = [A.alloc([128, 512], F32) for _ in range(5)]
            R_pacc2 = [Res(f"pacc{i}") for i in range(5)]
            pcv2 = [A.alloc([128, 5, 512], BF16) for _ in range(2)]
            R_pcv2 = [[Res(f"pcv{i}_{b}") for b in range(5)] for i in range(2)]
            pxt = [A.alloc([128, 640], BF16) for _ in range(2)]
            R_pxt = [Res("pxt0"), Res("pxt1")]
            pxd = [A.alloc([128, 512], BF16) for _ in range(2)]
            R_pxd = [Res("pxd0"), Res("pxd1")]
            pST = [A.alloc([128, 512], F32)] * 2
            R_pST, ds_pST = [Res("pST0")] * 2, [DSem("pST0")] * 2
            pc = {"x": 0}
            for g in range(8):
                wxa, Rxa = load_p(w_in[:, XS0 + 512 * g:XS0 + 512 * (g + 1)], 512)
                wxb, Rxb = load_p(w_in[:, B0 + 128 * g:B0 + 128 * (g + 1)], 128)
                STp, RSTp = pST[g % 2], R_pST[g % 2]
                memset("pool", STp, 0.0, w=[RSTp])
                for blk in range(5):
                    memset("pool", pxp[:, blk, 0:3], 0.0, w=[R_pxp[blk]])
                blocks_done = {}
                blk_active = {}
                free_pacc = [0, 1]
                free_tail = [0, 1]
                tails_tr = {}
                upd_done = {-1: True}

                def blk_gen(s, blk):
                    tiles = [R_hTp[t] for t in range(4 * s, 4 * s + 4)]
                    wsl, Rws, c0 = (wxa, Rxa, blk * 128) if blk < 4 else (wxb, Rxb, 0)
                    cbi = (4 * g + blk) if blk < 4 else (32 + g)
                    while blk_active.get(blk):
                        yield
                    blk_active[blk] = True
                    pslot = blk
                    pacc_, R_pacc_ = pacc2[pslot], R_pacc2[pslot]
                    pcv_, R_pcv_ = pcv2[s % 2], R_pcv2[s % 2]
                    bank = (0, 1, 3, 4, 5)[blk]
                    for k in range(16):
                        mm(PS[bank], wsl[:, k, c0:c0 + 128], hTp[:, k, s * 512:(s + 1) * 512],
                           start=(k == 0), stop=(k == 15), r=[Rws] + tiles, w=[RPS[bank]])
                    xp = pxp[:, blk, :]
                    cp("act", xp[:, 3:515], PS[bank], r=[RPS[bank]], w=[R_pxp[blk]])
                    yield
                    ts("dve", pacc_, xp[:, 0:512], cw_pk[:, cbi, 0:1], None, ALU.mult, r=[R_pxp[blk]] + RC, w=[R_pacc_])
                    yield
                    for j in range(1, 4):
                        stt("dve", pacc_, xp[:, j:j + 512], cw_pk[:, cbi, j:j + 1], pacc_, ALU.mult, ALU.add,
                            r=[R_pxp[blk], R_pacc_] + RC, w=[R_pacc_])
                        yield
                    while s >= 2 and tails_tr.get(s - 2, 0) < 4:
                        yield
                    act(pcv_[:, blk, :], pacc_, AF.Silu, bias=cb_pk[:, cbi:cbi + 1], r=[R_pacc_] + RC, w=[R_pcv_[blk]])
                    cp("pool", xp[:, 0:3], xp[:, 512:515], r=[R_pxp[blk]], w=[R_pxp[blk]])
                    blocks_done[s] = blocks_done.get(s, 0) + 1
                    blk_active[blk] = False

                def tail_gen(s, q):
                    t = 4 * s + q
                    pcv_, R_pcv_ = pcv2[s % 2], R_pcv2[s % 2]
                    while blocks_done.get(s, 0) < 5 or not free_tail:
                        yield
                    xi = free_tail.pop()
                    bT = 7 if xi == 0 else 2
                    for blk in range(5):
                        tr(PSb(bT)[:, blk * 128:(blk + 1) * 128], pcv_[:, blk, q * 128:(q + 1) * 128], ident_b,
                           r=[R_pcv_[blk]] + RC, w=[RPS[bT]])
                    tails_tr[s] = tails_tr.get(s, 0) + 1
                    cp("act", pxt[xi], PSb(bT)[:, 0:640], r=[RPS[bT]], w=[R_pxt[xi]])
                    yield
                    tt("pool", pxd[xi].rearrange("p (h d) -> p h d", h=8), pxt[xi][:, 0:512].rearrange("p (h d) -> p h d", h=8),
                       dtdP[:, t, 8 * g:8 * g + 8].unsqueeze(2).broadcast_to([128, 8, 64]), ALU.mult,
                       r=[R_pxt[xi], R_pdt[t]], w=[R_pxd[xi]])
                    yield
                    while not upd_done.get(t - 1):
                        yield
                    mm(PS[6], pxt[xi][:, 512:640], pxd[xi], r=[R_pxt[xi], R_pxd[xi]], w=[RPS[6]])
                    tt("dve", STp.rearrange("p (h d) -> p h d", h=8), STp.rearrange("p (h d) -> p h d", h=8),
                       eaeP[:, t, 8 * g:8 * g + 8].unsqueeze(2).broadcast_to([128, 8, 64]), ALU.mult,
                       r=[RSTp, R_pdt[t]], w=[RSTp])
                    tt("dve", STp, STp, PS[6], ALU.add, r=[RSTp, RPS[6]], w=[RSTp])
                    upd_done[t] = True
                    free_tail.append(xi)

                def all_gens():
                    for s in range(NPF // 4):
                        for blk in range(5):
                            yield blk_gen(s, blk)
                        for q in range(4):
                            yield tail_gen(s, q)

                run_interleaved(all_gens(), 6)
                dma("sp", Sst_d[g].ap(), STp, r=[RSTp], w=[R_SstD[g]], dsem=ds_pST[g % 2])
            A.pop()
            P.barrier()

        hT = A.alloc([128, 16, TOK], BF16)
        R_hT = [Res(f"hT{t}") for t in range(NT)]

        NSLOT = 2
        wslot = [A.alloc([128, 16, 512], BF16) for _ in range(NSLOT)]
        R_w = [Res(f"w{i}") for i in range(NSLOT)]
        ds_w = [DSem(f"w{i}") for i in range(NSLOT)]
        wstate = {"n": 0}

        def load_w(segs):
            i = wstate["n"] % NSLOT
            wstate["n"] += 1
            for k, (src, off) in enumerate(segs):
                n = src.shape[1]
                dma("pool", wslot[i][:, :, off:off + n], src.rearrange("(k p) n -> p k n", p=128),
                    w=[R_w[i]] if k == 0 else [], dsem=ds_w[i])
                if k > 0:
                    R_w[i].w = P.all_dma[-1]
            return wslot[i], R_w[i]

        def CK(name):
            if stop_after == name:
                raise _Stop()

        try:
            A.push()
            xbuf = [A.alloc([128, D], F32) for _ in range(2)]
            R_x = [Res("x0"), Res("x1")]
            ds_x = [DSem("x0"), DSem("x1")]
            junk = A.alloc([128, D], F32)
            R_junk = Res("junk")
            ssq = A.alloc([128, NT], F32)
            rstd = A.alloc([128, NT], F32)
            R_st = [Res(f"st{t}") for t in range(NT)]
            for t in range(NT):
                xb, rx = xbuf[t % 2], R_x[t % 2]
                dma("sp", xb, xin[t * 128:(t + 1) * 128, :], w=[rx], dsem=ds_x[t % 2])
                act(junk, xb, AF.Square, r=[rx], w=[R_junk, R_st[t]], accum_out=ssq[:, t:t + 1])
                ts("dve", rstd[:, t:t + 1], ssq[:, t:t + 1], 1.0 / D, EPS, ALU.mult, ALU.add, r=[R_st[t]], w=[R_st[t]])
                act(rstd[:, t:t + 1], rstd[:, t:t + 1], AF.Sqrt, r=[R_st[t]], w=[R_st[t]])
                P.add("dve", lambda e, o=rstd[:, t:t + 1]: e.reciprocal(out=o, in_=o), r=[R_st[t]], w=[R_st[t]])
                tt("pool", xb, xb, rstd[:, t:t + 1].broadcast_to([128, D]), ALU.mult, r=[rx, R_st[t]], w=[rx])
                for b in range(4):
                    bank = b % 2
                    for q in range(4):
                        kb = b * 4 + q
                        mm(PS[bank][:, q * 128:(q + 1) * 128], xb[:, kb * 128:(kb + 1) * 128], ident_f,
                           r=[rx] + RC, w=[RPS[bank]])
                    eng = "dve" if b % 2 == 0 else "pool"
                    src = PS[bank].rearrange("p (a b) -> p a b", a=4)
                    dst = hT[:, b * 4:(b + 1) * 4, t * 128:(t + 1) * 128]
                    sc = npre_pk[:, b * 4:(b + 1) * 4].unsqueeze(2).broadcast_to([128, 4, 128])
                    tt("dve", dst, src, sc, ALU.mult, r=[RPS[bank]] + RC, w=[R_hT[t]])
            A.pop()
            P.barrier()
            CK("p0")
            A.push()
            qT = A.alloc([128, 2, NPT], BF16)
            R_qT = Res("qT")
            kT = A.alloc([128, TOK], BF16)
            R_kT = Res("kT")
            vb = A.alloc([128, NT, 64], BF16)
            R_vb = [Res(f"vb{t}") for t in range(NT)]
            sz = A.alloc([128, NT, 256], BF16)
            R_sz = [Res(f"sz{t}") for t in range(NT)]
            kvnew = A.alloc([128, 2, 2, 512], F32)
            R_kvnew = Res("kvnew")
            biasP = A.alloc([128, 4, 256], F32)
            biasS = A.alloc([128, 4, 256], F32)
            R_bias = Res("bias")
            ds_bias = DSem("bias")
            Sb = [A.alloc([128, 4, 256], F32) for _ in range(2)]
            Pb = [A.alloc([128, 4, 256], BF16) for _ in range(2)]
            PTs = [A.alloc([128, 8, 128], BF16) for _ in range(2)]
            stat = [A.alloc([128, 8, 4], F32) for _ in range(2)]
            An = [A.alloc([128, 256], F32) for _ in range(2)]
            Ag = [A.alloc([128, 256], BF16) for _ in range(2)]
            R_it = [[Res(f"it{i}_{n}") for n in range(8)] for i in range(2)]
            ATs = A.alloc([128, 2, NPT], BF16)
            R_ATs = Res("ATs")
            ds_AT = DSem("AT")
            R_ATd = Res("ATd")
            kc = A.alloc([128, 16, 128], BF16)
            vc = A.alloc([128, 16, 64], BF16)
            R_kc, R_vc = Res("kc"), Res("vc")
            ds_kc = DSem("kc")
            ds_vc = DSem("vc")
            KcT = A.alloc([128, 16, 128], BF16)
            R_KcT = Res("KcT")
            Zq = [A.alloc([128, 16, 248], BF16) for _ in range(2)]
            R_Zq = [Res("Zq0"), Res("Zq1")]
            ZP = A.alloc([128, 2, 16, 248], BF16)
            R_ZP = Res("ZP")
            memset("pool", Zq[0], 0.0, w=[R_Zq[0]])
            memset("pool", Zq[1], 0.0, w=[R_Zq[1]])
            memset("pool", ZP, 0.0, w=[R_ZP])
            itc = {"n": 0, "bank": 0}

            for g in range(8):
                wt, Rw = load_w([(w_in[:, Q0 + 256 * g:Q0 + 256 * (g + 1)], 0),
                                 (w_in[:, K0 + 64 * g:K0 + 64 * (g + 1)], 256),
                                 (w_in[:, K0 + 64 * g:K0 + 64 * (g + 1)], 320),
                                 (w_in[:, V0 + 64 * g:V0 + 64 * (g + 1)], 384)])
                wt2, Rw2 = load_w([(w_in[:, ZA0 + 256 * g:ZA0 + 256 * (g + 1)], 0)])
                dma("sp", biasP, bias_p[4 * g:4 * g + 4].rearrange("h p k -> p h k"), w=[R_bias], dsem=ds_bias)
                dma("sp", biasS, bias_s[4 * g:4 * g + 4].rearrange("h p k -> p h k"), dsem=ds_bias)
                R_bias.w = P.all_dma[-1]
                dma("pool", kc[:, :, 0:64], cache_k[:, :, 64 * g:64 * (g + 1)].rearrange("s k d -> k s d"), w=[R_kc], dsem=ds_kc)
                dma("pool", kc[:, :, 64:128], cache_k[:, :, 64 * g:64 * (g + 1)].rearrange("s k d -> k s d"), dsem=ds_kc)
                R_kc.w = P.all_dma[-1]
                dma("pool", vc, cache_v[:, :, 64 * g:64 * (g + 1)].rearrange("s k d -> k s d"), w=[R_vc], dsem=ds_vc)

                if g == 0:
                    CK("a0")
                for blk in range(3):
                    ranges = [(128, 512), (640, 512), (1152, 128)] if blk < 2 else [(0, 512), (512, 512), (1024, 256)]
                    for (t0, n) in ranges:
                        bank = ipbank()
                        tiles = [R_hT[t] for t in range(t0 // 128, (t0 + n) // 128)]
                        for k in range(16):
                            mm(PS[bank][:, 0:n], wt[:, k, blk * 128:(blk + 1) * 128], hT[:, k, t0:t0 + n],
                               start=(k == 0), stop=(k == 15), r=[Rw] + tiles, w=[RPS[bank]])
                        if blk < 2:
                            ts("dve", qT[:, blk, t0 - 128:t0 - 128 + n], PS[bank][:, 0:n], 0.125, None, ALU.mult,
                               r=[RPS[bank]], w=[R_qT])
                        else:
                            cp("act", kT[:, t0:t0 + n], PS[bank][:, 0:n], r=[RPS[bank]], w=[R_kT])
                if g == 0:
                    CK("a1")
                for t in range(NT):
                    bank = ipbank()
                    for k in range(16):
                        mm(PS[bank][:, 0:128], hT[:, k, t * 128:(t + 1) * 128], wt[:, k, 320:448],
                           start=(k == 0), stop=(k == 15), r=[Rw, R_hT[t]], w=[RPS[bank]])
                    if t >= 1:
                        for k in range(16):
                            mm(PS[bank][:, 128:384], hT[:, k, t * 128:(t + 1) * 128], wt2[:, k, 0:256],
                               start=(k == 0), stop=(k == 15), r=[Rw2, R_hT[t]], w=[RPS[bank]])
                    cp("dve", vb[:, t, :], PS[bank][:, 64:128], r=[RPS[bank]], w=[R_vb[t]])
                    if t >= 8:
                        cp("dve", kvnew[:, t - 8, :, 64 * g:64 * (g + 1)], PS[bank][:, 0:128].rearrange("p (a b) -> p a b", a=2),
                           r=[RPS[bank]], w=[R_kvnew])
                    if t >= 1:
                        act(sz[:, t, :], PS[bank][:, 128:384], AF.Silu, r=[RPS[bank]], w=[R_sz[t]])

                if g == 0:
                    CK("a_inproj")
                for hf in range(2):
                    for s8 in range(8):
                        s = hf * 8 + s8
                        tr(PSb(7)[:, s8 * 128:(s8 + 1) * 128], kc[:, s, :], ident_b, r=[R_kc] + RC, w=[RPS[7]])
                    cp("act", KcT[:, hf * 8:(hf + 1) * 8, :], PSb(7).rearrange("p (a b) -> p a b", a=8), r=[RPS[7]], w=[R_KcT])
                for b in range(2):
                    cp("pool", Zq[b][:, :, 120:128], qT[:, b, 1024:1152].rearrange("p (s t) -> p s t", s=16),
                       r=[R_qT], w=[R_Zq[b]])

                if g == 0:
                    CK("a3")
                def attn_gen(i):
                    sample = (i == 9)
                    it = (i - 1) % 2
                    bS = (2, 3) if it == 0 else (0, 1)
                    bPT = 4 if it == 0 else 7
                    bO = 5 if it == 0 else 6
                    RS, RP, RPT, RST, RAN, RAG = R_it[it][0:6]
                    S_, P_, PT_, st_, An_, Ag_ = Sb[it], Pb[it], PTs[it], stat[it], An[it], Ag[it]
                    rmax, mx, negm, rsum, smm, es, den, rec = [st_[:, n, :] for n in range(8)]
                    for j in range(4):
                        b, jj = j // 2, j % 2
                        pr = slice(jj * 64, (jj + 1) * 64)
                        reg = PS[bS[jj]][:, b * 256:(b + 1) * 256]
                        if not sample:
                            mm(reg, qT[pr, b, (i - 1) * 128:i * 128], kT[pr, (i - 1) * 128:(i + 1) * 128],
                               r=[R_qT, R_kT], w=[RPS[bS[jj]]])
                        else:
                            for s in range(16):
                                mm(reg[:, 0:128], Zq[b][pr, s, 120 - 8 * s:248 - 8 * s], KcT[pr, s, :],
                                   start=(s == 0), stop=(s == 15), r=[R_Zq[b], R_KcT], w=[RPS[bS[jj]]])
                            mm(reg[:, 128:256], qT[pr, b, 1024:1152], kT[pr, 1152:1280], r=[R_qT, R_kT], w=[RPS[bS[jj]]])
                    bias_t = biasS if sample else biasP
                    for jj in range(2):
                        tt("dve", S_.rearrange("p (b j) k -> p b j k", j=2)[:, :, jj, :], PS[bS[jj]].rearrange("p (a b) -> p a b", a=2),
                           bias_t.rearrange("p (b j) k -> p b j k", j=2)[:, :, jj, :], ALU.add, r=[RPS[bS[jj]], R_bias], w=[RS])
                    yield
                    if i == 1:
                        tt("dve", S_, S_, hmask.unsqueeze(1).broadcast_to([128, 4, 256]), ALU.add, r=[RS] + RC, w=[RS])
                    P.add("dve", lambda e, o=rmax, s_=S_: e.tensor_reduce(out=o, in_=s_, axis=AX.X, op=ALU.max), r=[RS], w=[RST])
                    tt("dve", mx, rmax, sink_bc[:, 4 * g:4 * g + 4], ALU.max, r=[RST] + RC, w=[RST])
                    ts("dve", negm, mx, -1.0, None, ALU.mult, r=[RST], w=[RST])
                    tt("dve", smm, sink_bc[:, 4 * g:4 * g + 4], mx, ALU.subtract, r=[RST] + RC, w=[RST])
                    yield
                    for j in range(4):
                        act(P_[:, j, :], S_[:, j, :], AF.Exp, bias=negm[:, j:j + 1], accum_out=rsum[:, j:j + 1],
                            r=[RS, RST], w=[RP, RST])
                    act(es, smm, AF.Exp, r=[RST], w=[RST])
                    yield
                    tt("dve", den, rsum, es, ALU.add, r=[RST], w=[RST])
                    P.add("dve", lambda e, o=rec, d_=den: e.reciprocal(out=o, in_=d_), r=[RST], w=[RST])
                    yield
                    for j in range(4):
                        for hf in range(2):
                            tr(PSb(bPT)[:, (2 * j + hf) * 128:(2 * j + hf + 1) * 128], P_[:, j, hf * 128:(hf + 1) * 128], ident_b,
                               r=[RP] + RC, w=[RPS[bPT]])
                    cp("act", PT_, PSb(bPT).rearrange("p (a b) -> p a b", a=8), r=[RPS[bPT]], w=[RPT])
                    yield
                    if not sample:
                        for j in range(4):
                            mm(PS[bO][:, j * 64:(j + 1) * 64], PT_[:, 2 * j, :], vb[:, i - 1, :], start=True, stop=False,
                               r=[RPT, R_vb[i - 1]], w=[RPS[bO]])
                            mm(PS[bO][:, j * 64:(j + 1) * 64], PT_[:, 2 * j + 1, :], vb[:, i, :], start=False, stop=True,
                               r=[RPT, R_vb[i]], w=[RPS[bO]])
                    else:
                        for b in range(2):
                            src = PT_.rearrange("p (j h) k -> p j h k", h=2)[:, 2 * b:2 * b + 2, 0, :]
                            cp("pool", ZP[:, :, :, 120:128], src.rearrange("p j (s t) -> p j s t", s=16), r=[RPT], w=[R_ZP])
                            for jj in range(2):
                                j = 2 * b + jj
                                for s in range(16):
                                    mm(PS[bO][:, j * 64:(j + 1) * 64], ZP[:, jj, s, 120 - 8 * s:248 - 8 * s], vc[:, s, :],
                                       start=(s == 0), stop=False, r=[R_ZP, R_vc], w=[RPS[bO]])
                                mm(PS[bO][:, j * 64:(j + 1) * 64], PT_[:, 2 * j + 1, :], vb[:, 9, :], start=False, stop=True,
                                   r=[RPT, R_vb[9]], w=[RPS[bO]])
                    tt("dve", An_.rearrange("p (a b) -> p a b", a=4), PS[bO][:, 0:256].rearrange("p (a b) -> p a b", a=4),
                       rec.unsqueeze(2).broadcast_to([128, 4, 64]), ALU.mult, r=[RPS[bO], RST], w=[RAN])
                    tt("pool", Ag_, An_, sz[:, i, :], ALU.mult, r=[RAN, R_sz[i]], w=[RAG])
                    yield
                    for b in range(2):
                        tr(PSb(bO)[:, 512 + b * 128:512 + (b + 1) * 128], Ag_[:, b * 128:(b + 1) * 128], ident_b, r=[RAG] + RC, w=[RPS[bO]])
                    cp("act", ATs[:, :, (i - 1) * 128:i * 128], PSb(bO)[:, 512:768].rearrange("p (a b) -> p a b", a=2),
                       r=[RPS[bO]], w=[R_ATs])
                run_interleaved((attn_gen(i) for i in range(1, NT)), 2)
                dma("sp", AT_d[2 * g:2 * g + 2].rearrange("b p t -> p b t"), ATs, r=[R_ATs], w=[R_ATd], dsem=ds_AT)
                if g == 0:
                    CK("a_g0")

            CK("a_attn")
            ds_kv = DSem("kvout")
            dma("sp", kp_out[:, :], kvnew[:, 0, 0, :], r=[R_kvnew], dsem=ds_kv, outflag=True)
            dma("sp", vp_out[:, :], kvnew[:, 0, 1, :], r=[R_kvnew], dsem=ds_kv, outflag=True)
            for s in range(16):
                dma("sp", ks_out[s, 120:128, :], kvnew[8 * s:8 * s + 8, 1, 0, :], r=[R_kvnew], dsem=ds_kv, outflag=True)
                dma("sp", vs_out[s, 120:128, :], kvnew[8 * s:8 * s + 8, 1, 1, :], r=[R_kvnew], dsem=ds_kv, outflag=True)
            dma("sp", ks_out[:, 0:120, :], cache_k[:, 8:128, :], dsem=ds_kv, outflag=True)
            dma("sp", vs_out[:, 0:120, :], cache_v[:, 8:128, :], dsem=ds_kv, outflag=True)
            A.pop()
            P.barrier()
            A.push()
            NTT = 9
            dt_all = A.alloc([128, NTT, 64], F32)
            adt_all = A.alloc([128, NTT, 64], F32)
            eat_all = A.alloc([128, NTT, 64], F32)
            dtd_all = A.alloc([128, NTT, 64], F32)
            eaend_all = A.alloc([128, NTT, 64], F32)
            eatg_all = None if prefix else A.alloc([128, NTT, 64], F32)
            R_dt = [Res(f"dt{t}") for t in range(NTT)]
            A_bc = A.alloc([128, 64], F32)
            cum_bc = A.alloc([128, 64], F32)
            R_cum = Res("cum")
            E_samp = A.alloc([128, 512], F32)
            R_Es = Res("Es")
            A.push()
            b0tmp = A.alloc([128, 8, 64], F32)
            R_b0 = Res("b0tmp")
            Xeo = A.alloc([128, 2, 512], F32)
            R_Xeo = Res("Xeo")

            act(A_bc, alog_bc, AF.Exp, r=RC, w=[R_cum])
            ts("dve", A_bc, A_bc, -1.0, None, ALU.mult, r=[R_cum], w=[R_cum])
            memset("pool", cum_bc, 0.0, w=[R_cum])
            wt, Rw = load_w([(w_in[:, DT0:DT0 + 64], 0)])
            for t in range(1, NT):
                ti = t - 1
                smp = (t == 9)
                bank = ipbank()
                for k in range(16):
                    mm(PS[bank][:, 0:64], hT[:, k, t * 128:(t + 1) * 128], wt[:, k, 0:64],
                       start=(k == 0), stop=(k == 15), r=[Rw, R_hT[t]], w=[RPS[bank]])
                xx, ax, ee, ll, tmp = [b0tmp[:, n, :] for n in range(5)]
                tt("dve", xx, PS[bank][:, 0:64], dtb_bc, ALU.add, r=[RPS[bank]] + RC, w=[R_b0])
                act(ax, xx, AF.Abs, r=[R_b0], w=[R_b0])
                act(ee, ax, AF.Exp, scale=-1.0, r=[R_b0], w=[R_b0])
                ts("dve", ee, ee, 1.0, None, ALU.add, r=[R_b0], w=[R_b0])
                act(ll, ee, AF.Ln, r=[R_b0], w=[R_b0])
                stt("dve", dt_all[:, ti, :], xx, 0.0, ll, ALU.max, ALU.add, r=[R_b0], w=[R_dt[ti]])
                tt("dve", adt_all[:, ti, :], dt_all[:, ti, :], A_bc, ALU.mult, r=[R_dt[ti], R_cum], w=[R_dt[ti]])
                b2 = ipbank()
                mm(PS[b2][:, 0:64], TRIS if smp else TRI, adt_all[:, ti, :], r=[R_dt[ti]] + RC, w=[RPS[b2]])
                mm(PS[b2][:, 64:128], SAME if smp else ONES, adt_all[:, ti, :], r=[R_dt[ti]] + RC, w=[RPS[b2]])
                at_ = b0tmp[:, 5, :]
                cp("dve", at_, PS[b2][:, 0:64], r=[RPS[b2]], w=[R_b0])
                act(eat_all[:, ti, :], PS[b2][:, 0:64], AF.Exp, r=[RPS[b2]], w=[R_dt[ti]])
                tt("dve", tmp, PS[b2][:, 64:128], at_, ALU.subtract, r=[RPS[b2], R_b0], w=[R_b0])
                act(tmp, tmp, AF.Exp, r=[R_b0], w=[R_b0])
                tt("dve", dtd_all[:, ti, :], dt_all[:, ti, :], tmp, ALU.mult, r=[R_b0, R_dt[ti]], w=[R_dt[ti]])
                act(eaend_all[:, ti, :], PS[b2][:, 64:128], AF.Exp, r=[RPS[b2]], w=[R_dt[ti]])
                if not smp and prefix:
                    tt("dve", cum_bc, PS[b2][:, 64:128], cum_bc, ALU.add, r=[RPS[b2], R_cum], w=[R_cum])
                elif not smp:
                    atg = b0tmp[:, 6, :]
                    tt("dve", atg, at_, cum_bc, ALU.add, r=[R_b0, R_cum], w=[R_b0])
                    act(eatg_all[:, ti, :], atg, AF.Exp, r=[R_b0], w=[R_dt[ti]])
                    tt("dve", cum_bc, PS[b2][:, 64:128], cum_bc, ALU.add, r=[RPS[b2], R_cum], w=[R_cum])
                else:
                    adv = adt_all[:, ti, :].rearrange("p (i two) -> p i two", two=2)
                    for par in range(2):
                        tt("pool", Xeo[:, par, :].rearrange("p (s i) -> p s i", s=16),
                           adv[:, :, par].unsqueeze(1).broadcast_to([128, 16, 32]),
                           M16.unsqueeze(2).broadcast_to([128, 16, 32]), ALU.mult, r=[R_dt[ti]] + RC, w=[R_Xeo])
                    b3 = ipbank()
                    mm(PS[b3][0:64, :], ONES[:, 0:64], Xeo[:, 0, :], r=[R_Xeo] + RC, w=[RPS[b3]])
                    mm(PS[b3][64:128, :], ONES[:, 0:64], Xeo[:, 1, :], r=[R_Xeo] + RC, w=[RPS[b3]])
                    act(E_samp, PS[b3], AF.Exp, r=[RPS[b3]], w=[R_Es])
            A.pop()
            P.barrier()
            CK("b0")

            sc = A.alloc([128, 768], F32)
            R_sc, ds_sc = Res("sc"), DSem("sc")
            xpP = [A.alloc([128, 1027], F32) for _ in range(2)]
            xpS = [A.alloc([128, 16, 11], F32) for _ in range(2)]
            R_xp = [Res("xp0"), Res("xp1")]
            accP2 = [A.alloc([128, 1024], F32) for _ in range(2)]
            accS2 = [A.alloc([128, 16, 8], F32) for _ in range(2)]
            R_accP2, R_accS2 = [Res("accP0"), Res("accP1")], [Res("accS0"), Res("accS1")]
            convT = A.alloc([128, 5, NPT], BF16)
            R_cv = [Res(f"cv{b}") for b in range(5)]
            CTk = [A.alloc([128, NPT], BF16) for _ in range(2)]
            R_CT = [Res("CT0"), Res("CT1")]
            crow = sc
            R_crow, ds_crow = R_sc, DSem("crow")
            xtok = [A.alloc([128, 640], BF16) for _ in range(2)]
            R_xtok = [Res("xtok0"), Res("xtok1")]
            ylocal = A.alloc([128, NTT, 512], F32)
            R_yl = [Res(f"yl{t}") for t in range(NTT)]
            cbm = [A.alloc([128, 128], F32) for _ in range(2)]
            R_cbm = [Res("cbm0"), Res("cbm1")]
            xdt = [A.alloc([128, 512], BF16) for _ in range(2)]
            xdd = [A.alloc([128, 512], BF16) for _ in range(2)]
            R_xd = [Res("xd0"), Res("xd1")]
            Lt = [A.alloc([128, 4, 128], F32) for _ in range(2)]
            R_L = [Res("L0"), Res("L1")]
            Et = [A.alloc([128, 512], F32) for _ in range(2)]
            R_E = [Res("E0"), Res("E1")]
            MT = [A.alloc([128, 4, 128], BF16) for _ in range(2)]
            R_MT = [Res("MT0"), Res("MT1")]
            t12 = [A.alloc([128, 512], F32) for _ in range(2)]
            t3 = [None, None]
            R_t12, R_t3 = [Res("t12a"), Res("t12b")], [Res("t3a"), Res("t3b")]
            STf = [A.alloc([128, 520], F32) for _ in range(1 if prefix else 2)] * (2 if prefix else 1)
            R_STf = [Res("STf0")] * 2 if prefix else [Res("STf0"), Res("STf1")]
            STb = A.alloc([128, 512], BF16)
            R_STb = Res("STb")
            S0nat = [A.alloc([128, 4, 128], F32) for _ in range(2)]
            R_S0, ds_S0 = [Res("S00"), Res("S01")], [DSem("S00"), DSem("S01")]
            S0T = [A.alloc([128, 512], BF16) for _ in range(2)]
            R_S0T = [Res("S0T0"), Res("S0T1")]
            ZC = A.alloc([128, 16, 248], BF16)
            R_ZC = Res("ZC")
            Bm = A.alloc([128, 16, 128], BF16)
            R_Bm = Res("Bm")
            snew = [A.alloc([128, 4, 128], F32)] * 2
            R_sn, ds_sn = [Res("sn0")] * 2, [DSem("sn0")] * 2
            agl = None if prefix else A.alloc([128, 520], F32)
            R_agl, ds_agl = Res("agl"), DSem("agl")
            Sst = None if prefix else A.alloc([128, 512], F32)
            Sstb = None if prefix else A.alloc([128, 512], BF16)
            R_Sst, R_Sstb = Res("Sst"), Res("Sstb")
            cci = A.alloc([128, 4, 8], F32)
            R_cci = Res("cci")
            szs2 = [A.alloc([128, 512], F32) for _ in range(2)]
            yb2 = [A.alloc([128, 512], F32) for _ in range(2)]
            R_szs2, R_yb2 = [Res("szsa"), Res("szsb")], [Res("yba"), Res("ybb")]
            gst2 = [A.alloc([128, 4], F32) for _ in range(2)]
            R_gst2 = [Res("gsta"), Res("gstb")]
            szs, yb = szs2[0], yb2[0]
            gn = [A.alloc([128, 512], BF16) for _ in range(2)]
            R_szs, R_yb, R_gn = Res("szs"), Res("yb"), [Res("gn0"), Res("gn1")]
            gst = A.alloc([128, 4], F32)
            R_gst = Res("gst")
            ssTs = [A.alloc([128, 4, 128], BF16) for _ in range(2)]
            R_ssT, ds_ssT = [Res("ssT0"), Res("ssT1")], [DSem("ssT0"), DSem("ssT1")]
            spst = snew[0]
            R_spst, ds_spst = R_sn[0], DSem("spst")
            R_agin = [Res(f"agin{g}") for g in range(8)]
            R_agout = [Res(f"agout{g}") for g in range(8)]
            ds_agin = DSem("agin")
            R_STd = Res("STd")
            ds_stin = DSem("stin")
            memset("pool", ZC, 0.0, w=[R_ZC])
            hsel = A.alloc([128, 16, 48], BF16)
            R_hsel = Res("hsel")
            cp("pool", hsel.rearrange("p k (s j) -> p k s j", s=16),
               hT[:, :, 1152:1280].rearrange("p k (s t) -> p k s t", s=16)[:, :, :, 5:8], r=[R_hT[9]], w=[R_hsel])
            cnt = {"x": 0, "mt": 0, "s0": 0, "sn": 0, "gn": 0, "ss": 0}

            def bcast_h(ap_h8):
                return ap_h8.unsqueeze(2).broadcast_to([128, 8, 64])

            def v864(ap):
                return ap.rearrange("p (h d) -> p h d", h=8)

            def B1a(g):
                wa, Rwa = load_w([(w_in[:, XS0 + 512 * g:XS0 + 512 * (g + 1)], 0)])
                wb_, Rwb = load_w([(w_in[:, B0 + 128 * g:B0 + 128 * (g + 1)], 0),
                                   (w_in[:, C0 + 128 * g:C0 + 128 * (g + 1)], 128)])
                dma("sp", sc[0:48, 0:512], state_conv[:, 512 * g:512 * (g + 1)], w=[R_sc], dsem=ds_sc)
                dma("sp", sc[0:48, 512:640], state_conv[:, 4096 + 128 * g:4096 + 128 * (g + 1)], dsem=ds_sc)
                dma("sp", sc[0:48, 640:768], state_conv[:, 5120 + 128 * g:5120 + 128 * (g + 1)], dsem=ds_sc)
                R_sc.w = P.all_dma[-1]
                def blk_gen_b(blk):
                    wsl, Rws, c0 = (wa, Rwa, blk * 128) if blk < 4 else (wb_, Rwb, (blk - 4) * 128)
                    cbi = (4 * g + blk) if blk < 4 else (32 + g if blk == 4 else 40 + g)
                    xi = blk % 2
                    xp, xs_, Rxp = xpP[xi], xpS[xi], R_xp[xi]
                    accP, accS, R_aP, R_aS = accP2[xi], accS2[xi], R_accP2[xi], R_accS2[xi]
                    for ri, (t0, n) in enumerate([(0, 512), (512, 512), (1024, 256)]):
                        bank = ipbank()
                        tiles = [R_hT[t] for t in range(t0 // 128, (t0 + n) // 128)]
                        for k in range(16):
                            mm(PS[bank][:, 0:n], wsl[:, k, c0:c0 + 128], hT[:, k, t0:t0 + n],
                               start=(k == 0), stop=(k == 15), r=[Rws] + tiles, w=[RPS[bank]])
                        if ri == 0:
                            cp("act", xp[:, 0:387], PS[bank][:, 125:512], r=[RPS[bank]], w=[Rxp])
                        elif ri == 1:
                            cp("act", xp[:, 387:899], PS[bank][:, 0:512], r=[RPS[bank]], w=[Rxp])
                        else:
                            cp("act", xp[:, 899:1027], PS[bank][:, 0:128], r=[RPS[bank]], w=[Rxp])
                            cp("act", xs_[:, :, 3:11], PS[bank][:, 128:256].rearrange("p (s t) -> p s t", s=16),
                               r=[RPS[bank]], w=[Rxp])
                    bank = ipbank()
                    mm(PS[bank][:, 0:48], sc[0:48, blk * 128:(blk + 1) * 128], ident_f[0:48, 0:48], r=[R_sc] + RC, w=[RPS[bank]])
                    cp("act", xs_[:, :, 0:3], PS[bank][:, 0:48].rearrange("p (s t) -> p s t", s=16), r=[RPS[bank]], w=[Rxp])
                    yield
                    ts("dve", accP, xp[:, 0:1024], cw_pk[:, cbi, 0:1], None, ALU.mult, r=[Rxp] + RC, w=[R_aP])
                    ts("dve", accS, xs_[:, :, 0:8], cw_pk[:, cbi, 0:1], None, ALU.mult, r=[Rxp] + RC, w=[R_aS])
                    yield
                    for j in range(1, 4):
                        stt("dve", accP, xp[:, j:j + 1024], cw_pk[:, cbi, j:j + 1], accP, ALU.mult, ALU.add, r=[Rxp, R_aP] + RC, w=[R_aP])
                        stt("dve", accS, xs_[:, :, j:j + 8], cw_pk[:, cbi, j:j + 1], accS, ALU.mult, ALU.add, r=[Rxp, R_aS] + RC, w=[R_aS])
                        yield
                    if blk < 5:
                        dst, Rd = convT[:, blk, :], R_cv[blk]
                    else:
                        dst, Rd = CTk[g % 2], R_CT[g % 2]
                    act(dst[:, 0:1024], accP, AF.Silu, bias=cb_pk[:, cbi:cbi + 1], r=[R_aP] + RC, w=[Rd])
                    act(dst[:, 1024:1152].rearrange("p (s t) -> p s t", s=16), accS, AF.Silu, bias=cb_pk[:, cbi:cbi + 1],
                        r=[R_aS] + RC, w=[Rd])

                run_interleaved((blk_gen_b(blk) for blk in range(6)), 2)
                for (lhs_of_k, nrow, dst_out, Rt) in (
                        (lambda k: hT[:, k, 1149:1152], 3, cp_out, R_hT[8]),
                        (lambda k: hsel[:, k, :], 48, cs_out, R_hsel)):
                    bank = ipbank()
                    for k in range(16):
                        mm(PS[bank][0:nrow, 0:512], lhs_of_k(k), wa[:, k, 0:512], start=(k == 0), stop=(k == 15),
                           r=[Rwa, Rt], w=[RPS[bank]])
                    cp("dve", crow[0:nrow, 0:512], PS[bank][0:nrow, 0:512], r=[RPS[bank]], w=[R_crow])
                    bank = ipbank()
                    for k in range(16):
                        mm(PS[bank][0:nrow, 0:256], lhs_of_k(k), wb_[:, k, 0:256], start=(k == 0), stop=(k == 15),
                           r=[Rwb, Rt], w=[RPS[bank]])
                    cp("dve", crow[0:nrow, 512:768], PS[bank][0:nrow, 0:256], r=[RPS[bank]], w=[R_crow])
                    dma("sp", dst_out[:, 512 * g:512 * (g + 1)], crow[0:nrow, 0:512], r=[R_crow], dsem=ds_crow, outflag=True)
                    dma("sp", dst_out[:, 4096 + 128 * g:4096 + 128 * (g + 1)], crow[0:nrow, 512:640], r=[R_crow], dsem=ds_crow, outflag=True)
                    dma("sp", dst_out[:, 5120 + 128 * g:5120 + 128 * (g + 1)], crow[0:nrow, 640:768], r=[R_crow], dsem=ds_crow, outflag=True)

            def B1b(g):
                hs = slice(8 * g, 8 * g + 8)
                CT, RCT = CTk[g % 2], R_CT[g % 2]
                STc, RST_ = STf[g % 2], R_STf[g % 2]
                cp("pool", ZC[:, :, 120:128], CT[:, 1024:1152].rearrange("p (s t) -> p s t", s=16), r=[RCT], w=[R_ZC])
                if prefix:
                    dma("sp", STc[:, 0:512], Sst_d[g].ap(), r=[R_SstD[g]], w=[RST_], dsem=ds_stin)
                    cp("act", STb, STc[:, 0:512], r=[RST_], w=[R_STb])
                st_done = {0: True}

                def tile_gen(t):
                    ti = t - 1
                    smp = (t == 9)
                    par = ti % 2
                    bT = 7 if par == 0 else 2
                    bY = 4 if par == 0 else 0
                    bYI = 5 if par == 0 else 1
                    Lt_, R_L_, Et_, R_E_ = Lt[par], R_L[par], Et[par], R_E[par]
                    t12_, R_t12_, t3_, R_t3_ = t12[par], R_t12[par], t3[par], R_t3[par]
                    tsl = slice(ti * 128, (ti + 1) * 128)
                    xi = par
                    xk, Rxk = xtok[xi], R_xtok[xi]
                    for blk in range(5):
                        tr(PSb(bT)[:, blk * 128:(blk + 1) * 128], convT[:, blk, tsl], ident_b, r=[R_cv[blk]] + RC, w=[RPS[bT]])
                    cp("act", xk, PSb(bT)[:, 0:640], r=[RPS[bT]], w=[Rxk])
                    yield
                    mm(PS[bT][:, 384:512], convT[:, 4, tsl], CT[:, tsl], r=[R_cv[4], RCT], w=[RPS[bT]])
                    cb_, Rcb = cbm[ti % 2], R_cbm[ti % 2]
                    tt("dve", cb_, PS[bT][:, 384:512], TRIS if smp else TRI, ALU.mult, r=[RPS[bT]] + RC, w=[Rcb])
                    yield
                    xdt_, xdd_, Rxd = xdt[ti % 2], xdd[ti % 2], R_xd[ti % 2]
                    tt("pool", v864(xdt_), v864(xk[:, 0:512]), bcast_h(dt_all[:, ti, hs]), ALU.mult, r=[Rxk, R_dt[ti]], w=[Rxd])
                    tt("pool", v864(xdd_), v864(xk[:, 0:512]), bcast_h(dtd_all[:, ti, hs]), ALU.mult, r=[Rxk, R_dt[ti]], w=[Rxd])
                    yield
                    for hb in range(2):
                        h0 = 8 * g + 4 * hb
                        tt("pool", Lt_, MGT.unsqueeze(1).broadcast_to([128, 4, 128]),
                           adt_all[:, ti, h0:h0 + 4].unsqueeze(2).broadcast_to([128, 4, 128]), ALU.mult,
                           r=[R_dt[ti]] + RC, w=[R_L_])
                        yield
                        for r_ in range(4):
                            mm(PS[3][:, r_ * 128:(r_ + 1) * 128], Lt_[:, r_, :], TRI, r=[R_L_] + RC, w=[RPS[3]])
                        act(Et_, PS[3], AF.Exp, r=[RPS[3]], w=[R_E_])
                        yield
                        mi = par
                        tt("dve", MT[mi], Et_.rearrange("p (a b) -> p a b", a=4), cb_.unsqueeze(1).broadcast_to([128, 4, 128]),
                           ALU.mult, r=[R_E_, Rcb], w=[R_MT[mi]])
                        yield
                        for r_ in range(4):
                            hh = 4 * hb + r_
                            mm(PS[bY][:, hh * 64:(hh + 1) * 64], MT[mi][:, r_, :], xdt_[:, hh * 64:(hh + 1) * 64],
                               r=[R_MT[mi], Rxd], w=[RPS[bY]])
                        yield
                    have_inter = smp or t >= 2 or prefix
                    while not smp and not st_done.get(ti):
                        yield
                    if smp:
                        s0toks = []
                        for s in range(16):
                            si = cnt["s0"] % 2
                            cnt["s0"] += 1
                            dma("sp", S0nat[si], state_ssm[s, 512 * g:512 * (g + 1), :].rearrange("(q p) n -> p q n", p=128),
                                w=[R_S0[si]], dsem=ds_S0[si])
                            for q in range(4):
                                mm(PS[7][:, q * 128:(q + 1) * 128], S0nat[si][:, q, :], ident_f, r=[R_S0[si]] + RC, w=[RPS[7]])
                            cp("act", S0T[si], PS[7], r=[RPS[7]], w=[R_S0T[si]])
                            mm(PS[bYI], ZC[:, s, 120 - 8 * s:248 - 8 * s], S0T[si], start=(s == 0), stop=(s == 15),
                               r=[R_ZC, R_S0T[si]], w=[RPS[bYI]])
                            if s == 0:
                                tt("pool", Bm, xk[:, 512:640].unsqueeze(1).broadcast_to([128, 16, 128]),
                                   M16.unsqueeze(2).broadcast_to([128, 16, 128]), ALU.mult, r=[Rxk] + RC, w=[R_Bm])
                            for q in range(4):
                                mm(PS[6][:, q * 128:(q + 1) * 128], xdd_[:, q * 128:(q + 1) * 128], Bm[:, s, :],
                                   r=[Rxd, R_Bm], w=[RPS[6]])
                            ni = cnt["sn"] % 2
                            cnt["sn"] += 1
                            ecol = E_samp[:, s * 32 + 4 * g:s * 32 + 4 * g + 4]
                            tt("dve", snew[ni], S0nat[si], ecol.unsqueeze(2).broadcast_to([128, 4, 128]), ALU.mult,
                               r=[R_S0[si], R_Es], w=[R_sn[ni]])
                            tt("dve", snew[ni], snew[ni], PS[6].rearrange("p (q n) -> p q n", q=4), ALU.add,
                               r=[R_sn[ni], RPS[6]], w=[R_sn[ni]])
                            dma("pool", ss_out[s, 512 * g:512 * (g + 1), :].rearrange("(q p) n -> p q n", p=128), snew[ni],
                                r=[R_sn[ni]], dsem=ds_sn[ni], outflag=True)
                            yield
                    elif t >= 2 or prefix:
                        mm(PS[bYI], CT[:, tsl], STb, r=[RCT, R_STb], w=[RPS[bYI]])
                    if have_inter:
                        yield
                        tt("dve", v864(t12_), v864(PS[bYI]), bcast_h(eat_all[:, ti, hs]), ALU.mult, r=[RPS[bYI], R_dt[ti]], w=[R_t12_])
                        tt("dve", t12_, t12_, PS[bY], ALU.add, r=[R_t12_, RPS[bY]], w=[R_t12_])
                    else:
                        cp("dve", t12_, PS[bY], r=[RPS[bY]], w=[R_t12_])
                    tt("pool", v864(ylocal[:, ti, :]), v864(xk[:, 0:512]), bcast_h(dsk_bc[:, hs]), ALU.mult, r=[Rxk] + RC, w=[R_yl[ti]])
                    yield
                    tt("pool", ylocal[:, ti, :], ylocal[:, ti, :], t12_, ALU.add, r=[R_t12_, R_yl[ti]], w=[R_yl[ti]])
                    if not smp:
                        mm(PS[6], xk[:, 512:640], xdd_, r=[Rxk, Rxd], w=[RPS[6]])
                        if t == 1 and not prefix:
                            cp("dve", STc[:, 0:512], PS[6], r=[RPS[6]], w=[RST_])
                        else:
                            tt("dve", v864(STc[:, 0:512]), v864(STc[:, 0:512]), bcast_h(eaend_all[:, ti, hs]), ALU.mult,
                               r=[RST_, R_dt[ti]], w=[RST_])
                            tt("dve", STc[:, 0:512], STc[:, 0:512], PS[6], ALU.add, r=[RST_, RPS[6]], w=[RST_])
                        if t < 8:
                            cp("act", STb, STc[:, 0:512], r=[RST_], w=[R_STb])
                        st_done[t] = True

                run_interleaved((tile_gen(t) for t in range(1, NT)), 2)
                if prefix:
                    for q in range(4):
                        mm(PS[7][:, q * 128:(q + 1) * 128], STc[:, q * 128:(q + 1) * 128], ident_f, r=[RST_] + RC, w=[RPS[7]])
                    cp("act", spst, PS[7].rearrange("p (q n) -> p q n", q=4), r=[RPS[7]], w=[R_spst])
                    dma("sp", sp_out[512 * g:512 * (g + 1), :].rearrange("(q p) n -> p q n", p=128), spst, r=[R_spst],
                        dsem=ds_spst, outflag=True)
                    return
                cp("dve", STc[:, 512:520], cum_bc[:, hs], r=[R_cum], w=[RST_])
                dma("sp", ag_in[g].ap(), STc, r=[RST_], w=[R_agin[g]], dsem=ds_agin)
                dcc = DSem(f"cc{g}", inc=1)
                if nocc:
                    dma("sp", ag_out[g].ap()[0:128, :], ag_in[g].ap(), r=[R_agin[g]], w=[R_agout[g]], dsem=DSem(f"ccx{g}"))
                else:
                  P.add("pool", lambda e, g=g: e.collective_compute(
                    "AllGather", ALU.bypass, replica_groups=[list(range(NCORES))],
                    ins=[ag_in[g].ap().opt()], outs=[ag_out[g].ap().opt()]),
                    r=[R_agin[g]], w=[R_agout[g]], dsem=dcc)

            def B2(g):
                hs = slice(8 * g, 8 * g + 8)
                CT, RCT = CTk[g % 2], R_CT[g % 2]
                STc, RST_ = STf[g % 2], R_STf[g % 2]
                wz, Rwz = load_w([(w_in[:, ZS0 + 512 * g:ZS0 + 512 * (g + 1)], 0)])
                if not prefix:
                    memset("pool", Sst, 0.0, w=[R_Sst])
                for i in range(0 if prefix else NCORES):
                    dma("sp", agl, ag_out[g].ap()[i * 128:(i + 1) * 128, :], r=[R_agout[g]], w=[R_agl], dsem=ds_agl)
                    ei, ci = cci[:, 0, :], cci[:, 1, :]
                    act(ei, agl[:, 512:520], AF.Exp, r=[R_agl], w=[R_cci])
                    ts("dve", ci, ei, -1.0, None, ALU.add, r=[R_cci], w=[R_cci])
                    ts("dve", ci, ci, alpha_bc[:, i:i + 1], 1.0, ALU.mult, ALU.add, r=[R_cci] + RC, w=[R_cci])
                    tt("dve", v864(Sst), v864(Sst), bcast_h(ci), ALU.mult, r=[R_Sst, R_cci], w=[R_Sst])
                    stt("dve", Sst, agl[:, 0:512], alpha_bc[:, i:i + 1], Sst, ALU.mult, ALU.add, r=[R_agl, R_Sst] + RC, w=[R_Sst])
                if not prefix:
                    cp("act", Sstb, Sst, r=[R_Sst], w=[R_Sstb])
                eo = cci[:, 2, :]
                if not prefix:
                    act(eo, STc[:, 512:520], AF.Exp, r=[RST_], w=[R_cci])
                    tt("dve", v864(Sst), v864(Sst), bcast_h(eo), ALU.mult, r=[R_Sst, R_cci, R_Sstb], w=[R_Sst])
                    tt("dve", Sst, Sst, STc[:, 0:512], ALU.add, r=[R_Sst, RST_], w=[R_Sst])
                    for q in range(4):
                        mm(PS[7][:, q * 128:(q + 1) * 128], Sst[:, q * 128:(q + 1) * 128], ident_f, r=[R_Sst] + RC, w=[RPS[7]])
                    cp("act", spst, PS[7].rearrange("p (q n) -> p q n", q=4), r=[RPS[7]], w=[R_spst])
                    dma("sp", sp_out[512 * g:512 * (g + 1), :].rearrange("(q p) n -> p q n", p=128), spst, r=[R_spst],
                        dsem=ds_spst, outflag=True)
                def tile_gen2(t):
                    ti = t - 1
                    par = ti % 2
                    bT = 7 if par == 0 else 2
                    szs, R_szs, yb, R_yb, gst, R_gst = szs2[par], R_szs2[par], yb2[par], R_yb2[par], gst2[par], R_gst2[par]
                    tsl = slice(ti * 128, (ti + 1) * 128)
                    bank = ipbank()
                    for k in range(16):
                        mm(PS[bank], hT[:, k, t * 128:(t + 1) * 128], wz[:, k, 0:512], start=(k == 0), stop=(k == 15),
                           r=[Rwz, R_hT[t]], w=[RPS[bank]])
                    act(szs, PS[bank], AF.Silu, r=[RPS[bank]], w=[R_szs])
                    yield
                    if t <= 8 and not prefix:
                        mm(PS[5], CT[:, tsl], Sstb, r=[RCT, R_Sstb], w=[RPS[5]])
                        tt("dve", v864(yb), v864(PS[5]), bcast_h(eatg_all[:, ti, hs]), ALU.mult, r=[RPS[5], R_dt[ti]], w=[R_yb])
                        tt("pool", yb, yb, ylocal[:, ti, :], ALU.add, r=[R_yb, R_yl[ti]], w=[R_yb])
                        tt("pool", yb, yb, szs, ALU.mult, r=[R_yb, R_szs], w=[R_yb])
                    else:
                        tt("pool", yb, ylocal[:, ti, :], szs, ALU.mult, r=[R_yl[ti], R_szs], w=[R_yb])
                    yield
                    act(szs, yb, AF.Square, accum_out=gst[:, 0:1], r=[R_yb], w=[R_szs, R_gst])
                    yield
                    ts("dve", gst[:, 1:2], gst[:, 0:1], 1.0 / 512.0, EPS, ALU.mult, ALU.add, r=[R_gst], w=[R_gst])
                    yield
                    act(gst[:, 1:2], gst[:, 1:2], AF.Sqrt, r=[R_gst], w=[R_gst])
                    yield
                    P.add("dve", lambda e, o=gst[:, 2:3], i_=gst[:, 1:2]: e.reciprocal(out=o, in_=i_), r=[R_gst], w=[R_gst])
                    yield
                    gi = par
                    ts("dve", gn[gi], yb, gst[:, 2:3], None, ALU.mult, r=[R_yb, R_gst], w=[R_gn[gi]])
                    yield
                    for q in range(4):
                        tr(PSb(bT)[:, q * 128:(q + 1) * 128], gn[gi][:, q * 128:(q + 1) * 128], ident_b, r=[R_gn[gi]] + RC, w=[RPS[bT]])
                    yield
                    oi = cnt["ss"] % 2
                    cnt["ss"] += 1
                    tt("dve", ssTs[oi], PSb(bT)[:, 0:512].rearrange("p (q n) -> p q n", q=4),
                       ssn_pk[:, 4 * g:4 * g + 4].unsqueeze(2).broadcast_to([128, 4, 128]), ALU.mult,
                       r=[RPS[bT]] + RC, w=[R_ssT[oi]])
                    dma("sp", ST_d[4 * g:4 * g + 4, :, tsl].rearrange("b p t -> p b t"), ssTs[oi], r=[R_ssT[oi]], w=[R_STd],
                        dsem=ds_ssT[oi])

                run_interleaved((tile_gen2(t) for t in range(1, NT)), 2)

            for g in range(8):
                B1a(g)
                if g > 0:
                    B2(g - 1)
                B1b(g)
                if g == 0:
                    CK("b_g0")
            B2(7)
            A.pop()
            P.barrier()
            CK("b")
            A.push()
            mergedT = A.alloc([128, 16, NPT], BF16)
            R_mg = [Res(f"mg{t}") for t in range(9)]
            A.push()
            cslots = [wslot[0][:, :, 0:256], wslot[0][:, :, 256:512], wslot[1][:, :, 0:256], wslot[1][:, :, 256:512]]
            cslots += [A.alloc([128, 16, 256], BF16) for _ in range(4)]
            NCS = len(cslots)
            R_cs = [Res(f"cs{i}") for i in range(NCS)]
            ds_cs = [DSem(f"cs{i}") for i in range(NCS)]
            cstate = {"n": 0}

            def load_c(src):
                i = cstate["n"] % NCS
                cstate["n"] += 1
                dma("pool", cslots[i], src.rearrange("(k p) n -> p k n", p=128), w=[R_cs[i]], dsem=ds_cs[i])
                return cslots[i], R_cs[i]

            ATg = [A.alloc([128, 16, 256], BF16) for _ in range(2)]
            STg = [A.alloc([128, 32, 256], BF16) for _ in range(2)]
            R_xg, ds_xg = [Res("xg0"), Res("xg1")], [DSem("xg0"), DSem("xg1")]
            sg = [A.alloc([128, 512], F32) for _ in range(2)]
            R_sg = [Res("sg0"), Res("sg1")]
            mtmp = [A.alloc([128, 512], F32) for _ in range(2)]
            R_mt = [Res("mt0"), Res("mt1")]
            TGS = [(0, 256), (256, 256), (512, 256), (768, 256), (1024, 128)]
            ccnt = {"x": 0, "m": 0}
            for fc in range(8):
                f0 = 256 * fc
                wga, Rga = load_c(w_in[:, GA0 + f0:GA0 + f0 + 256])
                wgs, Rgs = load_c(w_in[:, GS0 + f0:GS0 + f0 + 256])
                wab, Rab = load_c(w_ab[:, f0:f0 + 256])
                ws0, Rs0 = load_c(w_sb[0:2048, f0:f0 + 256])
                ws1, Rs1 = load_c(w_sb[2048:4096, f0:f0 + 256])
                for (o0, n) in TGS:
                    xi = ccnt["x"] % 2
                    ccnt["x"] += 1
                    dma("sp", ATg[xi][:, :, 0:n], AT_d[:, :, o0:o0 + n].rearrange("b p t -> p b t"), r=[R_ATd], w=[R_xg[xi]], dsem=ds_xg[xi])
                    dma("sp", STg[xi][:, :, 0:n], ST_d[:, :, o0:o0 + n].rearrange("b p t -> p b t"), r=[R_STd], dsem=ds_xg[xi])
                    R_xg[xi].w = P.all_dma[-1]
                    htiles = [R_hT[t] for t in range((o0 + 128) // 128, (o0 + 128 + n) // 128)]
                    for sb in range(2):
                        cs_ = slice(sb * 128, (sb + 1) * 128)
                        bG, bP = (2, 3) if ccnt["m"] % 2 == 0 else (4, 5)
                        for k in range(16):
                            mm(PS[bG][:, 0:n], wga[:, k, cs_], hT[:, k, o0 + 128:o0 + 128 + n], start=(k == 0), stop=(k == 15),
                               r=[Rga] + htiles, w=[RPS[bG]])
                        for k in range(16):
                            mm(PS[bG][:, 256:256 + n], wgs[:, k, cs_], hT[:, k, o0 + 128:o0 + 128 + n], start=(k == 0), stop=(k == 15),
                               r=[Rgs] + htiles, w=[RPS[bG]])
                        for k in range(16):
                            mm(PS[bP][:, 0:n], wab[:, k, cs_], ATg[xi][:, k, 0:n], start=(k == 0), stop=(k == 15),
                               r=[Rab, R_xg[xi]], w=[RPS[bP]])
                        for k in range(32):
                            wsx, Rsx = (ws0, Rs0) if k < 16 else (ws1, Rs1)
                            mm(PS[bP][:, 256:256 + n], wsx[:, k % 16, cs_], STg[xi][:, k, 0:n], start=(k == 0), stop=(k == 31),
                               r=[Rsx, R_xg[xi]], w=[RPS[bP]])
                        mi = ccnt["m"] % 2
                        ccnt["m"] += 1
                        act(sg[mi], PS[bG], AF.Sigmoid, r=[RPS[bG]], w=[R_sg[mi]])
                        tt("dve", mtmp[mi], sg[mi], PS[bP], ALU.mult, r=[R_sg[mi], RPS[bP]], w=[R_mt[mi]])
                        mts = [R_mg[t] for t in range(o0 // 128, (o0 + n) // 128)]
                        tt("pool", mergedT[:, 2 * fc + sb, o0:o0 + n], mtmp[mi][:, 0:n], mtmp[mi][:, 256:256 + n], ALU.add,
                           r=[R_mt[mi]], w=mts)
            A.pop()
            P.barrier()
            CK("c")
            hT_flat = hT.rearrange("p a b -> p (a b)")
            wo = [hT_flat[:, 0:8192].rearrange("p (k n) -> p k n", k=16),
                  hT_flat[:, 8192:16384].rearrange("p (k n) -> p k n", k=16), wslot[0], wslot[1]]
            R_wo, ds_wo = [Res(f"wo{i}") for i in range(4)], [DSem(f"wo{i}") for i in range(4)]
            for c in range(4):
                dma("pool", wo[c], w_o[:, 512 * c:512 * (c + 1)].rearrange("(k p) n -> p k n", p=128), w=[R_wo[c]], dsem=ds_wo[c])
            npost_bc = A.alloc([128, D], F32)
            R_np, ds_np = Res("np"), DSem("np")
            dma("sp", npost_bc, norm_post.partition_broadcast(128), w=[R_np], dsem=ds_np)
            o32 = [A.alloc([128, D], F32) for _ in range(2)]
            xr = [A.alloc([128, D], F32) for _ in range(2)]
            R_o32, R_xr = [Res("o0"), Res("o1")], [Res("xr0"), Res("xr1")]
            ds_xr, ds_y = [DSem("xr0"), DSem("xr1")], [DSem("y0"), DSem("y1")]
            dst_ = A.alloc([128, 2, 8], F32)
            R_dst = [Res("dst0"), Res("dst1")]
            djunk = A.alloc([128, 512], BF16)
            R_dj = Res("dj")
            def d_gen(t):
                ti = t - 1
                oi = ti % 2
                bb = 4 if oi == 0 else 0
                dma("sp", xr[oi], xin[t * 128:(t + 1) * 128, :], w=[R_xr[oi]], dsem=ds_xr[oi])
                st_ = dst_[:, oi, :]
                for c in range(4):
                    for k in range(16):
                        mm(PS[bb + c], mergedT[:, k, ti * 128:(ti + 1) * 128], wo[c][:, k, :], start=(k == 0), stop=(k == 15),
                           r=[R_mg[ti], R_wo[c]], w=[RPS[bb + c]])
                    act(djunk, PS[bb + c], AF.Square, accum_out=st_[:, c:c + 1], r=[RPS[bb + c]], w=[R_dj, R_dst[oi]])
                    cp("dve", o32[oi][:, 512 * c:512 * (c + 1)], PS[bb + c], r=[RPS[bb + c]], w=[R_o32[oi]])
                    yield
                tt("dve", st_[:, 4:5], st_[:, 0:1], st_[:, 1:2], ALU.add, r=[R_dst[oi]], w=[R_dst[oi]])
                tt("dve", st_[:, 5:6], st_[:, 2:3], st_[:, 3:4], ALU.add, r=[R_dst[oi]], w=[R_dst[oi]])
                tt("dve", st_[:, 4:5], st_[:, 4:5], st_[:, 5:6], ALU.add, r=[R_dst[oi]], w=[R_dst[oi]])
                ts("dve", st_[:, 4:5], st_[:, 4:5], 1.0 / D, EPS, ALU.mult, ALU.add, r=[R_dst[oi]], w=[R_dst[oi]])
                yield
                act(st_[:, 4:5], st_[:, 4:5], AF.Sqrt, r=[R_dst[oi]], w=[R_dst[oi]])
                P.add("dve", lambda e, o=st_[:, 6:7], i_=st_[:, 4:5]: e.reciprocal(out=o, in_=i_), r=[R_dst[oi]], w=[R_dst[oi]])
                yield
                ts("dve", o32[oi], o32[oi], st_[:, 6:7], None, ALU.mult, r=[R_o32[oi], R_dst[oi]], w=[R_o32[oi]])
                yield
                tt("pool", o32[oi], o32[oi], npost_bc, ALU.mult, r=[R_o32[oi], R_np], w=[R_o32[oi]])
                tt("pool", o32[oi], o32[oi], xr[oi], ALU.add, r=[R_o32[oi], R_xr[oi]], w=[R_o32[oi]])
                dma("sp", y_out[ti * 128:(ti + 1) * 128, :], o32[oi], r=[R_o32[oi]], dsem=ds_y[oi], outflag=True)
            run_interleaved((d_gen(t) for t in range(1, NT)), 2)
            A.pop()
        except _Stop:
            pass
        P.wait_tokens("sp", P.out_toks + P.all_dma)
        sems = [estack.enter_context(nc.semaphore(f"s{i}")) for i in range(len(P.keys))]
        P.emit(nc, sems)
    return nc, P


_NC_CACHE = {}


def make_in_maps(x_prompt, x_sample, cache_k, cache_v, state_ssm, state_conv, norm_pre, w_in, conv_w,
                 conv_b, dt_bias, a_log, d_skip, ssm_norm, attn_sinks, w_attn_br, w_ssm_br, w_out, norm_post):
    f = lambda a: np.ascontiguousarray(np.asarray(a, dtype=np.float32))
    cst_np = _build_consts()
    bp, bs = _alibi_tables()
    shared = {
        "w_in": f(w_in[0]), "w_attn_br": f(w_attn_br[0]), "w_ssm_br": f(w_ssm_br[0]), "w_out": f(w_out[0]),
        "norm_pre": f(norm_pre[0]), "conv_w": f(conv_w[0]), "conv_b": f(conv_b[0]), "dt_bias": f(dt_bias[0]),
        "a_log": f(a_log[0]), "d_skip": f(d_skip[0]), "ssm_norm": f(ssm_norm[0]), "attn_sinks": f(attn_sinks[0]),
        "norm_post": f(norm_post[0]), "cst": cst_np, "bias_p": bp, "bias_s": bs,
    }
    in_maps = []
    for c in range(NCORES):
        b, j = c // 4, c % 4
        xin = np.zeros((TOK, D), np.float32)
        if j > 0:
            xin[0:128] = x_prompt[b, 1024 * j - 128:1024 * j]
        xin[128:1152] = x_prompt[b, 1024 * j:1024 * (j + 1)]
        xin[1152:1280] = np.asarray(x_sample[16 * c:16 * (c + 1)]).reshape(128, D)
        hm = np.zeros((128, 256), np.float32)
        if j == 0:
            hm[:, 0:128] = NEG
        al = np.zeros((8,), np.float32)
        for i in range(4 * b, c):
            al[i] = 1.0
        xprev = np.zeros((3072, D), np.float32)
        pflag = np.zeros((24,), np.float32)
        if j > 0:
            xprev[3072 - 1024 * j:] = x_prompt[b, 0:1024 * j]
            pflag[24 - 8 * j:] = 1.0
        m = dict(shared)
        m.update({
            "xprev": xprev, "pflag": pflag,
            "xin": xin,
            "cache_k": f(np.asarray(cache_k[0, 16 * c:16 * (c + 1)]).reshape(16, 128, 512)),
            "cache_v": f(np.asarray(cache_v[0, 16 * c:16 * (c + 1)]).reshape(16, 128, 512)),
            "state_ssm": f(np.asarray(state_ssm[0, 16 * c:16 * (c + 1)]).reshape(16, 4096, 128)),
            "state_conv": f(np.asarray(state_conv[0, 16 * c:16 * (c + 1)]).reshape(48, 6144)),
            "halo_mask": hm, "alpha": al,
        })
        in_maps.append(m)
    return in_maps


def kernel(x_prompt, x_sample, cache_k, cache_v, state_ssm, state_conv, norm_pre, w_in, conv_w,
           conv_b, dt_bias, a_log, d_skip, ssm_norm, attn_sinks, w_attn_br, w_ssm_br, w_out, norm_post):
    in_maps = make_in_maps(x_prompt, x_sample, cache_k, cache_v, state_ssm, state_conv, norm_pre, w_in, conv_w,
                           conv_b, dt_bias, a_log, d_skip, ssm_norm, attn_sinks, w_attn_br, w_ssm_br, w_out, norm_post)
    if "nc" not in _NC_CACHE:
        _NC_CACHE["nc"] = build_nc()[0]
    nc = _NC_CACHE["nc"]
    res = run_bass_kernel_spmd(nc, in_maps, core_ids=list(range(NCORES)))
    return assemble(res.results)


def assemble(r):
    f32 = np.float32
    y_prompt = np.zeros((2, 4096, D), f32)
    y_sample = np.zeros((128, 8, D), f32)
    k_p = np.zeros((1, 2, 128, 8, 64), f32)
    v_p = np.zeros((1, 2, 128, 8, 64), f32)
    s_p = np.zeros((1, 2, 64, 64, 128), f32)
    c_p = np.zeros((1, 2, 3, 6144), f32)
    k_s = np.zeros((1, 128, 128, 8, 64), f32)
    v_s = np.zeros((1, 128, 128, 8, 64), f32)
    s_s = np.zeros((1, 128, 64, 64, 128), f32)
    c_s = np.zeros((1, 128, 3, 6144), f32)
    for c in range(NCORES):
        b, j = c // 4, c % 4
        o = r[c]
        y = np.asarray(o["y"], f32)
        y_prompt[b, 1024 * j:1024 * (j + 1)] = y[0:1024]
        y_sample[16 * c:16 * (c + 1)] = y[1024:1152].reshape(16, 8, D)
        k_s[0, 16 * c:16 * (c + 1)] = np.asarray(o["ks"], f32).reshape(16, 128, 8, 64)
        v_s[0, 16 * c:16 * (c + 1)] = np.asarray(o["vs"], f32).reshape(16, 128, 8, 64)
        s_s[0, 16 * c:16 * (c + 1)] = np.asarray(o["ss"], f32).reshape(16, 64, 64, 128)
        c_s[0, 16 * c:16 * (c + 1)] = np.asarray(o["cs"], f32).reshape(16, 3, 6144)
        if j == 3:
            k_p[0, b] = np.asarray(o["kp"], f32).reshape(128, 8, 64)
            v_p[0, b] = np.asarray(o["vp"], f32).reshape(128, 8, 64)
            s_p[0, b] = np.asarray(o["sp"], f32).reshape(64, 64, 128)
            c_p[0, b] = np.asarray(o["cp"], f32)
    return (y_prompt, y_sample, k_p, v_p, s_p, c_p, k_s, v_s, s_s, c_s)
```

```python
import numpy as np
import concourse.bass as bass
import concourse.mybir as mybir
from concourse.bass_utils import run_bass_kernel_spmd

F32 = mybir.dt.float32
BF16 = mybir.dt.bfloat16
AF = mybir.ActivationFunctionType
ALU = mybir.AluOpType
AX = mybir.AxisListType

NCORES = 8
D = 2048
NT = 10
TOK = NT * 128
NPT = 1152
IN_DIM = 19520
Q0, K0, V0, ZA0, XS0, B0, C0, ZS0, DT0, GA0, GS0 = 0, 2048, 2560, 3072, 5120, 9216, 10240, 11264, 15360, 15424, 17472
EPS = 1e-6
NEG = -30000.0
EPOCH = 60000


class _Stop(Exception):
    pass


class Res:
    __slots__ = ("name", "w", "r", "excl")

    def __init__(self, name="", excl=False):
        self.name = name
        self.w = None
        self.r = {}
        self.excl = excl


class Tok:
    __slots__ = ("key", "val", "clock")

    def __init__(self, key, val, clock):
        self.key, self.val, self.clock = key, val, clock


class DSem:
    def __init__(self, name, inc=16):
        self.key = ("d", name)
        self.count = 0
        self.inc = inc


class Prog:
    ENG = ("pe", "act", "dve", "pool", "sp")

    def __init__(self):
        self.streams = {e: [] for e in self.ENG}
        self.seq = {e: 0 for e in self.ENG}
        self.known = {e: {} for e in self.ENG}
        self.keys = {}
        self.out_toks = []
        self.all_dma = []

    def _deps(self, eng, r, w):
        toks = []
        for res in r:
            if res.w is not None:
                toks.append(res.w)
            if res.excl:
                for k, t in res.r.items():
                    if k[0] != eng:
                        toks.append(t)
        for res in w:
            if res.w is not None:
                toks.append(res.w)
            toks.extend(res.r.values())
        return toks

    def _waits(self, eng, toks):
        known = self.known[eng]
        need = {}
        for t in toks:
            if t.key[0] == "pe" and eng == "pe":
                continue
            if known.get(t.key, 0) >= t.val:
                continue
            if need.get(t.key, 0) < t.val:
                need[t.key] = t.val
        for t in toks:
            for k, v in t.clock.items():
                if known.get(k, 0) < v:
                    known[k] = v
        return list(need.items())

    def add(self, eng, fn, r=(), w=(), dsem=None, out=False):
        toks = self._deps(eng, r, w)
        waits = self._waits(eng, toks)
        if dsem is not None:
            dsem.count += dsem.inc
            key, val, inc = dsem.key, dsem.count, dsem.inc
        else:
            s = self.seq[eng]
            self.seq[eng] = s + 1
            key, val, inc = (eng, s // EPOCH), s % EPOCH + 1, 1
        self.keys[key] = True
        clock = dict(self.known[eng])
        clock[key] = val
        tok = Tok(key, val, clock)
        if eng == "pe" and dsem is None:
            self.known[eng][key] = val
        for res in w:
            res.w = tok
            res.r = {}
        for res in r:
            old = res.r.get(key)
            if old is None or old.val < val:
                res.r[key] = tok
        self.streams[eng].append((waits, fn, (key, inc)))
        if dsem is not None:
            self.all_dma.append(tok)
        if out:
            self.out_toks.append(tok)
        return tok

    def wait_tokens(self, eng, toks):
        waits = self._waits(eng, toks)
        if waits:
            self.streams[eng].append((waits, None, None))

    def barrier(self):
        last = []
        for e in self.ENG:
            if e == "sp":
                continue
            s = self.seq[e]
            if s > 0:
                s -= 1
                last.append(Tok((e, s // EPOCH), s % EPOCH + 1, {}))
        toks = last + self.all_dma
        self.all_dma = []
        for e in self.ENG:
            self.wait_tokens(e, toks)

    def emit(self, nc, sems):
        semmap = {}
        keys = list(self.keys.keys())
        assert len(keys) <= len(sems), (len(keys), len(sems))
        for k, s in zip(keys, sems):
            semmap[k] = s
        engobj = {"pe": "tensor", "act": "scalar", "dve": "vector", "pool": "gpsimd", "sp": "sync"}
        with nc.Block() as block:
            def mk(ename):
                def body(e):
                    for waits, fn, inc in self.streams[ename]:
                        for k, v in waits:
                            e.wait_ge(semmap[k], v)
                        if fn is not None:
                            ins = fn(e)
                            ins.then_inc(semmap[inc[0]], inc[1])
                return body
            block.tensor(mk("pe"))
            block.scalar(mk("act"))
            block.vector(mk("dve"))
            block.gpsimd(mk("pool"))
            block.sync(mk("sp"))


CST_LAYOUT = {}


def _build_consts():
    cols = []
    off = 0

    def put(name, arr):
        nonlocal off
        arr = np.asarray(arr, np.float32)
        assert arr.shape[0] == 128
        CST_LAYOUT[name] = (off, arr.shape[1])
        cols.append(arr)
        off += arr.shape[1]

    i = np.arange(128)
    put("ident", np.eye(128))
    put("tri", (i[:, None] <= i[None, :]).astype(np.float32))
    same = (i[:, None] // 8 == i[None, :] // 8)
    put("tris", ((i[:, None] <= i[None, :]) & same).astype(np.float32))
    put("mgt", (i[:, None] > i[None, :]).astype(np.float32))
    put("same", same.astype(np.float32))
    put("ones", np.ones((128, 128)))
    put("m16", (i[:, None] // 8 == np.arange(16)[None, :]).astype(np.float32))
    return np.concatenate(cols, axis=1)


def _alibi_tables():
    slopes = np.exp2(-8.0 * np.arange(1, 33, dtype=np.float32) / 32.0).astype(np.float32)
    ql = np.arange(128)[:, None]
    kl = np.arange(256)[None, :]
    dist = (128 + ql) - kl
    valid = (dist >= 0) & (dist < 128)
    bp = np.where(valid[None], -slopes[:, None, None] * dist[None].astype(np.float32), NEG).astype(np.float32)
    t = (np.arange(128) % 8)[:, None]
    s = (np.arange(128) // 8)[:, None]
    j = np.arange(128)[None, :]
    dist_c = 128 + t - j
    valid_c = dist_c < 128
    t2 = (np.arange(128) % 8)[None, :]
    s2 = (np.arange(128) // 8)[None, :]
    dist_n = t - t2
    valid_n = (s == s2) & (dist_n >= 0)
    dist_s = np.concatenate([dist_c, dist_n], axis=1)
    valid_s = np.concatenate([valid_c, valid_n], axis=1)
    bs = np.where(valid_s[None], -slopes[:, None, None] * dist_s[None].astype(np.float32), NEG).astype(np.float32)
    return bp, bs


class Arena:
    def __init__(self, t_f32, nbytes):
        self.t = t_f32
        self.n = nbytes
        self.off = 0
        self.marks = []

    def push(self):
        self.marks.append(self.off)

    def pop(self):
        self.off = self.marks.pop()

    def alloc(self, shape, dt):
        esz = 4 if dt == F32 else 2
        free = 1
        for s in shape[1:]:
            free *= s
        nb = (free * esz + 63) // 64 * 64
        assert self.off + nb <= self.n, ("SBUF arena overflow", self.off, nb, self.n)
        a = self.t[0:shape[0], self.off // 4:(self.off + nb) // 4]
        self.off += nb
        if dt != F32:
            a = a.bitcast(dt)
        a = a[:, 0:free]
        if len(shape) == 3:
            a = a.rearrange("p (a b) -> p a b", a=shape[1])
        elif len(shape) == 4:
            a = a.rearrange("p (a b c) -> p a b c", a=shape[1], b=shape[2])
        return a


def build_nc(stop_after=None, debug=False, nocc=False, prefix=True):
    nc = bass.Bass("TRN2", target_bir_lowering=False)
    cst_np = _build_consts()
    NCST = cst_np.shape[1]

    def din(name, shape):
        return nc.dram_tensor(name, list(shape), F32, kind="ExternalInput").ap()

    def dout(name, shape):
        return nc.dram_tensor(name, list(shape), F32, kind="ExternalOutput").ap()

    xin = din("xin", [TOK, D])
    w_in = din("w_in", [D, IN_DIM])
    w_ab = din("w_attn_br", [D, D])
    w_sb = din("w_ssm_br", [2 * D, D])
    w_o = din("w_out", [D, D])
    cache_k = din("cache_k", [16, 128, 512])
    cache_v = din("cache_v", [16, 128, 512])
    state_ssm = din("state_ssm", [16, 4096, 128])
    state_conv = din("state_conv", [48, 6144])
    norm_pre = din("norm_pre", [D])
    conv_w = din("conv_w", [4, 6144])
    conv_b = din("conv_b", [6144])
    dt_bias = din("dt_bias", [64])
    a_log = din("a_log", [64])
    d_skip = din("d_skip", [64])
    ssm_norm = din("ssm_norm", [4096])
    sinks = din("attn_sinks", [32])
    norm_post = din("norm_post", [D])
    cst = din("cst", [128, NCST])
    bias_p = din("bias_p", [32, 128, 256])
    bias_s = din("bias_s", [32, 128, 256])
    halo_mask = din("halo_mask", [128, 256])
    alpha = din("alpha", [8])
    xprev = din("xprev", [3072, D])
    pflag = din("pflag", [24])

    y_out = dout("y", [NPT, D])
    kp_out = dout("kp", [128, 512])
    vp_out = dout("vp", [128, 512])
    sp_out = dout("sp", [4096, 128])
    cp_out = dout("cp", [3, 6144])
    ks_out = dout("ks", [16, 128, 512])
    vs_out = dout("vs", [16, 128, 512])
    ss_out = dout("ss", [16, 4096, 128])
    cs_out = dout("cs", [48, 6144])

    if debug:
        AT_d = nc.dram_tensor("AT_d", [16, 128, NPT], BF16, kind="ExternalOutput").ap()
        ST_d = nc.dram_tensor("ST_d", [32, 128, NPT], BF16, kind="ExternalOutput").ap()
    else:
        AT_d = nc.dram_tensor("AT_d", [16, 128, NPT], BF16).ap()
        ST_d = nc.dram_tensor("ST_d", [32, 128, NPT], BF16).ap()
    Sst_d = [nc.dram_tensor(f"Sst_d{g}", [128, 512], F32) for g in range(8)]
    ag_in = [nc.dram_tensor(f"ag_in{g}", [128, 520], F32) for g in range(8)]
    ag_out = [nc.dram_tensor(f"ag_out{g}", [NCORES * 128, 520], F32) for g in range(8)]

    P = Prog()
    ARENA_BYTES = 207 * 1024

    from contextlib import ExitStack
    with ExitStack() as estack:
        arena_t = estack.enter_context(nc.sbuf_tensor("arena", [128, ARENA_BYTES // 4], F32))
        psum = [estack.enter_context(nc.psum_tensor(f"ps{i}", [128, 512], F32)) for i in range(8)]
        A = Arena(arena_t, ARENA_BYTES)
        PS = [p[:] for p in psum]
        RPS = [Res(f"ps{i}", excl=True) for i in range(8)]

        def PSb(i):
            return PS[i].bitcast(BF16)

        def dma(q, out, in_, r=(), w=(), dsem=None, outflag=False, slow=False):
            if slow:
                return P.add(q, lambda e: e.dma_start(out=out, in_=in_, allow_slow_non_contiguous=True), r=r, w=w, dsem=dsem, out=outflag)
            return P.add(q, lambda e: e.dma_start(out=out, in_=in_), r=r, w=w, dsem=dsem, out=outflag)

        def mm(out, lhsT, rhs, start=True, stop=True, r=(), w=()):
            return P.add("pe", lambda e: e.matmul(out, lhsT=lhsT, rhs=rhs, start=start, stop=stop), r=r, w=w)

        def tr(out, in_, ident, r=(), w=()):
            return P.add("pe", lambda e: e.transpose(out=out, in_=in_, identity=ident), r=r, w=w)

        def act(out, in_, func, r=(), w=(), bias=None, scale=None, accum_out=None):
            kw = {}
            if bias is not None:
                kw["bias"] = bias
            if scale is not None:
                kw["scale"] = scale
            if accum_out is not None:
                kw["accum_out"] = accum_out
            return P.add("act", lambda e: e.activation(out=out, in_=in_, func=func, **kw), r=r, w=w)

        def tt(eng, out, in0, in1, op, r=(), w=()):
            return P.add(eng, lambda e: e.tensor_tensor(out=out, in0=in0, in1=in1, op=op), r=r, w=w)

        def ts(eng, out, in0, s1, s2, op0, op1=None, r=(), w=(), accum_out=None):
            kw = {}
            if op1 is not None:
                kw["op1"] = op1
            if accum_out is not None:
                kw["accum_out"] = accum_out
            return P.add(eng, lambda e: e.tensor_scalar(out=out, in0=in0, scalar1=s1, scalar2=s2, op0=op0, **kw), r=r, w=w)

        def stt(eng, out, in0, scalar, in1, op0, op1, r=(), w=()):
            return P.add(eng, lambda e: e.scalar_tensor_tensor(out=out, in0=in0, scalar=scalar, in1=in1, op0=op0, op1=op1), r=r, w=w)

        def cp(eng, out, in_, r=(), w=()):
            if eng == "act":
                return P.add("act", lambda e: e.copy(out=out, in_=in_), r=r, w=w)
            return P.add(eng, lambda e: e.tensor_copy(out=out, in_=in_), r=r, w=w)

        def run_interleaved(gens, width):
            it_ = iter(gens)
            active = []
            while True:
                while len(active) < width:
                    try:
                        active.append(next(it_))
                    except StopIteration:
                        break
                if not active:
                    break
                for g_ in list(active):
                    try:
                        next(g_)
                    except StopIteration:
                        active.remove(g_)

        def memset(eng, ap, val, w=()):
            return P.add(eng, lambda e: e.memset(ap, val), w=w)

        R_const = Res("const")
        ds_const = DSem("const")
        cstt = A.alloc([128, NCST], F32)
        dma("sp", cstt, cst[:, :], dsem=ds_const)

        def C(name):
            o, n = CST_LAYOUT[name]
            return cstt[:, o:o + n]
        ident_f, TRI, TRIS, MGT, SAME, ONES, M16 = C("ident"), C("tri"), C("tris"), C("mgt"), C("same"), C("ones"), C("m16")
        ident_b = A.alloc([128, 128], BF16)
        o_id = CST_LAYOUT["ident"][0]
        ds_idb = DSem("idb")
        R_idb = Res("idb")
        dma("pool", ident_b, cst[:, o_id:o_id + 128], w=[R_idb], dsem=ds_idb)
        npre_pk = A.alloc([128, 16], F32)
        cw_pk = A.alloc([128, 48, 4], F32)
        cb_pk = A.alloc([128, 48], F32)
        ssn_pk = A.alloc([128, 32], F32)
        npost_pk = None
        pk_srcs = [(norm_pre, 16), (conv_b, 48), (ssm_norm, 32), (conv_w[0], 48), (conv_w[1], 48), (conv_w[2], 48), (conv_w[3], 48)]
        dtb_bc = A.alloc([128, 64], F32)
        dma("sp", dtb_bc, dt_bias.partition_broadcast(128), dsem=ds_const)
        alog_bc = A.alloc([128, 64], F32)
        dma("sp", alog_bc, a_log.partition_broadcast(128), dsem=ds_const)
        dsk_bc = A.alloc([128, 64], F32)
        dma("sp", dsk_bc, d_skip.partition_broadcast(128), dsem=ds_const)
        sink_bc = A.alloc([128, 32], F32)
        dma("sp", sink_bc, sinks.partition_broadcast(128), dsem=ds_const)
        alpha_bc = A.alloc([128, 8], F32)
        dma("sp", alpha_bc, alpha.partition_broadcast(128), dsem=ds_const)
        hmask = A.alloc([128, 256], F32)
        dma("sp", hmask, halo_mask[:, :], dsem=ds_const)
        R_const.w = Tok(ds_const.key, ds_const.count, {ds_const.key: ds_const.count})
        RC = [R_const]
        A.push()
        pk_tmp = A.alloc([128, 6, 128], F32)
        R_pk = Res("pk")
        ds_pk = DSem("pk")
        pk_dst = [npre_pk, cb_pk, ssn_pk] + [cw_pk[:, :, j] for j in range(4)]
        for i, ((src, nb), dst) in enumerate(zip(pk_srcs, pk_dst)):
            slot = i % 6
            if i == 6:
                P.barrier()
            dma("sp", pk_tmp[0:nb, slot, :], src.rearrange("(b p) -> b p", p=128), w=[R_pk], dsem=ds_pk)
            mm(PS[0][:, 0:nb], pk_tmp[0:nb, slot, :], ident_f[0:nb, 0:nb], r=[R_pk] + RC, w=[RPS[0]])
            cp("dve", dst, PS[0][:, 0:nb], r=[RPS[0]], w=[R_pk])
        RC = [R_const, R_pk, R_idb]
        A.pop()
        P.barrier()

        NPF = 24
        ibank = {"b": 0}

        def ipbank():
            b = ibank["b"]
            ibank["b"] = 1 - b
            return b

        def norm_tile(src_rows, xb, rx, dsx, dst, Rdst, stcol, Rst, junk_, Rjunk):
            dma("sp", xb, src_rows, w=[rx], dsem=dsx)
            act(junk_, xb, AF.Square, r=[rx], w=[Rjunk, Rst], accum_out=stcol)
            ts("dve", stcol, stcol, 1.0 / D, EPS, ALU.mult, ALU.add, r=[Rst], w=[Rst])
            act(stcol, stcol, AF.Sqrt, r=[Rst], w=[Rst])
            P.add("dve", lambda e, o=stcol: e.reciprocal(out=o, in_=o), r=[Rst], w=[Rst])
            tt("pool", xb, xb, stcol.broadcast_to([128, D]), ALU.mult, r=[rx, Rst], w=[rx])
            for b in range(4):
                bank = b % 2
                for q in range(4):
                    kb = b * 4 + q
                    mm(PS[bank][:, q * 128:(q + 1) * 128], xb[:, kb * 128:(kb + 1) * 128], ident_f,
                       r=[rx] + RC, w=[RPS[bank]])
                src = PS[bank].rearrange("p (a b) -> p a b", a=4)
                sc_ = npre_pk[:, b * 4:(b + 1) * 4].unsqueeze(2).broadcast_to([128, 4, 128])
                tt("dve", dst(b), src, sc_, ALU.mult, r=[RPS[bank]] + RC, w=[Rdst])

        R_SstD = [Res(f"SstD{g}") for g in range(8)]
        if prefix:
            A.push()
            hTp = A.alloc([128, 16, NPF * 128], BF16)
            R_hTp = [Res(f"hTp{t}") for t in range(NPF)]
            pw = [A.alloc([128, 16, 512], BF16) for _ in range(2)]
            R_pw, ds_pw = [Res("pw0"), Res("pw1")], [DSem("pw0"), DSem("pw1")]
            pwn = {"n": 0}

            def load_p(src, n):
                i = pwn["n"] % 2
                pwn["n"] += 1
                dma("pool", pw[i][:, :, 0:n], src.rearrange("(k p) n -> p k n", p=128), w=[R_pw[i]], dsem=ds_pw[i])
                return pw[i], R_pw[i]

            pxb = [A.alloc([128, D], F32) for _ in range(2)]
            R_px, ds_px = [Res("px0"), Res("px1")], [DSem("px0"), DSem("px1")]
            pjunk = A.alloc([128, D], BF16)
            R_pj = Res("pjunk")
            pst = A.alloc([128, NPF], F32)
            R_pst = [Res(f"pst{t}") for t in range(NPF)]
            pflag_bc = A.alloc([128, NPF], F32)
            R_pf, ds_pf = Res("pflag"), DSem("pflag")
            dma("sp", pflag_bc, pflag.partition_broadcast(128), w=[R_pf], dsem=ds_pf)
            dtdP = A.alloc([128, NPF, 64], F32)
            eaeP = A.alloc([128, NPF, 64], F32)
            R_pdt = [Res(f"pdt{t}") for t in range(NPF)]
            pA_bc = A.alloc([128, 64], F32)
            R_pA = Res("pA")
            ptmp = A.alloc([128, 8, 64], F32)
            R_pt = Res("ptmp")
            for t in range(NPF):
                norm_tile(xprev[t * 128:(t + 1) * 128, :], pxb[t % 2], R_px[t % 2], ds_px[t % 2],
                          lambda b, t=t: hTp[:, b * 4:(b + 1) * 4, t * 128:(t + 1) * 128], R_hTp[t],
                          pst[:, t:t + 1], R_pst[t], pjunk, R_pj)
            act(pA_bc, alog_bc, AF.Exp, r=RC, w=[R_pA])
            ts("dve", pA_bc, pA_bc, -1.0, None, ALU.mult, r=[R_pA], w=[R_pA])
            wtd, Rwd = load_p(w_in[:, DT0:DT0 + 64], 64)
            for t in range(NPF):
                bank = ipbank()
                for k in range(16):
                    mm(PS[bank][:, 0:64], hTp[:, k, t * 128:(t + 1) * 128], wtd[:, k, 0:64],
                       start=(k == 0), stop=(k == 15), r=[Rwd, R_hTp[t]], w=[RPS[bank]])
                xx, ax, ee, ll, tmp, dtp, adtp, at_ = [ptmp[:, n, :] for n in range(8)]
                tt("dve", xx, PS[bank][:, 0:64], dtb_bc, ALU.add, r=[RPS[bank]] + RC, w=[R_pt])
                act(ax, xx, AF.Abs, r=[R_pt], w=[R_pt])
                act(ee, ax, AF.Exp, scale=-1.0, r=[R_pt], w=[R_pt])
                ts("dve", ee, ee, 1.0, None, ALU.add, r=[R_pt], w=[R_pt])
                act(ll, ee, AF.Ln, r=[R_pt], w=[R_pt])
                stt("dve", dtp, xx, 0.0, ll, ALU.max, ALU.add, r=[R_pt], w=[R_pt])
                ts("dve", dtp, dtp, pflag_bc[:, t:t + 1], None, ALU.mult, r=[R_pt, R_pf], w=[R_pt])
                tt("dve", adtp, dtp, pA_bc, ALU.mult, r=[R_pt, R_pA], w=[R_pt])
                b2 = ipbank()
                mm(PS[b2][:, 0:64], TRI, adtp, r=[R_pt] + RC, w=[RPS[b2]])
                mm(PS[b2][:, 64:128], ONES, adtp, r=[R_pt] + RC, w=[RPS[b2]])
                cp("dve", at_, PS[b2][:, 0:64], r=[RPS[b2]], w=[R_pt])
                tt("dve", tmp, PS[b2][:, 64:128], at_, ALU.subtract, r=[RPS[b2], R_pt], w=[R_pt])
                act(tmp, tmp, AF.Exp, r=[R_pt], w=[R_pt])
                tt("dve", dtdP[:, t, :], dtp, tmp, ALU.mult, r=[R_pt], w=[R_pdt[t]])
                act(eaeP[:, t, :], PS[b2][:, 64:128], AF.Exp, r=[RPS[b2]], w=[R_pdt[t]])
            pxp = A.alloc([128, 5, 515], F32)
            R_pxp = [Res(f"pxp{b}") for b in range(5)]
            pacc2 = [A.alloc([128, 512], F32) for _ in range(5)]
            R_pacc2 = [Res(f"pacc{i}") for i in range(5)]
            pcv2 = [A.alloc([128, 5, 512], BF16) for _ in range(2)]
            R_pcv2 = [[Res(f"pcv{i}_{b}") for b in range(5)] for i in range(2)]
            pxt = [A.alloc([128, 640], BF16) for _ in range(2)]
            R_pxt = [Res("pxt0"), Res("pxt1")]
            pxd = [A.alloc([128, 512], BF16) for _ in range(2)]
            R_pxd = [Res("pxd0"), Res("pxd1")]
            pST = [A.alloc([128, 512], F32)] * 2
            R_pST, ds_pST = [Res("pST0")] * 2, [DSem("pST0")] * 2
            pc = {"x": 0}
            for g in range(8):
                wxa, Rxa = load_p(w_in[:, XS0 + 512 * g:XS0 + 512 * (g + 1)], 512)
                wxb, Rxb = load_p(w_in[:, B0 + 128 * g:B0 + 128 * (g + 1)], 128)
                STp, RSTp = pST[g % 2], R_pST[g % 2]
                memset("pool", STp, 0.0, w=[RSTp])
                for blk in range(5):
                    memset("pool", pxp[:, blk, 0:3], 0.0, w=[R_pxp[blk]])
                blocks_done = {}
                blk_active = {}
                free_pacc = [0, 1]
                free_tail = [0, 1]
                tails_tr = {}
                upd_done = {-1: True}

                def blk_gen(s, blk):
                    tiles = [R_hTp[t] for t in range(4 * s, 4 * s + 4)]
                    wsl, Rws, c0 = (wxa, Rxa, blk * 128) if blk < 4 else (wxb, Rxb, 0)
                    cbi = (4 * g + blk) if blk < 4 else (32 + g)
                    while blk_active.get(blk):
                        yield
                    blk_active[blk] = True
                    pslot = blk
                    pacc_, R_pacc_ = pacc2[pslot], R_pacc2[pslot]
                    pcv_, R_pcv_ = pcv2[s % 2], R_pcv2[s % 2]
                    bank = (0, 1, 3, 4, 5)[blk]
                    for k in range(16):
                        mm(PS[bank], wsl[:, k, c0:c0 + 128], hTp[:, k, s * 512:(s + 1) * 512],
                           start=(k == 0), stop=(k == 15), r=[Rws] + tiles, w=[RPS[bank]])
                    xp = pxp[:, blk, :]
                    cp("act", xp[:, 3:515], PS[bank], r=[RPS[bank]], w=[R_pxp[blk]])
                    yield
                    ts("dve", pacc_, xp[:, 0:512], cw_pk[:, cbi, 0:1], None, ALU.mult, r=[R_pxp[blk]] + RC, w=[R_pacc_])
                    yield
                    for j in range(1, 4):
                        stt("dve", pacc_, xp[:, j:j + 512], cw_pk[:, cbi, j:j + 1], pacc_, ALU.mult, ALU.add,
                            r=[R_pxp[blk], R_pacc_] + RC, w=[R_pacc_])
                        yield
                    while s >= 2 and tails_tr.get(s - 2, 0) < 4:
                        yield
                    act(pcv_[:, blk, :], pacc_, AF.Silu, bias=cb_pk[:, cbi:cbi + 1], r=[R_pacc_] + RC, w=[R_pcv_[blk]])
                    cp("pool", xp[:, 0:3], xp[:, 512:515], r=[R_pxp[blk]], w=[R_pxp[blk]])
                    blocks_done[s] = blocks_done.get(s, 0) + 1
                    blk_active[blk] = False

                def tail_gen(s, q):
                    t = 4 * s + q
                    pcv_, R_pcv_ = pcv2[s % 2], R_pcv2[s % 2]
                    while blocks_done.get(s, 0) < 5 or not free_tail:
                        yield
                    xi = free_tail.pop()
                    bT = 7 if xi == 0 else 2
                    for blk in range(5):
                        tr(PSb(bT)[:, blk * 128:(blk + 1) * 128], pcv_[:, blk, q * 128:(q + 1) * 128], ident_b,
                           r=[R_pcv_[blk]] + RC, w=[RPS[bT]])
                    tails_tr[s] = tails_tr.get(s, 0) + 1
                    cp("act", pxt[xi], PSb(bT)[:, 0:640], r=[RPS[bT]], w=[R_pxt[xi]])
                    yield
                    tt("pool", pxd[xi].rearrange("p (h d) -> p h d", h=8), pxt[xi][:, 0:512].rearrange("p (h d) -> p h d", h=8),
                       dtdP[:, t, 8 * g:8 * g + 8].unsqueeze(2).broadcast_to([128, 8, 64]), ALU.mult,
                       r=[R_pxt[xi], R_pdt[t]], w=[R_pxd[xi]])
                    yield
                    while not upd_done.get(t - 1):
                        yield
                    mm(PS[6], pxt[xi][:, 512:640], pxd[xi], r=[R_pxt[xi], R_pxd[xi]], w=[RPS[6]])
                    tt("dve", STp.rearrange("p (h d) -> p h d", h=8), STp.rearrange("p (h d) -> p h d", h=8),
                       eaeP[:, t, 8 * g:8 * g + 8].unsqueeze(2).broadcast_to([128, 8, 64]), ALU.mult,
                       r=[RSTp, R_pdt[t]], w=[RSTp])
                    tt("dve", STp, STp, PS[6], ALU.add, r=[RSTp, RPS[6]], w=[RSTp])
                    upd_done[t] = True
                    free_tail.append(xi)

                def all_gens():
                    for s in range(NPF // 4):
                        for blk in range(5):
                            yield blk_gen(s, blk)
                        for q in range(4):
                            yield tail_gen(s, q)

                run_interleaved(all_gens(), 6)
                dma("sp", Sst_d[g].ap(), STp, r=[RSTp], w=[R_SstD[g]], dsem=ds_pST[g % 2])
            A.pop()
            P.barrier()

        hT = A.alloc([128, 16, TOK], BF16)
        R_hT = [Res(f"hT{t}") for t in range(NT)]

        NSLOT = 2
        wslot = [A.alloc([128, 16, 512], BF16) for _ in range(NSLOT)]
        R_w = [Res(f"w{i}") for i in range(NSLOT)]
        ds_w = [DSem(f"w{i}") for i in range(NSLOT)]
        wstate = {"n": 0}

        def load_w(segs):
            i = wstate["n"] % NSLOT
            wstate["n"] += 1
            for k, (src, off) in enumerate(segs):
                n = src.shape[1]
                dma("pool", wslot[i][:, :, off:off + n], src.rearrange("(k p) n -> p k n", p=128),
                    w=[R_w[i]] if k == 0 else [], dsem=ds_w[i])
                if k > 0:
                    R_w[i].w = P.all_dma[-1]
            return wslot[i], R_w[i]

        def CK(name):
            if stop_after == name:
                raise _Stop()

        try:
            A.push()
            xbuf = [A.alloc([128, D], F32) for _ in range(2)]
            R_x = [Res("x0"), Res("x1")]
            ds_x = [DSem("x0"), DSem("x1")]
            junk = A.alloc([128, D], F32)
            R_junk = Res("junk")
            ssq = A.alloc([128, NT], F32)
            rstd = A.alloc([128, NT], F32)
            R_st = [Res(f"st{t}") for t in range(NT)]
            for t in range(NT):
                xb, rx = xbuf[t % 2], R_x[t % 2]
                dma("sp", xb, xin[t * 128:(t + 1) * 128, :], w=[rx], dsem=ds_x[t % 2])
                act(junk, xb, AF.Square, r=[rx], w=[R_junk, R_st[t]], accum_out=ssq[:, t:t + 1])
                ts("dve", rstd[:, t:t + 1], ssq[:, t:t + 1], 1.0 / D, EPS, ALU.mult, ALU.add, r=[R_st[t]], w=[R_st[t]])
                act(rstd[:, t:t + 1], rstd[:, t:t + 1], AF.Sqrt, r=[R_st[t]], w=[R_st[t]])
                P.add("dve", lambda e, o=rstd[:, t:t + 1]: e.reciprocal(out=o, in_=o), r=[R_st[t]], w=[R_st[t]])
                tt("pool", xb, xb, rstd[:, t:t + 1].broadcast_to([128, D]), ALU.mult, r=[rx, R_st[t]], w=[rx])
                for b in range(4):
                    bank = b % 2
                    for q in range(4):
                        kb = b * 4 + q
                        mm(PS[bank][:, q * 128:(q + 1) * 128], xb[:, kb * 128:(kb + 1) * 128], ident_f,
                           r=[rx] + RC, w=[RPS[bank]])
                    eng = "dve" if b % 2 == 0 else "pool"
                    src = PS[bank].rearrange("p (a b) -> p a b", a=4)
                    dst = hT[:, b * 4:(b + 1) * 4, t * 128:(t + 1) * 128]
                    sc = npre_pk[:, b * 4:(b + 1) * 4].unsqueeze(2).broadcast_to([128, 4, 128])
                    tt("dve", dst, src, sc, ALU.mult, r=[RPS[bank]] + RC, w=[R_hT[t]])
            A.pop()
            P.barrier()
            CK("p0")
            A.push()
            qT = A.alloc([128, 2, NPT], BF16)
            R_qT = Res("qT")
            kT = A.alloc([128, TOK], BF16)
            R_kT = Res("kT")
            vb = A.alloc([128, NT, 64], BF16)
            R_vb = [Res(f"vb{t}") for t in range(NT)]
            sz = A.alloc([128, NT, 256], BF16)
            R_sz = [Res(f"sz{t}") for t in range(NT)]
            kvnew = A.alloc([128, 2, 2, 512], F32)
            R_kvnew = Res("kvnew")
            biasP = A.alloc([128, 4, 256], F32)
            biasS = A.alloc([128, 4, 256], F32)
            R_bias = Res("bias")
            ds_bias = DSem("bias")
            Sb = [A.alloc([128, 4, 256], F32) for _ in range(2)]
            Pb = [A.alloc([128, 4, 256], BF16) for _ in range(2)]
            PTs = [A.alloc([128, 8, 128], BF16) for _ in range(2)]
            stat = [A.alloc([128, 8, 4], F32) for _ in range(2)]
            An = [A.alloc([128, 256], F32) for _ in range(2)]
            Ag = [A.alloc([128, 256], BF16) for _ in range(2)]
            R_it = [[Res(f"it{i}_{n}") for n in range(8)] for i in range(2)]
            ATs = A.alloc([128, 2, NPT], BF16)
            R_ATs = Res("ATs")
            ds_AT = DSem("AT")
            R_ATd = Res("ATd")
            kc = A.alloc([128, 16, 128], BF16)
            vc = A.alloc([128, 16, 64], BF16)
            R_kc, R_vc = Res("kc"), Res("vc")
            ds_kc = DSem("kc")
            ds_vc = DSem("vc")
            KcT = A.alloc([128, 16, 128], BF16)
            R_KcT = Res("KcT")
            Zq = [A.alloc([128, 16, 248], BF16) for _ in range(2)]
            R_Zq = [Res("Zq0"), Res("Zq1")]
            ZP = A.alloc([128, 2, 16, 248], BF16)
            R_ZP = Res("ZP")
            memset("pool", Zq[0], 0.0, w=[R_Zq[0]])
            memset("pool", Zq[1], 0.0, w=[R_Zq[1]])
            memset("pool", ZP, 0.0, w=[R_ZP])
            itc = {"n": 0, "bank": 0}

            for g in range(8):
                wt, Rw = load_w([(w_in[:, Q0 + 256 * g:Q0 + 256 * (g + 1)], 0),
                                 (w_in[:, K0 + 64 * g:K0 + 64 * (g + 1)], 256),
                                 (w_in[:, K0 + 64 * g:K0 + 64 * (g + 1)], 320),
                                 (w_in[:, V0 + 64 * g:V0 + 64 * (g + 1)], 384)])
                wt2, Rw2 = load_w([(w_in[:, ZA0 + 256 * g:ZA0 + 256 * (g + 1)], 0)])
                dma("sp", biasP, bias_p[4 * g:4 * g + 4].rearrange("h p k -> p h k"), w=[R_bias], dsem=ds_bias)
                dma("sp", biasS, bias_s[4 * g:4 * g + 4].rearrange("h p k -> p h k"), dsem=ds_bias)
                R_bias.w = P.all_dma[-1]
                dma("pool", kc[:, :, 0:64], cache_k[:, :, 64 * g:64 * (g + 1)].rearrange("s k d -> k s d"), w=[R_kc], dsem=ds_kc)
                dma("pool", kc[:, :, 64:128], cache_k[:, :, 64 * g:64 * (g + 1)].rearrange("s k d -> k s d"), dsem=ds_kc)
                R_kc.w = P.all_dma[-1]
                dma("pool", vc, cache_v[:, :, 64 * g:64 * (g + 1)].rearrange("s k d -> k s d"), w=[R_vc], dsem=ds_vc)

                if g == 0:
                    CK("a0")
                for blk in range(3):
                    ranges = [(128, 512), (640, 512), (1152, 128)] if blk < 2 else [(0, 512), (512, 512), (1024, 256)]
                    for (t0, n) in ranges:
                        bank = ipbank()
                        tiles = [R_hT[t] for t in range(t0 // 128, (t0 + n) // 128)]
                        for k in range(16):
                            mm(PS[bank][:, 0:n], wt[:, k, blk * 128:(blk + 1) * 128], hT[:, k, t0:t0 + n],
                               start=(k == 0), stop=(k == 15), r=[Rw] + tiles, w=[RPS[bank]])
                        if blk < 2:
                            ts("dve", qT[:, blk, t0 - 128:t0 - 128 + n], PS[bank][:, 0:n], 0.125, None, ALU.mult,
                               r=[RPS[bank]], w=[R_qT])
                        else:
                            cp("act", kT[:, t0:t0 + n], PS[bank][:, 0:n], r=[RPS[bank]], w=[R_kT])
                if g == 0:
                    CK("a1")
                for t in range(NT):
                    bank = ipbank()
                    for k in range(16):
                        mm(PS[bank][:, 0:128], hT[:, k, t * 128:(t + 1) * 128], wt[:, k, 320:448],
                           start=(k == 0), stop=(k == 15), r=[Rw, R_hT[t]], w=[RPS[bank]])
                    if t >= 1:
                        for k in range(16):
                            mm(PS[bank][:, 128:384], hT[:, k, t * 128:(t + 1) * 128], wt2[:, k, 0:256],
                               start=(k == 0), stop=(k == 15), r=[Rw2, R_hT[t]], w=[RPS[bank]])
                    cp("dve", vb[:, t, :], PS[bank][:, 64:128], r=[RPS[bank]], w=[R_vb[t]])
                    if t >= 8:
                        cp("dve", kvnew[:, t - 8, :, 64 * g:64 * (g + 1)], PS[bank][:, 0:128].rearrange("p (a b) -> p a b", a=2),
                           r=[RPS[bank]], w=[R_kvnew])
                    if t >= 1:
                        act(sz[:, t, :], PS[bank][:, 128:384], AF.Silu, r=[RPS[bank]], w=[R_sz[t]])

                if g == 0:
                    CK("a_inproj")
                for hf in range(2):
                    for s8 in range(8):
                        s = hf * 8 + s8
                        tr(PSb(7)[:, s8 * 128:(s8 + 1) * 128], kc[:, s, :], ident_b, r=[R_kc] + RC, w=[RPS[7]])
                    cp("act", KcT[:, hf * 8:(hf + 1) * 8, :], PSb(7).rearrange("p (a b) -> p a b", a=8), r=[RPS[7]], w=[R_KcT])
                for b in range(2):
                    cp("pool", Zq[b][:, :, 120:128], qT[:, b, 1024:1152].rearrange("p (s t) -> p s t", s=16),
                       r=[R_qT], w=[R_Zq[b]])

                if g == 0:
                    CK("a3")
                def attn_gen(i):
                    sample = (i == 9)
                    it = (i - 1) % 2
                    bS = (2, 3) if it == 0 else (0, 1)
                    bPT = 4 if it == 0 else 7
                    bO = 5 if it == 0 else 6
                    RS, RP, RPT, RST, RAN, RAG = R_it[it][0:6]
                    S_, P_, PT_, st_, An_, Ag_ = Sb[it], Pb[it], PTs[it], stat[it], An[it], Ag[it]
                    rmax, mx, negm, rsum, smm, es, den, rec = [st_[:, n, :] for n in range(8)]
                    for j in range(4):
                        b, jj = j // 2, j % 2
                        pr = slice(jj * 64, (jj + 1) * 64)
                        reg = PS[bS[jj]][:, b * 256:(b + 1) * 256]
                        if not sample:
                            mm(reg, qT[pr, b, (i - 1) * 128:i * 128], kT[pr, (i - 1) * 128:(i + 1) * 128],
                               r=[R_qT, R_kT], w=[RPS[bS[jj]]])
                        else:
                            for s in range(16):
                                mm(reg[:, 0:128], Zq[b][pr, s, 120 - 8 * s:248 - 8 * s], KcT[pr, s, :],
                                   start=(s == 0), stop=(s == 15), r=[R_Zq[b], R_KcT], w=[RPS[bS[jj]]])
                            mm(reg[:, 128:256], qT[pr, b, 1024:1152], kT[pr, 1152:1280], r=[R_qT, R_kT], w=[RPS[bS[jj]]])
                    bias_t = biasS if sample else biasP
                    for jj in range(2):
                        tt("dve", S_.rearrange("p (b j) k -> p b j k", j=2)[:, :, jj, :], PS[bS[jj]].rearrange("p (a b) -> p a b", a=2),
                           bias_t.rearrange("p (b j) k -> p b j k", j=2)[:, :, jj, :], ALU.add, r=[RPS[bS[jj]], R_bias], w=[RS])
                    yield
                    if i == 1:
                        tt("dve", S_, S_, hmask.unsqueeze(1).broadcast_to([128, 4, 256]), ALU.add, r=[RS] + RC, w=[RS])
                    P.add("dve", lambda e, o=rmax, s_=S_: e.tensor_reduce(out=o, in_=s_, axis=AX.X, op=ALU.max), r=[RS], w=[RST])
                    tt("dve", mx, rmax, sink_bc[:, 4 * g:4 * g + 4], ALU.max, r=[RST] + RC, w=[RST])
                    ts("dve", negm, mx, -1.0, None, ALU.mult, r=[RST], w=[RST])
                    tt("dve", smm, sink_bc[:, 4 * g:4 * g + 4], mx, ALU.subtract, r=[RST] + RC, w=[RST])
                    yield
                    for j in range(4):
                        act(P_[:, j, :], S_[:, j, :], AF.Exp, bias=negm[:, j:j + 1], accum_out=rsum[:, j:j + 1],
                            r=[RS, RST], w=[RP, RST])
                    act(es, smm, AF.Exp, r=[RST], w=[RST])
                    yield
                    tt("dve", den, rsum, es, ALU.add, r=[RST], w=[RST])
                    P.add("dve", lambda e, o=rec, d_=den: e.reciprocal(out=o, in_=d_), r=[RST], w=[RST])
                    yield
                    for j in range(4):
                        for hf in range(2):
                            tr(PSb(bPT)[:, (2 * j + hf) * 128:(2 * j + hf + 1) * 128], P_[:, j, hf * 128:(hf + 1) * 128], ident_b,
                               r=[RP] + RC, w=[RPS[bPT]])
                    cp("act", PT_, PSb(bPT).rearrange("p (a b) -> p a b", a=8), r=[RPS[bPT]], w=[RPT])
                    yield
                    if not sample:
                        for j in range(4):
                            mm(PS[bO][:, j * 64:(j + 1) * 64], PT_[:, 2 * j, :], vb[:, i - 1, :], start=True, stop=False,
                               r=[RPT, R_vb[i - 1]], w=[RPS[bO]])
                            mm(PS[bO][:, j * 64:(j + 1) * 64], PT_[:, 2 * j + 1, :], vb[:, i, :], start=False, stop=True,
                               r=[RPT, R_vb[i]], w=[RPS[bO]])
                    else:
                        for b in range(2):
                            src = PT_.rearrange("p (j h) k -> p j h k", h=2)[:, 2 * b:2 * b + 2, 0, :]
                            cp("pool", ZP[:, :, :, 120:128], src.rearrange("p j (s t) -> p j s t", s=16), r=[RPT], w=[R_ZP])
                            for jj in range(2):
                                j = 2 * b + jj
                                for s in range(16):
                                    mm(PS[bO][:, j * 64:(j + 1) * 64], ZP[:, jj, s, 120 - 8 * s:248 - 8 * s], vc[:, s, :],
                                       start=(s == 0), stop=False, r=[R_ZP, R_vc], w=[RPS[bO]])
                                mm(PS[bO][:, j * 64:(j + 1) * 64], PT_[:, 2 * j + 1, :], vb[:, 9, :], start=False, stop=True,
                                   r=[RPT, R_vb[9]], w=[RPS[bO]])
                    tt("dve", An_.rearrange("p (a b) -> p a b", a=4), PS[bO][:, 0:256].rearrange("p (a b) -> p a b", a=4),
                       rec.unsqueeze(2).broadcast_to([128, 4, 64]), ALU.mult, r=[RPS[bO], RST], w=[RAN])
                    tt("pool", Ag_, An_, sz[:, i, :], ALU.mult, r=[RAN, R_sz[i]], w=[RAG])
                    yield
                    for b in range(2):
                        tr(PSb(bO)[:, 512 + b * 128:512 + (b + 1) * 128], Ag_[:, b * 128:(b + 1) * 128], ident_b, r=[RAG] + RC, w=[RPS[bO]])
                    cp("act", ATs[:, :, (i - 1) * 128:i * 128], PSb(bO)[:, 512:768].rearrange("p (a b) -> p a b", a=2),
                       r=[RPS[bO]], w=[R_ATs])
                run_interleaved((attn_gen(i) for i in range(1, NT)), 2)
                dma("sp", AT_d[2 * g:2 * g + 2].rearrange("b p t -> p b t"), ATs, r=[R_ATs], w=[R_ATd], dsem=ds_AT)
                if g == 0:
                    CK("a_g0")

            CK("a_attn")
            ds_kv = DSem("kvout")
            dma("sp", kp_out[:, :], kvnew[:, 0, 0, :], r=[R_kvnew], dsem=ds_kv, outflag=True)
            dma("sp", vp_out[:, :], kvnew[:, 0, 1, :], r=[R_kvnew], dsem=ds_kv, outflag=True)
            for s in range(16):
                dma("sp", ks_out[s, 120:128, :], kvnew[8 * s:8 * s + 8, 1, 0, :], r=[R_kvnew], dsem=ds_kv, outflag=True)
                dma("sp", vs_out[s, 120:128, :], kvnew[8 * s:8 * s + 8, 1, 1, :], r=[R_kvnew], dsem=ds_kv, outflag=True)
            dma("sp", ks_out[:, 0:120, :], cache_k[:, 8:128, :], dsem=ds_kv, outflag=True)
            dma("sp", vs_out[:, 0:120, :], cache_v[:, 8:128, :], dsem=ds_kv, outflag=True)
            A.pop()
            P.barrier()
            A.push()
            NTT = 9
            dt_all = A.alloc([128, NTT, 64], F32)
            adt_all = A.alloc([128, NTT, 64], F32)
            eat_all = A.alloc([128, NTT, 64], F32)
            dtd_all = A.alloc([128, NTT, 64], F32)
            eaend_all = A.alloc([128, NTT, 64], F32)
            eatg_all = None if prefix else A.alloc([128, NTT, 64], F32)
            R_dt = [Res(f"dt{t}") for t in range(NTT)]
            A_bc = A.alloc([128, 64], F32)
            cum_bc = A.alloc([128, 64], F32)
            R_cum = Res("cum")
            E_samp = A.alloc([128, 512], F32)
            R_Es = Res("Es")
            A.push()
            b0tmp = A.alloc([128, 8, 64], F32)
            R_b0 = Res("b0tmp")
            Xeo = A.alloc([128, 2, 512], F32)
            R_Xeo = Res("Xeo")

            act(A_bc, alog_bc, AF.Exp, r=RC, w=[R_cum])
            ts("dve", A_bc, A_bc, -1.0, None, ALU.mult, r=[R_cum], w=[R_cum])
            memset("pool", cum_bc, 0.0, w=[R_cum])
            wt, Rw = load_w([(w_in[:, DT0:DT0 + 64], 0)])
            for t in range(1, NT):
                ti = t - 1
                smp = (t == 9)
                bank = ipbank()
                for k in range(16):
                    mm(PS[bank][:, 0:64], hT[:, k, t * 128:(t + 1) * 128], wt[:, k, 0:64],
                       start=(k == 0), stop=(k == 15), r=[Rw, R_hT[t]], w=[RPS[bank]])
                xx, ax, ee, ll, tmp = [b0tmp[:, n, :] for n in range(5)]
                tt("dve", xx, PS[bank][:, 0:64], dtb_bc, ALU.add, r=[RPS[bank]] + RC, w=[R_b0])
                act(ax, xx, AF.Abs, r=[R_b0], w=[R_b0])
                act(ee, ax, AF.Exp, scale=-1.0, r=[R_b0], w=[R_b0])
                ts("dve", ee, ee, 1.0, None, ALU.add, r=[R_b0], w=[R_b0])
                act(ll, ee, AF.Ln, r=[R_b0], w=[R_b0])
                stt("dve", dt_all[:, ti, :], xx, 0.0, ll, ALU.max, ALU.add, r=[R_b0], w=[R_dt[ti]])
                tt("dve", adt_all[:, ti, :], dt_all[:, ti, :], A_bc, ALU.mult, r=[R_dt[ti], R_cum], w=[R_dt[ti]])
                b2 = ipbank()
                mm(PS[b2][:, 0:64], TRIS if smp else TRI, adt_all[:, ti, :], r=[R_dt[ti]] + RC, w=[RPS[b2]])
                mm(PS[b2][:, 64:128], SAME if smp else ONES, adt_all[:, ti, :], r=[R_dt[ti]] + RC, w=[RPS[b2]])
                at_ = b0tmp[:, 5, :]
                cp("dve", at_, PS[b2][:, 0:64], r=[RPS[b2]], w=[R_b0])
                act(eat_all[:, ti, :], PS[b2][:, 0:64], AF.Exp, r=[RPS[b2]], w=[R_dt[ti]])
                tt("dve", tmp, PS[b2][:, 64:128], at_, ALU.subtract, r=[RPS[b2], R_b0], w=[R_b0])
                act(tmp, tmp, AF.Exp, r=[R_b0], w=[R_b0])
                tt("dve", dtd_all[:, ti, :], dt_all[:, ti, :], tmp, ALU.mult, r=[R_b0, R_dt[ti]], w=[R_dt[ti]])
                act(eaend_all[:, ti, :], PS[b2][:, 64:128], AF.Exp, r=[RPS[b2]], w=[R_dt[ti]])
                if not smp and prefix:
                    tt("dve", cum_bc, PS[b2][:, 64:128], cum_bc, ALU.add, r=[RPS[b2], R_cum], w=[R_cum])
                elif not smp:
                    atg = b0tmp[:, 6, :]
                    tt("dve", atg, at_, cum_bc, ALU.add, r=[R_b0, R_cum], w=[R_b0])
                    act(eatg_all[:, ti, :], atg, AF.Exp, r=[R_b0], w=[R_dt[ti]])
                    tt("dve", cum_bc, PS[b2][:, 64:128], cum_bc, ALU.add, r=[RPS[b2], R_cum], w=[R_cum])
                else:
                    adv = adt_all[:, ti, :].rearrange("p (i two) -> p i two", two=2)
                    for par in range(2):
                        tt("pool", Xeo[:, par, :].rearrange("p (s i) -> p s i", s=16),
                           adv[:, :, par].unsqueeze(1).broadcast_to([128, 16, 32]),
                           M16.unsqueeze(2).broadcast_to([128, 16, 32]), ALU.mult, r=[R_dt[ti]] + RC, w=[R_Xeo])
                    b3 = ipbank()
                    mm(PS[b3][0:64, :], ONES[:, 0:64], Xeo[:, 0, :], r=[R_Xeo] + RC, w=[RPS[b3]])
                    mm(PS[b3][64:128, :], ONES[:, 0:64], Xeo[:, 1, :], r=[R_Xeo] + RC, w=[RPS[b3]])
                    act(E_samp, PS[b3], AF.Exp, r=[RPS[b3]], w=[R_Es])
            A.pop()
            P.barrier()
            CK("b0")

            sc = A.alloc([128, 768], F32)
            R_sc, ds_sc = Res("sc"), DSem("sc")
            xpP = [A.alloc([128, 1027], F32) for _ in range(2)]
            xpS = [A.alloc([128, 16, 11], F32) for _ in range(2)]
            R_xp = [Res("xp0"), Res("xp1")]
            accP2 = [A.alloc([128, 1024], F32) for _ in range(2)]
            accS2 = [A.alloc([128, 16, 8], F32) for _ in range(2)]
            R_accP2, R_accS2 = [Res("accP0"), Res("accP1")], [Res("accS0"), Res("accS1")]
            convT = A.alloc([128, 5, NPT], BF16)
            R_cv = [Res(f"cv{b}") for b in range(5)]
            CTk = [A.alloc([128, NPT], BF16) for _ in range(2)]
            R_CT = [Res("CT0"), Res("CT1")]
            crow = sc
            R_crow, ds_crow = R_sc, DSem("crow")
            xtok = [A.alloc([128, 640], BF16) for _ in range(2)]
            R_xtok = [Res("xtok0"), Res("xtok1")]
            ylocal = A.alloc([128, NTT, 512], F32)
            R_yl = [Res(f"yl{t}") for t in range(NTT)]
            cbm = [A.alloc([128, 128], F32) for _ in range(2)]
            R_cbm = [Res("cbm0"), Res("cbm1")]
            xdt = [A.alloc([128, 512], BF16) for _ in range(2)]
            xdd = [A.alloc([128, 512], BF16) for _ in range(2)]
            R_xd = [Res("xd0"), Res("xd1")]
            Lt = [A.alloc([128, 4, 128], F32) for _ in range(2)]
            R_L = [Res("L0"), Res("L1")]
            Et = [A.alloc([128, 512], F32) for _ in range(2)]
            R_E = [Res("E0"), Res("E1")]
            MT = [A.alloc([128, 4, 128], BF16) for _ in range(2)]
            R_MT = [Res("MT0"), Res("MT1")]
            t12 = [A.alloc([128, 512], F32) for _ in range(2)]
            t3 = [None, None]
            R_t12, R_t3 = [Res("t12a"), Res("t12b")], [Res("t3a"), Res("t3b")]
            STf = [A.alloc([128, 520], F32) for _ in range(1 if prefix else 2)] * (2 if prefix else 1)
            R_STf = [Res("STf0")] * 2 if prefix else [Res("STf0"), Res("STf1")]
            STb = A.alloc([128, 512], BF16)
            R_STb = Res("STb")
            S0nat = [A.alloc([128, 4, 128], F32) for _ in range(2)]
            R_S0, ds_S0 = [Res("S00"), Res("S01")], [DSem("S00"), DSem("S01")]
            S0T = [A.alloc([128, 512], BF16) for _ in range(2)]
            R_S0T = [Res("S0T0"), Res("S0T1")]
            ZC = A.alloc([128, 16, 248], BF16)
            R_ZC = Res("ZC")
            Bm = A.alloc([128, 16, 128], BF16)
            R_Bm = Res("Bm")
            snew = [A.alloc([128, 4, 128], F32)] * 2
            R_sn, ds_sn = [Res("sn0")] * 2, [DSem("sn0")] * 2
            agl = None if prefix else A.alloc([128, 520], F32)
            R_agl, ds_agl = Res("agl"), DSem("agl")
            Sst = None if prefix else A.alloc([128, 512], F32)
            Sstb = None if prefix else A.alloc([128, 512], BF16)
            R_Sst, R_Sstb = Res("Sst"), Res("Sstb")
            cci = A.alloc([128, 4, 8], F32)
            R_cci = Res("cci")
            szs2 = [A.alloc([128, 512], F32) for _ in range(2)]
            yb2 = [A.alloc([128, 512], F32) for _ in range(2)]
            R_szs2, R_yb2 = [Res("szsa"), Res("szsb")], [Res("yba"), Res("ybb")]
            gst2 = [A.alloc([128, 4], F32) for _ in range(2)]
            R_gst2 = [Res("gsta"), Res("gstb")]
            szs, yb = szs2[0], yb2[0]
            gn = [A.alloc([128, 512], BF16) for _ in range(2)]
            R_szs, R_yb, R_gn = Res("szs"), Res("yb"), [Res("gn0"), Res("gn1")]
            gst = A.alloc([128, 4], F32)
            R_gst = Res("gst")
            ssTs = [A.alloc([128, 4, 128], BF16) for _ in range(2)]
            R_ssT, ds_ssT = [Res("ssT0"), Res("ssT1")], [DSem("ssT0"), DSem("ssT1")]
            spst = snew[0]
            R_spst, ds_spst = R_sn[0], DSem("spst")
            R_agin = [Res(f"agin{g}") for g in range(8)]
            R_agout = [Res(f"agout{g}") for g in range(8)]
            ds_agin = DSem("agin")
            R_STd = Res("STd")
            ds_stin = DSem("stin")
            memset("pool", ZC, 0.0, w=[R_ZC])
            hsel = A.alloc([128, 16, 48], BF16)
            R_hsel = Res("hsel")
            cp("pool", hsel.rearrange("p k (s j) -> p k s j", s=16),
               hT[:, :, 1152:1280].rearrange("p k (s t) -> p k s t", s=16)[:, :, :, 5:8], r=[R_hT[9]], w=[R_hsel])
            cnt = {"x": 0, "mt": 0, "s0": 0, "sn": 0, "gn": 0, "ss": 0}

            def bcast_h(ap_h8):
                return ap_h8.unsqueeze(2).broadcast_to([128, 8, 64])

            def v864(ap):
                return ap.rearrange("p (h d) -> p h d", h=8)

            def B1a(g):
                wa, Rwa = load_w([(w_in[:, XS0 + 512 * g:XS0 + 512 * (g + 1)], 0)])
                wb_, Rwb = load_w([(w_in[:, B0 + 128 * g:B0 + 128 * (g + 1)], 0),
                                   (w_in[:, C0 + 128 * g:C0 + 128 * (g + 1)], 128)])
                dma("sp", sc[0:48, 0:512], state_conv[:, 512 * g:512 * (g + 1)], w=[R_sc], dsem=ds_sc)
                dma("sp", sc[0:48, 512:640], state_conv[:, 4096 + 128 * g:4096 + 128 * (g + 1)], dsem=ds_sc)
                dma("sp", sc[0:48, 640:768], state_conv[:, 5120 + 128 * g:5120 + 128 * (g + 1)], dsem=ds_sc)
                R_sc.w = P.all_dma[-1]
                def blk_gen_b(blk):
                    wsl, Rws, c0 = (wa, Rwa, blk * 128) if blk < 4 else (wb_, Rwb, (blk - 4) * 128)
                    cbi = (4 * g + blk) if blk < 4 else (32 + g if blk == 4 else 40 + g)
                    xi = blk % 2
                    xp, xs_, Rxp = xpP[xi], xpS[xi], R_xp[xi]
                    accP, accS, R_aP, R_aS = accP2[xi], accS2[xi], R_accP2[xi], R_accS2[xi]
                    for ri, (t0, n) in enumerate([(0, 512), (512, 512), (1024, 256)]):
                        bank = ipbank()
                        tiles = [R_hT[t] for t in range(t0 // 128, (t0 + n) // 128)]
                        for k in range(16):
                            mm(PS[bank][:, 0:n], wsl[:, k, c0:c0 + 128], hT[:, k, t0:t0 + n],
                               start=(k == 0), stop=(k == 15), r=[Rws] + tiles, w=[RPS[bank]])
                        if ri == 0:
                            cp("act", xp[:, 0:387], PS[bank][:, 125:512], r=[RPS[bank]], w=[Rxp])
                        elif ri == 1:
                            cp("act", xp[:, 387:899], PS[bank][:, 0:512], r=[RPS[bank]], w=[Rxp])
                        else:
                            cp("act", xp[:, 899:1027], PS[bank][:, 0:128], r=[RPS[bank]], w=[Rxp])
                            cp("act", xs_[:, :, 3:11], PS[bank][:, 128:256].rearrange("p (s t) -> p s t", s=16),
                               r=[RPS[bank]], w=[Rxp])
                    bank = ipbank()
                    mm(PS[bank][:, 0:48], sc[0:48, blk * 128:(blk + 1) * 128], ident_f[0:48, 0:48], r=[R_sc] + RC, w=[RPS[bank]])
                    cp("act", xs_[:, :, 0:3], PS[bank][:, 0:48].rearrange("p (s t) -> p s t", s=16), r=[RPS[bank]], w=[Rxp])
                    yield
                    ts("dve", accP, xp[:, 0:1024], cw_pk[:, cbi, 0:1], None, ALU.mult, r=[Rxp] + RC, w=[R_aP])
                    ts("dve", accS, xs_[:, :, 0:8], cw_pk[:, cbi, 0:1], None, ALU.mult, r=[Rxp] + RC, w=[R_aS])
                    yield
                    for j in range(1, 4):
                        stt("dve", accP, xp[:, j:j + 1024], cw_pk[:, cbi, j:j + 1], accP, ALU.mult, ALU.add, r=[Rxp, R_aP] + RC, w=[R_aP])
                        stt("dve", accS, xs_[:, :, j:j + 8], cw_pk[:, cbi, j:j + 1], accS, ALU.mult, ALU.add, r=[Rxp, R_aS] + RC, w=[R_aS])
                        yield
                    if blk < 5:
                        dst, Rd = convT[:, blk, :], R_cv[blk]
                    else:
                        dst, Rd = CTk[g % 2], R_CT[g % 2]
                    act(dst[:, 0:1024], accP, AF.Silu, bias=cb_pk[:, cbi:cbi + 1], r=[R_aP] + RC, w=[Rd])
                    act(dst[:, 1024:1152].rearrange("p (s t) -> p s t", s=16), accS, AF.Silu, bias=cb_pk[:, cbi:cbi + 1],
                        r=[R_aS] + RC, w=[Rd])

                run_interleaved((blk_gen_b(blk) for blk in range(6)), 2)
                for (lhs_of_k, nrow, dst_out, Rt) in (
                        (lambda k: hT[:, k, 1149:1152], 3, cp_out, R_hT[8]),
                        (lambda k: hsel[:, k, :], 48, cs_out, R_hsel)):
                    bank = ipbank()
                    for k in range(16):
                        mm(PS[bank][0:nrow, 0:512], lhs_of_k(k), wa[:, k, 0:512], start=(k == 0), stop=(k == 15),
                           r=[Rwa, Rt], w=[RPS[bank]])
                    cp("dve", crow[0:nrow, 0:512], PS[bank][0:nrow, 0:512], r=[RPS[bank]], w=[R_crow])
                    bank = ipbank()
                    for k in range(16):
                        mm(PS[bank][0:nrow, 0:256], lhs_of_k(k), wb_[:, k, 0:256], start=(k == 0), stop=(k == 15),
                           r=[Rwb, Rt], w=[RPS[bank]])
                    cp("dve", crow[0:nrow, 512:768], PS[bank][0:nrow, 0:256], r=[RPS[bank]], w=[R_crow])
                    dma("pool", dst_out[:, 512 * g:512 * (g + 1)], crow[0:nrow, 0:512], r=[R_crow], dsem=ds_crow, outflag=True)
                    dma("pool", dst_out[:, 4096 + 128 * g:4096 + 128 * (g + 1)], crow[0:nrow, 512:640], r=[R_crow], dsem=ds_crow, outflag=True)
                    dma("pool", dst_out[:, 5120 + 128 * g:5120 + 128 * (g + 1)], crow[0:nrow, 640:768], r=[R_crow], dsem=ds_crow, outflag=True)

            def B1b(g):
                hs = slice(8 * g, 8 * g + 8)
                CT, RCT = CTk[g % 2], R_CT[g % 2]
                STc, RST_ = STf[g % 2], R_STf[g % 2]
                cp("pool", ZC[:, :, 120:128], CT[:, 1024:1152].rearrange("p (s t) -> p s t", s=16), r=[RCT], w=[R_ZC])
                if prefix:
                    dma("sp", STc[:, 0:512], Sst_d[g].ap(), r=[R_SstD[g]], w=[RST_], dsem=ds_stin)
                    cp("act", STb, STc[:, 0:512], r=[RST_], w=[R_STb])
                st_done = {0: True}

                def tile_gen(t):
                    ti = t - 1
                    smp = (t == 9)
                    par = ti % 2
                    bT = 7 if par == 0 else 2
                    bY = 4 if par == 0 else 0
                    bYI = 5 if par == 0 else 1
                    Lt_, R_L_, Et_, R_E_ = Lt[par], R_L[par], Et[par], R_E[par]
                    t12_, R_t12_, t3_, R_t3_ = t12[par], R_t12[par], t3[par], R_t3[par]
                    tsl = slice(ti * 128, (ti + 1) * 128)
                    xi = par
                    xk, Rxk = xtok[xi], R_xtok[xi]
                    for blk in range(5):
                        tr(PSb(bT)[:, blk * 128:(blk + 1) * 128], convT[:, blk, tsl], ident_b, r=[R_cv[blk]] + RC, w=[RPS[bT]])
                    cp("act", xk, PSb(bT)[:, 0:640], r=[RPS[bT]], w=[Rxk])
                    yield
                    mm(PS[bT][:, 384:512], convT[:, 4, tsl], CT[:, tsl], r=[R_cv[4], RCT], w=[RPS[bT]])
                    cb_, Rcb = cbm[ti % 2], R_cbm[ti % 2]
                    tt("dve", cb_, PS[bT][:, 384:512], TRIS if smp else TRI, ALU.mult, r=[RPS[bT]] + RC, w=[Rcb])
                    yield
                    xdt_, xdd_, Rxd = xdt[ti % 2], xdd[ti % 2], R_xd[ti % 2]
                    tt("pool", v864(xdt_), v864(xk[:, 0:512]), bcast_h(dt_all[:, ti, hs]), ALU.mult, r=[Rxk, R_dt[ti]], w=[Rxd])
                    tt("pool", v864(xdd_), v864(xk[:, 0:512]), bcast_h(dtd_all[:, ti, hs]), ALU.mult, r=[Rxk, R_dt[ti]], w=[Rxd])
                    yield
                    for hb in range(2):
                        h0 = 8 * g + 4 * hb
                        tt("pool", Lt_, MGT.unsqueeze(1).broadcast_to([128, 4, 128]),
                           adt_all[:, ti, h0:h0 + 4].unsqueeze(2).broadcast_to([128, 4, 128]), ALU.mult,
                           r=[R_dt[ti]] + RC, w=[R_L_])
                        yield
                        for r_ in range(4):
                            mm(PS[3][:, r_ * 128:(r_ + 1) * 128], Lt_[:, r_, :], TRI, r=[R_L_] + RC, w=[RPS[3]])
                        act(Et_, PS[3], AF.Exp, r=[RPS[3]], w=[R_E_])
                        yield
                        mi = par
                        tt("dve", MT[mi], Et_.rearrange("p (a b) -> p a b", a=4), cb_.unsqueeze(1).broadcast_to([128, 4, 128]),
                           ALU.mult, r=[R_E_, Rcb], w=[R_MT[mi]])
                        yield
                        for r_ in range(4):
                            hh = 4 * hb + r_
                            mm(PS[bY][:, hh * 64:(hh + 1) * 64], MT[mi][:, r_, :], xdt_[:, hh * 64:(hh + 1) * 64],
                               r=[R_MT[mi], Rxd], w=[RPS[bY]])
                        yield
                    have_inter = smp or t >= 2 or prefix
                    while not smp and not st_done.get(ti):
                        yield
                    if smp:
                        s0toks = []
                        for s in range(16):
                            si = cnt["s0"] % 2
                            cnt["s0"] += 1
                            dma("sp", S0nat[si], state_ssm[s, 512 * g:512 * (g + 1), :].rearrange("(q p) n -> p q n", p=128),
                                w=[R_S0[si]], dsem=ds_S0[si])
                            for q in range(4):
                                mm(PS[7][:, q * 128:(q + 1) * 128], S0nat[si][:, q, :], ident_f, r=[R_S0[si]] + RC, w=[RPS[7]])
                            cp("act", S0T[si], PS[7], r=[RPS[7]], w=[R_S0T[si]])
                            mm(PS[bYI], ZC[:, s, 120 - 8 * s:248 - 8 * s], S0T[si], start=(s == 0), stop=(s == 15),
                               r=[R_ZC, R_S0T[si]], w=[RPS[bYI]])
                            if s == 0:
                                tt("pool", Bm, xk[:, 512:640].unsqueeze(1).broadcast_to([128, 16, 128]),
                                   M16.unsqueeze(2).broadcast_to([128, 16, 128]), ALU.mult, r=[Rxk] + RC, w=[R_Bm])
                            for q in range(4):
                                mm(PS[6][:, q * 128:(q + 1) * 128], xdd_[:, q * 128:(q + 1) * 128], Bm[:, s, :],
                                   r=[Rxd, R_Bm], w=[RPS[6]])
                            ni = cnt["sn"] % 2
                            cnt["sn"] += 1
                            ecol = E_samp[:, s * 32 + 4 * g:s * 32 + 4 * g + 4]
                            tt("dve", snew[ni], S0nat[si], ecol.unsqueeze(2).broadcast_to([128, 4, 128]), ALU.mult,
                               r=[R_S0[si], R_Es], w=[R_sn[ni]])
                            tt("dve", snew[ni], snew[ni], PS[6].rearrange("p (q n) -> p q n", q=4), ALU.add,
                               r=[R_sn[ni], RPS[6]], w=[R_sn[ni]])
                            dma("pool", ss_out[s, 512 * g:512 * (g + 1), :].rearrange("(q p) n -> p q n", p=128), snew[ni],
                                r=[R_sn[ni]], dsem=ds_sn[ni], outflag=True)
                            yield
                    elif t >= 2 or prefix:
                        mm(PS[bYI], CT[:, tsl], STb, r=[RCT, R_STb], w=[RPS[bYI]])
                    if have_inter:
                        yield
                        tt("dve", v864(t12_), v864(PS[bYI]), bcast_h(eat_all[:, ti, hs]), ALU.mult, r=[RPS[bYI], R_dt[ti]], w=[R_t12_])
                        tt("dve", t12_, t12_, PS[bY], ALU.add, r=[R_t12_, RPS[bY]], w=[R_t12_])
                    else:
                        cp("dve", t12_, PS[bY], r=[RPS[bY]], w=[R_t12_])
                    tt("pool", v864(ylocal[:, ti, :]), v864(xk[:, 0:512]), bcast_h(dsk_bc[:, hs]), ALU.mult, r=[Rxk] + RC, w=[R_yl[ti]])
                    yield
                    tt("pool", ylocal[:, ti, :], ylocal[:, ti, :], t12_, ALU.add, r=[R_t12_, R_yl[ti]], w=[R_yl[ti]])
                    if not smp:
                        mm(PS[6], xk[:, 512:640], xdd_, r=[Rxk, Rxd], w=[RPS[6]])
                        if t == 1 and not prefix:
                            cp("dve", STc[:, 0:512], PS[6], r=[RPS[6]], w=[RST_])
                        else:
                            tt("dve", v864(STc[:, 0:512]), v864(STc[:, 0:512]), bcast_h(eaend_all[:, ti, hs]), ALU.mult,
                               r=[RST_, R_dt[ti]], w=[RST_])
                            tt("dve", STc[:, 0:512], STc[:, 0:512], PS[6], ALU.add, r=[RST_, RPS[6]], w=[RST_])
                        if t < 8:
                            cp("act", STb, STc[:, 0:512], r=[RST_], w=[R_STb])
                        st_done[t] = True

                run_interleaved((tile_gen(t) for t in range(1, NT)), 2)
                if prefix:
                    for q in range(4):
                        mm(PS[7][:, q * 128:(q + 1) * 128], STc[:, q * 128:(q + 1) * 128], ident_f, r=[RST_] + RC, w=[RPS[7]])
                    cp("act", spst, PS[7].rearrange("p (q n) -> p q n", q=4), r=[RPS[7]], w=[R_spst])
                    dma("sp", sp_out[512 * g:512 * (g + 1), :].rearrange("(q p) n -> p q n", p=128), spst, r=[R_spst],
                        dsem=ds_spst, outflag=True)
                    return
                cp("dve", STc[:, 512:520], cum_bc[:, hs], r=[R_cum], w=[RST_])
                dma("sp", ag_in[g].ap(), STc, r=[RST_], w=[R_agin[g]], dsem=ds_agin)
                dcc = DSem(f"cc{g}", inc=1)
                if nocc:
                    dma("sp", ag_out[g].ap()[0:128, :], ag_in[g].ap(), r=[R_agin[g]], w=[R_agout[g]], dsem=DSem(f"ccx{g}"))
                else:
                  P.add("pool", lambda e, g=g: e.collective_compute(
                    "AllGather", ALU.bypass, replica_groups=[list(range(NCORES))],
                    ins=[ag_in[g].ap().opt()], outs=[ag_out[g].ap().opt()]),
                    r=[R_agin[g]], w=[R_agout[g]], dsem=dcc)

            def B2(g):
                hs = slice(8 * g, 8 * g + 8)
                CT, RCT = CTk[g % 2], R_CT[g % 2]
                STc, RST_ = STf[g % 2], R_STf[g % 2]
                wz, Rwz = load_w([(w_in[:, ZS0 + 512 * g:ZS0 + 512 * (g + 1)], 0)])
                if not prefix:
                    memset("pool", Sst, 0.0, w=[R_Sst])
                for i in range(0 if prefix else NCORES):
                    dma("sp", agl, ag_out[g].ap()[i * 128:(i + 1) * 128, :], r=[R_agout[g]], w=[R_agl], dsem=ds_agl)
                    ei, ci = cci[:, 0, :], cci[:, 1, :]
                    act(ei, agl[:, 512:520], AF.Exp, r=[R_agl], w=[R_cci])
                    ts("dve", ci, ei, -1.0, None, ALU.add, r=[R_cci], w=[R_cci])
                    ts("dve", ci, ci, alpha_bc[:, i:i + 1], 1.0, ALU.mult, ALU.add, r=[R_cci] + RC, w=[R_cci])
                    tt("dve", v864(Sst), v864(Sst), bcast_h(ci), ALU.mult, r=[R_Sst, R_cci], w=[R_Sst])
                    stt("dve", Sst, agl[:, 0:512], alpha_bc[:, i:i + 1], Sst, ALU.mult, ALU.add, r=[R_agl, R_Sst] + RC, w=[R_Sst])
                if not prefix:
                    cp("act", Sstb, Sst, r=[R_Sst], w=[R_Sstb])
                eo = cci[:, 2, :]
                if not prefix:
                    act(eo, STc[:, 512:520], AF.Exp, r=[RST_], w=[R_cci])
                    tt("dve", v864(Sst), v864(Sst), bcast_h(eo), ALU.mult, r=[R_Sst, R_cci, R_Sstb], w=[R_Sst])
                    tt("dve", Sst, Sst, STc[:, 0:512], ALU.add, r=[R_Sst, RST_], w=[R_Sst])
                    for q in range(4):
                        mm(PS[7][:, q * 128:(q + 1) * 128], Sst[:, q * 128:(q + 1) * 128], ident_f, r=[R_Sst] + RC, w=[RPS[7]])
                    cp("act", spst, PS[7].rearrange("p (q n) -> p q n", q=4), r=[RPS[7]], w=[R_spst])
                    dma("sp", sp_out[512 * g:512 * (g + 1), :].rearrange("(q p) n -> p q n", p=128), spst, r=[R_spst],
                        dsem=ds_spst, outflag=True)
                def tile_gen2(t):
                    ti = t - 1
                    par = ti % 2
                    bT = 7 if par == 0 else 2
                    szs, R_szs, yb, R_yb, gst, R_gst = szs2[par], R_szs2[par], yb2[par], R_yb2[par], gst2[par], R_gst2[par]
                    tsl = slice(ti * 128, (ti + 1) * 128)
                    bank = ipbank()
                    for k in range(16):
                        mm(PS[bank], hT[:, k, t * 128:(t + 1) * 128], wz[:, k, 0:512], start=(k == 0), stop=(k == 15),
                           r=[Rwz, R_hT[t]], w=[RPS[bank]])
                    act(szs, PS[bank], AF.Silu, r=[RPS[bank]], w=[R_szs])
                    yield
                    if t <= 8 and not prefix:
                        mm(PS[5], CT[:, tsl], Sstb, r=[RCT, R_Sstb], w=[RPS[5]])
                        tt("dve", v864(yb), v864(PS[5]), bcast_h(eatg_all[:, ti, hs]), ALU.mult, r=[RPS[5], R_dt[ti]], w=[R_yb])
                        tt("pool", yb, yb, ylocal[:, ti, :], ALU.add, r=[R_yb, R_yl[ti]], w=[R_yb])
                        tt("pool", yb, yb, szs, ALU.mult, r=[R_yb, R_szs], w=[R_yb])
                    else:
                        tt("pool", yb, ylocal[:, ti, :], szs, ALU.mult, r=[R_yl[ti], R_szs], w=[R_yb])
                    yield
                    act(szs, yb, AF.Square, accum_out=gst[:, 0:1], r=[R_yb], w=[R_szs, R_gst])
                    yield
                    ts("dve", gst[:, 1:2], gst[:, 0:1], 1.0 / 512.0, EPS, ALU.mult, ALU.add, r=[R_gst], w=[R_gst])
                    yield
                    act(gst[:, 1:2], gst[:, 1:2], AF.Sqrt, r=[R_gst], w=[R_gst])
                    yield
                    P.add("dve", lambda e, o=gst[:, 2:3], i_=gst[:, 1:2]: e.reciprocal(out=o, in_=i_), r=[R_gst], w=[R_gst])
                    yield
                    gi = par
                    ts("dve", gn[gi], yb, gst[:, 2:3], None, ALU.mult, r=[R_yb, R_gst], w=[R_gn[gi]])
                    yield
                    for q in range(4):
                        tr(PSb(bT)[:, q * 128:(q + 1) * 128], gn[gi][:, q * 128:(q + 1) * 128], ident_b, r=[R_gn[gi]] + RC, w=[RPS[bT]])
                    yield
                    oi = cnt["ss"] % 2
                    cnt["ss"] += 1
                    tt("dve", ssTs[oi], PSb(bT)[:, 0:512].rearrange("p (q n) -> p q n", q=4),
                       ssn_pk[:, 4 * g:4 * g + 4].unsqueeze(2).broadcast_to([128, 4, 128]), ALU.mult,
                       r=[RPS[bT]] + RC, w=[R_ssT[oi]])
                    dma("sp", ST_d[4 * g:4 * g + 4, :, tsl].rearrange("b p t -> p b t"), ssTs[oi], r=[R_ssT[oi]], w=[R_STd],
                        dsem=ds_ssT[oi])

                run_interleaved((tile_gen2(t) for t in range(1, NT)), 2)

            for g in range(8):
                B1a(g)
                if g > 0:
                    B2(g - 1)
                B1b(g)
                if g == 0:
                    CK("b_g0")
            B2(7)
            A.pop()
            P.barrier()
            CK("b")
            A.push()
            mergedT = A.alloc([128, 16, NPT], BF16)
            R_mg = [Res(f"mg{t}") for t in range(9)]
            A.push()
            cslots = [wslot[0][:, :, 0:256], wslot[0][:, :, 256:512], wslot[1][:, :, 0:256], wslot[1][:, :, 256:512]]
            cslots += [A.alloc([128, 16, 256], BF16) for _ in range(4)]
            NCS = len(cslots)
            R_cs = [Res(f"cs{i}") for i in range(NCS)]
            ds_cs = [DSem(f"cs{i}") for i in range(NCS)]
            cstate = {"n": 0}

            def load_c(src):
                i = cstate["n"] % NCS
                cstate["n"] += 1
                dma("pool", cslots[i], src.rearrange("(k p) n -> p k n", p=128), w=[R_cs[i]], dsem=ds_cs[i])
                return cslots[i], R_cs[i]

            ATg = [A.alloc([128, 16, 256], BF16) for _ in range(2)]
            STg = [A.alloc([128, 32, 256], BF16) for _ in range(2)]
            R_xg, ds_xg = [Res("xg0"), Res("xg1")], [DSem("xg0"), DSem("xg1")]
            sg = [A.alloc([128, 512], F32) for _ in range(2)]
            R_sg = [Res("sg0"), Res("sg1")]
            mtmp = [A.alloc([128, 512], F32) for _ in range(2)]
            R_mt = [Res("mt0"), Res("mt1")]
            TGS = [(0, 256), (256, 256), (512, 256), (768, 256), (1024, 128)]
            ccnt = {"x": 0, "m": 0}
            for fc in range(8):
                f0 = 256 * fc
                wga, Rga = load_c(w_in[:, GA0 + f0:GA0 + f0 + 256])
                wgs, Rgs = load_c(w_in[:, GS0 + f0:GS0 + f0 + 256])
                wab, Rab = load_c(w_ab[:, f0:f0 + 256])
                ws0, Rs0 = load_c(w_sb[0:2048, f0:f0 + 256])
                ws1, Rs1 = load_c(w_sb[2048:4096, f0:f0 + 256])
                for (o0, n) in TGS:
                    xi = ccnt["x"] % 2
                    ccnt["x"] += 1
                    dma("sp", ATg[xi][:, :, 0:n], AT_d[:, :, o0:o0 + n].rearrange("b p t -> p b t"), r=[R_ATd], w=[R_xg[xi]], dsem=ds_xg[xi])
                    dma("sp", STg[xi][:, :, 0:n], ST_d[:, :, o0:o0 + n].rearrange("b p t -> p b t"), r=[R_STd], dsem=ds_xg[xi])
                    R_xg[xi].w = P.all_dma[-1]
                    htiles = [R_hT[t] for t in range((o0 + 128) // 128, (o0 + 128 + n) // 128)]
                    for sb in range(2):
                        cs_ = slice(sb * 128, (sb + 1) * 128)
                        bG, bP = (2, 3) if ccnt["m"] % 2 == 0 else (4, 5)
                        for k in range(16):
                            mm(PS[bG][:, 0:n], wga[:, k, cs_], hT[:, k, o0 + 128:o0 + 128 + n], start=(k == 0), stop=(k == 15),
                               r=[Rga] + htiles, w=[RPS[bG]])
                        for k in range(16):
                            mm(PS[bG][:, 256:256 + n], wgs[:, k, cs_], hT[:, k, o0 + 128:o0 + 128 + n], start=(k == 0), stop=(k == 15),
                               r=[Rgs] + htiles, w=[RPS[bG]])
                        for k in range(16):
                            mm(PS[bP][:, 0:n], wab[:, k, cs_], ATg[xi][:, k, 0:n], start=(k == 0), stop=(k == 15),
                               r=[Rab, R_xg[xi]], w=[RPS[bP]])
                        for k in range(32):
                            wsx, Rsx = (ws0, Rs0) if k < 16 else (ws1, Rs1)
                            mm(PS[bP][:, 256:256 + n], wsx[:, k % 16, cs_], STg[xi][:, k, 0:n], start=(k == 0), stop=(k == 31),
                               r=[Rsx, R_xg[xi]], w=[RPS[bP]])
                        mi = ccnt["m"] % 2
                        ccnt["m"] += 1
                        act(sg[mi], PS[bG], AF.Sigmoid, r=[RPS[bG]], w=[R_sg[mi]])
                        tt("dve", mtmp[mi], sg[mi], PS[bP], ALU.mult, r=[R_sg[mi], RPS[bP]], w=[R_mt[mi]])
                        mts = [R_mg[t] for t in range(o0 // 128, (o0 + n) // 128)]
                        tt("pool", mergedT[:, 2 * fc + sb, o0:o0 + n], mtmp[mi][:, 0:n], mtmp[mi][:, 256:256 + n], ALU.add,
                           r=[R_mt[mi]], w=mts)
            A.pop()
            P.barrier()
            CK("c")
            hT_flat = hT.rearrange("p a b -> p (a b)")
            wo = [hT_flat[:, 0:8192].rearrange("p (k n) -> p k n", k=16),
                  hT_flat[:, 8192:16384].rearrange("p (k n) -> p k n", k=16), wslot[0], wslot[1]]
            R_wo, ds_wo = [Res(f"wo{i}") for i in range(4)], [DSem(f"wo{i}") for i in range(4)]
            for c in range(4):
                dma("pool", wo[c], w_o[:, 512 * c:512 * (c + 1)].rearrange("(k p) n -> p k n", p=128), w=[R_wo[c]], dsem=ds_wo[c])
            npost_bc = A.alloc([128, D], F32)
            R_np, ds_np = Res("np"), DSem("np")
            dma("sp", npost_bc, norm_post.partition_broadcast(128), w=[R_np], dsem=ds_np)
            o32 = [A.alloc([128, D], F32) for _ in range(2)]
            xr = [A.alloc([128, D], F32) for _ in range(2)]
            R_o32, R_xr = [Res("o0"), Res("o1")], [Res("xr0"), Res("xr1")]
            ds_xr, ds_y = [DSem("xr0"), DSem("xr1")], [DSem("y0"), DSem("y1")]
            dst_ = A.alloc([128, 2, 8], F32)
            R_dst = [Res("dst0"), Res("dst1")]
            djunk = A.alloc([128, 512], BF16)
            R_dj = Res("dj")
            def d_gen(t):
                ti = t - 1
                oi = ti % 2
                bb = 4 if oi == 0 else 0
                dma("sp", xr[oi], xin[t * 128:(t + 1) * 128, :], w=[R_xr[oi]], dsem=ds_xr[oi])
                st_ = dst_[:, oi, :]
                for c in range(4):
                    for k in range(16):
                        mm(PS[bb + c], mergedT[:, k, ti * 128:(ti + 1) * 128], wo[c][:, k, :], start=(k == 0), stop=(k == 15),
                           r=[R_mg[ti], R_wo[c]], w=[RPS[bb + c]])
                    act(djunk, PS[bb + c], AF.Square, accum_out=st_[:, c:c + 1], r=[RPS[bb + c]], w=[R_dj, R_dst[oi]])
                    cp("dve", o32[oi][:, 512 * c:512 * (c + 1)], PS[bb + c], r=[RPS[bb + c]], w=[R_o32[oi]])
                    yield
                tt("dve", st_[:, 4:5], st_[:, 0:1], st_[:, 1:2], ALU.add, r=[R_dst[oi]], w=[R_dst[oi]])
                tt("dve", st_[:, 5:6], st_[:, 2:3], st_[:, 3:4], ALU.add, r=[R_dst[oi]], w=[R_dst[oi]])
                tt("dve", st_[:, 4:5], st_[:, 4:5], st_[:, 5:6], ALU.add, r=[R_dst[oi]], w=[R_dst[oi]])
                ts("dve", st_[:, 4:5], st_[:, 4:5], 1.0 / D, EPS, ALU.mult, ALU.add, r=[R_dst[oi]], w=[R_dst[oi]])
                yield
                act(st_[:, 4:5], st_[:, 4:5], AF.Sqrt, r=[R_dst[oi]], w=[R_dst[oi]])
                P.add("dve", lambda e, o=st_[:, 6:7], i_=st_[:, 4:5]: e.reciprocal(out=o, in_=i_), r=[R_dst[oi]], w=[R_dst[oi]])
                yield
                ts("dve", o32[oi], o32[oi], st_[:, 6:7], None, ALU.mult, r=[R_o32[oi], R_dst[oi]], w=[R_o32[oi]])
                yield
                tt("pool", o32[oi], o32[oi], npost_bc, ALU.mult, r=[R_o32[oi], R_np], w=[R_o32[oi]])
                tt("pool", o32[oi], o32[oi], xr[oi], ALU.add, r=[R_o32[oi], R_xr[oi]], w=[R_o32[oi]])
                dma("pool", y_out[ti * 128:(ti + 1) * 128, :], o32[oi], r=[R_o32[oi]], dsem=ds_y[oi], outflag=True)
            run_interleaved((d_gen(t) for t in range(1, NT)), 2)
            A.pop()
        except _Stop:
            pass
        P.wait_tokens("sp", P.out_toks + P.all_dma)
        sems = [estack.enter_context(nc.semaphore(f"s{i}")) for i in range(len(P.keys))]
        P.emit(nc, sems)
    return nc, P


_NC_CACHE = {}


def make_in_maps(x_prompt, x_sample, cache_k, cache_v, state_ssm, state_conv, norm_pre, w_in, conv_w,
                 conv_b, dt_bias, a_log, d_skip, ssm_norm, attn_sinks, w_attn_br, w_ssm_br, w_out, norm_post):
    f = lambda a: np.ascontiguousarray(np.asarray(a, dtype=np.float32))
    cst_np = _build_consts()
    bp, bs = _alibi_tables()
    shared = {
        "w_in": f(w_in[0]), "w_attn_br": f(w_attn_br[0]), "w_ssm_br": f(w_ssm_br[0]), "w_out": f(w_out[0]),
        "norm_pre": f(norm_pre[0]), "conv_w": f(conv_w[0]), "conv_b": f(conv_b[0]), "dt_bias": f(dt_bias[0]),
        "a_log": f(a_log[0]), "d_skip": f(d_skip[0]), "ssm_norm": f(ssm_norm[0]), "attn_sinks": f(attn_sinks[0]),
        "norm_post": f(norm_post[0]), "cst": cst_np, "bias_p": bp, "bias_s": bs,
    }
    in_maps = []
    for c in range(NCORES):
        b, j = c // 4, c % 4
        xin = np.zeros((TOK, D), np.float32)
        if j > 0:
            xin[0:128] = x_prompt[b, 1024 * j - 128:1024 * j]
        xin[128:1152] = x_prompt[b, 1024 * j:1024 * (j + 1)]
        xin[1152:1280] = np.asarray(x_sample[16 * c:16 * (c + 1)]).reshape(128, D)
        hm = np.zeros((128, 256), np.float32)
        if j == 0:
            hm[:, 0:128] = NEG
        al = np.zeros((8,), np.float32)
        for i in range(4 * b, c):
            al[i] = 1.0
        xprev = np.zeros((3072, D), np.float32)
        pflag = np.zeros((24,), np.float32)
        if j > 0:
            xprev[3072 - 1024 * j:] = x_prompt[b, 0:1024 * j]
            pflag[24 - 8 * j:] = 1.0
        m = dict(shared)
        m.update({
            "xprev": xprev, "pflag": pflag,
            "xin": xin,
            "cache_k": f(np.asarray(cache_k[0, 16 * c:16 * (c + 1)]).reshape(16, 128, 512)),
            "cache_v": f(np.asarray(cache_v[0, 16 * c:16 * (c + 1)]).reshape(16, 128, 512)),
            "state_ssm": f(np.asarray(state_ssm[0, 16 * c:16 * (c + 1)]).reshape(16, 4096, 128)),
            "state_conv": f(np.asarray(state_conv[0, 16 * c:16 * (c + 1)]).reshape(48, 6144)),
            "halo_mask": hm, "alpha": al,
        })
        in_maps.append(m)
    return in_maps


def kernel(x_prompt, x_sample, cache_k, cache_v, state_ssm, state_conv, norm_pre, w_in, conv_w,
           conv_b, dt_bias, a_log, d_skip, ssm_norm, attn_sinks, w_attn_br, w_ssm_br, w_out, norm_post):
    in_maps = make_in_maps(x_prompt, x_sample, cache_k, cache_v, state_ssm, state_conv, norm_pre, w_in, conv_w,
                           conv_b, dt_bias, a_log, d_skip, ssm_norm, attn_sinks, w_attn_br, w_ssm_br, w_out, norm_post)
    if "nc" not in _NC_CACHE:
        _NC_CACHE["nc"] = build_nc()[0]
    nc = _NC_CACHE["nc"]
    res = run_bass_kernel_spmd(nc, in_maps, core_ids=list(range(NCORES)))
    return assemble(res.results)


def assemble(r):
    f32 = np.float32
    y_prompt = np.zeros((2, 4096, D), f32)
    y_sample = np.zeros((128, 8, D), f32)
    k_p = np.zeros((1, 2, 128, 8, 64), f32)
    v_p = np.zeros((1, 2, 128, 8, 64), f32)
    s_p = np.zeros((1, 2, 64, 64, 128), f32)
    c_p = np.zeros((1, 2, 3, 6144), f32)
    k_s = np.zeros((1, 128, 128, 8, 64), f32)
    v_s = np.zeros((1, 128, 128, 8, 64), f32)
    s_s = np.zeros((1, 128, 64, 64, 128), f32)
    c_s = np.zeros((1, 128, 3, 6144), f32)
    for c in range(NCORES):
        b, j = c // 4, c % 4
        o = r[c]
        y = np.asarray(o["y"], f32)
        y_prompt[b, 1024 * j:1024 * (j + 1)] = y[0:1024]
        y_sample[16 * c:16 * (c + 1)] = y[1024:1152].reshape(16, 8, D)
        k_s[0, 16 * c:16 * (c + 1)] = np.asarray(o["ks"], f32).reshape(16, 128, 8, 64)
        v_s[0, 16 * c:16 * (c + 1)] = np.asarray(o["vs"], f32).reshape(16, 128, 8, 64)
        s_s[0, 16 * c:16 * (c + 1)] = np.asarray(o["ss"], f32).reshape(16, 64, 64, 128)
        c_s[0, 16 * c:16 * (c + 1)] = np.asarray(o["cs"], f32).reshape(16, 3, 6144)
        if j == 3:
            k_p[0, b] = np.asarray(o["kp"], f32).reshape(128, 8, 64)
            v_p[0, b] = np.asarray(o["vp"], f32).reshape(128, 8, 64)
            s_p[0, b] = np.asarray(o["sp"], f32).reshape(64, 64, 128)
            c_p[0, b] = np.asarray(o["cp"], f32)
    return (y_prompt, y_sample, k_p, v_p, s_p, c_p, k_s, v_s, s_s, c_s)
```

```python
import numpy as np
import concourse.bass as bass
import concourse.mybir as mybir
from concourse.bass_utils import run_bass_kernel_spmd

F32 = mybir.dt.float32
BF16 = mybir.dt.bfloat16
AF = mybir.ActivationFunctionType
ALU = mybir.AluOpType
AX = mybir.AxisListType

NCORES = 8
D = 2048
NT = 10
TOK = NT * 128
NPT = 1152
IN_DIM = 19520
Q0, K0, V0, ZA0, XS0, B0, C0, ZS0, DT0, GA0, GS0 = 0, 2048, 2560, 3072, 5120, 9216, 10240, 11264, 15360, 15424, 17472
EPS = 1e-6
NEG = -30000.0
EPOCH = 60000


class _Stop(Exception):
    pass


class Res:
    __slots__ = ("name", "w", "r", "excl")

    def __init__(self, name="", excl=False):
        self.name = name
        self.w = None
        self.r = {}
        self.excl = excl


class Tok:
    __slots__ = ("key", "val", "clock")

    def __init__(self, key, val, clock):
        self.key, self.val, self.clock = key, val, clock


class DSem:
    def __init__(self, name, inc=16):
        self.key = ("d", name)
        self.count = 0
        self.inc = inc


class Prog:
    ENG = ("pe", "act", "dve", "pool", "sp")

    def __init__(self):
        self.streams = {e: [] for e in self.ENG}
        self.seq = {e: 0 for e in self.ENG}
        self.known = {e: {} for e in self.ENG}
        self.keys = {}
        self.out_toks = []
        self.all_dma = []

    def _deps(self, eng, r, w):
        toks = []
        for res in r:
            if res.w is not None:
                toks.append(res.w)
            if res.excl:
                for k, t in res.r.items():
                    if k[0] != eng:
                        toks.append(t)
        for res in w:
            if res.w is not None:
                toks.append(res.w)
            toks.extend(res.r.values())
        return toks

    def _waits(self, eng, toks):
        known = self.known[eng]
        need = {}
        for t in toks:
            if t.key[0] == "pe" and eng == "pe":
                continue
            if known.get(t.key, 0) >= t.val:
                continue
            if need.get(t.key, 0) < t.val:
                need[t.key] = t.val
        for t in toks:
            for k, v in t.clock.items():
                if known.get(k, 0) < v:
                    known[k] = v
        return list(need.items())

    def add(self, eng, fn, r=(), w=(), dsem=None, out=False):
        toks = self._deps(eng, r, w)
        waits = self._waits(eng, toks)
        if dsem is not None:
            dsem.count += dsem.inc
            key, val, inc = dsem.key, dsem.count, dsem.inc
        else:
            s = self.seq[eng]
            self.seq[eng] = s + 1
            key, val, inc = (eng, s // EPOCH), s % EPOCH + 1, 1
        self.keys[key] = True
        clock = dict(self.known[eng])
        clock[key] = val
        tok = Tok(key, val, clock)
        if eng == "pe" and dsem is None:
            self.known[eng][key] = val
        for res in w:
            res.w = tok
            res.r = {}
        for res in r:
            old = res.r.get(key)
            if old is None or old.val < val:
                res.r[key] = tok
        self.streams[eng].append((waits, fn, (key, inc)))
        if dsem is not None:
            self.all_dma.append(tok)
        if out:
            self.out_toks.append(tok)
        return tok

    def wait_tokens(self, eng, toks):
        waits = self._waits(eng, toks)
        if waits:
            self.streams[eng].append((waits, None, None))

    def barrier(self):
        last = []
        for e in self.ENG:
            if e == "sp":
                continue
            s = self.seq[e]
            if s > 0:
                s -= 1
                last.append(Tok((e, s // EPOCH), s % EPOCH + 1, {}))
        toks = last + self.all_dma
        self.all_dma = []
        for e in self.ENG:
            self.wait_tokens(e, toks)

    def emit(self, nc, sems):
        semmap = {}
        keys = list(self.keys.keys())
        assert len(keys) <= len(sems), (len(keys), len(sems))
        for k, s in zip(keys, sems):
            semmap[k] = s
        engobj = {"pe": "tensor", "act": "scalar", "dve": "vector", "pool": "gpsimd", "sp": "sync"}
        with nc.Block() as block:
            def mk(ename):
                def body(e):
                    for waits, fn, inc in self.streams[ename]:
                        for k, v in waits:
                            e.wait_ge(semmap[k], v)
                        if fn is not None:
                            ins = fn(e)
                            ins.then_inc(semmap[inc[0]], inc[1])
                return body
            block.tensor(mk("pe"))
            block.scalar(mk("act"))
            block.vector(mk("dve"))
            block.gpsimd(mk("pool"))
            block.sync(mk("sp"))


CST_LAYOUT = {}


def _build_consts():
    cols = []
    off = 0

    def put(name, arr):
        nonlocal off
        arr = np.asarray(arr, np.float32)
        assert arr.shape[0] == 128
        CST_LAYOUT[name] = (off, arr.shape[1])
        cols.append(arr)
        off += arr.shape[1]

    i = np.arange(128)
    put("ident", np.eye(128))
    put("tri", (i[:, None] <= i[None, :]).astype(np.float32))
    same = (i[:, None] // 8 == i[None, :] // 8)
    put("tris", ((i[:, None] <= i[None, :]) & same).astype(np.float32))
    put("mgt", (i[:, None] > i[None, :]).astype(np.float32))
    put("same", same.astype(np.float32))
    put("ones", np.ones((128, 128)))
    put("m16", (i[:, None] // 8 == np.arange(16)[None, :]).astype(np.float32))
    return np.concatenate(cols, axis=1)


def _alibi_tables():
    slopes = np.exp2(-8.0 * np.arange(1, 33, dtype=np.float32) / 32.0).astype(np.float32)
    ql = np.arange(128)[:, None]
    kl = np.arange(256)[None, :]
    dist = (128 + ql) - kl
    valid = (dist >= 0) & (dist < 128)
    bp = np.where(valid[None], -slopes[:, None, None] * dist[None].astype(np.float32), NEG).astype(np.float32)
    t = (np.arange(128) % 8)[:, None]
    s = (np.arange(128) // 8)[:, None]
    j = np.arange(128)[None, :]
    dist_c = 128 + t - j
    valid_c = dist_c < 128
    t2 = (np.arange(128) % 8)[None, :]
    s2 = (np.arange(128) // 8)[None, :]
    dist_n = t - t2
    valid_n = (s == s2) & (dist_n >= 0)
    dist_s = np.concatenate([dist_c, dist_n], axis=1)
    valid_s = np.concatenate([valid_c, valid_n], axis=1)
    bs = np.where(valid_s[None], -slopes[:, None, None] * dist_s[None].astype(np.float32), NEG).astype(np.float32)
    return bp, bs


class Arena:
    def __init__(self, t_f32, nbytes):
        self.t = t_f32
        self.n = nbytes
        self.off = 0
        self.marks = []

    def push(self):
        self.marks.append(self.off)

    def pop(self):
        self.off = self.marks.pop()

    def alloc(self, shape, dt):
        esz = 4 if dt == F32 else 2
        free = 1
        for s in shape[1:]:
            free *= s
        nb = (free * esz + 63) // 64 * 64
        assert self.off + nb <= self.n, ("SBUF arena overflow", self.off, nb, self.n)
        a = self.t[0:shape[0], self.off // 4:(self.off + nb) // 4]
        self.off += nb
        if dt != F32:
            a = a.bitcast(dt)
        a = a[:, 0:free]
        if len(shape) == 3:
            a = a.rearrange("p (a b) -> p a b", a=shape[1])
        elif len(shape) == 4:
            a = a.rearrange("p (a b c) -> p a b c", a=shape[1], b=shape[2])
        return a


def build_nc(stop_after=None, debug=False, nocc=False, prefix=True):
    nc = bass.Bass("TRN2", target_bir_lowering=False)
    cst_np = _build_consts()
    NCST = cst_np.shape[1]

    def din(name, shape):
        return nc.dram_tensor(name, list(shape), F32, kind="ExternalInput").ap()

    def dout(name, shape):
        return nc.dram_tensor(name, list(shape), F32, kind="ExternalOutput").ap()

    xin = din("xin", [TOK, D])
    w_in = din("w_in", [D, IN_DIM])
    w_ab = din("w_attn_br", [D, D])
    w_sb = din("w_ssm_br", [2 * D, D])
    w_o = din("w_out", [D, D])
    cache_k = din("cache_k", [16, 128, 512])
    cache_v = din("cache_v", [16, 128, 512])
    state_ssm = din("state_ssm", [16, 4096, 128])
    state_conv = din("state_conv", [48, 6144])
    norm_pre = din("norm_pre", [D])
    conv_w = din("conv_w", [4, 6144])
    conv_b = din("conv_b", [6144])
    dt_bias = din("dt_bias", [64])
    a_log = din("a_log", [64])
    d_skip = din("d_skip", [64])
    ssm_norm = din("ssm_norm", [4096])
    sinks = din("attn_sinks", [32])
    norm_post = din("norm_post", [D])
    cst = din("cst", [128, NCST])
    bias_p = din("bias_p", [32, 128, 256])
    bias_s = din("bias_s", [32, 128, 256])
    halo_mask = din("halo_mask", [128, 256])
    alpha = din("alpha", [8])
    xprev = din("xprev", [3072, D])
    pflag = din("pflag", [24])

    y_out = dout("y", [NPT, D])
    kp_out = dout("kp", [128, 512])
    vp_out = dout("vp", [128, 512])
    sp_out = dout("sp", [4096, 128])
    cp_out = dout("cp", [3, 6144])
    ks_out = dout("ks", [16, 128, 512])
    vs_out = dout("vs", [16, 128, 512])
    ss_out = dout("ss", [16, 4096, 128])
    cs_out = dout("cs", [48, 6144])

    if debug:
        AT_d = nc.dram_tensor("AT_d", [16, 128, NPT], BF16, kind="ExternalOutput").ap()
        ST_d = nc.dram_tensor("ST_d", [32, 128, NPT], BF16, kind="ExternalOutput").ap()
    else:
        AT_d = nc.dram_tensor("AT_d", [16, 128, NPT], BF16).ap()
        ST_d = nc.dram_tensor("ST_d", [32, 128, NPT], BF16).ap()
    Sst_d = [nc.dram_tensor(f"Sst_d{g}", [128, 512], F32) for g in range(8)]
    ag_in = [nc.dram_tensor(f"ag_in{g}", [128, 520], F32) for g in range(8)]
    ag_out = [nc.dram_tensor(f"ag_out{g}", [NCORES * 128, 520], F32) for g in range(8)]

    P = Prog()
    ARENA_BYTES = 207 * 1024

    from contextlib import ExitStack
    with ExitStack() as estack:
        arena_t = estack.enter_context(nc.sbuf_tensor("arena", [128, ARENA_BYTES // 4], F32))
        psum = [estack.enter_context(nc.psum_tensor(f"ps{i}", [128, 512], F32)) for i in range(8)]
        A = Arena(arena_t, ARENA_BYTES)
        PS = [p[:] for p in psum]
        RPS = [Res(f"ps{i}", excl=True) for i in range(8)]

        def PSb(i):
            return PS[i].bitcast(BF16)

        def dma(q, out, in_, r=(), w=(), dsem=None, outflag=False, slow=False):
            if slow:
                return P.add(q, lambda e: e.dma_start(out=out, in_=in_, allow_slow_non_contiguous=True), r=r, w=w, dsem=dsem, out=outflag)
            return P.add(q, lambda e: e.dma_start(out=out, in_=in_), r=r, w=w, dsem=dsem, out=outflag)

        def mm(out, lhsT, rhs, start=True, stop=True, r=(), w=()):
            return P.add("pe", lambda e: e.matmul(out, lhsT=lhsT, rhs=rhs, start=start, stop=stop), r=r, w=w)

        def tr(out, in_, ident, r=(), w=()):
            return P.add("pe", lambda e: e.transpose(out=out, in_=in_, identity=ident), r=r, w=w)

        def act(out, in_, func, r=(), w=(), bias=None, scale=None, accum_out=None):
            kw = {}
            if bias is not None:
                kw["bias"] = bias
            if scale is not None:
                kw["scale"] = scale
            if accum_out is not None:
                kw["accum_out"] = accum_out
            return P.add("act", lambda e: e.activation(out=out, in_=in_, func=func, **kw), r=r, w=w)

        def tt(eng, out, in0, in1, op, r=(), w=()):
            return P.add(eng, lambda e: e.tensor_tensor(out=out, in0=in0, in1=in1, op=op), r=r, w=w)

        def ts(eng, out, in0, s1, s2, op0, op1=None, r=(), w=(), accum_out=None):
            kw = {}
            if op1 is not None:
                kw["op1"] = op1
            if accum_out is not None:
                kw["accum_out"] = accum_out
            return P.add(eng, lambda e: e.tensor_scalar(out=out, in0=in0, scalar1=s1, scalar2=s2, op0=op0, **kw), r=r, w=w)

        def stt(eng, out, in0, scalar, in1, op0, op1, r=(), w=()):
            return P.add(eng, lambda e: e.scalar_tensor_tensor(out=out, in0=in0, scalar=scalar, in1=in1, op0=op0, op1=op1), r=r, w=w)

        def cp(eng, out, in_, r=(), w=()):
            if eng == "act":
                return P.add("act", lambda e: e.copy(out=out, in_=in_), r=r, w=w)
            return P.add(eng, lambda e: e.tensor_copy(out=out, in_=in_), r=r, w=w)

        def run_interleaved(gens, width):
            it_ = iter(gens)
            active = []
            while True:
                while len(active) < width:
                    try:
                        active.append(next(it_))
                    except StopIteration:
                        break
                if not active:
                    break
                for g_ in list(active):
                    try:
                        next(g_)
                    except StopIteration:
                        active.remove(g_)

        def memset(eng, ap, val, w=()):
            return P.add(eng, lambda e: e.memset(ap, val), w=w)

        R_const = Res("const")
        ds_const = DSem("const")
        cstt = A.alloc([128, NCST], F32)
        dma("sp", cstt, cst[:, :], dsem=ds_const)

        def C(name):
            o, n = CST_LAYOUT[name]
            return cstt[:, o:o + n]
        ident_f, TRI, TRIS, MGT, SAME, ONES, M16 = C("ident"), C("tri"), C("tris"), C("mgt"), C("same"), C("ones"), C("m16")
        ident_b = A.alloc([128, 128], BF16)
        o_id = CST_LAYOUT["ident"][0]
        ds_idb = DSem("idb")
        R_idb = Res("idb")
        dma("pool", ident_b, cst[:, o_id:o_id + 128], w=[R_idb], dsem=ds_idb)
        npre_pk = A.alloc([128, 16], F32)
        cw_pk = A.alloc([128, 48, 4], F32)
        cb_pk = A.alloc([128, 48], F32)
        ssn_pk = A.alloc([128, 32], F32)
        npost_pk = None
        pk_srcs = [(norm_pre, 16), (conv_b, 48), (ssm_norm, 32), (conv_w[0], 48), (conv_w[1], 48), (conv_w[2], 48), (conv_w[3], 48)]
        dtb_bc = A.alloc([128, 64], F32)
        dma("sp", dtb_bc, dt_bias.partition_broadcast(128), dsem=ds_const)
        alog_bc = A.alloc([128, 64], F32)
        dma("sp", alog_bc, a_log.partition_broadcast(128), dsem=ds_const)
        dsk_bc = A.alloc([128, 64], F32)
        dma("sp", dsk_bc, d_skip.partition_broadcast(128), dsem=ds_const)
        sink_bc = A.alloc([128, 32], F32)
        dma("sp", sink_bc, sinks.partition_broadcast(128), dsem=ds_const)
        alpha_bc = A.alloc([128, 8], F32)
        dma("sp", alpha_bc, alpha.partition_broadcast(128), dsem=ds_const)
        hmask = A.alloc([128, 256], F32)
        dma("sp", hmask, halo_mask[:, :], dsem=ds_const)
        R_const.w = Tok(ds_const.key, ds_const.count, {ds_const.key: ds_const.count})
        RC = [R_const]
        A.push()
        pk_tmp = A.alloc([128, 6, 128], F32)
        R_pk = Res("pk")
        ds_pk = DSem("pk")
        pk_dst = [npre_pk, cb_pk, ssn_pk] + [cw_pk[:, :, j] for j in range(4)]
        for i, ((src, nb), dst) in enumerate(zip(pk_srcs, pk_dst)):
            slot = i % 6
            if i == 6:
                P.barrier()
            dma("sp", pk_tmp[0:nb, slot, :], src.rearrange("(b p) -> b p", p=128), w=[R_pk], dsem=ds_pk)
            mm(PS[0][:, 0:nb], pk_tmp[0:nb, slot, :], ident_f[0:nb, 0:nb], r=[R_pk] + RC, w=[RPS[0]])
            cp("dve", dst, PS[0][:, 0:nb], r=[RPS[0]], w=[R_pk])
        RC = [R_const, R_pk, R_idb]
        A.pop()
        P.barrier()

        NPF = 24
        ibank = {"b": 0}

        def ipbank():
            b = ibank["b"]
            ibank["b"] = 1 - b
            return b

        def norm_tile(src_rows, xb, rx, dsx, dst, Rdst, stcol, Rst, junk_, Rjunk):
            dma("sp", xb, src_rows, w=[rx], dsem=dsx)
            act(junk_, xb, AF.Square, r=[rx], w=[Rjunk, Rst], accum_out=stcol)
            ts("dve", stcol, stcol, 1.0 / D, EPS, ALU.mult, ALU.add, r=[Rst], w=[Rst])
            act(stcol, stcol, AF.Sqrt, r=[Rst], w=[Rst])
            P.add("dve", lambda e, o=stcol: e.reciprocal(out=o, in_=o), r=[Rst], w=[Rst])
            tt("pool", xb, xb, stcol.broadcast_to([128, D]), ALU.mult, r=[rx, Rst], w=[rx])
            for b in range(4):
                bank = b % 2
                for q in range(4):
                    kb = b * 4 + q
                    mm(PS[bank][:, q * 128:(q + 1) * 128], xb[:, kb * 128:(kb + 1) * 128], ident_f,
                       r=[rx] + RC, w=[RPS[bank]])
                src = PS[bank].rearrange("p (a b) -> p a b", a=4)
                sc_ = npre_pk[:, b * 4:(b + 1) * 4].unsqueeze(2).broadcast_to([128, 4, 128])
                tt("dve", dst(b), src, sc_, ALU.mult, r=[RPS[bank]] + RC, w=[Rdst])

        R_SstD = [Res(f"SstD{g}") for g in range(8)]
        if prefix:
            A.push()
            hTp = A.alloc([128, 16, NPF * 128], BF16)
            R_hTp = [Res(f"hTp{t}") for t in range(NPF)]
            pw = [A.alloc([128, 16, 512], BF16) for _ in range(2)]
            R_pw, ds_pw = [Res("pw0"), Res("pw1")], [DSem("pw0"), DSem("pw1")]
            pwn = {"n": 0}

            def load_p(src, n):
                i = pwn["n"] % 2
                pwn["n"] += 1
                dma("pool", pw[i][:, :, 0:n], src.rearrange("(k p) n -> p k n", p=128), w=[R_pw[i]], dsem=ds_pw[i])
                return pw[i], R_pw[i]

            pxb = [A.alloc([128, D], F32) for _ in range(2)]
            R_px, ds_px = [Res("px0"), Res("px1")], [DSem("px0"), DSem("px1")]
            pjunk = A.alloc([128, D], BF16)
            R_pj = Res("pjunk")
            pst = A.alloc([128, NPF], F32)
            R_pst = [Res(f"pst{t}") for t in range(NPF)]
            pflag_bc = A.alloc([128, NPF], F32)
            R_pf, ds_pf = Res("pflag"), DSem("pflag")
            dma("sp", pflag_bc, pflag.partition_broadcast(128), w=[R_pf], dsem=ds_pf)
            dtdP = A.alloc([128, NPF, 64], F32)
            eaeP = A.alloc([128, NPF, 64], F32)
            R_pdt = [Res(f"pdt{t}") for t in range(NPF)]
            pA_bc = A.alloc([128, 64], F32)
            R_pA = Res("pA")
            ptmp = A.alloc([128, 8, 64], F32)
            R_pt = Res("ptmp")
            for t in range(NPF):
                norm_tile(xprev[t * 128:(t + 1) * 128, :], pxb[t % 2], R_px[t % 2], ds_px[t % 2],
                          lambda b, t=t: hTp[:, b * 4:(b + 1) * 4, t * 128:(t + 1) * 128], R_hTp[t],
                          pst[:, t:t + 1], R_pst[t], pjunk, R_pj)
            act(pA_bc, alog_bc, AF.Exp, r=RC, w=[R_pA])
            ts("dve", pA_bc, pA_bc, -1.0, None, ALU.mult, r=[R_pA], w=[R_pA])
            wtd, Rwd = load_p(w_in[:, DT0:DT0 + 64], 64)
            for t in range(NPF):
                bank = ipbank()
                for k in range(16):
                    mm(PS[bank][:, 0:64], hTp[:, k, t * 128:(t + 1) * 128], wtd[:, k, 0:64],
                       start=(k == 0), stop=(k == 15), r=[Rwd, R_hTp[t]], w=[RPS[bank]])
                xx, ax, ee, ll, tmp, dtp, adtp, at_ = [ptmp[:, n, :] for n in range(8)]
                tt("dve", xx, PS[bank][:, 0:64], dtb_bc, ALU.add, r=[RPS[bank]] + RC, w=[R_pt])
                act(ax, xx, AF.Abs, r=[R_pt], w=[R_pt])
                act(ee, ax, AF.Exp, scale=-1.0, r=[R_pt], w=[R_pt])
                ts("dve", ee, ee, 1.0, None, ALU.add, r=[R_pt], w=[R_pt])
                act(ll, ee, AF.Ln, r=[R_pt], w=[R_pt])
                stt("dve", dtp, xx, 0.0, ll, ALU.max, ALU.add, r=[R_pt], w=[R_pt])
                ts("dve", dtp, dtp, pflag_bc[:, t:t + 1], None, ALU.mult, r=[R_pt, R_pf], w=[R_pt])
                tt("dve", adtp, dtp, pA_bc, ALU.mult, r=[R_pt, R_pA], w=[R_pt])
                b2 = ipbank()
                mm(PS[b2][:, 0:64], TRI, adtp, r=[R_pt] + RC, w=[RPS[b2]])
                mm(PS[b2][:, 64:128], ONES, adtp, r=[R_pt] + RC, w=[RPS[b2]])
                cp("dve", at_, PS[b2][:, 0:64], r=[RPS[b2]], w=[R_pt])
                tt("dve", tmp, PS[b2][:, 64:128], at_, ALU.subtract, r=[RPS[b2], R_pt], w=[R_pt])
                act(tmp, tmp, AF.Exp, r=[R_pt], w=[R_pt])
                tt("dve", dtdP[:, t, :], dtp, tmp, ALU.mult, r=[R_pt], w=[R_pdt[t]])
                act(eaeP[:, t, :], PS[b2][:, 64:128], AF.Exp, r=[RPS[b2]], w=[R_pdt[t]])
            pxp = A.alloc([128, 5, 515], F32)
            R_pxp = [Res(f"pxp{b}") for b in range(5)]
            pacc2 = [A.alloc([128, 512], F32) for _ in range(5)]
            R_pacc2 = [Res(f"pacc{i}") for i in range(5)]
            pcv2 = [A.alloc([128, 5, 512], BF16) for _ in range(2)]
            R_pcv2 = [[Res(f"pcv{i}_{b}") for b in range(5)] for i in range(2)]
            pxt = [A.alloc([128, 640], BF16) for _ in range(2)]
            R_pxt = [Res("pxt0"), Res("pxt1")]
            pxd = [A.alloc([128, 512], BF16) for _ in range(2)]
            R_pxd = [Res("pxd0"), Res("pxd1")]
            pST = [A.alloc([128, 512], F32)] * 2
            R_pST, ds_pST = [Res("pST0")] * 2, [DSem("pST0")] * 2
            pc = {"x": 0}
            for g in range(8):
                wxa, Rxa = load_p(w_in[:, XS0 + 512 * g:XS0 + 512 * (g + 1)], 512)
                wxb, Rxb = load_p(w_in[:, B0 + 128 * g:B0 + 128 * (g + 1)], 128)
                STp, RSTp = pST[g % 2], R_pST[g % 2]
                memset("pool", STp, 0.0, w=[RSTp])
                for blk in range(5):
                    memset("pool", pxp[:, blk, 0:3], 0.0, w=[R_pxp[blk]])
                blocks_done = {}
                blk_active = {}
                free_pacc = [0, 1]
                free_tail = [0, 1]
                tails_tr = {}
                upd_done = {-1: True}

                def blk_gen(s, blk):
                    tiles = [R_hTp[t] for t in range(4 * s, 4 * s + 4)]
                    wsl, Rws, c0 = (wxa, Rxa, blk * 128) if blk < 4 else (wxb, Rxb, 0)
                    cbi = (4 * g + blk) if blk < 4 else (32 + g)
                    while blk_active.get(blk):
                        yield
                    blk_active[blk] = True
                    pslot = blk
                    pacc_, R_pacc_ = pacc2[pslot], R_pacc2[pslot]
                    pcv_, R_pcv_ = pcv2[s % 2], R_pcv2[s % 2]
                    bank = (0, 1, 3, 4, 5)[blk]
                    for k in range(16):
                        mm(PS[bank], wsl[:, k, c0:c0 + 128], hTp[:, k, s * 512:(s + 1) * 512],
                           start=(k == 0), stop=(k == 15), r=[Rws] + tiles, w=[RPS[bank]])
                    xp = pxp[:, blk, :]
                    cp("act", xp[:, 3:515], PS[bank], r=[RPS[bank]], w=[R_pxp[blk]])
                    yield
                    ts("dve", pacc_, xp[:, 0:512], cw_pk[:, cbi, 0:1], None, ALU.mult, r=[R_pxp[blk]] + RC, w=[R_pacc_])
                    yield
                    for j in range(1, 4):
                        stt("dve", pacc_, xp[:, j:j + 512], cw_pk[:, cbi, j:j + 1], pacc_, ALU.mult, ALU.add,
                            r=[R_pxp[blk], R_pacc_] + RC, w=[R_pacc_])
                        yield
                    while s >= 2 and tails_tr.get(s - 2, 0) < 4:
                        yield
                    act(pcv_[:, blk, :], pacc_, AF.Silu, bias=cb_pk[:, cbi:cbi + 1], r=[R_pacc_] + RC, w=[R_pcv_[blk]])
                    cp("pool", xp[:, 0:3], xp[:, 512:515], r=[R_pxp[blk]], w=[R_pxp[blk]])
                    blocks_done[s] = blocks_done.get(s, 0) + 1
                    blk_active[blk] = False

                def tail_gen(s, q):
                    t = 4 * s + q
                    pcv_, R_pcv_ = pcv2[s % 2], R_pcv2[s % 2]
                    while blocks_done.get(s, 0) < 5 or not free_tail:
                        yield
                    xi = free_tail.pop()
                    bT = 7 if xi == 0 else 2
                    for blk in range(5):
                        tr(PSb(bT)[:, blk * 128:(blk + 1) * 128], pcv_[:, blk, q * 128:(q + 1) * 128], ident_b,
                           r=[R_pcv_[blk]] + RC, w=[RPS[bT]])
                    tails_tr[s] = tails_tr.get(s, 0) + 1
                    cp("act", pxt[xi], PSb(bT)[:, 0:640], r=[RPS[bT]], w=[R_pxt[xi]])
                    yield
                    tt("pool", pxd[xi].rearrange("p (h d) -> p h d", h=8), pxt[xi][:, 0:512].rearrange("p (h d) -> p h d", h=8),
                       dtdP[:, t, 8 * g:8 * g + 8].unsqueeze(2).broadcast_to([128, 8, 64]), ALU.mult,
                       r=[R_pxt[xi], R_pdt[t]], w=[R_pxd[xi]])
                    yield
                    while not upd_done.get(t - 1):
                        yield
                    mm(PS[6], pxt[xi][:, 512:640], pxd[xi], r=[R_pxt[xi], R_pxd[xi]], w=[RPS[6]])
                    tt("dve", STp.rearrange("p (h d) -> p h d", h=8), STp.rearrange("p (h d) -> p h d", h=8),
                       eaeP[:, t, 8 * g:8 * g + 8].unsqueeze(2).broadcast_to([128, 8, 64]), ALU.mult,
                       r=[RSTp, R_pdt[t]], w=[RSTp])
                    tt("dve", STp, STp, PS[6], ALU.add, r=[RSTp, RPS[6]], w=[RSTp])
                    upd_done[t] = True
                    free_tail.append(xi)

                def all_gens():
                    for s in range(NPF // 4):
                        for blk in range(5):
                            yield blk_gen(s, blk)
                        for q in range(4):
                            yield tail_gen(s, q)

                run_interleaved(all_gens(), 6)
                dma("sp", Sst_d[g].ap(), STp, r=[RSTp], w=[R_SstD[g]], dsem=ds_pST[g % 2])
            A.pop()
            P.barrier()

        hT = A.alloc([128, 16, TOK], BF16)
        R_hT = [Res(f"hT{t}") for t in range(NT)]

        NSLOT = 2
        wslot = [A.alloc([128, 16, 512], BF16) for _ in range(NSLOT)]
        R_w = [Res(f"w{i}") for i in range(NSLOT)]
        ds_w = [DSem(f"w{i}") for i in range(NSLOT)]
        wstate = {"n": 0}

        def load_w(segs):
            i = wstate["n"] % NSLOT
            wstate["n"] += 1
            for k, (src, off) in enumerate(segs):
                n = src.shape[1]
                dma("pool", wslot[i][:, :, off:off + n], src.rearrange("(k p) n -> p k n", p=128),
                    w=[R_w[i]] if k == 0 else [], dsem=ds_w[i])
                if k > 0:
                    R_w[i].w = P.all_dma[-1]
            return wslot[i], R_w[i]

        def CK(name):
            if stop_after == name:
                raise _Stop()

        try:
            A.push()
            xbuf = [A.alloc([128, D], F32) for _ in range(2)]
            R_x = [Res("x0"), Res("x1")]
            ds_x = [DSem("x0"), DSem("x1")]
            junk = A.alloc([128, D], F32)
            R_junk = Res("junk")
            ssq = A.alloc([128, NT], F32)
            rstd = A.alloc([128, NT], F32)
            R_st = [Res(f"st{t}") for t in range(NT)]
            for t in range(NT):
                xb, rx = xbuf[t % 2], R_x[t % 2]
                dma("sp", xb, xin[t * 128:(t + 1) * 128, :], w=[rx], dsem=ds_x[t % 2])
                act(junk, xb, AF.Square, r=[rx], w=[R_junk, R_st[t]], accum_out=ssq[:, t:t + 1])
                ts("dve", rstd[:, t:t + 1], ssq[:, t:t + 1], 1.0 / D, EPS, ALU.mult, ALU.add, r=[R_st[t]], w=[R_st[t]])
                act(rstd[:, t:t + 1], rstd[:, t:t + 1], AF.Sqrt, r=[R_st[t]], w=[R_st[t]])
                P.add("dve", lambda e, o=rstd[:, t:t + 1]: e.reciprocal(out=o, in_=o), r=[R_st[t]], w=[R_st[t]])
                tt("pool", xb, xb, rstd[:, t:t + 1].broadcast_to([128, D]), ALU.mult, r=[rx, R_st[t]], w=[rx])
                for b in range(4):
                    bank = b % 2
                    for q in range(4):
                        kb = b * 4 + q
                        mm(PS[bank][:, q * 128:(q + 1) * 128], xb[:, kb * 128:(kb + 1) * 128], ident_f,
                           r=[rx] + RC, w=[RPS[bank]])
                    eng = "dve" if b % 2 == 0 else "pool"
                    src = PS[bank].rearrange("p (a b) -> p a b", a=4)
                    dst = hT[:, b * 4:(b + 1) * 4, t * 128:(t + 1) * 128]
                    sc = npre_pk[:, b * 4:(b + 1) * 4].unsqueeze(2).broadcast_to([128, 4, 128])
                    tt("dve", dst, src, sc, ALU.mult, r=[RPS[bank]] + RC, w=[R_hT[t]])
            A.pop()
            P.barrier()
            CK("p0")
            A.push()
            qT = A.alloc([128, 2, NPT], BF16)
            R_qT = Res("qT")
            kT = A.alloc([128, TOK], BF16)
            R_kT = Res("kT")
            vb = A.alloc([128, NT, 64], BF16)
            R_vb = [Res(f"vb{t}") for t in range(NT)]
            sz = A.alloc([128, NT, 256], BF16)
            R_sz = [Res(f"sz{t}") for t in range(NT)]
            kvnew = A.alloc([128, 2, 2, 512], F32)
            R_kvnew = Res("kvnew")
            biasP = A.alloc([128, 4, 256], F32)
            biasS = A.alloc([128, 4, 256], F32)
            R_bias = Res("bias")
            ds_bias = DSem("bias")
            Sb = [A.alloc([128, 4, 256], F32) for _ in range(2)]
            Pb = [A.alloc([128, 4, 256], BF16) for _ in range(2)]
            PTs = [A.alloc([128, 8, 128], BF16) for _ in range(2)]
            stat = [A.alloc([128, 8, 4], F32) for _ in range(2)]
            An = [A.alloc([128, 256], F32) for _ in range(2)]
            Ag = [A.alloc([128, 256], BF16) for _ in range(2)]
            R_it = [[Res(f"it{i}_{n}") for n in range(8)] for i in range(2)]
            ATs = A.alloc([128, 2, NPT], BF16)
            R_ATs = Res("ATs")
            ds_AT = DSem("AT")
            R_ATd = Res("ATd")
            kc = A.alloc([128, 16, 128], BF16)
            vc = A.alloc([128, 16, 64], BF16)
            R_kc, R_vc = Res("kc"), Res("vc")
            ds_kc = DSem("kc")
            ds_vc = DSem("vc")
            KcT = A.alloc([128, 16, 128], BF16)
            R_KcT = Res("KcT")
            Zq = [A.alloc([128, 16, 248], BF16) for _ in range(2)]
            R_Zq = [Res("Zq0"), Res("Zq1")]
            ZP = A.alloc([128, 2, 16, 248], BF16)
            R_ZP = Res("ZP")
            memset("pool", Zq[0], 0.0, w=[R_Zq[0]])
            memset("pool", Zq[1], 0.0, w=[R_Zq[1]])
            memset("pool", ZP, 0.0, w=[R_ZP])
            itc = {"n": 0, "bank": 0}

            for g in range(8):
                wt, Rw = load_w([(w_in[:, Q0 + 256 * g:Q0 + 256 * (g + 1)], 0),
                                 (w_in[:, K0 + 64 * g:K0 + 64 * (g + 1)], 256),
                                 (w_in[:, K0 + 64 * g:K0 + 64 * (g + 1)], 320),
                                 (w_in[:, V0 + 64 * g:V0 + 64 * (g + 1)], 384)])
                wt2, Rw2 = load_w([(w_in[:, ZA0 + 256 * g:ZA0 + 256 * (g + 1)], 0)])
                dma("sp", biasP, bias_p[4 * g:4 * g + 4].rearrange("h p k -> p h k"), w=[R_bias], dsem=ds_bias)
                dma("sp", biasS, bias_s[4 * g:4 * g + 4].rearrange("h p k -> p h k"), dsem=ds_bias)
                R_bias.w = P.all_dma[-1]
                dma("pool", kc[:, :, 0:64], cache_k[:, :, 64 * g:64 * (g + 1)].rearrange("s k d -> k s d"), w=[R_kc], dsem=ds_kc)
                dma("pool", kc[:, :, 64:128], cache_k[:, :, 64 * g:64 * (g + 1)].rearrange("s k d -> k s d"), dsem=ds_kc)
                R_kc.w = P.all_dma[-1]
                dma("pool", vc, cache_v[:, :, 64 * g:64 * (g + 1)].rearrange("s k d -> k s d"), w=[R_vc], dsem=ds_vc)

                if g == 0:
                    CK("a0")
                for blk in range(3):
                    ranges = [(128, 512), (640, 512), (1152, 128)] if blk < 2 else [(0, 512), (512, 512), (1024, 256)]
                    for (t0, n) in ranges:
                        bank = ipbank()
                        tiles = [R_hT[t] for t in range(t0 // 128, (t0 + n) // 128)]
                        for k in range(16):
                            mm(PS[bank][:, 0:n], wt[:, k, blk * 128:(blk + 1) * 128], hT[:, k, t0:t0 + n],
                               start=(k == 0), stop=(k == 15), r=[Rw] + tiles, w=[RPS[bank]])
                        if blk < 2:
                            ts("dve", qT[:, blk, t0 - 128:t0 - 128 + n], PS[bank][:, 0:n], 0.125, None, ALU.mult,
                               r=[RPS[bank]], w=[R_qT])
                        else:
                            cp("act", kT[:, t0:t0 + n], PS[bank][:, 0:n], r=[RPS[bank]], w=[R_kT])
                if g == 0:
                    CK("a1")
                for t in range(NT):
                    bank = ipbank()
                    for k in range(16):
                        mm(PS[bank][:, 0:128], hT[:, k, t * 128:(t + 1) * 128], wt[:, k, 320:448],
                           start=(k == 0), stop=(k == 15), r=[Rw, R_hT[t]], w=[RPS[bank]])
                    if t >= 1:
                        for k in range(16):
                            mm(PS[bank][:, 128:384], hT[:, k, t * 128:(t + 1) * 128], wt2[:, k, 0:256],
                               start=(k == 0), stop=(k == 15), r=[Rw2, R_hT[t]], w=[RPS[bank]])
                    cp("dve", vb[:, t, :], PS[bank][:, 64:128], r=[RPS[bank]], w=[R_vb[t]])
                    if t >= 8:
                        cp("dve", kvnew[:, t - 8, :, 64 * g:64 * (g + 1)], PS[bank][:, 0:128].rearrange("p (a b) -> p a b", a=2),
                           r=[RPS[bank]], w=[R_kvnew])
                    if t >= 1:
                        act(sz[:, t, :], PS[bank][:, 128:384], AF.Silu, r=[RPS[bank]], w=[R_sz[t]])

                if g == 0:
                    CK("a_inproj")
                for hf in range(2):
                    for s8 in range(8):
                        s = hf * 8 + s8
                        tr(PSb(7)[:, s8 * 128:(s8 + 1) * 128], kc[:, s, :], ident_b, r=[R_kc] + RC, w=[RPS[7]])
                    cp("act", KcT[:, hf * 8:(hf + 1) * 8, :], PSb(7).rearrange("p (a b) -> p a b", a=8), r=[RPS[7]], w=[R_KcT])
                for b in range(2):
                    cp("pool", Zq[b][:, :, 120:128], qT[:, b, 1024:1152].rearrange("p (s t) -> p s t", s=16),
                       r=[R_qT], w=[R_Zq[b]])

                if g == 0:
                    CK("a3")
                def attn_gen(i):
                    sample = (i == 9)
                    it = (i - 1) % 2
                    bS = (2, 3) if it == 0 else (0, 1)
                    bPT = 4 if it == 0 else 7
                    bO = 5 if it == 0 else 6
                    RS, RP, RPT, RST, RAN, RAG = R_it[it][0:6]
                    S_, P_, PT_, st_, An_, Ag_ = Sb[it], Pb[it], PTs[it], stat[it], An[it], Ag[it]
                    rmax, mx, negm, rsum, smm, es, den, rec = [st_[:, n, :] for n in range(8)]
                    for j in range(4):
                        b, jj = j // 2, j % 2
                        pr = slice(jj * 64, (jj + 1) * 64)
                        reg = PS[bS[jj]][:, b * 256:(b + 1) * 256]
                        if not sample:
                            mm(reg, qT[pr, b, (i - 1) * 128:i * 128], kT[pr, (i - 1) * 128:(i + 1) * 128],
                               r=[R_qT, R_kT], w=[RPS[bS[jj]]])
                        else:
                            for s in range(16):
                                mm(reg[:, 0:128], Zq[b][pr, s, 120 - 8 * s:248 - 8 * s], KcT[pr, s, :],
                                   start=(s == 0), stop=(s == 15), r=[R_Zq[b], R_KcT], w=[RPS[bS[jj]]])
                            mm(reg[:, 128:256], qT[pr, b, 1024:1152], kT[pr, 1152:1280], r=[R_qT, R_kT], w=[RPS[bS[jj]]])
                    bias_t = biasS if sample else biasP
                    for jj in range(2):
                        tt("dve", S_.rearrange("p (b j) k -> p b j k", j=2)[:, :, jj, :], PS[bS[jj]].rearrange("p (a b) -> p a b", a=2),
                           bias_t.rearrange("p (b j) k -> p b j k", j=2)[:, :, jj, :], ALU.add, r=[RPS[bS[jj]], R_bias], w=[RS])
                    yield
                    if i == 1:
                        tt("dve", S_, S_, hmask.unsqueeze(1).broadcast_to([128, 4, 256]), ALU.add, r=[RS] + RC, w=[RS])
                    P.add("dve", lambda e, o=rmax, s_=S_: e.tensor_reduce(out=o, in_=s_, axis=AX.X, op=ALU.max), r=[RS], w=[RST])
                    tt("dve", mx, rmax, sink_bc[:, 4 * g:4 * g + 4], ALU.max, r=[RST] + RC, w=[RST])
                    ts("dve", negm, mx, -1.0, None, ALU.mult, r=[RST], w=[RST])
                    tt("dve", smm, sink_bc[:, 4 * g:4 * g + 4], mx, ALU.subtract, r=[RST] + RC, w=[RST])
                    yield
                    for j in range(4):
                        act(P_[:, j, :], S_[:, j, :], AF.Exp, bias=negm[:, j:j + 1], accum_out=rsum[:, j:j + 1],
                            r=[RS, RST], w=[RP, RST])
                    act(es, smm, AF.Exp, r=[RST], w=[RST])
                    yield
                    tt("dve", den, rsum, es, ALU.add, r=[RST], w=[RST])
                    P.add("dve", lambda e, o=rec, d_=den: e.reciprocal(out=o, in_=d_), r=[RST], w=[RST])
                    yield
                    for j in range(4):
                        for hf in range(2):
                            tr(PSb(bPT)[:, (2 * j + hf) * 128:(2 * j + hf + 1) * 128], P_[:, j, hf * 128:(hf + 1) * 128], ident_b,
                               r=[RP] + RC, w=[RPS[bPT]])
                    cp("act", PT_, PSb(bPT).rearrange("p (a b) -> p a b", a=8), r=[RPS[bPT]], w=[RPT])
                    yield
                    if not sample:
                        for j in range(4):
                            mm(PS[bO][:, j * 64:(j + 1) * 64], PT_[:, 2 * j, :], vb[:, i - 1, :], start=True, stop=False,
                               r=[RPT, R_vb[i - 1]], w=[RPS[bO]])
                            mm(PS[bO][:, j * 64:(j + 1) * 64], PT_[:, 2 * j + 1, :], vb[:, i, :], start=False, stop=True,
                               r=[RPT, R_vb[i]], w=[RPS[bO]])
                    else:
                        for b in range(2):
                            src = PT_.rearrange("p (j h) k -> p j h k", h=2)[:, 2 * b:2 * b + 2, 0, :]
                            cp("pool", ZP[:, :, :, 120:128], src.rearrange("p j (s t) -> p j s t", s=16), r=[RPT], w=[R_ZP])
                            for jj in range(2):
                                j = 2 * b + jj
                                for s in range(16):
                                    mm(PS[bO][:, j * 64:(j + 1) * 64], ZP[:, jj, s, 120 - 8 * s:248 - 8 * s], vc[:, s, :],
                                       start=(s == 0), stop=False, r=[R_ZP, R_vc], w=[RPS[bO]])
                                mm(PS[bO][:, j * 64:(j + 1) * 64], PT_[:, 2 * j + 1, :], vb[:, 9, :], start=False, stop=True,
                                   r=[RPT, R_vb[9]], w=[RPS[bO]])
                    tt("dve", An_.rearrange("p (a b) -> p a b", a=4), PS[bO][:, 0:256].rearrange("p (a b) -> p a b", a=4),
                       rec.unsqueeze(2).broadcast_to([128, 4, 64]), ALU.mult, r=[RPS[bO], RST], w=[RAN])
                    tt("pool", Ag_, An_, sz[:, i, :], ALU.mult, r=[RAN, R_sz[i]], w=[RAG])
                    yield
                    for b in range(2):
                        tr(PSb(bO)[:, 512 + b * 128:512 + (b + 1) * 128], Ag_[:, b * 128:(b + 1) * 128], ident_b, r=[RAG] + RC, w=[RPS[bO]])
                    cp("act", ATs[:, :, (i - 1) * 128:i * 128], PSb(bO)[:, 512:768].rearrange("p (a b) -> p a b", a=2),
                       r=[RPS[bO]], w=[R_ATs])
                run_interleaved((attn_gen(i) for i in range(1, NT)), 2)
                dma("sp", AT_d[2 * g:2 * g + 2].rearrange("b p t -> p b t"), ATs, r=[R_ATs], w=[R_ATd], dsem=ds_AT)
                if g == 0:
                    CK("a_g0")

            CK("a_attn")
            ds_kv = DSem("kvout")
            dma("sp", kp_out[:, :], kvnew[:, 0, 0, :], r=[R_kvnew], dsem=ds_kv, outflag=True)
            dma("sp", vp_out[:, :], kvnew[:, 0, 1, :], r=[R_kvnew], dsem=ds_kv, outflag=True)
            for s in range(16):
                dma("sp", ks_out[s, 120:128, :], kvnew[8 * s:8 * s + 8, 1, 0, :], r=[R_kvnew], dsem=ds_kv, outflag=True)
                dma("sp", vs_out[s, 120:128, :], kvnew[8 * s:8 * s + 8, 1, 1, :], r=[R_kvnew], dsem=ds_kv, outflag=True)
            dma("sp", ks_out[:, 0:120, :], cache_k[:, 8:128, :], dsem=ds_kv, outflag=True)
            dma("sp", vs_out[:, 0:120, :], cache_v[:, 8:128, :], dsem=ds_kv, outflag=True)
            A.pop()
            P.barrier()
            A.push()
            NTT = 9
            dt_all = A.alloc([128, NTT, 64], F32)
            adt_all = A.alloc([128, NTT, 64], F32)
            eat_all = A.alloc([128, NTT, 64], F32)
            dtd_all = A.alloc([128, NTT, 64], F32)
            eaend_all = A.alloc([128, NTT, 64], F32)
            eatg_all = None if prefix else A.alloc([128, NTT, 64], F32)
            R_dt = [Res(f"dt{t}") for t in range(NTT)]
            A_bc = A.alloc([128, 64], F32)
            cum_bc = A.alloc([128, 64], F32)
            R_cum = Res("cum")
            E_samp = A.alloc([128, 512], F32)
            R_Es = Res("Es")
            A.push()
            b0tmp = A.alloc([128, 8, 64], F32)
            R_b0 = Res("b0tmp")
            Xeo = A.alloc([128, 2, 512], F32)
            R_Xeo = Res("Xeo")

            act(A_bc, alog_bc, AF.Exp, r=RC, w=[R_cum])
            ts("dve", A_bc, A_bc, -1.0, None, ALU.mult, r=[R_cum], w=[R_cum])
            memset("pool", cum_bc, 0.0, w=[R_cum])
            wt, Rw = load_w([(w_in[:, DT0:DT0 + 64], 0)])
            for t in range(1, NT):
                ti = t - 1
                smp = (t == 9)
                bank = ipbank()
                for k in range(16):
                    mm(PS[bank][:, 0:64], hT[:, k, t * 128:(t + 1) * 128], wt[:, k, 0:64],
                       start=(k == 0), stop=(k == 15), r=[Rw, R_hT[t]], w=[RPS[bank]])
                xx, ax, ee, ll, tmp = [b0tmp[:, n, :] for n in range(5)]
                tt("dve", xx, PS[bank][:, 0:64], dtb_bc, ALU.add, r=[RPS[bank]] + RC, w=[R_b0])
                act(ax, xx, AF.Abs, r=[R_b0], w=[R_b0])
                act(ee, ax, AF.Exp, scale=-1.0, r=[R_b0], w=[R_b0])
                ts("dve", ee, ee, 1.0, None, ALU.add, r=[R_b0], w=[R_b0])
                act(ll, ee, AF.Ln, r=[R_b0], w=[R_b0])
                stt("dve", dt_all[:, ti, :], xx, 0.0, ll, ALU.max, ALU.add, r=[R_b0], w=[R_dt[ti]])
                tt("dve", adt_all[:, ti, :], dt_all[:, ti, :], A_bc, ALU.mult, r=[R_dt[ti], R_cum], w=[R_dt[ti]])
                b2 = ipbank()
                mm(PS[b2][:, 0:64], TRIS if smp else TRI, adt_all[:, ti, :], r=[R_dt[ti]] + RC, w=[RPS[b2]])
                mm(PS[b2][:, 64:128], SAME if smp else ONES, adt_all[:, ti, :], r=[R_dt[ti]] + RC, w=[RPS[b2]])
                at_ = b0tmp[:, 5, :]
                cp("dve", at_, PS[b2][:, 0:64], r=[RPS[b2]], w=[R_b0])
                act(eat_all[:, ti, :], PS[b2][:, 0:64], AF.Exp, r=[RPS[b2]], w=[R_dt[ti]])
                tt("dve", tmp, PS[b2][:, 64:128], at_, ALU.subtract, r=[RPS[b2], R_b0], w=[R_b0])
                act(tmp, tmp, AF.Exp, r=[R_b0], w=[R_b0])
                tt("dve", dtd_all[:, ti, :], dt_all[:, ti, :], tmp, ALU.mult, r=[R_b0, R_dt[ti]], w=[R_dt[ti]])
                act(eaend_all[:, ti, :], PS[b2][:, 64:128], AF.Exp, r=[RPS[b2]], w=[R_dt[ti]])
                if not smp and prefix:
                    tt("dve", cum_bc, PS[b2][:, 64:128], cum_bc, ALU.add, r=[RPS[b2], R_cum], w=[R_cum])
                elif not smp:
                    atg = b0tmp[:, 6, :]
                    tt("dve", atg, at_, cum_bc, ALU.add, r=[R_b0, R_cum], w=[R_b0])
                    act(eatg_all[:, ti, :], atg, AF.Exp, r=[R_b0], w=[R_dt[ti]])
                    tt("dve", cum_bc, PS[b2][:, 64:128], cum_bc, ALU.add, r=[RPS[b2], R_cum], w=[R_cum])
                else:
                    adv = adt_all[:, ti, :].rearrange("p (i two) -> p i two", two=2)
                    for par in range(2):
                        tt("pool", Xeo[:, par, :].rearrange("p (s i) -> p s i", s=16),
                           adv[:, :, par].unsqueeze(1).broadcast_to([128, 16, 32]),
                           M16.unsqueeze(2).broadcast_to([128, 16, 32]), ALU.mult, r=[R_dt[ti]] + RC, w=[R_Xeo])
                    b3 = ipbank()
                    mm(PS[b3][0:64, :], ONES[:, 0:64], Xeo[:, 0, :], r=[R_Xeo] + RC, w=[RPS[b3]])
                    mm(PS[b3][64:128, :], ONES[:, 0:64], Xeo[:, 1, :], r=[R_Xeo] + RC, w=[RPS[b3]])
                    act(E_samp, PS[b3], AF.Exp, r=[RPS[b3]], w=[R_Es])
            A.pop()
            P.barrier()
            CK("b0")

            sc = A.alloc([128, 768], F32)
            R_sc, ds_sc = Res("sc"), DSem("sc")
            xpP = [A.alloc([128, 1027], F32) for _ in range(2)]
            xpS = [A.alloc([128, 16, 11], F32) for _ in range(2)]
            R_xp = [Res("xp0"), Res("xp1")]
            accP2 = [A.alloc([128, 1024], F32) for _ in range(2)]
            accS2 = [A.alloc([128, 16, 8], F32) for _ in range(2)]
            R_accP2, R_accS2 = [Res("accP0"), Res("accP1")], [Res("accS0"), Res("accS1")]
            convT = A.alloc([128, 5, NPT], BF16)
            R_cv = [Res(f"cv{b}") for b in range(5)]
            CTk = [A.alloc([128, NPT], BF16) for _ in range(2)]
            R_CT = [Res("CT0"), Res("CT1")]
            crow = sc
            R_crow, ds_crow = R_sc, DSem("crow")
            xtok = [A.alloc([128, 640], BF16) for _ in range(2)]
            R_xtok = [Res("xtok0"), Res("xtok1")]
            ylocal = A.alloc([128, NTT, 512], F32)
            R_yl = [Res(f"yl{t}") for t in range(NTT)]
            cbm = [A.alloc([128, 128], F32) for _ in range(2)]
            R_cbm = [Res("cbm0"), Res("cbm1")]
            xdt = [A.alloc([128, 512], BF16) for _ in range(2)]
            xdd = [A.alloc([128, 512], BF16) for _ in range(2)]
            R_xd = [Res("xd0"), Res("xd1")]
            Lt = [A.alloc([128, 4, 128], F32) for _ in range(2)]
            R_L = [Res("L0"), Res("L1")]
            Et = [A.alloc([128, 512], F32) for _ in range(2)]
            R_E = [Res("E0"), Res("E1")]
            MT = [A.alloc([128, 4, 128], BF16) for _ in range(2)]
            R_MT = [Res("MT0"), Res("MT1")]
            t12 = [A.alloc([128, 512], F32) for _ in range(2)]
            t3 = [None, None]
            R_t12, R_t3 = [Res("t12a"), Res("t12b")], [Res("t3a"), Res("t3b")]
            STf = [A.alloc([128, 520], F32) for _ in range(1 if prefix else 2)] * (2 if prefix else 1)
            R_STf = [Res("STf0")] * 2 if prefix else [Res("STf0"), Res("STf1")]
            STb = A.alloc([128, 512], BF16)
            R_STb = Res("STb")
            S0nat = [A.alloc([128, 4, 128], F32) for _ in range(2)]
            R_S0, ds_S0 = [Res("S00"), Res("S01")], [DSem("S00"), DSem("S01")]
            S0T = [A.alloc([128, 512], BF16) for _ in range(2)]
            R_S0T = [Res("S0T0"), Res("S0T1")]
            ZC = A.alloc([128, 16, 248], BF16)
            R_ZC = Res("ZC")
            Bm = A.alloc([128, 16, 128], BF16)
            R_Bm = Res("Bm")
            snew = [A.alloc([128, 4, 128], F32)] * 2
            R_sn, ds_sn = [Res("sn0")] * 2, [DSem("sn0")] * 2
            agl = None if prefix else A.alloc([128, 520], F32)
            R_agl, ds_agl = Res("agl"), DSem("agl")
            Sst = None if prefix else A.alloc([128, 512], F32)
            Sstb = None if prefix else A.alloc([128, 512], BF16)
            R_Sst, R_Sstb = Res("Sst"), Res("Sstb")
            cci = A.alloc([128, 4, 8], F32)
            R_cci = Res("cci")
            szs2 = [A.alloc([128, 512], F32) for _ in range(2)]
            yb2 = [A.alloc([128, 512], F32) for _ in range(2)]
            R_szs2, R_yb2 = [Res("szsa"), Res("szsb")], [Res("yba"), Res("ybb")]
            gst2 = [A.alloc([128, 4], F32) for _ in range(2)]
            R_gst2 = [Res("gsta"), Res("gstb")]
            szs, yb = szs2[0], yb2[0]
            gn = [A.alloc([128, 512], BF16) for _ in range(2)]
            R_szs, R_yb, R_gn = Res("szs"), Res("yb"), [Res("gn0"), Res("gn1")]
            gst = A.alloc([128, 4], F32)
            R_gst = Res("gst")
            ssTs = [A.alloc([128, 4, 128], BF16) for _ in range(2)]
            R_ssT, ds_ssT = [Res("ssT0"), Res("ssT1")], [DSem("ssT0"), DSem("ssT1")]
            spst = snew[0]
            R_spst, ds_spst = R_sn[0], DSem("spst")
            R_agin = [Res(f"agin{g}") for g in range(8)]
            R_agout = [Res(f"agout{g}") for g in range(8)]
            ds_agin = DSem("agin")
            R_STd = Res("STd")
            ds_stin = DSem("stin")
            memset("pool", ZC, 0.0, w=[R_ZC])
            hsel = A.alloc([128, 16, 48], BF16)
            R_hsel = Res("hsel")
            cp("pool", hsel.rearrange("p k (s j) -> p k s j", s=16),
               hT[:, :, 1152:1280].rearrange("p k (s t) -> p k s t", s=16)[:, :, :, 5:8], r=[R_hT[9]], w=[R_hsel])
            cnt = {"x": 0, "mt": 0, "s0": 0, "sn": 0, "gn": 0, "ss": 0}

            def bcast_h(ap_h8):
                return ap_h8.unsqueeze(2).broadcast_to([128, 8, 64])

            def v864(ap):
                return ap.rearrange("p (h d) -> p h d", h=8)

            def B1a(g):
                wa, Rwa = load_w([(w_in[:, XS0 + 512 * g:XS0 + 512 * (g + 1)], 0)])
                wb_, Rwb = load_w([(w_in[:, B0 + 128 * g:B0 + 128 * (g + 1)], 0),
                                   (w_in[:, C0 + 128 * g:C0 + 128 * (g + 1)], 128)])
                dma("sp", sc[0:48, 0:512], state_conv[:, 512 * g:512 * (g + 1)], w=[R_sc], dsem=ds_sc)
                dma("sp", sc[0:48, 512:640], state_conv[:, 4096 + 128 * g:4096 + 128 * (g + 1)], dsem=ds_sc)
                dma("sp", sc[0:48, 640:768], state_conv[:, 5120 + 128 * g:5120 + 128 * (g + 1)], dsem=ds_sc)
                R_sc.w = P.all_dma[-1]
                def blk_gen_b(blk):
                    wsl, Rws, c0 = (wa, Rwa, blk * 128) if blk < 4 else (wb_, Rwb, (blk - 4) * 128)
                    cbi = (4 * g + blk) if blk < 4 else (32 + g if blk == 4 else 40 + g)
                    xi = blk % 2
                    xp, xs_, Rxp = xpP[xi], xpS[xi], R_xp[xi]
                    accP, accS, R_aP, R_aS = accP2[xi], accS2[xi], R_accP2[xi], R_accS2[xi]
                    for ri, (t0, n) in enumerate([(0, 512), (512, 512), (1024, 256)]):
                        bank = ipbank()
                        tiles = [R_hT[t] for t in range(t0 // 128, (t0 + n) // 128)]
                        for k in range(16):
                            mm(PS[bank][:, 0:n], wsl[:, k, c0:c0 + 128], hT[:, k, t0:t0 + n],
                               start=(k == 0), stop=(k == 15), r=[Rws] + tiles, w=[RPS[bank]])
                        if ri == 0:
                            cp("act", xp[:, 0:387], PS[bank][:, 125:512], r=[RPS[bank]], w=[Rxp])
                        elif ri == 1:
                            cp("act", xp[:, 387:899], PS[bank][:, 0:512], r=[RPS[bank]], w=[Rxp])
                        else:
                            cp("act", xp[:, 899:1027], PS[bank][:, 0:128], r=[RPS[bank]], w=[Rxp])
                            cp("act", xs_[:, :, 3:11], PS[bank][:, 128:256].rearrange("p (s t) -> p s t", s=16),
                               r=[RPS[bank]], w=[Rxp])
                    bank = ipbank()
                    mm(PS[bank][:, 0:48], sc[0:48, blk * 128:(blk + 1) * 128], ident_f[0:48, 0:48], r=[R_sc] + RC, w=[RPS[bank]])
                    cp("act", xs_[:, :, 0:3], PS[bank][:, 0:48].rearrange("p (s t) -> p s t", s=16), r=[RPS[bank]], w=[Rxp])
                    yield
                    ts("dve", accP, xp[:, 0:1024], cw_pk[:, cbi, 0:1], None, ALU.mult, r=[Rxp] + RC, w=[R_aP])
                    ts("dve", accS, xs_[:, :, 0:8], cw_pk[:, cbi, 0:1], None, ALU.mult, r=[Rxp] + RC, w=[R_aS])
                    yield
                    for j in range(1, 4):
                        stt("dve", accP, xp[:, j:j + 1024], cw_pk[:, cbi, j:j + 1], accP, ALU.mult, ALU.add, r=[Rxp, R_aP] + RC, w=[R_aP])
                        stt("dve", accS, xs_[:, :, j:j + 8], cw_pk[:, cbi, j:j + 1], accS, ALU.mult, ALU.add, r=[Rxp, R_aS] + RC, w=[R_aS])
                        yield
                    if blk < 5:
                        dst, Rd = convT[:, blk, :], R_cv[blk]
                    else:
                        dst, Rd = CTk[g % 2], R_CT[g % 2]
                    act(dst[:, 0:1024], accP, AF.Silu, bias=cb_pk[:, cbi:cbi + 1], r=[R_aP] + RC, w=[Rd])
                    act(dst[:, 1024:1152].rearrange("p (s t) -> p s t", s=16), accS, AF.Silu, bias=cb_pk[:, cbi:cbi + 1],
                        r=[R_aS] + RC, w=[Rd])

                run_interleaved((blk_gen_b(blk) for blk in range(6)), 2)
                for (lhs_of_k, nrow, dst_out, Rt) in (
                        (lambda k: hT[:, k, 1149:1152], 3, cp_out, R_hT[8]),
                        (lambda k: hsel[:, k, :], 48, cs_out, R_hsel)):
                    bank = ipbank()
                    for k in range(16):
                        mm(PS[bank][0:nrow, 0:512], lhs_of_k(k), wa[:, k, 0:512], start=(k == 0), stop=(k == 15),
                           r=[Rwa, Rt], w=[RPS[bank]])
                    cp("dve", crow[0:nrow, 0:512], PS[bank][0:nrow, 0:512], r=[RPS[bank]], w=[R_crow])
                    bank = ipbank()
                    for k in range(16):
                        mm(PS[bank][0:nrow, 0:256], lhs_of_k(k), wb_[:, k, 0:256], start=(k == 0), stop=(k == 15),
                           r=[Rwb, Rt], w=[RPS[bank]])
                    cp("dve", crow[0:nrow, 512:768], PS[bank][0:nrow, 0:256], r=[RPS[bank]], w=[R_crow])
                    dma("sp", dst_out[:, 512 * g:512 * (g + 1)], crow[0:nrow, 0:512], r=[R_crow], dsem=ds_crow, outflag=True)
                    dma("sp", dst_out[:, 4096 + 128 * g:4096 + 128 * (g + 1)], crow[0:nrow, 512:640], r=[R_crow], dsem=ds_crow, outflag=True)
                    dma("sp", dst_out[:, 5120 + 128 * g:5120 + 128 * (g + 1)], crow[0:nrow, 640:768], r=[R_crow], dsem=ds_crow, outflag=True)

            def B1b(g):
                hs = slice(8 * g, 8 * g + 8)
                CT, RCT = CTk[g % 2], R_CT[g % 2]
                STc, RST_ = STf[g % 2], R_STf[g % 2]
                cp("pool", ZC[:, :, 120:128], CT[:, 1024:1152].rearrange("p (s t) -> p s t", s=16), r=[RCT], w=[R_ZC])
                if prefix:
                    dma("sp", STc[:, 0:512], Sst_d[g].ap(), r=[R_SstD[g]], w=[RST_], dsem=ds_stin)
                    cp("act", STb, STc[:, 0:512], r=[RST_], w=[R_STb])
                st_done = {0: True}

                def tile_gen(t):
                    ti = t - 1
                    smp = (t == 9)
                    par = ti % 2
                    bT = 7 if par == 0 else 2
                    bY = 4 if par == 0 else 0
                    bYI = 5 if par == 0 else 1
                    Lt_, R_L_, Et_, R_E_ = Lt[par], R_L[par], Et[par], R_E[par]
                    t12_, R_t12_, t3_, R_t3_ = t12[par], R_t12[par], t3[par], R_t3[par]
                    tsl = slice(ti * 128, (ti + 1) * 128)
                    xi = par
                    xk, Rxk = xtok[xi], R_xtok[xi]
                    for blk in range(5):
                        tr(PSb(bT)[:, blk * 128:(blk + 1) * 128], convT[:, blk, tsl], ident_b, r=[R_cv[blk]] + RC, w=[RPS[bT]])
                    cp("act", xk, PSb(bT)[:, 0:640], r=[RPS[bT]], w=[Rxk])
                    yield
                    mm(PS[bT][:, 384:512], convT[:, 4, tsl], CT[:, tsl], r=[R_cv[4], RCT], w=[RPS[bT]])
                    cb_, Rcb = cbm[ti % 2], R_cbm[ti % 2]
                    tt("dve", cb_, PS[bT][:, 384:512], TRIS if smp else TRI, ALU.mult, r=[RPS[bT]] + RC, w=[Rcb])
                    yield
                    xdt_, xdd_, Rxd = xdt[ti % 2], xdd[ti % 2], R_xd[ti % 2]
                    tt("pool", v864(xdt_), v864(xk[:, 0:512]), bcast_h(dt_all[:, ti, hs]), ALU.mult, r=[Rxk, R_dt[ti]], w=[Rxd])
                    tt("pool", v864(xdd_), v864(xk[:, 0:512]), bcast_h(dtd_all[:, ti, hs]), ALU.mult, r=[Rxk, R_dt[ti]], w=[Rxd])
                    yield
                    for hb in range(2):
                        h0 = 8 * g + 4 * hb
                        tt("pool", Lt_, MGT.unsqueeze(1).broadcast_to([128, 4, 128]),
                           adt_all[:, ti, h0:h0 + 4].unsqueeze(2).broadcast_to([128, 4, 128]), ALU.mult,
                           r=[R_dt[ti]] + RC, w=[R_L_])
                        yield
                        for r_ in range(4):
                            mm(PS[3][:, r_ * 128:(r_ + 1) * 128], Lt_[:, r_, :], TRI, r=[R_L_] + RC, w=[RPS[3]])
                        act(Et_, PS[3], AF.Exp, r=[RPS[3]], w=[R_E_])
                        yield
                        mi = par
                        tt("dve", MT[mi], Et_.rearrange("p (a b) -> p a b", a=4), cb_.unsqueeze(1).broadcast_to([128, 4, 128]),
                           ALU.mult, r=[R_E_, Rcb], w=[R_MT[mi]])
                        yield
                        for r_ in range(4):
                            hh = 4 * hb + r_
                            mm(PS[bY][:, hh * 64:(hh + 1) * 64], MT[mi][:, r_, :], xdt_[:, hh * 64:(hh + 1) * 64],
                               r=[R_MT[mi], Rxd], w=[RPS[bY]])
                        yield
                    have_inter = smp or t >= 2 or prefix
                    while not smp and not st_done.get(ti):
                        yield
                    if smp:
                        s0toks = []
                        for s in range(16):
                            si = cnt["s0"] % 2
                            cnt["s0"] += 1
                            dma("sp", S0nat[si], state_ssm[s, 512 * g:512 * (g + 1), :].rearrange("(q p) n -> p q n", p=128),
                                w=[R_S0[si]], dsem=ds_S0[si])
                            for q in range(4):
                                mm(PS[7][:, q * 128:(q + 1) * 128], S0nat[si][:, q, :], ident_f, r=[R_S0[si]] + RC, w=[RPS[7]])
                            cp("act", S0T[si], PS[7], r=[RPS[7]], w=[R_S0T[si]])
                            mm(PS[bYI], ZC[:, s, 120 - 8 * s:248 - 8 * s], S0T[si], start=(s == 0), stop=(s == 15),
                               r=[R_ZC, R_S0T[si]], w=[RPS[bYI]])
                            if s == 0:
                                tt("pool", Bm, xk[:, 512:640].unsqueeze(1).broadcast_to([128, 16, 128]),
                                   M16.unsqueeze(2).broadcast_to([128, 16, 128]), ALU.mult, r=[Rxk] + RC, w=[R_Bm])
                            for q in range(4):
                                mm(PS[6][:, q * 128:(q + 1) * 128], xdd_[:, q * 128:(q + 1) * 128], Bm[:, s, :],
                                   r=[Rxd, R_Bm], w=[RPS[6]])
                            ni = cnt["sn"] % 2
                            cnt["sn"] += 1
                            ecol = E_samp[:, s * 32 + 4 * g:s * 32 + 4 * g + 4]
                            tt("dve", snew[ni], S0nat[si], ecol.unsqueeze(2).broadcast_to([128, 4, 128]), ALU.mult,
                               r=[R_S0[si], R_Es], w=[R_sn[ni]])
                            tt("dve", snew[ni], snew[ni], PS[6].rearrange("p (q n) -> p q n", q=4), ALU.add,
                               r=[R_sn[ni], RPS[6]], w=[R_sn[ni]])
                            dma("pool", ss_out[s, 512 * g:512 * (g + 1), :].rearrange("(q p) n -> p q n", p=128), snew[ni],
                                r=[R_sn[ni]], dsem=ds_sn[ni], outflag=True)
                            yield
                    elif t >= 2 or prefix:
                        mm(PS[bYI], CT[:, tsl], STb, r=[RCT, R_STb], w=[RPS[bYI]])
                    if have_inter:
                        yield
                        tt("dve", v864(t12_), v864(PS[bYI]), bcast_h(eat_all[:, ti, hs]), ALU.mult, r=[RPS[bYI], R_dt[ti]], w=[R_t12_])
                        tt("dve", t12_, t12_, PS[bY], ALU.add, r=[R_t12_, RPS[bY]], w=[R_t12_])
                    else:
                        cp("dve", t12_, PS[bY], r=[RPS[bY]], w=[R_t12_])
                    tt("pool", v864(ylocal[:, ti, :]), v864(xk[:, 0:512]), bcast_h(dsk_bc[:, hs]), ALU.mult, r=[Rxk] + RC, w=[R_yl[ti]])
                    yield
                    tt("pool", ylocal[:, ti, :], ylocal[:, ti, :], t12_, ALU.add, r=[R_t12_, R_yl[ti]], w=[R_yl[ti]])
                    if not smp:
                        mm(PS[6], xk[:, 512:640], xdd_, r=[Rxk, Rxd], w=[RPS[6]])
                        if t == 1 and not prefix:
                            cp("dve", STc[:, 0:512], PS[6], r=[RPS[6]], w=[RST_])
                        else:
                            tt("dve", v864(STc[:, 0:512]), v864(STc[:, 0:512]), bcast_h(eaend_all[:, ti, hs]), ALU.mult,
                               r=[RST_, R_dt[ti]], w=[RST_])
                            tt("dve", STc[:, 0:512], STc[:, 0:512], PS[6], ALU.add, r=[RST_, RPS[6]], w=[RST_])
                        if t < 8:
                            cp("act", STb, STc[:, 0:512], r=[RST_], w=[R_STb])
                        st_done[t] = True

                run_interleaved((tile_gen(t) for t in range(1, NT)), 2)
                if prefix:
                    for q in range(4):
                        mm(PS[7][:, q * 128:(q + 1) * 128], STc[:, q * 128:(q + 1) * 128], ident_f, r=[RST_] + RC, w=[RPS[7]])
                    cp("act", spst, PS[7].rearrange("p (q n) -> p q n", q=4), r=[RPS[7]], w=[R_spst])
                    dma("sp", sp_out[512 * g:512 * (g + 1), :].rearrange("(q p) n -> p q n", p=128), spst, r=[R_spst],
                        dsem=ds_spst, outflag=True)
                    return
                cp("dve", STc[:, 512:520], cum_bc[:, hs], r=[R_cum], w=[RST_])
                dma("sp", ag_in[g].ap(), STc, r=[RST_], w=[R_agin[g]], dsem=ds_agin)
                dcc = DSem(f"cc{g}", inc=1)
                if nocc:
                    dma("sp", ag_out[g].ap()[0:128, :], ag_in[g].ap(), r=[R_agin[g]], w=[R_agout[g]], dsem=DSem(f"ccx{g}"))
                else:
                  P.add("pool", lambda e, g=g: e.collective_compute(
                    "AllGather", ALU.bypass, replica_groups=[list(range(NCORES))],
                    ins=[ag_in[g].ap().opt()], outs=[ag_out[g].ap().opt()]),
                    r=[R_agin[g]], w=[R_agout[g]], dsem=dcc)

            def B2(g):
                hs = slice(8 * g, 8 * g + 8)
                CT, RCT = CTk[g % 2], R_CT[g % 2]
                STc, RST_ = STf[g % 2], R_STf[g % 2]
                wz, Rwz = load_w([(w_in[:, ZS0 + 512 * g:ZS0 + 512 * (g + 1)], 0)])
                if not prefix:
                    memset("pool", Sst, 0.0, w=[R_Sst])
                for i in range(0 if prefix else NCORES):
                    dma("sp", agl, ag_out[g].ap()[i * 128:(i + 1) * 128, :], r=[R_agout[g]], w=[R_agl], dsem=ds_agl)
                    ei, ci = cci[:, 0, :], cci[:, 1, :]
                    act(ei, agl[:, 512:520], AF.Exp, r=[R_agl], w=[R_cci])
                    ts("dve", ci, ei, -1.0, None, ALU.add, r=[R_cci], w=[R_cci])
                    ts("dve", ci, ci, alpha_bc[:, i:i + 1], 1.0, ALU.mult, ALU.add, r=[R_cci] + RC, w=[R_cci])
                    tt("dve", v864(Sst), v864(Sst), bcast_h(ci), ALU.mult, r=[R_Sst, R_cci], w=[R_Sst])
                    stt("dve", Sst, agl[:, 0:512], alpha_bc[:, i:i + 1], Sst, ALU.mult, ALU.add, r=[R_agl, R_Sst] + RC, w=[R_Sst])
                if not prefix:
                    cp("act", Sstb, Sst, r=[R_Sst], w=[R_Sstb])
                eo = cci[:, 2, :]
                if not prefix:
                    act(eo, STc[:, 512:520], AF.Exp, r=[RST_], w=[R_cci])
                    tt("dve", v864(Sst), v864(Sst), bcast_h(eo), ALU.mult, r=[R_Sst, R_cci, R_Sstb], w=[R_Sst])
                    tt("dve", Sst, Sst, STc[:, 0:512], ALU.add, r=[R_Sst, RST_], w=[R_Sst])
                    for q in range(4):
                        mm(PS[7][:, q * 128:(q + 1) * 128], Sst[:, q * 128:(q + 1) * 128], ident_f, r=[R_Sst] + RC, w=[RPS[7]])
                    cp("act", spst, PS[7].rearrange("p (q n) -> p q n", q=4), r=[RPS[7]], w=[R_spst])
                    dma("sp", sp_out[512 * g:512 * (g + 1), :].rearrange("(q p) n -> p q n", p=128), spst, r=[R_spst],
                        dsem=ds_spst, outflag=True)
                def tile_gen2(t):
                    ti = t - 1
                    par = ti % 2
                    bT = 7 if par == 0 else 2
                    szs, R_szs, yb, R_yb, gst, R_gst = szs2[par], R_szs2[par], yb2[par], R_yb2[par], gst2[par], R_gst2[par]
                    tsl = slice(ti * 128, (ti + 1) * 128)
                    bank = ipbank()
                    for k in range(16):
                        mm(PS[bank], hT[:, k, t * 128:(t + 1) * 128], wz[:, k, 0:512], start=(k == 0), stop=(k == 15),
                           r=[Rwz, R_hT[t]], w=[RPS[bank]])
                    act(szs, PS[bank], AF.Silu, r=[RPS[bank]], w=[R_szs])
                    yield
                    if t <= 8 and not prefix:
                        mm(PS[5], CT[:, tsl], Sstb, r=[RCT, R_Sstb], w=[RPS[5]])
                        tt("dve", v864(yb), v864(PS[5]), bcast_h(eatg_all[:, ti, hs]), ALU.mult, r=[RPS[5], R_dt[ti]], w=[R_yb])
                        tt("pool", yb, yb, ylocal[:, ti, :], ALU.add, r=[R_yb, R_yl[ti]], w=[R_yb])
                        tt("pool", yb, yb, szs, ALU.mult, r=[R_yb, R_szs], w=[R_yb])
                    else:
                        tt("pool", yb, ylocal[:, ti, :], szs, ALU.mult, r=[R_yl[ti], R_szs], w=[R_yb])
                    yield
                    act(szs, yb, AF.Square, accum_out=gst[:, 0:1], r=[R_yb], w=[R_szs, R_gst])
                    yield
                    ts("dve", gst[:, 1:2], gst[:, 0:1], 1.0 / 512.0, EPS, ALU.mult, ALU.add, r=[R_gst], w=[R_gst])
                    yield
                    act(gst[:, 1:2], gst[:, 1:2], AF.Sqrt, r=[R_gst], w=[R_gst])
                    yield
                    P.add("dve", lambda e, o=gst[:, 2:3], i_=gst[:, 1:2]: e.reciprocal(out=o, in_=i_), r=[R_gst], w=[R_gst])
                    yield
                    gi = par
                    ts("dve", gn[gi], yb, gst[:, 2:3], None, ALU.mult, r=[R_yb, R_gst], w=[R_gn[gi]])
                    yield
                    for q in range(4):
                        tr(PSb(bT)[:, q * 128:(q + 1) * 128], gn[gi][:, q * 128:(q + 1) * 128], ident_b, r=[R_gn[gi]] + RC, w=[RPS[bT]])
                    yield
                    oi = cnt["ss"] % 2
                    cnt["ss"] += 1
                    tt("dve", ssTs[oi], PSb(bT)[:, 0:512].rearrange("p (q n) -> p q n", q=4),
                       ssn_pk[:, 4 * g:4 * g + 4].unsqueeze(2).broadcast_to([128, 4, 128]), ALU.mult,
                       r=[RPS[bT]] + RC, w=[R_ssT[oi]])
                    dma("pool", ST_d[4 * g:4 * g + 4, :, tsl].rearrange("b p t -> p b t"), ssTs[oi], r=[R_ssT[oi]], w=[R_STd],
                        dsem=ds_ssT[oi])

                run_interleaved((tile_gen2(t) for t in range(1, NT)), 2)

            for g in range(8):
                B1a(g)
                if g > 0:
                    B2(g - 1)
                B1b(g)
                if g == 0:
                    CK("b_g0")
            B2(7)
            A.pop()
            P.barrier()
            CK("b")
            A.push()
            mergedT = A.alloc([128, 16, NPT], BF16)
            R_mg = [Res(f"mg{t}") for t in range(9)]
            A.push()
            cslots = [wslot[0][:, :, 0:256], wslot[0][:, :, 256:512], wslot[1][:, :, 0:256], wslot[1][:, :, 256:512]]
            cslots += [A.alloc([128, 16, 256], BF16) for _ in range(4)]
            NCS = len(cslots)
            R_cs = [Res(f"cs{i}") for i in range(NCS)]
            ds_cs = [DSem(f"cs{i}") for i in range(NCS)]
            cstate = {"n": 0}

            def load_c(src):
                i = cstate["n"] % NCS
                cstate["n"] += 1
                dma("pool", cslots[i], src.rearrange("(k p) n -> p k n", p=128), w=[R_cs[i]], dsem=ds_cs[i])
                return cslots[i], R_cs[i]

            ATg = [A.alloc([128, 16, 256], BF16) for _ in range(2)]
            STg = [A.alloc([128, 32, 256], BF16) for _ in range(2)]
            R_xg, ds_xg = [Res("xg0"), Res("xg1")], [DSem("xg0"), DSem("xg1")]
            sg = [A.alloc([128, 512], F32) for _ in range(2)]
            R_sg = [Res("sg0"), Res("sg1")]
            mtmp = [A.alloc([128, 512], F32) for _ in range(2)]
            R_mt = [Res("mt0"), Res("mt1")]
            TGS = [(0, 256), (256, 256), (512, 256), (768, 256), (1024, 128)]
            ccnt = {"x": 0, "m": 0}
            for fc in range(8):
                f0 = 256 * fc
                wga, Rga = load_c(w_in[:, GA0 + f0:GA0 + f0 + 256])
                wgs, Rgs = load_c(w_in[:, GS0 + f0:GS0 + f0 + 256])
                wab, Rab = load_c(w_ab[:, f0:f0 + 256])
                ws0, Rs0 = load_c(w_sb[0:2048, f0:f0 + 256])
                ws1, Rs1 = load_c(w_sb[2048:4096, f0:f0 + 256])
                for (o0, n) in TGS:
                    xi = ccnt["x"] % 2
                    ccnt["x"] += 1
                    dma("sp", ATg[xi][:, :, 0:n], AT_d[:, :, o0:o0 + n].rearrange("b p t -> p b t"), r=[R_ATd], w=[R_xg[xi]], dsem=ds_xg[xi])
                    dma("sp", STg[xi][:, :, 0:n], ST_d[:, :, o0:o0 + n].rearrange("b p t -> p b t"), r=[R_STd], dsem=ds_xg[xi])
                    R_xg[xi].w = P.all_dma[-1]
                    htiles = [R_hT[t] for t in range((o0 + 128) // 128, (o0 + 128 + n) // 128)]
                    for sb in range(2):
                        cs_ = slice(sb * 128, (sb + 1) * 128)
                        bG, bP = (2, 3) if ccnt["m"] % 2 == 0 else (4, 5)
                        for k in range(16):
                            mm(PS[bG][:, 0:n], wga[:, k, cs_], hT[:, k, o0 + 128:o0 + 128 + n], start=(k == 0), stop=(k == 15),
                               r=[Rga] + htiles, w=[RPS[bG]])
                        for k in range(16):
                            mm(PS[bG][:, 256:256 + n], wgs[:, k, cs_], hT[:, k, o0 + 128:o0 + 128 + n], start=(k == 0), stop=(k == 15),
                               r=[Rgs] + htiles, w=[RPS[bG]])
                        for k in range(16):
                            mm(PS[bP][:, 0:n], wab[:, k, cs_], ATg[xi][:, k, 0:n], start=(k == 0), stop=(k == 15),
                               r=[Rab, R_xg[xi]], w=[RPS[bP]])
                        for k in range(32):
                            wsx, Rsx = (ws0, Rs0) if k < 16 else (ws1, Rs1)
                            mm(PS[bP][:, 256:256 + n], wsx[:, k % 16, cs_], STg[xi][:, k, 0:n], start=(k == 0), stop=(k == 31),
                               r=[Rsx, R_xg[xi]], w=[RPS[bP]])
                        mi = ccnt["m"] % 2
                        ccnt["m"] += 1
                        act(sg[mi], PS[bG], AF.Sigmoid, r=[RPS[bG]], w=[R_sg[mi]])
                        tt("dve", mtmp[mi], sg[mi], PS[bP], ALU.mult, r=[R_sg[mi], RPS[bP]], w=[R_mt[mi]])
                        mts = [R_mg[t] for t in range(o0 // 128, (o0 + n) // 128)]
                        tt("pool", mergedT[:, 2 * fc + sb, o0:o0 + n], mtmp[mi][:, 0:n], mtmp[mi][:, 256:256 + n], ALU.add,
                           r=[R_mt[mi]], w=mts)
            A.pop()
            P.barrier()
            CK("c")
            hT_flat = hT.rearrange("p a b -> p (a b)")
            wo = [hT_flat[:, 0:8192].rearrange("p (k n) -> p k n", k=16),
                  hT_flat[:, 8192:16384].rearrange("p (k n) -> p k n", k=16), wslot[0], wslot[1]]
            R_wo, ds_wo = [Res(f"wo{i}") for i in range(4)], [DSem(f"wo{i}") for i in range(4)]
            for c in range(4):
                dma("pool", wo[c], w_o[:, 512 * c:512 * (c + 1)].rearrange("(k p) n -> p k n", p=128), w=[R_wo[c]], dsem=ds_wo[c])
            npost_bc = A.alloc([128, D], F32)
            R_np, ds_np = Res("np"), DSem("np")
            dma("sp", npost_bc, norm_post.partition_broadcast(128), w=[R_np], dsem=ds_np)
            o32 = [A.alloc([128, D], F32) for _ in range(2)]
            xr = [A.alloc([128, D], F32) for _ in range(2)]
            R_o32, R_xr = [Res("o0"), Res("o1")], [Res("xr0"), Res("xr1")]
            ds_xr, ds_y = [DSem("xr0"), DSem("xr1")], [DSem("y0"), DSem("y1")]
            dst_ = A.alloc([128, 2, 8], F32)
            R_dst = [Res("dst0"), Res("dst1")]
            djunk = A.alloc([128, 512], BF16)
            R_dj = Res("dj")
            def d_gen(t):
                ti = t - 1
                oi = ti % 2
                bb = 4 if oi == 0 else 0
                dma("sp", xr[oi], xin[t * 128:(t + 1) * 128, :], w=[R_xr[oi]], dsem=ds_xr[oi])
                st_ = dst_[:, oi, :]
                for c in range(4):
                    for k in range(16):
                        mm(PS[bb + c], mergedT[:, k, ti * 128:(ti + 1) * 128], wo[c][:, k, :], start=(k == 0), stop=(k == 15),
                           r=[R_mg[ti], R_wo[c]], w=[RPS[bb + c]])
                    act(djunk, PS[bb + c], AF.Square, accum_out=st_[:, c:c + 1], r=[RPS[bb + c]], w=[R_dj, R_dst[oi]])
                    cp("dve", o32[oi][:, 512 * c:512 * (c + 1)], PS[bb + c], r=[RPS[bb + c]], w=[R_o32[oi]])
                    yield
                tt("dve", st_[:, 4:5], st_[:, 0:1], st_[:, 1:2], ALU.add, r=[R_dst[oi]], w=[R_dst[oi]])
                tt("dve", st_[:, 5:6], st_[:, 2:3], st_[:, 3:4], ALU.add, r=[R_dst[oi]], w=[R_dst[oi]])
                tt("dve", st_[:, 4:5], st_[:, 4:5], st_[:, 5:6], ALU.add, r=[R_dst[oi]], w=[R_dst[oi]])
                ts("dve", st_[:, 4:5], st_[:, 4:5], 1.0 / D, EPS, ALU.mult, ALU.add, r=[R_dst[oi]], w=[R_dst[oi]])
                yield
                act(st_[:, 4:5], st_[:, 4:5], AF.Sqrt, r=[R_dst[oi]], w=[R_dst[oi]])
                P.add("dve", lambda e, o=st_[:, 6:7], i_=st_[:, 4:5]: e.reciprocal(out=o, in_=i_), r=[R_dst[oi]], w=[R_dst[oi]])
                yield
                ts("dve", o32[oi], o32[oi], st_[:, 6:7], None, ALU.mult, r=[R_o32[oi], R_dst[oi]], w=[R_o32[oi]])
                yield
                tt("pool", o32[oi], o32[oi], npost_bc, ALU.mult, r=[R_o32[oi], R_np], w=[R_o32[oi]])
                tt("pool", o32[oi], o32[oi], xr[oi], ALU.add, r=[R_o32[oi], R_xr[oi]], w=[R_o32[oi]])
                dma("sp", y_out[ti * 128:(ti + 1) * 128, :], o32[oi], r=[R_o32[oi]], dsem=ds_y[oi], outflag=True)
            run_interleaved((d_gen(t) for t in range(1, NT)), 2)
            A.pop()
        except _Stop:
            pass
        P.wait_tokens("sp", P.out_toks + P.all_dma)
        sems = [estack.enter_context(nc.semaphore(f"s{i}")) for i in range(len(P.keys))]
        P.emit(nc, sems)
    return nc, P


_NC_CACHE = {}


def make_in_maps(x_prompt, x_sample, cache_k, cache_v, state_ssm, state_conv, norm_pre, w_in, conv_w,
                 conv_b, dt_bias, a_log, d_skip, ssm_norm, attn_sinks, w_attn_br, w_ssm_br, w_out, norm_post):
    f = lambda a: np.ascontiguousarray(np.asarray(a, dtype=np.float32))
    cst_np = _build_consts()
    bp, bs = _alibi_tables()
    shared = {
        "w_in": f(w_in[0]), "w_attn_br": f(w_attn_br[0]), "w_ssm_br": f(w_ssm_br[0]), "w_out": f(w_out[0]),
        "norm_pre": f(norm_pre[0]), "conv_w": f(conv_w[0]), "conv_b": f(conv_b[0]), "dt_bias": f(dt_bias[0]),
        "a_log": f(a_log[0]), "d_skip": f(d_skip[0]), "ssm_norm": f(ssm_norm[0]), "attn_sinks": f(attn_sinks[0]),
        "norm_post": f(norm_post[0]), "cst": cst_np, "bias_p": bp, "bias_s": bs,
    }
    in_maps = []
    for c in range(NCORES):
        b, j = c // 4, c % 4
        xin = np.zeros((TOK, D), np.float32)
        if j > 0:
            xin[0:128] = x_prompt[b, 1024 * j - 128:1024 * j]
        xin[128:1152] = x_prompt[b, 1024 * j:1024 * (j + 1)]
        xin[1152:1280] = np.asarray(x_sample[16 * c:16 * (c + 1)]).reshape(128, D)
        hm = np.zeros((128, 256), np.float32)
        if j == 0:
            hm[:, 0:128] = NEG
        al = np.zeros((8,), np.float32)
        for i in range(4 * b, c):
            al[i] = 1.0
        xprev = np.zeros((3072, D), np.float32)
        pflag = np.zeros((24,), np.float32)
        if j > 0:
            xprev[3072 - 1024 * j:] = x_prompt[b, 0:1024 * j]
            pflag[24 - 8 * j:] = 1.0
        m = dict(shared)
        m.update({
            "xprev": xprev, "pflag": pflag,
            "xin": xin,
            "cache_k": f(np.asarray(cache_k[0, 16 * c:16 * (c + 1)]).reshape(16, 128, 512)),
            "cache_v": f(np.asarray(cache_v[0, 16 * c:16 * (c + 1)]).reshape(16, 128, 512)),
            "state_ssm": f(np.asarray(state_ssm[0, 16 * c:16 * (c + 1)]).reshape(16, 4096, 128)),
            "state_conv": f(np.asarray(state_conv[0, 16 * c:16 * (c + 1)]).reshape(48, 6144)),
            "halo_mask": hm, "alpha": al,
        })
        in_maps.append(m)
    return in_maps


def kernel(x_prompt, x_sample, cache_k, cache_v, state_ssm, state_conv, norm_pre, w_in, conv_w,
           conv_b, dt_bias, a_log, d_skip, ssm_norm, attn_sinks, w_attn_br, w_ssm_br, w_out, norm_post):
    in_maps = make_in_maps(x_prompt, x_sample, cache_k, cache_v, state_ssm, state_conv, norm_pre, w_in, conv_w,
                           conv_b, dt_bias, a_log, d_skip, ssm_norm, attn_sinks, w_attn_br, w_ssm_br, w_out, norm_post)
    if "nc" not in _NC_CACHE:
        _NC_CACHE["nc"] = build_nc()[0]
    nc = _NC_CACHE["nc"]
    res = run_bass_kernel_spmd(nc, in_maps, core_ids=list(range(NCORES)))
    return assemble(res.results)


def assemble(r):
    f32 = np.float32
    y_prompt = np.zeros((2, 4096, D), f32)
    y_sample = np.zeros((128, 8, D), f32)
    k_p = np.zeros((1, 2, 128, 8, 64), f32)
    v_p = np.zeros((1, 2, 128, 8, 64), f32)
    s_p = np.zeros((1, 2, 64, 64, 128), f32)
    c_p = np.zeros((1, 2, 3, 6144), f32)
    k_s = np.zeros((1, 128, 128, 8, 64), f32)
    v_s = np.zeros((1, 128, 128, 8, 64), f32)
    s_s = np.zeros((1, 128, 64, 64, 128), f32)
    c_s = np.zeros((1, 128, 3, 6144), f32)
    for c in range(NCORES):
        b, j = c // 4, c % 4
        o = r[c]
        y = np.asarray(o["y"], f32)
        y_prompt[b, 1024 * j:1024 * (j + 1)] = y[0:1024]
        y_sample[16 * c:16 * (c + 1)] = y[1024:1152].reshape(16, 8, D)
        k_s[0, 16 * c:16 * (c + 1)] = np.asarray(o["ks"], f32).reshape(16, 128, 8, 64)
        v_s[0, 16 * c:16 * (c + 1)] = np.asarray(o["vs"], f32).reshape(16, 128, 8, 64)
        s_s[0, 16 * c:16 * (c + 1)] = np.asarray(o["ss"], f32).reshape(16, 64, 64, 128)
        c_s[0, 16 * c:16 * (c + 1)] = np.asarray(o["cs"], f32).reshape(16, 3, 6144)
        if j == 3:
            k_p[0, b] = np.asarray(o["kp"], f32).reshape(128, 8, 64)
            v_p[0, b] = np.asarray(o["vp"], f32).reshape(128, 8, 64)
            s_p[0, b] = np.asarray(o["sp"], f32).reshape(64, 64, 128)
            c_p[0, b] = np.asarray(o["cp"], f32)
    return (y_prompt, y_sample, k_p, v_p, s_p, c_p, k_s, v_s, s_s, c_s)
```
